# Optimizing a Trainium2 kernel written in Bass

```python
import math
import jax
import jax.numpy as jnp
from jax import lax
import numpy as np

D_MODEL = 1024
BATCH = 8
SEQ = 4096
DEPTH = 2

GRID_W = 64
CTX_LEN = 256
EPS = 1e-6
ROPE_BASE = 10000.0
NEG_INF = -1e30
Q_BLOCK = 128

MLA_HEADS = 8
MLA_NOPE = 64
MLA_ROPE = 32
MLA_V = 64
MLA_QK = MLA_NOPE + MLA_ROPE
MLA_Q_LORA = 384
MLA_KV_LORA = 256

S5_WIDTH = 512
S5_GROUP = 16
S5_GROUPS = S5_WIDTH // S5_GROUP
S5_STATE = 64
S5_DT_MIN = 1e-3
S5_DT_MAX = 1e-1
S5_MAX_RE = -1e-4

SWA_HEADS = 8
SWA_KV_HEADS = 2
SWA_GROUP = SWA_HEADS // SWA_KV_HEADS
SWA_HEAD_DIM = 64
SWA_WINDOW = 128
SWA_BLOCK = 128
SWA_SPAN = SWA_BLOCK + 2 * SWA_WINDOW

N_BRANCH = 3
BRANCH_WIDTH = 512
FFN_HIDDEN = ((8 * D_MODEL + 3 * 256 - 1) // (3 * 256)) * 256

IN_SIZES = (MLA_Q_LORA, MLA_KV_LORA, MLA_ROPE, S5_WIDTH, SWA_HEADS * SWA_HEAD_DIM,
            2 * SWA_KV_HEADS * SWA_HEAD_DIM, N_BRANCH * D_MODEL)
IN_SPLITS = tuple(int(s) for s in np.cumsum(IN_SIZES)[:-1])
D_IN = sum(IN_SIZES)

kernel_name = 'hybrid_mla_s5_swa_dit_block'


def rms_norm(x, g):
    xf = x.astype(jnp.float32)
    y = xf * lax.rsqrt(jnp.mean(xf * xf, axis=-1, keepdims=True) + EPS)
    return (y * g.astype(jnp.float32)).astype(x.dtype)


def modulate(h, shift, scale):
    return h * (1 + scale[:, None]) + shift[:, None]


def axial_rope_angles(row, col, rot_dim):
    n_axis = rot_dim // 4
    freqs = ROPE_BASE ** (-jnp.arange(n_axis, dtype=jnp.float32) / n_axis)
    return jnp.concatenate([row[:, None] * freqs, col[:, None] * freqs], axis=-1)


def apply_rope(x, ang):
    cos = jnp.cos(ang)[:, None, :]
    sin = jnp.sin(ang)[:, None, :]
    xf = x.astype(jnp.float32)
    half = x.shape[-1] // 2
    x1, x2 = xf[..., :half], xf[..., half:]
    return jnp.concatenate([x1 * cos - x2 * sin, x2 * cos + x1 * sin], axis=-1).astype(x.dtype)


def full_attention(q, k, v):
    s = jnp.einsum('bqhd,bkhd->bhqk', q, k).astype(jnp.float32) * (q.shape[-1] ** -0.5)
    p = jax.nn.softmax(s, axis=-1).astype(v.dtype)
    return jnp.einsum('bhqk,bkhv->bqhv', p, v)


def blocked_dense_attention(q, k, v):
    b, n, h, d = q.shape
    nb = n // Q_BLOCK
    qb = jnp.moveaxis(q.reshape(b, nb, Q_BLOCK, h, d), 1, 0)
    out = lax.map(lambda qblk: full_attention(qblk, k, v), qb)
    return jnp.moveaxis(out, 0, 1).reshape(b, n, h * v.shape[-1])


def mla_project(q_in, kv_in, kr_in, qa_g, kva_g, w_uq, w_ukv, qk_g, ang):
    b, n, _ = q_in.shape
    q = (rms_norm(q_in, qa_g) @ w_uq).reshape(b, n, MLA_HEADS, MLA_QK)
    kv = (rms_norm(kv_in, kva_g) @ w_ukv).reshape(b, n, MLA_HEADS, MLA_NOPE + MLA_V)
    k_nope, v = kv[..., :MLA_NOPE], kv[..., MLA_NOPE:]
    k_pe = jnp.broadcast_to(kr_in[:, :, None, :], (b, n, MLA_HEADS, MLA_ROPE))
    k = jnp.concatenate([k_nope, k_pe], axis=-1)
    q = rms_norm(q, qk_g[0])
    k = rms_norm(k, qk_g[1])
    if ang is not None:
        q = jnp.concatenate([q[..., :MLA_NOPE], apply_rope(q[..., MLA_NOPE:], ang)], axis=-1)
        k = jnp.concatenate([k[..., :MLA_NOPE], apply_rope(k[..., MLA_NOPE:], ang)], axis=-1)
    return q, k, v


def swa_project(q_in, kv_in, qk_g, ang):
    b, n, _ = q_in.shape
    q = rms_norm(q_in.reshape(b, n, SWA_HEADS, SWA_HEAD_DIM), qk_g[0])
    k, v = jnp.split(kv_in.reshape(b, n, 2 * SWA_KV_HEADS, SWA_HEAD_DIM), 2, axis=2)
    k = rms_norm(k, qk_g[1])
    if ang is not None:
        q = apply_rope(q, ang)
        k = apply_rope(k, ang)
    return q.reshape(b, n, SWA_KV_HEADS, SWA_GROUP, SWA_HEAD_DIM), k, v


def sink_softmax(parts, sink):
    s0 = parts[0]
    s_sink = jnp.broadcast_to(sink.astype(jnp.float32).reshape(SWA_KV_HEADS, SWA_GROUP)[None, :, :, None, None],
                              s0.shape[:-1] + (1,))
    p = jax.nn.softmax(jnp.concatenate([s_sink] + parts, axis=-1), axis=-1)
    return p[..., 1:]


def swa_latent(q, k, v, k_ctx, v_ctx, sink):
    b, n = q.shape[0], q.shape[1]
    nb = n // SWA_BLOCK
    n_ctx = k_ctx.shape[1]
    scale = SWA_HEAD_DIM ** -0.5
    qb = jnp.moveaxis(q.reshape(b, nb, SWA_BLOCK, SWA_KV_HEADS, SWA_GROUP, SWA_HEAD_DIM), 1, 0)
    pad = ((0, 0), (SWA_WINDOW, SWA_WINDOW), (0, 0), (0, 0))
    kpad = jnp.pad(k, pad)
    vpad = jnp.pad(v, pad)

    def one(args):
        qblk, idx = args
        start = idx * SWA_BLOCK
        kb = lax.dynamic_slice_in_dim(kpad, start, SWA_SPAN, axis=1)
        vb = lax.dynamic_slice_in_dim(vpad, start, SWA_SPAN, axis=1)
        s_band = jnp.einsum('bqkgd,bjkd->bkgqj', qblk, kb).astype(jnp.float32) * scale
        qpos = start + jnp.arange(SWA_BLOCK)
        kpos = start - SWA_WINDOW + jnp.arange(SWA_SPAN)
        valid = (jnp.abs(qpos[:, None] - kpos[None, :]) <= SWA_WINDOW) & (kpos[None, :] >= 0) & (kpos[None, :] < n)
        s_band = jnp.where(valid, s_band, NEG_INF)
        s_ctx = jnp.einsum('bqkgd,bjkd->bkgqj', qblk, k_ctx).astype(jnp.float32) * scale
        p = sink_softmax([s_ctx, s_band], sink)
        p_ctx = p[..., :n_ctx].astype(v.dtype)
        p_band = p[..., n_ctx:].astype(v.dtype)
        return (jnp.einsum('bkgqj,bjkd->bqkgd', p_band, vb)
                + jnp.einsum('bkgqj,bjkd->bqkgd', p_ctx, v_ctx))

    out = lax.map(one, (qb, jnp.arange(nb)))
    return jnp.moveaxis(out, 0, 1).reshape(b, n, SWA_HEADS * SWA_HEAD_DIM)


def swa_context(q, k, v, sink):
    b, n = q.shape[0], q.shape[1]
    s = jnp.einsum('bqkgd,bjkd->bkgqj', q, k).astype(jnp.float32) * (SWA_HEAD_DIM ** -0.5)
    p = sink_softmax([s], sink).astype(v.dtype)
    return jnp.einsum('bkgqj,bjkd->bqkgd', p, v).reshape(b, n, SWA_HEADS * SWA_HEAD_DIM)


def s5_params(lam_re, lam_im, log_step, b_re, b_im, c_re, c_im):
    lam = lax.complex(jnp.minimum(lam_re.astype(jnp.float32), S5_MAX_RE), lam_im.astype(jnp.float32))
    dt = jnp.exp(log_step.astype(jnp.float32))[:, None]
    a_bar = jnp.exp(lam * dt)
    b_mat = lax.complex(b_re.astype(jnp.float32), b_im.astype(jnp.float32))
    b_bar = ((a_bar - 1.0) / lam)[..., None] * b_mat
    c_mat = lax.complex(c_re.astype(jnp.float32), c_im.astype(jnp.float32))
    return a_bar, b_bar, c_mat


def s5_combine(e1, e2):
    a1, b1 = e1
    a2, b2 = e2
    return a1 * a2, a2 * b1 + b2


def s5_scan(u, a_bar, b_bar, h0, reverse):
    n = u.shape[1]
    bu = jnp.einsum('bngi,gpi->bngp', u.astype(jnp.complex64), b_bar)
    a = jnp.broadcast_to(a_bar[None, None], (1, n) + a_bar.shape)
    a_cum, h = lax.associative_scan(s5_combine, (a, bu), axis=1, reverse=reverse)
    if h0 is not None:
        h = h + a_cum * h0[:, None]
    return h


def s5_readout(h_f, h_b, u, c_f, c_b, d, w_glu, b_glu, dtype):
    b, n = u.shape[0], u.shape[1]
    y = jnp.real(jnp.einsum('bngp,gip->bngi', h_f, c_f) + jnp.einsum('bngp,gip->bngi', h_b, c_b))
    y = y + d.astype(jnp.float32).reshape(S5_GROUPS, S5_GROUP) * u
    y = jax.nn.gelu(y.reshape(b, n, S5_WIDTH).astype(dtype))
    return y * jax.nn.sigmoid(y @ w_glu + b_glu)


def merge_branches(mla_o, s5_o, swa_o, gate_in, w_branch, w_out):
    b, n, _ = mla_o.shape
    branches = jnp.stack([mla_o, s5_o, swa_o], axis=0)
    proj = jnp.einsum('rbnc,rcd->rbnd', branches, w_branch)
    gates = jax.nn.sigmoid(jnp.moveaxis(gate_in.reshape(b, n, N_BRANCH, D_MODEL), 2, 0))
    return jnp.sum(gates * proj, axis=0) @ w_out


def swiglu(h, w_in, w_out):
    a, g = jnp.split(h @ w_in, 2, axis=-1)
    return (jax.nn.silu(a) * g) @ w_out


def setup_inputs(seed: int = 0):
    key = jax.random.key(seed)
    ks = jax.random.split(key, 32)
    f32 = jnp.float32

    def nrm(k, shape, scale):
        return jax.random.normal(k, shape, f32) * scale

    def gain(k, shape):
        return 1.0 + 0.05 * jax.random.normal(k, shape, f32)

    L = DEPTH
    n_idx = jnp.arange(S5_STATE, dtype=f32)
    return {
        'x': nrm(ks[0], (BATCH, SEQ, D_MODEL), 1.0),
        'c': nrm(ks[1], (BATCH, D_MODEL), 1.0),
        'ctx': nrm(ks[2], (BATCH, CTX_LEN, D_MODEL), 1.0),
        'c_ctx': nrm(ks[3], (D_MODEL,), 1.0),
        'w_ada': nrm(ks[4], (L, D_MODEL, 6 * D_MODEL), 0.5 * D_MODEL ** -0.5),
        'b_ada': nrm(ks[5], (L, 6 * D_MODEL), 0.02),
        'norm1_g': gain(ks[6], (L, D_MODEL)),
        'norm2_g': gain(ks[7], (L, D_MODEL)),
        'w_in': nrm(ks[8], (L, D_MODEL, D_IN), D_MODEL ** -0.5),
        'mla_qa_g': gain(ks[9], (L, MLA_Q_LORA)),
        'mla_kva_g': gain(ks[10], (L, MLA_KV_LORA)),
        'mla_w_uq': nrm(ks[11], (L, MLA_Q_LORA, MLA_HEADS * MLA_QK), MLA_Q_LORA ** -0.5),
        'mla_w_ukv': nrm(ks[12], (L, MLA_KV_LORA, MLA_HEADS * (MLA_NOPE + MLA_V)), MLA_KV_LORA ** -0.5),
        'mla_qk_g': gain(ks[13], (L, 2, MLA_QK)),
        's5_lambda_re': -0.5 + nrm(ks[14], (L, 2, S5_GROUPS, S5_STATE), 0.01),
        's5_lambda_im': jnp.pi * n_idx + nrm(ks[15], (L, 2, S5_GROUPS, S5_STATE), 0.01),
        's5_log_step': jax.random.uniform(ks[16], (L, 2, S5_GROUPS), f32, math.log(S5_DT_MIN), math.log(S5_DT_MAX)),
        's5_b_re': nrm(ks[17], (L, 2, S5_GROUPS, S5_STATE, S5_GROUP), (2 * S5_GROUP) ** -0.5),
        's5_b_im': nrm(ks[18], (L, 2, S5_GROUPS, S5_STATE, S5_GROUP), (2 * S5_GROUP) ** -0.5),
        's5_c_re': nrm(ks[19], (L, 2, S5_GROUPS, S5_GROUP, S5_STATE), 0.5),
        's5_c_im': nrm(ks[20], (L, 2, S5_GROUPS, S5_GROUP, S5_STATE), 0.5),
        's5_d': nrm(ks[21], (L, S5_WIDTH), 1.0),
        's5_w_glu': nrm(ks[22], (L, S5_WIDTH, S5_WIDTH), S5_WIDTH ** -0.5),
        's5_b_glu': nrm(ks[23], (L, S5_WIDTH), 0.02),
        'swa_qk_g': gain(ks[24], (L, 2, SWA_HEAD_DIM)),
        'swa_sink': nrm(ks[25], (L, SWA_HEADS), 1.0),
        'w_branch': nrm(ks[26], (L, N_BRANCH, BRANCH_WIDTH, D_MODEL), BRANCH_WIDTH ** -0.5),
        'w_out': nrm(ks[27], (L, D_MODEL, D_MODEL), D_MODEL ** -0.5),
        'ffn_w_in': nrm(ks[28], (L, D_MODEL, 2 * FFN_HIDDEN), D_MODEL ** -0.5),
        'ffn_w_out': nrm(ks[29], (L, FFN_HIDDEN, D_MODEL), FFN_HIDDEN ** -0.5),
    }


def reference(x, c, ctx, c_ctx, w_ada, b_ada, norm1_g, norm2_g, w_in,
              mla_qa_g, mla_kva_g, mla_w_uq, mla_w_ukv, mla_qk_g,
              s5_lambda_re, s5_lambda_im, s5_log_step, s5_b_re, s5_b_im,
              s5_c_re, s5_c_im, s5_d, s5_w_glu, s5_b_glu,
              swa_qk_g, swa_sink, w_branch, w_out, ffn_w_in, ffn_w_out):
    b, n_lat, _ = x.shape
    rows = n_lat // GRID_W
    row = jnp.repeat(jnp.arange(rows, dtype=jnp.float32), GRID_W)
    col = jnp.tile(jnp.arange(GRID_W, dtype=jnp.float32), rows)
    ang_mla = axial_rope_angles(row, col, MLA_ROPE)
    ang_swa = axial_rope_angles(row, col, SWA_HEAD_DIM)
    sc_x = jax.nn.silu(c)
    sc_c = jax.nn.silu(c_ctx)[None]
    h_x, h_c = x, ctx
    for l in range(DEPTH):
        need_ctx_out = l < DEPTH - 1
        sh1x, sc1x, g1x, sh2x, sc2x, g2x = jnp.split(sc_x @ w_ada[l] + b_ada[l], 6, axis=-1)
        sh1c, sc1c, g1c, sh2c, sc2c, g2c = jnp.split(sc_c @ w_ada[l] + b_ada[l], 6, axis=-1)

        hx = modulate(rms_norm(h_x, norm1_g[l]), sh1x, sc1x)
        hc = modulate(rms_norm(h_c, norm1_g[l]), sh1c, sc1c)
        px = jnp.split(hx @ w_in[l], IN_SPLITS, axis=-1)
        pc = jnp.split(hc @ w_in[l], IN_SPLITS, axis=-1)

        mq_x, mk_x, mv_x = mla_project(px[0], px[1], px[2], mla_qa_g[l], mla_kva_g[l],
                                       mla_w_uq[l], mla_w_ukv[l], mla_qk_g[l], ang_mla)
        mq_c, mk_c, mv_c = mla_project(pc[0], pc[1], pc[2], mla_qa_g[l], mla_kva_g[l],
                                       mla_w_uq[l], mla_w_ukv[l], mla_qk_g[l], None)
        k_all = jnp.concatenate([mk_c, mk_x], axis=1)
        v_all = jnp.concatenate([mv_c, mv_x], axis=1)
        mla_x = blocked_dense_attention(mq_x, k_all, v_all)

        a_f, bb_f, cc_f = s5_params(s5_lambda_re[l, 0], s5_lambda_im[l, 0], s5_log_step[l, 0],
                                    s5_b_re[l, 0], s5_b_im[l, 0], s5_c_re[l, 0], s5_c_im[l, 0])
        a_b, bb_b, cc_b = s5_params(s5_lambda_re[l, 1], s5_lambda_im[l, 1], s5_log_step[l, 1],
                                    s5_b_re[l, 1], s5_b_im[l, 1], s5_c_re[l, 1], s5_c_im[l, 1])
        u_x = px[3].astype(jnp.float32).reshape(b, n_lat, S5_GROUPS, S5_GROUP)
        u_c = pc[3].astype(jnp.float32).reshape(b, pc[3].shape[1], S5_GROUPS, S5_GROUP)
        st_cf = s5_scan(u_c, a_f, bb_f, None, False)
        st_cb = s5_scan(u_c, a_b, bb_b, None, True)
        st_xf = s5_scan(u_x, a_f, bb_f, st_cf[:, -1], False)
        st_xb = s5_scan(u_x, a_b, bb_b, st_cb[:, 0], True)
        s5_x = s5_readout(st_xf, st_xb, u_x, cc_f, cc_b, s5_d[l], s5_w_glu[l], s5_b_glu[l], h_x.dtype)

        sq_x, sk_x, sv_x = swa_project(px[4], px[5], swa_qk_g[l], ang_swa)
        sq_c, sk_c, sv_c = swa_project(pc[4], pc[5], swa_qk_g[l], None)
        swa_x = swa_latent(sq_x, sk_x, sv_x, sk_c, sv_c, swa_sink[l])

        mix_x = merge_branches(mla_x, s5_x, swa_x, px[6], w_branch[l], w_out[l])
        h_x = h_x + g1x[:, None] * mix_x
        if need_ctx_out:
            mla_c = full_attention(mq_c, mk_c, mv_c).reshape(b, pc[0].shape[1], MLA_HEADS * MLA_V)
            s5_c = s5_readout(st_cf, st_cb, u_c, cc_f, cc_b, s5_d[l], s5_w_glu[l], s5_b_glu[l], h_c.dtype)
            swa_c = swa_context(sq_c, sk_c, sv_c, swa_sink[l])
            mix_c = merge_branches(mla_c, s5_c, swa_c, pc[6], w_branch[l], w_out[l])
            h_c = h_c + g1c[:, None] * mix_c

        fx = modulate(rms_norm(h_x, norm2_g[l]), sh2x, sc2x)
        h_x = h_x + g2x[:, None] * swiglu(fx, ffn_w_in[l], ffn_w_out[l])
        if need_ctx_out:
            fc = modulate(rms_norm(h_c, norm2_g[l]), sh2c, sc2c)
            h_c = h_c + g2c[:, None] * swiglu(fc, ffn_w_in[l], ffn_w_out[l])
    return h_x
```

```python
import contextlib
import math
import numpy as np
import concourse.bass as bass
import concourse.mybir as mybir
from concourse.bass_utils import run_bass_kernel_spmd

F32 = mybir.dt.float32
BF16 = mybir.dt.bfloat16
I32 = mybir.dt.int32
AF = mybir.ActivationFunctionType
ALU = mybir.AluOpType

D = 1024
NX = 4096
NC_ = 256
T = NX + NC_
L = 2
EPS = 1e-6
FH = 2816
DIN = 5024
O_KR = 640
O_U = 672
O_SQ = 1184
O_SK = 1696
O_SV = 1824
O_G = 1952
import os as _os
D_UENG = _os.environ.get("UENG", "dve,dve,dve,dve").split(",")
D_XENG = _os.environ.get("XENG", "pool")
D_TENG = _os.environ.get("TENG", "dve,dve,dve,dve").split(",")
D_EVAC = _os.environ.get("D_EVAC", "0") == "1"
OVERLAP = _os.environ.get("OVERLAP", "0") == "1"
OV_K = int(_os.environ.get("OV_K", "9"))
TWO_PI = 2.0 * math.pi

STS = [(0, 0, 256, True)] + [(1 + k, 256 + 512 * k, 512, False) for k in range(8)]


class Buf:
    __slots__ = ("name", "w", "r")

    def __init__(self, name):
        self.name = name
        self.w = {}
        self.r = {}


def _merge(d, s):
    for k, v in s.items():
        if d.get(k, 0) < v:
            d[k] = v


class Sched:
    def __init__(self, nc, stack):
        self.nc = nc
        self.stack = stack
        self.E = {"pe": nc.tensor, "act": nc.scalar, "dve": nc.vector,
                  "pool": nc.gpsimd, "sp": nc.sync}
        self.semh = {}
        self.cnt = {}
        self.seen = {k: {} for k in self.E}
        for k in self.E:
            self.semh[k] = stack.enter_context(nc.semaphore("s_" + k))
            self.cnt[k] = 0
        self.nins = 0
        self.nwait = 0
        self.alias = {}
        self.free = []
        self.dkeys = []

    def _sem(self, key):
        if key in self.alias:
            return self.alias[key]
        if self.free:
            ck = self.free.pop()
        else:
            ck = "dq%d" % len(self.dkeys)
            self.dkeys.append(ck)
            self.semh[ck] = self.stack.enter_context(self.nc.semaphore(ck))
            self.cnt[ck] = 0
        self.alias[key] = ck
        return ck

    def _wait(self, eng, deps):
        seen = self.seen[eng]
        for key, val in deps.items():
            if val <= 0 or seen.get(key, 0) >= val:
                continue
            self.E[eng].wait_ge(self.semh[key], val)
            seen[key] = val
            self.nwait += 1

    def _deps(self, reads, writes):
        deps = {}
        for b in reads:
            _merge(deps, b.w)
        for b in writes:
            _merge(deps, b.w)
            _merge(deps, b.r)
        return deps

    def op(self, eng, fn, reads=(), writes=()):
        deps = self._deps(reads, writes)
        if eng == "pe":
            deps.pop("pe", None)
        self._wait(eng, deps)
        ins = fn(self.E[eng])
        self.cnt[eng] += 1
        v = self.cnt[eng]
        ins.then_inc(self.semh[eng], 1)
        for b in reads:
            if b.r.get(eng, 0) < v:
                b.r[eng] = v
        for b in writes:
            b.w = {eng: v}
            b.r = {}
        self.nins += 1
        return ins

    def dma(self, out, in_, reads=(), writes=(), key=None, eng="sp"):
        key = self._sem(key)
        deps = self._deps(reads, writes)
        deps.pop(key, None)
        self._wait(eng, deps)
        ins = self.E[eng].dma_start(out=out, in_=in_)
        self.cnt[key] += 16
        v = self.cnt[key]
        ins.then_inc(self.semh[key], 16)
        for b in reads:
            if b.r.get(key, 0) < v:
                b.r[key] = v
        for b in writes:
            b.w = {key: v}
            b.r = {}
        self.nins += 1
        return ins

    def barrier(self):
        deps = {k: v for k, v in self.cnt.items() if v > 0}
        for e in self.E:
            self._wait(e, deps)
        self.alias = {}
        self.free = list(self.dkeys)


class TT:
    __slots__ = ("t", "b")

    def __init__(self, t, b):
        self.t = t
        self.b = b


class Ctx:
    pass


def build(dbg=False, nlayers=L, stop_after=None, phases="aABCDEF"):
    nc = bass.Bass("TRN2", target_bir_lowering=False)
    g = Ctx()
    g.nc = nc

    def din(name, shape, dt=F32):
        return nc.dram_tensor(name, list(shape), dt, kind="ExternalInput").ap()

    dbg_names = []

    def dscr(name, shape, dt):
        if dbg:
            dbg_names.append(name)
            return nc.dram_tensor(name, list(shape), dt, kind="ExternalOutput").ap()
        return nc.dram_tensor(name, list(shape), dt).ap()

    xT = din("xT", [D, T])
    ccol = din("ccol", [128, 8, 2])
    w_ada = din("w_ada", [L, D, 6 * D])
    badac = din("badac", [L, 128, 48])
    g1c = din("g1c", [L, 128, 8])
    g2c = din("g2c", [L, 128, 8])
    w_in = din("w_in", [L, D, DIN])
    qagc = din("qagc", [L, 128, 3])
    kvagc = din("kvagc", [L, 128, 2])
    w_uq = din("w_uq", [L, 384, 768])
    w_ukv = din("w_ukv", [L, 256, 1024])
    mqkg = din("mqkg", [L, 96, 2])
    swag = din("swag", [L, 128, 2])
    sinkb = din("sinkb", [L, 128, 8])
    s5lre = din("s5lre", [L, 128, 2, 16])
    s5lim = din("s5lim", [L, 128, 2, 16])
    s5ls = din("s5ls", [L, 128, 2, 16])
    s5bre = din("s5bre", [L, 128, 2, 16, 16])
    s5bim = din("s5bim", [L, 128, 2, 16, 16])
    s5cre = din("s5cre", [L, 128, 2, 16, 16])
    s5cim = din("s5cim", [L, 128, 2, 16, 16])
    s5dc = din("s5dc", [L, 128, 4])
    s5bgc = din("s5bgc", [L, 128, 4])
    w_glu = din("w_glu", [L, 512, 512])
    w_br = din("w_br", [L, 3, 512, D])
    w_o = din("w_o", [L, D, D])
    f_in = din("f_in", [L, D, 2 * FH])
    f_out = din("f_out", [L, FH, D])
    cMc = din("cMc", [96, NX]); cMs = din("cMs", [96, NX])
    cSc = din("cSc", [128, NX]); cSs = din("cSs", [128, NX])
    cPm = din("cPm", [96, 96]); cPs = din("cPs", [128, 128])
    cShift = din("cShift", [32, 96])
    cMge = din("cMge", [128, 128]); cMle = din("cMle", [128, 128])
    cBones = din("cBones", [128, 128])
    cIota = din("cIota", [128, 512])
    cId = din("cId", [128, 128])

    outT = nc.dram_tensor("outT", [D, NX], F32, kind="ExternalOutput").ap()

    hres = [dscr("hres%d" % i, [D, T], F32) for i in range(2)]
    hTs = dscr("hTs", [128, 8, T], BF16)
    QTs = dscr("QTs", [8, 96, T], BF16)
    KTs = dscr("KTs", [8, 96, T], BF16)
    VsM = dscr("VsM", [34, 128, 8, 128], BF16)
    uTs = dscr("uTs", [128, 4, T], BF16)
    sQs = dscr("sQs", [64, 8, T], BF16)
    sKs = dscr("sKs", [64, 2, T], BF16)
    sVs = dscr("sVs", [34, 128, 2, 128], BF16)
    mlaO = dscr("mlaO", [512, T], BF16)
    swaO = dscr("swaO", [512, T], BF16)
    s5O = dscr("s5O", [128, 4, T], BF16)
    ygs = dscr("ygs", [128, 4, T], BF16)
    fTs = dscr("fTs", [128, 8, T], BF16)
    actTs = dscr("actTs", [128, 22, T], BF16)

    dbufs = {}

    def db(name, idx=0):
        k = (name, idx)
        if k not in dbufs:
            dbufs[k] = Buf("%s_%s" % (name, idx))
        return dbufs[k]

    with contextlib.ExitStack() as top:
        S = Sched(nc, top)
        uid = [0]

        def sbt(ctx, name, shape, dt):
            uid[0] += 1
            nm = "%s_%d" % (name, uid[0])
            t = ctx.enter_context(nc.sbuf_tensor(nm, list(shape), dt))
            return TT(t, Buf(nm))

        def pst(ctx, name, shape=(128, 512), dt=F32):
            uid[0] += 1
            nm = "%s_%d" % (name, uid[0])
            t = ctx.enter_context(nc.psum_tensor(nm, list(shape), dt))
            return TT(t, Buf(nm))

        ones32 = sbt(top, "ones32", [128, 128], F32)
        S.op("pool", lambda e: e.memset(ones32.t[:], 1.0), writes=[ones32.b])
        bones32 = sbt(top, "bones32", [128, 128], F32)
        S.dma(bones32.t[:], cBones[:, :], writes=[bones32.b], key="c_bones")
        stgc = sbt(top, "stgc", [128, 128], F32)

        def const_bf(name, src, rows, cols):
            t = sbt(top, name, [rows, cols], BF16)
            S.dma(stgc.t[0:rows, 0:cols], src[:, :], writes=[stgc.b], key="c_stg")
            S.op("dve", lambda e: e.tensor_copy(out=t.t[:], in_=stgc.t[0:rows, 0:cols]),
                 reads=[stgc.b], writes=[t.b])
            return t

        Pm = const_bf("Pm", cPm, 96, 96)
        Ps = const_bf("Ps", cPs, 128, 128)
        shiftI = const_bf("shiftI", cShift, 32, 96)
        Mge = const_bf("Mge", cMge, 128, 128)
        Mle = const_bf("Mle", cMle, 128, 128)
        identb = const_bf("identb", cId, 128, 128)

        stg = [sbt(top, "stg%d" % i, [128, 2048], F32) for i in range(2)]
        stg_i = [0]

        def load_cast(dst_ap_fn, src3, A, B, scale_fn=None, eng="pool"):
            step = 1 if scale_fn is not None else max(1, 2048 // B)
            a0 = 0
            while a0 < A:
                a1 = min(A, a0 + step)
                s_ = stg[stg_i[0] % 2]
                stg_i[0] += 1
                na = a1 - a0
                view = s_.t[:, 0:na * B].rearrange("p (a b) -> p a b", b=B)
                S.dma(view, src3[:, a0:a1, :], writes=[s_.b], key=s_.b.name)
                dst, dbuf = dst_ap_fn(a0, a1)
                if scale_fn is None:
                    if eng == "act":
                        S.op(eng, lambda e: e.copy(out=dst, in_=view), reads=[s_.b], writes=[dbuf])
                    else:
                        S.op(eng, lambda e: e.tensor_copy(out=dst, in_=view), reads=[s_.b], writes=[dbuf])
                else:
                    assert na == 1
                    sc, scb = scale_fn(a0)
                    S.op(eng, lambda e: e.tensor_scalar(out=dst, in0=view, scalar1=sc, scalar2=None, op0=ALU.mult),
                         reads=[s_.b, scb], writes=[dbuf])
                a0 = a1

        def rsqrt_from(ctx_eng_out, out_t, in_ap, in_buf, scale, shape_ap=None):
            o = out_t.t[:] if shape_ap is None else shape_ap
            S.op("act", lambda e: e.activation(out=o, in_=in_ap, func=AF.Ln, bias=epsc.t[0:o.shape[0], 0:1], scale=scale),
                 reads=[in_buf, epsc.b], writes=[out_t.b])
            S.op("act", lambda e: e.activation(out=o, in_=o, func=AF.Exp, scale=-0.5), reads=[out_t.b], writes=[out_t.b])

        epsc = sbt(top, "epsc", [128, 1], F32)
        S.op("pool", lambda e: e.memset(epsc.t[:], EPS), writes=[epsc.b])

        modA1 = sbt(top, "modA1", [128, 8, 2], F32)
        modA2 = sbt(top, "modA2", [128, 8, 2], F32)
        ada = sbt(top, "ada", [128, 48, 2], F32)

        def phase_ada(l):
            with contextlib.ExitStack() as ph:
                cc = sbt(ph, "cc", [128, 8, 2], F32)
                S.dma(cc.t[:], ccol[:, :, :], writes=[cc.b], key="cc")
                sc = sbt(ph, "sc", [128, 8, 2], F32)
                S.op("act", lambda e: e.activation(out=sc.t[:], in_=cc.t[:], func=AF.Silu), reads=[cc.b], writes=[sc.b])
                bad = sbt(ph, "bad", [128, 48], F32)
                S.dma(bad.t[:], badac[l], writes=[bad.b], key="bad")
                gg = sbt(ph, "gg", [128, 16], F32)
                S.dma(gg.t[:, 0:8], g1c[l], writes=[gg.b], key="gg")
                S.dma(gg.t[:, 8:16], g2c[l], writes=[gg.b], key="gg")
                wa = [sbt(ph, "wa%d" % i, [128, 8, 512], F32) for i in range(2)]
                pa = pst(ph, "pa", [128, 96], F32)
                wsrc = w_ada[l].rearrange("(kc p) n -> p kc n", p=128)
                for cb in range(12):
                    w = wa[cb % 2]
                    S.dma(w.t[:], wsrc[:, :, cb * 512:(cb + 1) * 512], writes=[w.b], key=w.b.name)
                    for f4 in range(4):
                        fc = cb * 4 + f4
                        for kc in range(8):
                            S.op("pe", lambda e: e.matmul(pa.t[:, fc * 2:fc * 2 + 2], lhsT=w.t[:, kc, f4 * 128:(f4 + 1) * 128],
                                                          rhs=sc.t[:, kc, :], start=(kc == 0), stop=(kc == 7)),
                                 reads=[w.b, sc.b], writes=[pa.b])
                S.op("dve", lambda e: e.tensor_tensor(out=ada.t[:], in0=pa.t[:].rearrange("p (c s) -> p c s", s=2),
                                                      in1=bad.t[:].unsqueeze(2).to_broadcast([128, 48, 2]), op=ALU.add),
                     reads=[pa.b, bad.b], writes=[ada.b])
                for (mod, sc0, gofs) in ((modA1, 8, 0), (modA2, 32, 8)):
                    S.op("dve", lambda e: e.tensor_scalar(out=mod.t[:], in0=ada.t[:, sc0:sc0 + 8, :], scalar1=1.0, scalar2=None, op0=ALU.add),
                         reads=[ada.b], writes=[mod.b])
                    S.op("dve", lambda e: e.tensor_tensor(out=mod.t[:], in0=mod.t[:],
                                                          in1=gg.t[:, gofs:gofs + 8].unsqueeze(2).to_broadcast([128, 8, 2]), op=ALU.mult),
                         reads=[mod.b, gg.b], writes=[mod.b])
            S.barrier()

        def norm_mod(ph, xt, n, modA, shofs, si_ctx, sq, pss, rs, hT):
            s = 1 if si_ctx else 0
            S.op("act", lambda e: e.activation(out=sq.t[:, :, 0:n], in_=xt.t[:, :, 0:n], func=AF.Square), reads=[xt.b], writes=[sq.b])
            for kc in range(8):
                S.op("pe", lambda e: e.matmul(pss.t[:, 0:n], lhsT=ones32.t[:], rhs=sq.t[:, kc, 0:n], start=(kc == 0), stop=(kc == 7)),
                     reads=[ones32.b, sq.b], writes=[pss.b])
            rsqrt_from(None, rs, pss.t[:, 0:n], pss.b, 1.0 / D, shape_ap=rs.t[:, 0:n])
            S.op("dve", lambda e: e.tensor_tensor(out=sq.t[:, :, 0:n], in0=xt.t[:, :, 0:n],
                                                  in1=rs.t[:, 0:n].unsqueeze(1).to_broadcast([128, 8, n]), op=ALU.mult),
                 reads=[xt.b, rs.b], writes=[sq.b])
            for kc in range(8):
                S.op("act", lambda e: e.activation(out=hT.t[:, kc, 0:n], in_=sq.t[:, kc, 0:n], func=AF.Identity,
                                                   bias=ada.t[:, shofs + kc, s:s + 1], scale=modA.t[:, kc, s:s + 1]),
                     reads=[sq.b, ada.b, modA.b], writes=[hT.b])

        def phase_A(l, hsrc, do_ctx_q):
            with contextlib.ExitStack() as ph:
                Win = sbt(ph, "Win", [128, 8, O_G], BF16)
                wsrc = w_in[l].rearrange("(kc p) n -> p kc n", p=128)
                load_cast(lambda a0, a1: (Win.t[:, a0:a1, :], Win.b), wsrc[:, :, 0:O_G], 8, O_G)
                qag = sbt(ph, "qag", [128, 5], F32)
                S.dma(qag.t[:, 0:3], qagc[l], writes=[qag.b], key="qag")
                S.dma(qag.t[:, 3:5], kvagc[l], writes=[qag.b], key="qag")
                Wuq = sbt(ph, "Wuq", [128, 3, 768], BF16)
                load_cast(lambda a0, a1: (Wuq.t[:, a0:a1, :], Wuq.b), w_uq[l].rearrange("(kc p) n -> p kc n", p=128), 3, 768,
                          scale_fn=lambda a: (qag.t[:, a:a + 1], qag.b))
                Wkp = sbt(ph, "Wkp", [128, 2, 8, 96], BF16)
                S.op("pool", lambda e: e.memset(Wkp.t[:], 0.0), writes=[Wkp.b])
                Wv = sbt(ph, "Wv", [128, 2, 8, 64], BF16)
                ukv = w_ukv[l].rearrange("(kc p) n -> p kc n", p=128)
                for kc in range(2):
                    s_ = stg[stg_i[0] % 2]
                    stg_i[0] += 1
                    S.dma(s_.t[:, 0:1024], ukv[:, kc, :], writes=[s_.b], key=s_.b.name)
                    v3 = s_.t[:, 0:1024].rearrange("p (h c) -> p h c", c=128)
                    S.op("pool", lambda e: e.tensor_scalar(out=Wkp.t[:, kc, :, 0:64], in0=v3[:, :, 0:64], scalar1=qag.t[:, 3 + kc:4 + kc],
                                                           scalar2=None, op0=ALU.mult), reads=[s_.b, qag.b], writes=[Wkp.b])
                    S.op("pool", lambda e: e.tensor_scalar(out=Wv.t[:, kc, :, :], in0=v3[:, :, 64:128], scalar1=qag.t[:, 3 + kc:4 + kc],
                                                           scalar2=None, op0=ALU.mult), reads=[s_.b, qag.b], writes=[Wv.b])
                Wkd = sbt(ph, "Wkd", [128, 8, 2, 128], BF16)
                for kh in range(2):
                    for hf in range(2):
                        S.op("pool", lambda e: e.tensor_copy(out=Wkd.t[:, :, kh, hf * 64:(hf + 1) * 64],
                                                             in_=Win.t[:, :, O_SK + kh * 64:O_SK + (kh + 1) * 64]),
                             reads=[Win.b], writes=[Wkd.b])
                gq = sbt(ph, "gq", [128, 4], F32)
                S.dma(gq.t[0:96, 0:2], mqkg[l], writes=[gq.b], key="gq")
                S.dma(gq.t[:, 2:4], swag[l], writes=[gq.b], key="gq")

                xt = sbt(ph, "xt", [128, 8, 512], F32)
                sq = sbt(ph, "sq", [128, 8, 512], F32)
                rs = sbt(ph, "rs", [128, 512], F32)
                hT = sbt(ph, "hT", [128, 8, 512], BF16)
                q32 = sbt(ph, "q32", [128, 3, 512], F32)
                sqq = sbt(ph, "sqq", [128, 3, 512], F32)
                rq = sbt(ph, "rq", [128, 512], F32)
                qn = sbt(ph, "qn", [128, 3, 512], BF16)
                kvn = sbt(ph, "kvn", [128, 2, 512], BF16)
                krT = sbt(ph, "krT", [32, 512], BF16)
                uT = sbt(ph, "uT", [128, 4, 512], BF16)
                NH = 4
                hq32 = [sbt(ph, "hq32_%d" % i, [128, 512], F32) for i in range(NH)]
                hsq = [sbt(ph, "hsq_%d" % i, [128, 512], F32) for i in range(NH)]
                hqn = [sbt(ph, "hqn_%d" % i, [128, 512], F32) for i in range(NH)]
                hqb = [sbt(ph, "hqb_%d" % i, [128, 512], BF16) for i in range(NH)]
                hout = [sbt(ph, "hout_%d" % i, [128, 512], BF16) for i in range(NH)]
                va = [sbt(ph, "va_%d" % i, [128, 8, 128], BF16) for i in range(2)]
                sva = [sbt(ph, "sva_%d" % i, [128, 2, 128], BF16) for i in range(2)]
                for v_ in va + sva:
                    S.op("pool", lambda e: e.memset(v_.t[:], 1.0), writes=[v_.b])
                rope = sbt(ph, "rope", [128, 4, 512], F32)
                pss = pst(ph, "pss")
                pp = [pst(ph, "pp%d" % i) for i in range(2)]
                hp = [pst(ph, "hp%d" % i) for i in range(NH)]
                pv = pst(ph, "pv")
                cnt = [0]

                def headnorm(i, mm_fn, rows, gcol, onesT, dim, use_rope, Pmat, rc, rs_, n, dst_ap, dst_bufs):
                    ps, a32, asq, aqn, aqb, ao = hp[i], hq32[i], hsq[i], hqn[i], hqb[i], hout[i]
                    mm_fn(ps)
                    yield
                    S.op("act", lambda e: e.copy(out=a32.t[0:rows, 0:n], in_=ps.t[0:rows, 0:n]), reads=[ps.b], writes=[a32.b])
                    S.op("act", lambda e: e.activation(out=asq.t[0:rows, 0:n], in_=ps.t[0:rows, 0:n], func=AF.Square), reads=[ps.b], writes=[asq.b])
                    yield
                    S.op("pe", lambda e: e.matmul(ps.t[0:rows, 0:n], lhsT=onesT.t[0:rows, 0:rows], rhs=asq.t[0:rows, 0:n], start=True, stop=True),
                         reads=[onesT.b, asq.b], writes=[ps.b])
                    yield
                    rsqrt_from(None, asq, ps.t[0:rows, 0:n], ps.b, 1.0 / dim, shape_ap=asq.t[0:rows, 0:n])
                    yield
                    if not use_rope:
                        S.op("dve", lambda e: e.scalar_tensor_tensor(out=ao.t[0:rows, 0:n], in0=a32.t[0:rows, 0:n], scalar=gcol,
                                                                     in1=asq.t[0:rows, 0:n], op0=ALU.mult, op1=ALU.mult),
                             reads=[a32.b, asq.b, gq.b], writes=[ao.b])
                    else:
                        S.op("dve", lambda e: e.scalar_tensor_tensor(out=aqn.t[0:rows, 0:n], in0=a32.t[0:rows, 0:n], scalar=gcol,
                                                                     in1=asq.t[0:rows, 0:n], op0=ALU.mult, op1=ALU.mult),
                             reads=[a32.b, asq.b, gq.b], writes=[aqn.b])
                        yield
                        S.op("act", lambda e: e.copy(out=aqb.t[0:rows, 0:n], in_=aqn.t[0:rows, 0:n]), reads=[aqn.b], writes=[aqb.b])
                        yield
                        S.op("pe", lambda e: e.matmul(ps.t[0:rows, 0:n], lhsT=Pmat.t[0:rows, 0:rows], rhs=aqb.t[0:rows, 0:n], start=True, stop=True),
                             reads=[Pmat.b, aqb.b], writes=[ps.b])
                        S.op("pool", lambda e: e.tensor_tensor(out=a32.t[0:rows, 0:n], in0=aqn.t[0:rows, 0:n], in1=rope.t[0:rows, rc, 0:n], op=ALU.mult),
                             reads=[aqn.b, rope.b], writes=[a32.b])
                        yield
                        S.op("dve", lambda e: e.tensor_tensor(out=asq.t[0:rows, 0:n], in0=ps.t[0:rows, 0:n], in1=rope.t[0:rows, rs_, 0:n], op=ALU.mult),
                             reads=[ps.b, rope.b], writes=[asq.b])
                        yield
                        S.op("dve", lambda e: e.tensor_tensor(out=ao.t[0:rows, 0:n], in0=a32.t[0:rows, 0:n], in1=asq.t[0:rows, 0:n], op=ALU.add),
                             reads=[a32.b, asq.b], writes=[ao.b])
                    yield
                    if isinstance(dst_ap, list):
                        for (d_ap, r0, r1) in dst_ap:
                            S.dma(d_ap, ao.t[r0:r1, 0:n], reads=[ao.b], writes=dst_bufs, key=ao.b.name)
                    else:
                        S.dma(dst_ap, ao.t[0:rows, 0:n], reads=[ao.b], writes=dst_bufs, key=ao.b.name)

                def run_chains(jobs):
                    for b0 in range(0, len(jobs), NH):
                        gens = [jobs[b0 + k](k) for k in range(min(NH, len(jobs) - b0))]
                        while gens:
                            for g_ in list(gens):
                                try:
                                    next(g_)
                                except StopIteration:
                                    gens.remove(g_)

                for (si, t0, n, isc) in STS:
                    s = 1 if isc else 0
                    src = hsrc.rearrange("(kc p) t -> p kc t", p=128)
                    S.dma(xt.t[:, :, 0:n], src[:, :, t0:t0 + n], reads=[db("hsrc%d" % id(hsrc), si)], writes=[xt.b], key="xt")
                    if not isc:
                        x0 = t0 - NC_
                        S.dma(rope.t[0:96, 0, 0:n], cMc[:, x0:x0 + n], writes=[rope.b], key="rope")
                        S.dma(rope.t[0:96, 1, 0:n], cMs[:, x0:x0 + n], writes=[rope.b], key="rope")
                        S.dma(rope.t[:, 2, 0:n], cSc[:, x0:x0 + n], writes=[rope.b], key="rope")
                        S.dma(rope.t[:, 3, 0:n], cSs[:, x0:x0 + n], writes=[rope.b], key="rope")
                    norm_mod(ph, xt, n, modA1, 0, isc, sq, pss, rs, hT)
                    S.dma(hTs[:, :, t0:t0 + n], hT.t[:, :, 0:n], reads=[hT.b], writes=[db("hTs", si)], key="hT_st")

                    def proj(ps, c0, ncols=128):
                        for kc in range(8):
                            S.op("pe", lambda e: e.matmul(ps.t[0:ncols, 0:n], lhsT=Win.t[:, kc, c0:c0 + ncols], rhs=hT.t[:, kc, 0:n],
                                                          start=(kc == 0), stop=(kc == 7)), reads=[Win.b, hT.b], writes=[ps.b])
                    pi = [0]

                    def nextp():
                        pi[0] += 1
                        return pp[pi[0] % 2]

                    for (nch, c0, dst, dim) in ((3, 0, qn, 384), (2, 384, kvn, 256)):
                        for c in range(nch):
                            ps = nextp()
                            proj(ps, c0 + c * 128)
                            S.op("act", lambda e: e.copy(out=q32.t[:, c, 0:n], in_=ps.t[:, 0:n]), reads=[ps.b], writes=[q32.b])
                        S.op("pool", lambda e: e.tensor_tensor(out=sqq.t[:, 0:nch, 0:n], in0=q32.t[:, 0:nch, 0:n], in1=q32.t[:, 0:nch, 0:n], op=ALU.mult),
                             reads=[q32.b], writes=[sqq.b])
                        for c in range(nch):
                            S.op("pe", lambda e: e.matmul(pss.t[:, 0:n], lhsT=ones32.t[:], rhs=sqq.t[:, c, 0:n], start=(c == 0), stop=(c == nch - 1)),
                                 reads=[ones32.b, sqq.b], writes=[pss.b])
                        rsqrt_from(None, rq, pss.t[:, 0:n], pss.b, 1.0 / dim, shape_ap=rq.t[:, 0:n])
                        S.op("dve", lambda e: e.tensor_tensor(out=dst.t[:, 0:nch, 0:n], in0=q32.t[:, 0:nch, 0:n],
                                                              in1=rq.t[:, 0:n].unsqueeze(1).to_broadcast([128, nch, n]), op=ALU.mult),
                             reads=[q32.b, rq.b], writes=[dst.b])
                    ps = nextp()
                    proj(ps, O_KR, 32)
                    S.op("act", lambda e: e.copy(out=krT.t[:, 0:n], in_=ps.t[0:32, 0:n]), reads=[ps.b], writes=[krT.b])
                    jobs = []
                    for h in range(8):
                        if (not isc) or do_ctx_q:
                            def mmq(ps, h=h):
                                for c in range(3):
                                    S.op("pe", lambda e: e.matmul(ps.t[0:96, 0:n], lhsT=Wuq.t[:, c, h * 96:(h + 1) * 96], rhs=qn.t[:, c, 0:n],
                                                                  start=(c == 0), stop=(c == 2)), reads=[Wuq.b, qn.b], writes=[ps.b])
                            jobs.append(lambda i, h=h, mmq=mmq: headnorm(i, mmq, 96, gq.t[0:96, 0:1], ones32, 96.0, not isc, Pm, 0, 1, n,
                                                                           QTs[h, :, t0:t0 + n], [db("QTs", (h, si))]))

                        def mmk(ps, h=h):
                            for c in range(2):
                                S.op("pe", lambda e: e.matmul(ps.t[0:96, 0:n], lhsT=Wkp.t[:, c, h, :], rhs=kvn.t[:, c, 0:n],
                                                              start=(c == 0), stop=False), reads=[Wkp.b, kvn.b], writes=[ps.b])
                            S.op("pe", lambda e: e.matmul(ps.t[0:96, 0:n], lhsT=shiftI.t[:, :], rhs=krT.t[:, 0:n], start=False, stop=True),
                                 reads=[shiftI.b, krT.b], writes=[ps.b])
                        jobs.append(lambda i, h=h, mmk=mmk: headnorm(i, mmk, 96, gq.t[0:96, 1:2], ones32, 96.0, not isc, Pm, 0, 1, n,
                                                                       KTs[h, :, t0:t0 + n], [db("KTs", (h, si))]))
                    for c in range(4):
                        def mmsq(ps, c=c):
                            proj(ps, O_SQ + c * 128)
                        jobs.append(lambda i, c=c, mmsq=mmsq: headnorm(i, mmsq, 128, gq.t[:, 2:3], bones32, 64.0, not isc, Ps, 2, 3, n,
                                                                         [(sQs[:, 2 * c, t0:t0 + n], 0, 64), (sQs[:, 2 * c + 1, t0:t0 + n], 64, 128)],
                                                                         [db("sQs", (c, si))]))
                    for kh in range(2):
                        def mmsk(ps, kh=kh):
                            for kc in range(8):
                                S.op("pe", lambda e: e.matmul(ps.t[:, 0:n], lhsT=Wkd.t[:, kc, kh, :], rhs=hT.t[:, kc, 0:n],
                                                              start=(kc == 0), stop=(kc == 7)), reads=[Wkd.b, hT.b], writes=[ps.b])
                        jobs.append(lambda i, kh=kh, mmsk=mmsk: headnorm(i, mmsk, 128, gq.t[:, 3:4], bones32, 64.0, not isc, Ps, 2, 3, n,
                                                                           [(sKs[:, kh, t0:t0 + n], 0, 64)], [db("sKs", (kh, si))]))
                    run_chains(jobs)
                    for j in range(n // 128):
                        tile_i = t0 // 128 + j
                        v_ = va[tile_i % 2]
                        for c in range(2):
                            S.op("pe", lambda e: e.matmul(pv.t[:, 0:512], lhsT=kvn.t[:, c, j * 128:(j + 1) * 128],
                                                          rhs=Wv.t[:, c, :, :].rearrange("p h c -> p (h c)"),
                                                          start=(c == 0), stop=(c == 1)), reads=[kvn.b, Wv.b], writes=[pv.b])
                        S.op("act", lambda e: e.copy(out=v_.t[:, :, 0:64], in_=pv.t[:, 0:512].rearrange("p (h c) -> p h c", c=64)),
                             reads=[pv.b], writes=[v_.b])
                        S.dma(VsM[tile_i], v_.t[:], reads=[v_.b], writes=[db("VsM", tile_i)], key=v_.b.name)
                    for c in range(4):
                        ps = nextp()
                        proj(ps, O_U + c * 128)
                        S.op("act", lambda e: e.copy(out=uT.t[:, c, 0:n], in_=ps.t[:, 0:n]), reads=[ps.b], writes=[uT.b])
                    S.dma(uTs[:, :, t0:t0 + n], uT.t[:, :, 0:n], reads=[uT.b], writes=[db("uTs", si)], key="uT_st")
                    for j in range(n // 128):
                        tile_i = t0 // 128 + j
                        v_ = sva[tile_i % 2]
                        for kc in range(8):
                            S.op("pe", lambda e: e.matmul(pv.t[:, 0:128], lhsT=hT.t[:, kc, j * 128:(j + 1) * 128], rhs=Win.t[:, kc, O_SV:O_SV + 128],
                                                          start=(kc == 0), stop=(kc == 7)), reads=[hT.b, Win.b], writes=[pv.b])
                        S.op("act", lambda e: e.copy(out=v_.t[:, :, 0:64], in_=pv.t[:, 0:128].rearrange("p (h c) -> p h c", c=64)),
                             reads=[pv.b], writes=[v_.b])
                        S.dma(sVs[tile_i], v_.t[:], reads=[v_.b], writes=[db("sVs", tile_i)], key=v_.b.name)
            S.barrier()

        def attn_core(ph, nkeys_tiles, score_fn, pv_lhsT_fn, pv_reads, n, pS, pO, Pt, scale, mask_fn=None):
            nk = len(nkeys_tiles)
            NR = len(pS)

            def do_s(i):
                score_fn(nkeys_tiles[i], pS[i % NR])

            def do_e(i):
                ps_, p_ = pS[i % NR], Pt[i % NR]
                S.op("act", lambda e: e.activation(out=p_.t[:, 0:n], in_=ps_.t[:, 0:n], func=AF.Exp, scale=scale), reads=[ps_.b], writes=[p_.b])
                if mask_fn is not None:
                    mask_fn(nkeys_tiles[i], p_)

            def do_pv(i):
                p_ = Pt[i % NR]
                S.op("pe", lambda e: e.matmul(pO.t[:, 0:n], lhsT=pv_lhsT_fn(nkeys_tiles[i]), rhs=p_.t[:, 0:n], start=(i == 0), stop=(i == nk - 1)),
                     reads=[p_.b] + pv_reads, writes=[pO.b])

            do_s(0)
            if nk > 1:
                do_s(1)
            for i in range(nk):
                do_e(i)
                if i + 2 < nk:
                    do_s(i + 2)
                do_pv(i)
                yield

        def phase_B(l, do_ctx, ov=False):
            with contextlib.ExitStack() as ph:
                nb = 1 if ov else 2
                KT = [sbt(ph, "KT%d" % i, [96, T], BF16) for i in range(nb)]
                QT = [sbt(ph, "QT%d" % i, [96, T], BF16) for i in range(nb)]
                VH = [sbt(ph, "VH%d" % i, [128, 34, 128], BF16) for i in range(nb)]
                Pt = [sbt(ph, "Pt%d" % i, [128, 512], BF16) for i in range(2 if ov else 3)]
                rsum = [sbt(ph, "rsum%d" % i, [64, 512], F32) for i in range(nb)]
                oT = [sbt(ph, "oT%d" % i, [64, 512], BF16) for i in range(2)]
                pS = [pst(ph, "pS%d" % i) for i in range(2 if ov else 3)]
                pO = [pst(ph, "pO%d" % i) for i in range(nb)]
                allv = [db("VsM", i) for i in range(34)]
                it = [0]
                for h in range(8):
                    allq = [db("QTs", (h, si)) for si in range(9)]
                    allk = [db("KTs", (h, si)) for si in range(9)]
                    kt_, qt_, vh_ = KT[h % nb], QT[h % nb], VH[h % nb]
                    S.dma(kt_.t[:], KTs[h], reads=allk, writes=[kt_.b], key=kt_.b.name)
                    S.dma(qt_.t[:], QTs[h], reads=allq, writes=[qt_.b], key=qt_.b.name)
                    S.dma(vh_.t[:], VsM.rearrange("t p h c -> p t h c")[:, :, h, :], reads=allv, writes=[vh_.b], key=vh_.b.name)
                    for (si, t0, n, isc) in STS:
                        if isc and not do_ctx:
                            continue
                        keys = [0, 1] if isc else list(range(34))
                        po = pO[it[0] % nb]
                        rs_ = rsum[it[0] % nb]
                        o_ = oT[it[0] % 2]
                        it[0] += 1

                        def score(kt, ps_):
                            S.op("pe", lambda e: e.matmul(ps_.t[:, 0:n], lhsT=kt_.t[:, kt * 128:(kt + 1) * 128], rhs=qt_.t[:, t0:t0 + n], start=True, stop=True),
                                 reads=[kt_.b, qt_.b], writes=[ps_.b])
                        yield from attn_core(ph, keys, score, lambda kt: vh_.t[:, kt, :], [vh_.b], n, pS, po, Pt, 96.0 ** -0.5)
                        S.op("act", lambda e: e.copy(out=rs_.t[:, 0:n], in_=po.t[64:128, 0:n]), reads=[po.b], writes=[rs_.b])
                        S.op("dve", lambda e: e.reciprocal(out=rs_.t[:, 0:n], in_=rs_.t[:, 0:n]), reads=[rs_.b], writes=[rs_.b])
                        S.op("dve", lambda e: e.tensor_tensor(out=o_.t[:, 0:n], in0=po.t[0:64, 0:n], in1=rs_.t[:, 0:n], op=ALU.mult),
                             reads=[po.b, rs_.b], writes=[o_.b])
                        S.dma(mlaO[h * 64:(h + 1) * 64, t0:t0 + n], o_.t[:, 0:n], reads=[o_.b], writes=[db("mlaO", (h, si))], key=o_.b.name)
            S.barrier()

        def phase_C(l, do_ctx, ov=False):
            with contextlib.ExitStack() as ph:
                nb = 1 if ov else 2
                sk = sbt(ph, "sk", [64, T], BF16)
                sqt = [sbt(ph, "sqt%d" % i, [64, 4, 128], BF16) for i in range(2)]
                sv = sbt(ph, "sv", [128, 34, 128], BF16)
                Pt = [sbt(ph, "Pt%d" % i, [128, 512], BF16) for i in range(2 if ov else 3)]
                rsum = [sbt(ph, "rsum%d" % i, [64, 512], F32) for i in range(nb)]
                oT = [sbt(ph, "oT%d" % i, [64, 512], BF16) for i in range(2)]
                esk = sbt(ph, "esk", [128, 8], F32)
                S.dma(esk.t[:], sinkb[l], writes=[esk.b], key="esk")
                S.op("act", lambda e: e.activation(out=esk.t[:], in_=esk.t[:], func=AF.Exp), reads=[esk.b], writes=[esk.b])
                pS = [pst(ph, "pS%d" % i) for i in range(2 if ov else 3)]
                pO = [pst(ph, "pO%d" % i) for i in range(nb)]
                allv = [db("sVs", i) for i in range(34)]
                it = [0]
                for kh in range(2):
                    allq = [db("sQs", (c_, si)) for si in range(9) for c_ in (2 * kh, 2 * kh + 1)]
                    allk = [db("sKs", (kh, si)) for si in range(9)]
                    S.dma(sk.t[:], sKs[:, kh, :], reads=allk, writes=[sk.b], key="sk")
                    S.dma(sv.t[:], sVs.rearrange("t p h c -> p t h c")[:, :, kh, :], reads=allv, writes=[sv.b], key="sv")
                    qtiles = ([0, 1] if do_ctx else []) + list(range(2, 34))
                    for qt in qtiles:
                        q0 = qt * 128
                        if qt < 2:
                            keys = [(0, None), (1, None)]
                        else:
                            keys = [(0, None), (1, None)]
                            if qt > 2:
                                keys.append((qt - 1, Mge))
                            keys.append((qt, None))
                            if qt < 33:
                                keys.append((qt + 1, Mle))
                        po = pO[it[0] % nb]
                        rs_ = rsum[it[0] % nb]
                        o_ = oT[it[0] % 2]
                        sq_ = sqt[it[0] % 2]
                        it[0] += 1
                        S.dma(sq_.t[:], sQs[:, 4 * kh:4 * kh + 4, q0:q0 + 128], reads=allq, writes=[sq_.b], key=sq_.b.name)

                        def score(km, ps_):
                            kt = km[0]
                            for hd in range(4):
                                S.op("pe", lambda e: e.matmul(ps_.t[:, hd * 128:(hd + 1) * 128], lhsT=sk.t[:, kt * 128:(kt + 1) * 128],
                                                              rhs=sq_.t[:, hd, :], start=True, stop=True),
                                     reads=[sk.b, sq_.b], writes=[ps_.b])

                        def maskf(km, p_):
                            if km[1] is not None:
                                m = km[1]
                                p3 = p_.t[:, :].rearrange("p (h q) -> p h q", q=128)
                                S.op("pool", lambda e: e.tensor_tensor(out=p3, in0=p3, in1=m.t[:, :].unsqueeze(1).to_broadcast([128, 4, 128]), op=ALU.mult),
                                     reads=[p_.b, m.b], writes=[p_.b])
                        yield from attn_core(ph, keys, score, lambda km: sv.t[:, km[0], :], [sv.b], 512, pS, po, Pt, 0.125, mask_fn=maskf)
                        S.op("act", lambda e: e.copy(out=rs_.t[:, :], in_=po.t[64:128, :]), reads=[po.b], writes=[rs_.b])
                        r3 = rs_.t[:, :].rearrange("p (h q) -> p h q", q=128)
                        S.op("dve", lambda e: e.tensor_tensor(out=r3, in0=r3, in1=esk.t[0:64, 4 * kh:4 * kh + 4].unsqueeze(2).to_broadcast([64, 4, 128]), op=ALU.add),
                             reads=[rs_.b, esk.b], writes=[rs_.b])
                        S.op("dve", lambda e: e.reciprocal(out=rs_.t[:, :], in_=rs_.t[:, :]), reads=[rs_.b], writes=[rs_.b])
                        S.op("dve", lambda e: e.tensor_tensor(out=o_.t[:, :], in0=po.t[0:64, :], in1=rs_.t[:, :], op=ALU.mult),
                             reads=[po.b, rs_.b], writes=[o_.b])
                        dst = swaO.rearrange("(h d) t -> d h t", d=64)[:, 4 * kh:4 * kh + 4, q0:q0 + 128]
                        S.dma(dst, o_.t[:, :].rearrange("p (h q) -> p h q", q=128), reads=[o_.b], writes=[db("swaO", (kh, qt))], key=o_.b.name)
            S.barrier()

        def phase_D(l, do_ctx, ov=False):
            with contextlib.ExitStack() as ph:
                uTc2 = [sbt(ph, "uTc%d" % i, [128, T], BF16) for i in range(2)]
                acc = sbt(ph, "acc", [128, T], F32)
                def small(name, w=32):
                    return sbt(ph, name, [128, w], F32)
                lre, lim, dtt = small("lre"), small("lim"), small("dtt")
                S.dma(lre.t[:], s5lre[l].rearrange("p d q -> p (d q)"), writes=[lre.b], key="s5p1")
                S.dma(lim.t[:], s5lim[l].rearrange("p d q -> p (d q)"), writes=[lim.b], key="s5p2")
                S.dma(dtt.t[:], s5ls[l].rearrange("p d q -> p (d q)"), writes=[dtt.b], key="s5p3")
                S.op("act", lambda e: e.activation(out=dtt.t[:], in_=dtt.t[:], func=AF.Exp), reads=[dtt.b], writes=[dtt.b])
                S.op("dve", lambda e: e.tensor_scalar(out=lre.t[:], in0=lre.t[:], scalar1=-1e-4, scalar2=None, op0=ALU.min), reads=[lre.b], writes=[lre.b])
                rr, th, fq = small("rr"), small("th"), small("fq")
                S.op("dve", lambda e: e.tensor_tensor(out=rr.t[:], in0=lre.t[:], in1=dtt.t[:], op=ALU.mult), reads=[lre.b, dtt.b], writes=[rr.b])
                S.op("act", lambda e: e.activation(out=rr.t[:], in_=rr.t[:], func=AF.Exp), reads=[rr.b], writes=[rr.b])
                S.op("dve", lambda e: e.tensor_tensor(out=th.t[:], in0=lim.t[:], in1=dtt.t[:], op=ALU.mult), reads=[lim.b, dtt.b], writes=[th.b])
                S.op("dve", lambda e: e.tensor_scalar(out=fq.t[:], in0=th.t[:], scalar1=1.0 / TWO_PI, scalar2=None, op0=ALU.mult), reads=[th.b], writes=[fq.b])
                tmpi = sbt(ph, "tmpi", [128, 512], I32)
                tmpf = sbt(ph, "tmpf", [128, 512], F32)

                def sincos(dst_s, dst_c, ph_ap, ph_buf, w):
                    for (dst, shift) in ((dst_s, 0.0), (dst_c, 0.25)):
                        S.op("dve", lambda e: e.tensor_scalar(out=tmpf.t[:, 0:w], in0=ph_ap, scalar1=shift, scalar2=None, op0=ALU.add), reads=[ph_buf], writes=[tmpf.b])
                        S.op("dve", lambda e: e.tensor_copy(out=tmpi.t[:, 0:w], in_=tmpf.t[:, 0:w]), reads=[tmpf.b], writes=[tmpi.b])
                        S.op("dve", lambda e: e.tensor_copy(out=dst[1], in_=tmpi.t[:, 0:w]), reads=[tmpi.b], writes=[dst[0]])
                        S.op("dve", lambda e: e.tensor_tensor(out=tmpf.t[:, 0:w], in0=tmpf.t[:, 0:w], in1=dst[1], op=ALU.subtract), reads=[tmpf.b, dst[0]], writes=[tmpf.b])
                        S.op("act", lambda e: e.activation(out=dst[1], in_=tmpf.t[:, 0:w], func=AF.Sin, scale=TWO_PI), reads=[tmpf.b], writes=[dst[0]])

                sn, cs = small("sn"), small("cs")
                sincos((sn.b, sn.t[:]), (cs.b, cs.t[:]), fq.t[:], fq.b, 32)
                are, aim = small("are"), small("aim")
                S.op("dve", lambda e: e.tensor_tensor(out=are.t[:], in0=rr.t[:], in1=cs.t[:], op=ALU.mult), reads=[rr.b, cs.b], writes=[are.b])
                S.op("dve", lambda e: e.tensor_tensor(out=aim.t[:], in0=rr.t[:], in1=sn.t[:], op=ALU.mult), reads=[rr.b, sn.b], writes=[aim.b])
                am1, den, t1_, t2_, cre, cim = small("am1"), small("den"), small("t1_"), small("t2_"), small("cre"), small("cim")
                S.op("dve", lambda e: e.tensor_scalar(out=am1.t[:], in0=are.t[:], scalar1=-1.0, scalar2=None, op0=ALU.add), reads=[are.b], writes=[am1.b])
                S.op("dve", lambda e: e.tensor_tensor(out=den.t[:], in0=lre.t[:], in1=lre.t[:], op=ALU.mult), reads=[lre.b], writes=[den.b])
                S.op("dve", lambda e: e.tensor_tensor(out=t1_.t[:], in0=lim.t[:], in1=lim.t[:], op=ALU.mult), reads=[lim.b], writes=[t1_.b])
                S.op("dve", lambda e: e.tensor_tensor(out=den.t[:], in0=den.t[:], in1=t1_.t[:], op=ALU.add), reads=[den.b, t1_.b], writes=[den.b])
                S.op("dve", lambda e: e.reciprocal(out=den.t[:], in_=den.t[:]), reads=[den.b], writes=[den.b])
                S.op("dve", lambda e: e.tensor_tensor(out=t1_.t[:], in0=am1.t[:], in1=lre.t[:], op=ALU.mult), reads=[am1.b, lre.b], writes=[t1_.b])
                S.op("dve", lambda e: e.tensor_tensor(out=t2_.t[:], in0=aim.t[:], in1=lim.t[:], op=ALU.mult), reads=[aim.b, lim.b], writes=[t2_.b])
                S.op("dve", lambda e: e.tensor_tensor(out=cre.t[:], in0=t1_.t[:], in1=t2_.t[:], op=ALU.add), reads=[t1_.b, t2_.b], writes=[cre.b])
                S.op("dve", lambda e: e.tensor_tensor(out=cre.t[:], in0=cre.t[:], in1=den.t[:], op=ALU.mult), reads=[cre.b, den.b], writes=[cre.b])
                S.op("dve", lambda e: e.tensor_tensor(out=t1_.t[:], in0=aim.t[:], in1=lre.t[:], op=ALU.mult), reads=[aim.b, lre.b], writes=[t1_.b])
                S.op("dve", lambda e: e.tensor_tensor(out=t2_.t[:], in0=am1.t[:], in1=lim.t[:], op=ALU.mult), reads=[am1.b, lim.b], writes=[t2_.b])
                S.op("dve", lambda e: e.tensor_tensor(out=cim.t[:], in0=t1_.t[:], in1=t2_.t[:], op=ALU.subtract), reads=[t1_.b, t2_.b], writes=[cim.b])
                S.op("dve", lambda e: e.tensor_tensor(out=cim.t[:], in0=cim.t[:], in1=den.t[:], op=ALU.mult), reads=[cim.b, den.b], writes=[cim.b])
                fb, snB, csB = small("fb", 64), small("snB", 64), small("csB", 64)
                S.op("dve", lambda e: e.tensor_scalar(out=fb.t[:, 0:32], in0=fq.t[:], scalar1=256.0, scalar2=None, op0=ALU.mult), reads=[fq.b], writes=[fb.b])
                S.op("dve", lambda e: e.tensor_scalar(out=fb.t[:, 32:64], in0=fq.t[:], scalar1=512.0, scalar2=None, op0=ALU.mult), reads=[fq.b], writes=[fb.b])
                sincos((snB.b, snB.t[:]), (csB.b, csB.t[:]), fb.t[:], fb.b, 64)
                WB = sbt(ph, "WB", [128, 2, 2, 8, 128], BF16)
                CW = sbt(ph, "CW", [128, 3, 2, 16, 128], BF16)
                S.op("pool", lambda e: e.memset(CW.t[:], 0.0), writes=[CW.b])
                prep = contextlib.ExitStack()
                braw = sbt(prep, "braw", [128, 2, 2, 16, 16], F32)
                craw = sbt(prep, "craw", [128, 2, 2, 16, 16], F32)
                S.dma(braw.t[:, 0], s5bre[l], writes=[braw.b], key="braw")
                S.dma(braw.t[:, 1], s5bim[l], writes=[braw.b], key="braw")
                S.dma(craw.t[:, 0], s5cre[l], writes=[craw.b], key="craw")
                S.dma(craw.t[:, 1], s5cim[l], writes=[craw.b], key="craw")
                bbar = sbt(prep, "bbar", [128, 2, 2, 16, 16], F32)
                tb = sbt(prep, "tb", [128, 2, 16, 16], F32)
                cre3 = cre.t[:].rearrange("p (d q) -> p d q", q=16).unsqueeze(3).to_broadcast([128, 2, 16, 16])
                cim3 = cim.t[:].rearrange("p (d q) -> p d q", q=16).unsqueeze(3).to_broadcast([128, 2, 16, 16])
                S.op("dve", lambda e: e.tensor_tensor(out=bbar.t[:, 0], in0=braw.t[:, 0], in1=cre3, op=ALU.mult), reads=[braw.b, cre.b], writes=[bbar.b])
                S.op("dve", lambda e: e.tensor_tensor(out=tb.t[:], in0=braw.t[:, 1], in1=cim3, op=ALU.mult), reads=[braw.b, cim.b], writes=[tb.b])
                S.op("dve", lambda e: e.tensor_tensor(out=bbar.t[:, 0], in0=bbar.t[:, 0], in1=tb.t[:], op=ALU.subtract), reads=[bbar.b, tb.b], writes=[bbar.b])
                S.op("dve", lambda e: e.tensor_tensor(out=bbar.t[:, 1], in0=braw.t[:, 1], in1=cre3, op=ALU.mult), reads=[braw.b, cre.b], writes=[bbar.b])
                S.op("dve", lambda e: e.tensor_tensor(out=tb.t[:], in0=braw.t[:, 0], in1=cim3, op=ALU.mult), reads=[braw.b, cim.b], writes=[tb.b])
                S.op("dve", lambda e: e.tensor_tensor(out=bbar.t[:, 1], in0=bbar.t[:, 1], in1=tb.t[:], op=ALU.add), reads=[bbar.b, tb.b], writes=[bbar.b])
                Z = [sbt(prep, "Z%d" % i, [128, 64], BF16) for i in range(2)]
                pz = [pst(prep, "pz%d" % i, [128, 128], BF16) for i in range(2)]
                zi = 0
                for d_ in range(2):
                    for q in range(16):
                        c, q4 = q // 4, q % 4
                        hf, q2 = q4 // 2, q4 % 2
                        for ri in range(2):
                            z, pzz = Z[zi % 2], pz[zi % 2]
                            zi += 1
                            S.op("pool", lambda e: e.memset(z.t[:], 0.0), writes=[z.b])
                            S.op("dve", lambda e: e.tensor_copy(out=z.t[0:64, q2 * 32:q2 * 32 + 16], in_=bbar.t[0:64, ri, d_, q, :]), reads=[bbar.b], writes=[z.b])
                            S.op("dve", lambda e: e.tensor_copy(out=z.t[64:128, q2 * 32 + 16:q2 * 32 + 32], in_=bbar.t[64:128, ri, d_, q, :]), reads=[bbar.b], writes=[z.b])
                            S.op("pe", lambda e: e.transpose(pzz.t[0:64, :], z.t[:, :], identb.t[:, :]), reads=[z.b, identb.b], writes=[pzz.b])
                            S.op("act", lambda e: e.copy(out=WB.t[hf * 64:(hf + 1) * 64, ri, d_, c * 2 + q2, :], in_=pzz.t[0:64, :]), reads=[pzz.b], writes=[WB.b])
                            sgn = 1.0 if ri == 0 else -1.0
                            for e_ in range(2):
                                S.op("pool", lambda e: e.tensor_scalar(out=CW.t[e_ * 64:(e_ + 1) * 64, ri, d_, q, q4 * 32 + e_ * 16:q4 * 32 + e_ * 16 + 16],
                                                                       in0=craw.t[e_ * 64:(e_ + 1) * 64, ri, d_, q, :], scalar1=sgn, scalar2=None, op0=ALU.mult),
                                     reads=[craw.b], writes=[CW.b])
                                if ri == 0:
                                    S.op("pool", lambda e: e.tensor_scalar(out=CW.t[e_ * 64:(e_ + 1) * 64, 2, d_, q, q4 * 32 + e_ * 16:q4 * 32 + e_ * 16 + 16],
                                                                           in0=craw.t[e_ * 64:(e_ + 1) * 64, 0, d_, q, :], scalar1=-1.0, scalar2=None, op0=ALU.mult),
                                         reads=[craw.b], writes=[CW.b])
                S.barrier()
                prep.close()
                Wg = sbt(ph, "Wg", [128, 4, 512], BF16)
                load_cast(lambda a0, a1: (Wg.t[:, a0:a1, :], Wg.b), w_glu[l].rearrange("(kc p) n -> p kc n", p=128), 4, 512, eng="act")
                dcol = sbt(ph, "dcol", [128, 8], F32)
                S.dma(dcol.t[:, 0:4], s5dc[l], writes=[dcol.b], key="dcol")
                S.dma(dcol.t[:, 4:8], s5bgc[l], writes=[dcol.b], key="dcol")
                iot = sbt(ph, "iot", [128, 512], F32)
                S.dma(iot.t[:], cIota[:, :], writes=[iot.b], key="iot")
                wk = contextlib.ExitStack()
                tph = sbt(wk, "tph", [128, 512], F32)
                tC4 = [sbt(wk, "tC4_%d" % i, [128, 512], F32) for i in range(4)]
                tS4 = [sbt(wk, "tS4_%d" % i, [128, 512], F32) for i in range(4)]
                NS = 2 if ov else int(_os.environ.get("D_NS", "3"))
                W = {k: [sbt(wk, "w%s%d" % (k, i), [128, 512], F32) for i in range(NS)] for k in
                     ("t1", "t2", "t3", "t4", "xa", "xb", "ga", "gb")}
                Ub = {k: [sbt(wk, "b%s%d" % (k, i), [128, 512], BF16) for i in range(NS)] for k in ("u1", "u2", "u3", "u4")}
                ini4 = [sbt(wk, "ini4_%d" % i, [128, 4], F32) for i in range(4)]
                ygt = [sbt(wk, "ygt%d" % i, [128, 512], BF16) for i in range(2)]
                Xs = [[sbt(wk, "Xs%d%d" % (i, j), [128, 512], F32) for j in range(2)] for i in range(NS)] if D_EVAC else None
                pX = [[pst(wk, "pX%d%d" % (i, j)) for j in range(2)] for i in range(NS)]
                pY = [pst(wk, "pY%d" % i) for i in range(1 if ov else 2)]
                UENG = D_UENG

                def tt(eng, o, a, b_, op, rd, wr):
                    S.op(eng, lambda e: e.tensor_tensor(out=o, in0=a, in1=b_, op=op), reads=rd, writes=wr)
                it = 0
                pend = []
                for c in range(4):
                    uT = uTc2[c % 2]
                    S.dma(uT.t[:], uTs[:, c, :], reads=[db("uTs", si) for si in range(9)], writes=[uT.b], key=uT.b.name)
                    S.op("dve", lambda e: e.tensor_scalar(out=acc.t[:], in0=uT.t[:, :], scalar1=dcol.t[:, c:c + 1], scalar2=None, op0=ALU.mult),
                         reads=[uT.b, dcol.b], writes=[acc.b])
                    for d_ in range(2):
                        for q4 in range(4):
                            col = d_ * 16 + c * 4 + q4
                            S.op("dve", lambda e: e.tensor_scalar(out=tph.t[:], in0=iot.t[:], scalar1=fq.t[:, col:col + 1], scalar2=None, op0=ALU.mult),
                                 reads=[iot.b, fq.b], writes=[tph.b])
                            sincos((tS4[q4].b, tS4[q4].t[:]), (tC4[q4].b, tC4[q4].t[:]), tph.t[:], tph.b, 512)
                            S.op("pool", lambda e: e.memset(ini4[q4].t[:], 0.0), writes=[ini4[q4].b])
                        for fidx in range(9):
                            n = 256 if fidx == 0 else 512
                            if d_ == 0:
                                k0 = 0 if fidx == 0 else 256 + 512 * (fidx - 1)
                                cols = slice(k0, k0 + n)
                            else:
                                if fidx == 0:
                                    cols = slice(255, None, -1)
                                else:
                                    hi = NC_ + NX - 512 * (fidx - 1) - 1
                                    cols = slice(hi, hi - 512, -1)
                            py = pY[fidx % len(pY)]
                            for q4 in range(4):
                                if q4 > 0 or fidx > 0:
                                    yield
                                q = c * 4 + q4
                                hf, q2 = q4 // 2, q4 % 2
                                col = d_ * 16 + q
                                i = it % NS
                                it += 1
                                tC, tS, iv = tC4[q4], tS4[q4], ini4[q4]
                                px = pX[i]
                                for ri in range(2):
                                    S.op("pe", lambda e: e.matmul(px[ri].t[:, 0:n], lhsT=WB.t[hf * 64:(hf + 1) * 64, ri, d_, c * 2 + q2, :],
                                                                  rhs=uT.t[hf * 64:(hf + 1) * 64, cols], start=True, stop=True),
                                         reads=[WB.b, uT.b], writes=[px[ri].b])
                                for f_ in pend:
                                    f_()
                                del pend[:]
                                t1, t2, t3, t4 = W["t1"][i], W["t2"][i], W["t3"][i], W["t4"][i]
                                xa, xb, ga, gb = W["xa"][i], W["xb"][i], W["ga"][i], W["gb"][i]
                                u1, u2, u3, u4 = Ub["u1"][i], Ub["u2"][i], Ub["u3"][i], Ub["u4"][i]
                                if D_EVAC:
                                    xs0, xs1 = Xs[i][0], Xs[i][1]
                                    S.op("act", lambda e: e.copy(out=xs0.t[:, 0:n], in_=px[0].t[:, 0:n]), reads=[px[0].b], writes=[xs0.b])
                                    S.op("act", lambda e: e.copy(out=xs1.t[:, 0:n], in_=px[1].t[:, 0:n]), reads=[px[1].b], writes=[xs1.b])
                                else:
                                    xs0, xs1 = px[0], px[1]
                                tt(D_TENG[0], t1.t[:, 0:n], xs0.t[:, 0:n], tC.t[:, 0:n], ALU.mult, [xs0.b, tC.b], [t1.b])
                                tt(D_TENG[1], t2.t[:, 0:n], xs1.t[:, 0:n], tS.t[:, 0:n], ALU.mult, [xs1.b, tS.b], [t2.b])
                                tt(D_TENG[2], t3.t[:, 0:n], xs1.t[:, 0:n], tC.t[:, 0:n], ALU.mult, [xs1.b, tC.b], [t3.b])
                                tt(D_TENG[3], t4.t[:, 0:n], xs0.t[:, 0:n], tS.t[:, 0:n], ALU.mult, [xs0.b, tS.b], [t4.b])
                                tt(D_XENG, xa.t[:, 0:n], t1.t[:, 0:n], t2.t[:, 0:n], ALU.add, [t1.b, t2.b], [xa.b])
                                tt(D_XENG, xb.t[:, 0:n], t3.t[:, 0:n], t4.t[:, 0:n], ALU.subtract, [t3.b, t4.b], [xb.b])
                                rbc = rr.t[:, col:col + 1].to_broadcast([128, n])
                                S.op("dve", lambda e: e.tensor_tensor_scan(out=ga.t[:, 0:n], data0=rbc, data1=xa.t[:, 0:n], initial=iv.t[:, 0:1], op0=ALU.mult, op1=ALU.add),
                                     reads=[xa.b, rr.b, iv.b], writes=[ga.b])
                                S.op("dve", lambda e: e.tensor_tensor_scan(out=gb.t[:, 0:n], data0=rbc, data1=xb.t[:, 0:n], initial=iv.t[:, 1:2], op0=ALU.mult, op1=ALU.add),
                                     reads=[xb.b, rr.b, iv.b], writes=[gb.b])
                                if fidx < 8:
                                    bc = (0 if n == 256 else 32) + col
                                    cB, sB = csB.t[:, bc:bc + 1], snB.t[:, bc:bc + 1]
                                    S.op("pool", lambda e: e.tensor_scalar(out=iv.t[:, 2:3], in0=gb.t[:, n - 1:n], scalar1=sB, scalar2=None, op0=ALU.mult), reads=[gb.b, snB.b], writes=[iv.b])
                                    S.op("pool", lambda e: e.tensor_scalar(out=iv.t[:, 3:4], in0=ga.t[:, n - 1:n], scalar1=sB, scalar2=None, op0=ALU.mult), reads=[ga.b, snB.b], writes=[iv.b])
                                    S.op("dve", lambda e: e.scalar_tensor_tensor(out=iv.t[:, 0:1], in0=ga.t[:, n - 1:n], scalar=cB, in1=iv.t[:, 2:3], op0=ALU.mult, op1=ALU.subtract),
                                         reads=[ga.b, csB.b, iv.b], writes=[iv.b])
                                    S.op("dve", lambda e: e.scalar_tensor_tensor(out=iv.t[:, 1:2], in0=gb.t[:, n - 1:n], scalar=cB, in1=iv.t[:, 3:4], op0=ALU.mult, op1=ALU.add),
                                         reads=[gb.b, csB.b, iv.b], writes=[iv.b])
                                tt(UENG[0], u1.t[:, 0:n], ga.t[:, 0:n], tC.t[:, 0:n], ALU.mult, [ga.b, tC.b], [u1.b])
                                tt(UENG[1], u2.t[:, 0:n], gb.t[:, 0:n], tS.t[:, 0:n], ALU.mult, [gb.b, tS.b], [u2.b])
                                tt(UENG[2], u3.t[:, 0:n], ga.t[:, 0:n], tS.t[:, 0:n], ALU.mult, [ga.b, tS.b], [u3.b])
                                tt(UENG[3], u4.t[:, 0:n], gb.t[:, 0:n], tC.t[:, 0:n], ALU.mult, [gb.b, tC.b], [u4.b])
                                def readout(py=py, d_=d_, q=q, q4=q4, n=n, cols=cols, us=(u1, u2, u3, u4)):
                                    for j_, (ws, uu) in enumerate(((0, us[0]), (2, us[1]), (1, us[2]), (1, us[3]))):
                                        S.op("pe", lambda e: e.matmul(py.t[:, 0:n], lhsT=CW.t[:, ws, d_, q, :], rhs=uu.t[:, 0:n],
                                                                      start=(q4 == 0 and j_ == 0), stop=(q4 == 3 and j_ == 3)), reads=[CW.b, uu.b], writes=[py.b])
                                    if q4 == 3:
                                        S.op("dve", lambda e: e.tensor_tensor(out=acc.t[:, cols], in0=acc.t[:, cols], in1=py.t[:, 0:n], op=ALU.add), reads=[acc.b, py.b], writes=[acc.b])
                                pend.append(readout)
                        for f_ in pend:
                            f_()
                        del pend[:]
                    for a0 in range(0, T, 512):
                        w_ = min(512, T - a0)
                        i = (a0 // 512) % 2
                        t1, t2 = W["t1"][i], W["t2"][i]
                        yo = ygt[i]
                        S.op("pool", lambda e: e.tensor_tensor(out=t1.t[:, 0:w_], in0=acc.t[:, a0:a0 + w_], in1=acc.t[:, a0:a0 + w_], op=ALU.mult), reads=[acc.b], writes=[t1.b])
                        S.op("pool", lambda e: e.tensor_scalar(out=t1.t[:, 0:w_], in0=t1.t[:, 0:w_], scalar1=0.044715, scalar2=1.0, op0=ALU.mult, op1=ALU.add), reads=[t1.b], writes=[t1.b])
                        S.op("pool", lambda e: e.tensor_tensor(out=t1.t[:, 0:w_], in0=t1.t[:, 0:w_], in1=acc.t[:, a0:a0 + w_], op=ALU.mult), reads=[t1.b, acc.b], writes=[t1.b])
                        S.op("act", lambda e: e.activation(out=t2.t[:, 0:w_], in_=t1.t[:, 0:w_], func=AF.Sigmoid, scale=1.5957691216057308), reads=[t1.b], writes=[t2.b])
                        S.op("dve", lambda e: e.tensor_tensor(out=yo.t[:, 0:w_], in0=t2.t[:, 0:w_], in1=acc.t[:, a0:a0 + w_], op=ALU.mult), reads=[t2.b, acc.b], writes=[yo.b])
                        S.dma(ygs[:, c, a0:a0 + w_], yo.t[:, 0:w_], reads=[yo.b], writes=[db("ygs", (c, a0 // 512))], key=yo.b.name)
                yield "MAIN_DONE"
                S.barrier()
                wk.close()
                pY = [pst(ph, "pYg%d" % i) for i in range(2)]
                W = {"t2": [sbt(ph, "gsig%d" % i, [128, 512], F32) for i in range(2)]}
                so = [sbt(ph, "so%d" % i, [128, 4, 512], BF16) for i in range(2)]
                ygl = [sbt(ph, "ygl%d" % i, [128, 4, 512], BF16) for i in range(2)]
                for (si, t0, n, isc) in STS:
                    if isc and not do_ctx:
                        continue
                    o_ = so[si % 2]
                    yg = ygl[si % 2]
                    S.dma(yg.t[:, :, 0:n], ygs[:, :, t0:t0 + n], reads=[db("ygs", (c_, (t0 + j_) // 512)) for c_ in range(4) for j_ in (0, n - 1)],
                          writes=[yg.b], key=yg.b.name)
                    for oc in range(4):
                        py = pY[oc % 2]
                        for kc in range(4):
                            S.op("pe", lambda e: e.matmul(py.t[:, 0:n], lhsT=Wg.t[:, kc, oc * 128:(oc + 1) * 128], rhs=yg.t[:, kc, 0:n], start=(kc == 0), stop=(kc == 3)),
                                 reads=[Wg.b, yg.b], writes=[py.b])
                        t2 = W["t2"][oc % 2]
                        S.op("act", lambda e: e.activation(out=t2.t[:, 0:n], in_=py.t[:, 0:n], func=AF.Sigmoid, bias=dcol.t[:, 4 + oc:5 + oc], scale=1.0), reads=[py.b, dcol.b], writes=[t2.b])
                        S.op("dve", lambda e: e.tensor_tensor(out=o_.t[:, oc, 0:n], in0=t2.t[:, 0:n], in1=yg.t[:, oc, 0:n], op=ALU.mult), reads=[t2.b, yg.b], writes=[o_.b])
                    S.dma(s5O[:, :, t0:t0 + n], o_.t[:, :, 0:n], reads=[o_.b], writes=[db("s5O", si)], key=o_.b.name)
            S.barrier()

        def phase_E(l, hsrc, hdst, do_ctx):
            with contextlib.ExitStack() as ph:
                Wgt = sbt(ph, "Wgt", [128, 8, 3072], BF16)
                wsrc = w_in[l].rearrange("(kc p) n -> p kc n", p=128)
                for r in range(3):
                    load_cast(lambda a0, a1: (Wgt.t[:, a0:a1, r * 1024:(r + 1) * 1024], Wgt.b), wsrc[:, :, O_G + r * 1024:O_G + (r + 1) * 1024], 8, 1024,
                              eng=("pool" if r % 2 == 0 else "act"))
                Wb = sbt(ph, "Wb", [128, 3, 4, 1024], BF16)
                for r in range(3):
                    load_cast(lambda a0, a1: (Wb.t[:, r, a0:a1, :], Wb.b), w_br[l, r].rearrange("(kc p) n -> p kc n", p=128), 4, 1024,
                              eng=("act" if r % 2 == 0 else "pool"))
                Wo = sbt(ph, "Wo", [128, 8, 1024], BF16)
                load_cast(lambda a0, a1: (Wo.t[:, a0:a1, :], Wo.b), w_o[l].rearrange("(kc p) n -> p kc n", p=128), 8, 1024)
                hT = [sbt(ph, "hT%d" % i, [128, 8, 512], BF16) for i in range(2)]
                oB = [sbt(ph, "oB%d" % i, [128, 3, 4, 512], BF16) for i in range(2)]
                xt1 = sbt(ph, "xt", [128, 8, 512], F32)
                xt = [xt1, xt1]
                mT = sbt(ph, "mT", [128, 8, 512], BF16)
                sg = [sbt(ph, "sg%d" % i, [128, 512], BF16) for i in range(3)]
                gp = [sbt(ph, "gp%d" % i, [128, 512], F32) for i in range(3)]
                hn = xt1
                sq = sbt(ph, "sq", [128, 8, 512], F32)
                rs = sbt(ph, "rs", [128, 512], F32)
                fT = sbt(ph, "fT", [128, 8, 512], BF16)
                pG = [pst(ph, "pG%d" % i) for i in range(3)]
                pP = [pst(ph, "pP%d" % i) for i in range(3)]
                pM = pst(ph, "pM")
                pss = pst(ph, "pss")
                srcs = [(mlaO.rearrange("(kc p) t -> p kc t", p=128), "mlaO"), (s5O, "s5O"), (swaO.rearrange("(kc p) t -> p kc t", p=128), "swaO")]
                sts = [s for s in STS if (not s[3]) or do_ctx]

                def loads(k):
                    (si, t0, n, isc) = sts[k]
                    S.dma(hT[k % 2].t[:, :, 0:n], hTs[:, :, t0:t0 + n], reads=[db("hTs", si)], writes=[hT[k % 2].b], key=hT[k % 2].b.name)
                    for r in range(3):
                        if srcs[r][1] == "swaO":
                            rd = [db("swaO", (kh_, t0 // 128 + j)) for j in range(n // 128) for kh_ in range(2)]
                        elif srcs[r][1] == "mlaO":
                            rd = [db("mlaO", (h_i, si)) for h_i in range(8)]
                        else:
                            rd = [db(srcs[r][1], si)]
                        S.dma(oB[k % 2].t[:, r, :, 0:n], srcs[r][0][:, :, t0:t0 + n], reads=rd, writes=[oB[k % 2].b], key=oB[k % 2].b.name)

                def loadx(k):
                    (si, t0, n, isc) = sts[k]
                    src = hsrc.rearrange("(kc p) t -> p kc t", p=128)
                    S.dma(xt1.t[:, :, 0:n], src[:, :, t0:t0 + n], reads=[db("hsrc%d" % id(hsrc), si)], writes=[xt1.b], key=xt1.b.name)

                loads(0)
                for k, (si, t0, n, isc) in enumerate(sts):
                    loadx(k)
                    if k + 1 < len(sts):
                        loads(k + 1)
                    s = 1 if isc else 0
                    h_, o_, x_ = hT[k % 2], oB[k % 2], xt[k % 2]
                    for oc in range(8):
                        for r in range(3):
                            for kc in range(8):
                                S.op("pe", lambda e: e.matmul(pG[r].t[:, 0:n], lhsT=Wgt.t[:, kc, r * 1024 + oc * 128:r * 1024 + (oc + 1) * 128], rhs=h_.t[:, kc, 0:n],
                                                              start=(kc == 0), stop=(kc == 7)), reads=[Wgt.b, h_.b], writes=[pG[r].b])
                            for kc in range(4):
                                S.op("pe", lambda e: e.matmul(pP[r].t[:, 0:n], lhsT=Wb.t[:, r, kc, oc * 128:(oc + 1) * 128], rhs=o_.t[:, r, kc, 0:n],
                                                              start=(kc == 0), stop=(kc == 3)), reads=[Wb.b, o_.b], writes=[pP[r].b])
                        for r in range(3):
                            S.op("act", lambda e: e.activation(out=sg[r].t[:, 0:n], in_=pG[r].t[:, 0:n], func=AF.Sigmoid), reads=[pG[r].b], writes=[sg[r].b])
                            S.op("dve", lambda e: e.tensor_tensor(out=gp[r].t[:, 0:n], in0=pP[r].t[:, 0:n], in1=sg[r].t[:, 0:n], op=ALU.mult),
                                 reads=[pP[r].b, sg[r].b], writes=[gp[r].b])
                        S.op("pool", lambda e: e.tensor_tensor(out=gp[0].t[:, 0:n], in0=gp[0].t[:, 0:n], in1=gp[1].t[:, 0:n], op=ALU.add), reads=[gp[0].b, gp[1].b], writes=[gp[0].b])
                        S.op("pool", lambda e: e.tensor_tensor(out=mT.t[:, oc, 0:n], in0=gp[0].t[:, 0:n], in1=gp[2].t[:, 0:n], op=ALU.add), reads=[gp[0].b, gp[2].b], writes=[mT.b])
                    for oc in range(8):
                        for kc in range(8):
                            S.op("pe", lambda e: e.matmul(pM.t[:, 0:n], lhsT=Wo.t[:, kc, oc * 128:(oc + 1) * 128], rhs=mT.t[:, kc, 0:n], start=(kc == 0), stop=(kc == 7)),
                                 reads=[Wo.b, mT.b], writes=[pM.b])
                        S.op("dve", lambda e: e.scalar_tensor_tensor(out=hn.t[:, oc, 0:n], in0=pM.t[:, 0:n], scalar=ada.t[:, 16 + oc, s:s + 1], in1=x_.t[:, oc, 0:n],
                                                                     op0=ALU.mult, op1=ALU.add), reads=[pM.b, ada.b, x_.b], writes=[hn.b])
                    dst = hdst.rearrange("(kc p) t -> p kc t", p=128)
                    S.dma(dst[:, :, t0:t0 + n], hn.t[:, :, 0:n], reads=[hn.b], writes=[db("hsrc%d" % id(hdst), si)], key="hn_st")
                    norm_mod(ph, hn, n, modA2, 24, isc, sq, pss, rs, fT)
                    S.dma(fTs[:, :, t0:t0 + n], fT.t[:, :, 0:n], reads=[fT.b], writes=[db("fTs", si)], key="fT_st")
            S.barrier()

        def phase_F1(l, do_ctx):
            with contextlib.ExitStack() as ph:
                Wf = sbt(ph, "Wf", [128, 8, 2 * FH], BF16)
                wsrc = f_in[l].rearrange("(kc p) n -> p kc n", p=128)
                for cb in range(11):
                    load_cast(lambda a0, a1: (Wf.t[:, a0:a1, cb * 512:(cb + 1) * 512], Wf.b), wsrc[:, :, cb * 512:(cb + 1) * 512], 8, 512,
                              eng=("pool" if cb % 2 == 0 else "act"))
                fT = [sbt(ph, "fT%d" % i, [128, 8, 512], BF16) for i in range(2)]
                aT = [sbt(ph, "aT%d" % i, [128, 22, 512], BF16) for i in range(2)]
                sa = [sbt(ph, "sa%d" % i, [128, 512], F32) for i in range(2)]
                pA = [pst(ph, "pA%d" % i) for i in range(3)]
                pGt = [pst(ph, "pGt%d" % i) for i in range(3)]
                sts = [s for s in STS if (not s[3]) or do_ctx]
                S.dma(fT[0].t[:, :, 0:sts[0][2]], fTs[:, :, sts[0][1]:sts[0][1] + sts[0][2]], reads=[db("fTs", sts[0][0])], writes=[fT[0].b], key=fT[0].b.name)
                for k, (si, t0, n, isc) in enumerate(sts):
                    if k + 1 < len(sts):
                        (si2, t02, n2, _) = sts[k + 1]
                        S.dma(fT[(k + 1) % 2].t[:, :, 0:n2], fTs[:, :, t02:t02 + n2], reads=[db("fTs", si2)], writes=[fT[(k + 1) % 2].b], key=fT[(k + 1) % 2].b.name)
                    f_, a_ = fT[k % 2], aT[k % 2]
                    for hc in range(22):
                        pa_, pg_ = pA[hc % 3], pGt[hc % 3]
                        for kc in range(8):
                            S.op("pe", lambda e: e.matmul(pa_.t[:, 0:n], lhsT=Wf.t[:, kc, hc * 128:(hc + 1) * 128], rhs=f_.t[:, kc, 0:n], start=(kc == 0), stop=(kc == 7)),
                                 reads=[Wf.b, f_.b], writes=[pa_.b])
                        for kc in range(8):
                            S.op("pe", lambda e: e.matmul(pg_.t[:, 0:n], lhsT=Wf.t[:, kc, FH + hc * 128:FH + (hc + 1) * 128], rhs=f_.t[:, kc, 0:n], start=(kc == 0), stop=(kc == 7)),
                                 reads=[Wf.b, f_.b], writes=[pg_.b])
                        s_ = sa[hc % 2]
                        S.op("act", lambda e: e.activation(out=s_.t[:, 0:n], in_=pa_.t[:, 0:n], func=AF.Silu), reads=[pa_.b], writes=[s_.b])
                        S.op("dve", lambda e: e.tensor_tensor(out=a_.t[:, hc, 0:n], in0=pg_.t[:, 0:n], in1=s_.t[:, 0:n], op=ALU.mult), reads=[pg_.b, s_.b], writes=[a_.b])
                    S.dma(actTs[:, :, t0:t0 + n], a_.t[:, :, 0:n], reads=[a_.b], writes=[db("actTs", si)], key=a_.b.name)
            S.barrier()

        def phase_F2(l, hsrc, hdst, do_ctx, final):
            with contextlib.ExitStack() as ph:
                Wfo = sbt(ph, "Wfo", [128, 22, 1024], BF16)
                load_cast(lambda a0, a1: (Wfo.t[:, a0:a1, :], Wfo.b), f_out[l].rearrange("(kc p) n -> p kc n", p=128), 22, 1024)
                aT = [sbt(ph, "aT%d" % i, [128, 22, 512], BF16) for i in range(2)]
                xt = [sbt(ph, "xt%d" % i, [128, 8, 512], F32) for i in range(2)]
                ho = [sbt(ph, "ho%d" % i, [128, 8, 512], F32) for i in range(2)]
                pO = [pst(ph, "pO%d" % i) for i in range(3)]
                sts = [s for s in STS if (not s[3]) or do_ctx]
                src = hsrc.rearrange("(kc p) t -> p kc t", p=128)

                def loads(k):
                    (si, t0, n, isc) = sts[k]
                    S.dma(aT[k % 2].t[:, :, 0:n], actTs[:, :, t0:t0 + n], reads=[db("actTs", si)], writes=[aT[k % 2].b], key=aT[k % 2].b.name)
                    S.dma(xt[k % 2].t[:, :, 0:n], src[:, :, t0:t0 + n], reads=[db("hsrc%d" % id(hsrc), si)], writes=[xt[k % 2].b], key=xt[k % 2].b.name)
                loads(0)
                for k, (si, t0, n, isc) in enumerate(sts):
                    if k + 1 < len(sts):
                        loads(k + 1)
                    s = 1 if isc else 0
                    a_, x_, h_ = aT[k % 2], xt[k % 2], ho[k % 2]
                    for oc in range(8):
                        po = pO[oc % 3]
                        for hc in range(22):
                            S.op("pe", lambda e: e.matmul(po.t[:, 0:n], lhsT=Wfo.t[:, hc, oc * 128:(oc + 1) * 128], rhs=a_.t[:, hc, 0:n], start=(hc == 0), stop=(hc == 21)),
                                 reads=[Wfo.b, a_.b], writes=[po.b])
                        S.op("dve", lambda e: e.scalar_tensor_tensor(out=h_.t[:, oc, 0:n], in0=po.t[:, 0:n], scalar=ada.t[:, 40 + oc, s:s + 1], in1=x_.t[:, oc, 0:n],
                                                                     op0=ALU.mult, op1=ALU.add), reads=[po.b, ada.b, x_.b], writes=[h_.b])
                    if final:
                        dst = outT.rearrange("(kc p) t -> p kc t", p=128)
                        S.dma(dst[:, :, t0 - NC_:t0 - NC_ + n], h_.t[:, :, 0:n], reads=[h_.b], writes=[db("outT", si)], key=h_.b.name)
                    else:
                        dst = hdst.rearrange("(kc p) t -> p kc t", p=128)
                        S.dma(dst[:, :, t0:t0 + n], h_.t[:, :, 0:n], reads=[h_.b], writes=[db("hsrc%d" % id(hdst), si)], key=h_.b.name)
            S.barrier()

        def run_gen(g_):
            for _ in g_:
                pass

        def phase_BCD(l, do_ctx):
            gD = phase_D(l, do_ctx, ov=True)
            others = [phase_B(l, do_ctx, ov=True), phase_C(l, do_ctx, ov=True)]
            cur = 0
            while True:
                r = next(gD)
                if r == "MAIN_DONE":
                    break
                for _ in range(OV_K):
                    if cur < len(others):
                        try:
                            next(others[cur])
                        except StopIteration:
                            cur += 1
            while cur < len(others):
                run_gen(others[cur])
                cur += 1
            run_gen(gD)

        cur = xT
        for l in range(nlayers):
            last = (l == L - 1)
            do_ctx = not last
            if "a" in phases:
                phase_ada(l)
            if "A" in phases:
                phase_A(l, cur, do_ctx)
            if stop_after == "A":
                break
            if OVERLAP and "B" in phases and "C" in phases and "D" in phases:
                phase_BCD(l, do_ctx)
            else:
                if "B" in phases:
                    run_gen(phase_B(l, do_ctx))
                if stop_after == "B":
                    break
                if "C" in phases:
                    run_gen(phase_C(l, do_ctx))
                if stop_after == "C":
                    break
                if "D" in phases:
                    run_gen(phase_D(l, do_ctx))
            if stop_after == "D":
                break
            if "E" in phases:
                phase_E(l, cur, hres[0], do_ctx)
            if stop_after == "E":
                break
            if "F" in phases:
                phase_F1(l, do_ctx)
                phase_F2(l, hres[0], hres[1], do_ctx, last)
            cur = hres[1]
        S.barrier()
        g.stats = (S.nins, S.nwait, len(S.semh))
    return nc, dbg_names, g


def _consts():
    n_axis = 8
    freqs = (10000.0 ** (-np.arange(n_axis, dtype=np.float32) / n_axis)).astype(np.float32)
    rows = NX // 64
    row = np.repeat(np.arange(rows, dtype=np.float32), 64)
    col = np.tile(np.arange(64, dtype=np.float32), rows)
    angM = np.concatenate([row[:, None] * freqs, col[:, None] * freqs], axis=-1).astype(np.float32)
    n_axis_s = 16
    freqs_s = (10000.0 ** (-np.arange(n_axis_s, dtype=np.float32) / n_axis_s)).astype(np.float32)
    angS = np.concatenate([row[:, None] * freqs_s, col[:, None] * freqs_s], axis=-1).astype(np.float32)
    cMc = np.ones((96, NX), np.float32); cMs = np.zeros((96, NX), np.float32)
    cMc[64:80] = np.cos(angM).T; cMc[80:96] = np.cos(angM).T
    cMs[64:80] = np.sin(angM).T; cMs[80:96] = np.sin(angM).T
    cSc = np.zeros((128, NX), np.float32); cSs = np.zeros((128, NX), np.float32)
    for hh in range(2):
        for half in range(2):
            r0 = hh * 64 + half * 32
            cSc[r0:r0 + 32] = np.cos(angS).T
            cSs[r0:r0 + 32] = np.sin(angS).T
    cPm = np.zeros((96, 96), np.float32)
    for i in range(16):
        cPm[80 + i, 64 + i] = -1.0
        cPm[64 + i, 80 + i] = 1.0
    cPs = np.zeros((128, 128), np.float32)
    for hh in range(2):
        for i in range(32):
            cPs[hh * 64 + 32 + i, hh * 64 + i] = -1.0
            cPs[hh * 64 + i, hh * 64 + 32 + i] = 1.0
    cShift = np.zeros((32, 96), np.float32)
    for i in range(32):
        cShift[i, 64 + i] = 1.0
    j = np.arange(128)[:, None]; i = np.arange(128)[None, :]
    cMge = (j >= i).astype(np.float32)
    cMle = (j <= i).astype(np.float32)
    cBones = np.zeros((128, 128), np.float32)
    cBones[0:64, 0:64] = 1.0; cBones[64:128, 64:128] = 1.0
    cIota = np.tile(np.arange(512, dtype=np.float32)[None, :], (128, 1))
    cId = np.eye(128, dtype=np.float32)
    return dict(cMc=cMc, cMs=cMs, cSc=cSc, cSs=cSs, cPm=cPm, cPs=cPs, cShift=cShift, cMge=cMge, cMle=cMle,
                cBones=cBones, cIota=cIota, cId=cId)


def _colv(v, nch):
    return np.ascontiguousarray(np.transpose(v.reshape(v.shape[0], nch, 128), (0, 2, 1)))


def _shared_inputs(inp):
    f = lambda a: np.ascontiguousarray(np.asarray(a, dtype=np.float32))
    sh = {}
    sh["w_ada"] = f(inp["w_ada"]); sh["badac"] = _colv(f(inp["b_ada"]), 48)
    sh["g1c"] = _colv(f(inp["norm1_g"]), 8); sh["g2c"] = _colv(f(inp["norm2_g"]), 8)
    sh["w_in"] = f(inp["w_in"])
    sh["qagc"] = _colv(f(inp["mla_qa_g"]), 3); sh["kvagc"] = _colv(f(inp["mla_kva_g"]), 2)
    sh["w_uq"] = f(inp["mla_w_uq"]); sh["w_ukv"] = f(inp["mla_w_ukv"])
    sh["mqkg"] = np.ascontiguousarray(np.transpose(f(inp["mla_qk_g"]), (0, 2, 1)))
    sw = np.transpose(f(inp["swa_qk_g"]), (0, 2, 1))
    sh["swag"] = np.ascontiguousarray(np.concatenate([sw, sw], axis=1))
    sh["sinkb"] = np.ascontiguousarray(np.broadcast_to(f(inp["swa_sink"])[:, None, :], (L, 128, 8)))

    def pairlay(a):
        return np.ascontiguousarray(np.transpose(a.reshape(L, 2, 16, 2, 64), (0, 3, 4, 1, 2)).reshape(L, 128, 2, 16))
    sh["s5lre"] = pairlay(f(inp["s5_lambda_re"])); sh["s5lim"] = pairlay(f(inp["s5_lambda_im"]))
    ls = np.broadcast_to(f(inp["s5_log_step"])[:, :, :, None], (L, 2, 32, 64))
    sh["s5ls"] = pairlay(np.ascontiguousarray(ls))

    def pairlay_b(a):
        return np.ascontiguousarray(np.transpose(a.reshape(L, 2, 16, 2, 64, 16), (0, 3, 4, 1, 2, 5)).reshape(L, 128, 2, 16, 16))
    sh["s5bre"] = pairlay_b(f(inp["s5_b_re"])); sh["s5bim"] = pairlay_b(f(inp["s5_b_im"]))
    sh["s5cre"] = pairlay_b(np.ascontiguousarray(np.transpose(f(inp["s5_c_re"]), (0, 1, 2, 4, 3))))
    sh["s5cim"] = pairlay_b(np.ascontiguousarray(np.transpose(f(inp["s5_c_im"]), (0, 1, 2, 4, 3))))
    sh["s5dc"] = _colv(f(inp["s5_d"]), 4); sh["s5bgc"] = _colv(f(inp["s5_b_glu"]), 4)
    sh["w_glu"] = f(inp["s5_w_glu"]); sh["w_br"] = f(inp["w_branch"]); sh["w_o"] = f(inp["w_out"])
    sh["f_in"] = f(inp["ffn_w_in"]); sh["f_out"] = f(inp["ffn_w_out"])
    sh.update(_consts())
    return sh


def _core_inputs(inp, b, sh):
    x = np.asarray(inp["x"][b], np.float32); ctx = np.asarray(inp["ctx"][b], np.float32)
    m = dict(sh)
    m["xT"] = np.ascontiguousarray(np.concatenate([ctx, x], axis=0).T)
    cc = np.stack([np.asarray(inp["c"][b], np.float32), np.asarray(inp["c_ctx"], np.float32)], axis=-1)
    m["ccol"] = np.ascontiguousarray(np.transpose(cc.reshape(8, 128, 2), (1, 0, 2)))
    return m


_CACHE = {}


def kernel(**inputs):
    if "nc" not in _CACHE:
        _CACHE["nc"] = build()[0]
    nc = _CACHE["nc"]
    sh = _shared_inputs(inputs)
    in_maps = [_core_inputs(inputs, b, sh) for b in range(8)]
    res = run_bass_kernel_spmd(nc, in_maps, core_ids=list(range(8)))
    out = np.stack([np.ascontiguousarray(r["outT"].T) for r in res.results], axis=0)
    return out.astype(np.float32)
```

```python
import contextlib
import math
import numpy as np
import concourse.bass as bass
import concourse.mybir as mybir
from concourse.bass_utils import run_bass_kernel_spmd

F32 = mybir.dt.float32
BF16 = mybir.dt.bfloat16
I32 = mybir.dt.int32
AF = mybir.ActivationFunctionType
ALU = mybir.AluOpType

D = 1024
NX = 4096
NC_ = 256
T = NX + NC_
L = 2
EPS = 1e-6
FH = 2816
DIN = 5024
O_KR = 640
O_U = 672
O_SQ = 1184
O_SK = 1696
O_SV = 1824
O_G = 1952
import os as _os
D_UENG = _os.environ.get("UENG", "dve,dve,dve,dve").split(",")
D_XENG = _os.environ.get("XENG", "pool")
D_TENG = _os.environ.get("TENG", "dve,dve,dve,dve").split(",")
D_EVAC = _os.environ.get("D_EVAC", "0") == "1"
TWO_PI = 2.0 * math.pi

STS = [(0, 0, 256, True)] + [(1 + k, 256 + 512 * k, 512, False) for k in range(8)]


class Buf:
    __slots__ = ("name", "w", "r")

    def __init__(self, name):
        self.name = name
        self.w = {}
        self.r = {}


def _merge(d, s):
    for k, v in s.items():
        if d.get(k, 0) < v:
            d[k] = v


class Sched:
    def __init__(self, nc, stack):
        self.nc = nc
        self.stack = stack
        self.E = {"pe": nc.tensor, "act": nc.scalar, "dve": nc.vector,
                  "pool": nc.gpsimd, "sp": nc.sync}
        self.semh = {}
        self.cnt = {}
        self.seen = {k: {} for k in self.E}
        for k in self.E:
            self.semh[k] = stack.enter_context(nc.semaphore("s_" + k))
            self.cnt[k] = 0
        self.nins = 0
        self.nwait = 0
        self.alias = {}
        self.free = []
        self.dkeys = []

    def _sem(self, key):
        if key in self.alias:
            return self.alias[key]
        if self.free:
            ck = self.free.pop()
        else:
            ck = "dq%d" % len(self.dkeys)
            self.dkeys.append(ck)
            self.semh[ck] = self.stack.enter_context(self.nc.semaphore(ck))
            self.cnt[ck] = 0
        self.alias[key] = ck
        return ck

    def _wait(self, eng, deps):
        seen = self.seen[eng]
        for key, val in deps.items():
            if val <= 0 or seen.get(key, 0) >= val:
                continue
            self.E[eng].wait_ge(self.semh[key], val)
            seen[key] = val
            self.nwait += 1

    def _deps(self, reads, writes):
        deps = {}
        for b in reads:
            _merge(deps, b.w)
        for b in writes:
            _merge(deps, b.w)
            _merge(deps, b.r)
        return deps

    def op(self, eng, fn, reads=(), writes=()):
        deps = self._deps(reads, writes)
        if eng == "pe":
            deps.pop("pe", None)
        self._wait(eng, deps)
        ins = fn(self.E[eng])
        self.cnt[eng] += 1
        v = self.cnt[eng]
        ins.then_inc(self.semh[eng], 1)
        for b in reads:
            if b.r.get(eng, 0) < v:
                b.r[eng] = v
        for b in writes:
            b.w = {eng: v}
            b.r = {}
        self.nins += 1
        return ins

    def dma(self, out, in_, reads=(), writes=(), key=None, eng="sp"):
        key = self._sem(key)
        deps = self._deps(reads, writes)
        deps.pop(key, None)
        self._wait(eng, deps)
        ins = self.E[eng].dma_start(out=out, in_=in_)
        self.cnt[key] += 16
        v = self.cnt[key]
        ins.then_inc(self.semh[key], 16)
        for b in reads:
            if b.r.get(key, 0) < v:
                b.r[key] = v
        for b in writes:
            b.w = {key: v}
            b.r = {}
        self.nins += 1
        return ins

    def barrier(self):
        deps = {k: v for k, v in self.cnt.items() if v > 0}
        for e in self.E:
            self._wait(e, deps)
        self.alias = {}
        self.free = list(self.dkeys)


class TT:
    __slots__ = ("t", "b")

    def __init__(self, t, b):
        self.t = t
        self.b = b


class Ctx:
    pass


def build(dbg=False, nlayers=L, stop_after=None, phases="aABCDEF"):
    nc = bass.Bass("TRN2", target_bir_lowering=False)
    g = Ctx()
    g.nc = nc

    def din(name, shape, dt=F32):
        return nc.dram_tensor(name, list(shape), dt, kind="ExternalInput").ap()

    dbg_names = []

    def dscr(name, shape, dt):
        if dbg:
            dbg_names.append(name)
            return nc.dram_tensor(name, list(shape), dt, kind="ExternalOutput").ap()
        return nc.dram_tensor(name, list(shape), dt).ap()

    xT = din("xT", [D, T])
    ccol = din("ccol", [128, 8, 2])
    w_ada = din("w_ada", [L, D, 6 * D])
    badac = din("badac", [L, 128, 48])
    g1c = din("g1c", [L, 128, 8])
    g2c = din("g2c", [L, 128, 8])
    w_in = din("w_in", [L, D, DIN])
    qagc = din("qagc", [L, 128, 3])
    kvagc = din("kvagc", [L, 128, 2])
    w_uq = din("w_uq", [L, 384, 768])
    w_ukv = din("w_ukv", [L, 256, 1024])
    mqkg = din("mqkg", [L, 96, 2])
    swag = din("swag", [L, 128, 2])
    sinkb = din("sinkb", [L, 128, 8])
    s5lre = din("s5lre", [L, 128, 2, 16])
    s5lim = din("s5lim", [L, 128, 2, 16])
    s5ls = din("s5ls", [L, 128, 2, 16])
    s5bre = din("s5bre", [L, 128, 2, 16, 16])
    s5bim = din("s5bim", [L, 128, 2, 16, 16])
    s5cre = din("s5cre", [L, 128, 2, 16, 16])
    s5cim = din("s5cim", [L, 128, 2, 16, 16])
    s5dc = din("s5dc", [L, 128, 4])
    s5bgc = din("s5bgc", [L, 128, 4])
    w_glu = din("w_glu", [L, 512, 512])
    w_br = din("w_br", [L, 3, 512, D])
    w_o = din("w_o", [L, D, D])
    f_in = din("f_in", [L, D, 2 * FH])
    f_out = din("f_out", [L, FH, D])
    cMc = din("cMc", [96, NX]); cMs = din("cMs", [96, NX])
    cSc = din("cSc", [128, NX]); cSs = din("cSs", [128, NX])
    cPm = din("cPm", [96, 96]); cPs = din("cPs", [128, 128])
    cShift = din("cShift", [32, 96])
    cMge = din("cMge", [128, 128]); cMle = din("cMle", [128, 128])
    cBones = din("cBones", [128, 128])
    cIota = din("cIota", [128, 512])
    cId = din("cId", [128, 128])

    outT = nc.dram_tensor("outT", [D, NX], F32, kind="ExternalOutput").ap()

    hres = [dscr("hres%d" % i, [D, T], F32) for i in range(2)]
    hTs = dscr("hTs", [128, 8, T], BF16)
    QTs = dscr("QTs", [8, 96, T], BF16)
    KTs = dscr("KTs", [8, 96, T], BF16)
    VsM = dscr("VsM", [34, 128, 8, 128], BF16)
    uTs = dscr("uTs", [128, 4, T], BF16)
    sQs = dscr("sQs", [64, 8, T], BF16)
    sKs = dscr("sKs", [64, 2, T], BF16)
    sVs = dscr("sVs", [34, 128, 2, 128], BF16)
    mlaO = dscr("mlaO", [512, T], BF16)
    swaO = dscr("swaO", [512, T], BF16)
    s5O = dscr("s5O", [128, 4, T], BF16)
    ygs = dscr("ygs", [128, 4, T], BF16)
    fTs = dscr("fTs", [128, 8, T], BF16)
    actTs = dscr("actTs", [128, 22, T], BF16)

    dbufs = {}

    def db(name, idx=0):
        k = (name, idx)
        if k not in dbufs:
            dbufs[k] = Buf("%s_%s" % (name, idx))
        return dbufs[k]

    with contextlib.ExitStack() as top:
        S = Sched(nc, top)
        uid = [0]

        def sbt(ctx, name, shape, dt):
            uid[0] += 1
            nm = "%s_%d" % (name, uid[0])
            t = ctx.enter_context(nc.sbuf_tensor(nm, list(shape), dt))
            return TT(t, Buf(nm))

        def pst(ctx, name, shape=(128, 512), dt=F32):
            uid[0] += 1
            nm = "%s_%d" % (name, uid[0])
            t = ctx.enter_context(nc.psum_tensor(nm, list(shape), dt))
            return TT(t, Buf(nm))

        ones32 = sbt(top, "ones32", [128, 128], F32)
        S.op("pool", lambda e: e.memset(ones32.t[:], 1.0), writes=[ones32.b])
        bones32 = sbt(top, "bones32", [128, 128], F32)
        S.dma(bones32.t[:], cBones[:, :], writes=[bones32.b], key="c_bones")
        stgc = sbt(top, "stgc", [128, 128], F32)

        def const_bf(name, src, rows, cols):
            t = sbt(top, name, [rows, cols], BF16)
            S.dma(stgc.t[0:rows, 0:cols], src[:, :], writes=[stgc.b], key="c_stg")
            S.op("dve", lambda e: e.tensor_copy(out=t.t[:], in_=stgc.t[0:rows, 0:cols]),
                 reads=[stgc.b], writes=[t.b])
            return t

        Pm = const_bf("Pm", cPm, 96, 96)
        Ps = const_bf("Ps", cPs, 128, 128)
        shiftI = const_bf("shiftI", cShift, 32, 96)
        Mge = const_bf("Mge", cMge, 128, 128)
        Mle = const_bf("Mle", cMle, 128, 128)
        identb = const_bf("identb", cId, 128, 128)

        stg = [sbt(top, "stg%d" % i, [128, 2048], F32) for i in range(2)]
        stg_i = [0]

        def load_cast(dst_ap_fn, src3, A, B, scale_fn=None, eng="pool"):
            step = 1 if scale_fn is not None else max(1, 2048 // B)
            a0 = 0
            while a0 < A:
                a1 = min(A, a0 + step)
                s_ = stg[stg_i[0] % 2]
                stg_i[0] += 1
                na = a1 - a0
                view = s_.t[:, 0:na * B].rearrange("p (a b) -> p a b", b=B)
                S.dma(view, src3[:, a0:a1, :], writes=[s_.b], key=s_.b.name)
                dst, dbuf = dst_ap_fn(a0, a1)
                if scale_fn is None:
                    if eng == "act":
                        S.op(eng, lambda e: e.copy(out=dst, in_=view), reads=[s_.b], writes=[dbuf])
                    else:
                        S.op(eng, lambda e: e.tensor_copy(out=dst, in_=view), reads=[s_.b], writes=[dbuf])
                else:
                    assert na == 1
                    sc, scb = scale_fn(a0)
                    S.op(eng, lambda e: e.tensor_scalar(out=dst, in0=view, scalar1=sc, scalar2=None, op0=ALU.mult),
                         reads=[s_.b, scb], writes=[dbuf])
                a0 = a1

        def rsqrt_from(ctx_eng_out, out_t, in_ap, in_buf, scale, shape_ap=None):
            o = out_t.t[:] if shape_ap is None else shape_ap
            S.op("act", lambda e: e.activation(out=o, in_=in_ap, func=AF.Ln, bias=epsc.t[0:o.shape[0], 0:1], scale=scale),
                 reads=[in_buf, epsc.b], writes=[out_t.b])
            S.op("act", lambda e: e.activation(out=o, in_=o, func=AF.Exp, scale=-0.5), reads=[out_t.b], writes=[out_t.b])

        epsc = sbt(top, "epsc", [128, 1], F32)
        S.op("pool", lambda e: e.memset(epsc.t[:], EPS), writes=[epsc.b])

        modA1 = sbt(top, "modA1", [128, 8, 2], F32)
        modA2 = sbt(top, "modA2", [128, 8, 2], F32)
        ada = sbt(top, "ada", [128, 48, 2], F32)

        def phase_ada(l):
            with contextlib.ExitStack() as ph:
                cc = sbt(ph, "cc", [128, 8, 2], F32)
                S.dma(cc.t[:], ccol[:, :, :], writes=[cc.b], key="cc")
                sc = sbt(ph, "sc", [128, 8, 2], F32)
                S.op("act", lambda e: e.activation(out=sc.t[:], in_=cc.t[:], func=AF.Silu), reads=[cc.b], writes=[sc.b])
                bad = sbt(ph, "bad", [128, 48], F32)
                S.dma(bad.t[:], badac[l], writes=[bad.b], key="bad")
                gg = sbt(ph, "gg", [128, 16], F32)
                S.dma(gg.t[:, 0:8], g1c[l], writes=[gg.b], key="gg")
                S.dma(gg.t[:, 8:16], g2c[l], writes=[gg.b], key="gg")
                wa = [sbt(ph, "wa%d" % i, [128, 8, 512], F32) for i in range(2)]
                pa = pst(ph, "pa", [128, 96], F32)
                wsrc = w_ada[l].rearrange("(kc p) n -> p kc n", p=128)
                for cb in range(12):
                    w = wa[cb % 2]
                    S.dma(w.t[:], wsrc[:, :, cb * 512:(cb + 1) * 512], writes=[w.b], key=w.b.name)
                    for f4 in range(4):
                        fc = cb * 4 + f4
                        for kc in range(8):
                            S.op("pe", lambda e: e.matmul(pa.t[:, fc * 2:fc * 2 + 2], lhsT=w.t[:, kc, f4 * 128:(f4 + 1) * 128],
                                                          rhs=sc.t[:, kc, :], start=(kc == 0), stop=(kc == 7)),
                                 reads=[w.b, sc.b], writes=[pa.b])
                S.op("dve", lambda e: e.tensor_tensor(out=ada.t[:], in0=pa.t[:].rearrange("p (c s) -> p c s", s=2),
                                                      in1=bad.t[:].unsqueeze(2).to_broadcast([128, 48, 2]), op=ALU.add),
                     reads=[pa.b, bad.b], writes=[ada.b])
                for (mod, sc0, gofs) in ((modA1, 8, 0), (modA2, 32, 8)):
                    S.op("dve", lambda e: e.tensor_scalar(out=mod.t[:], in0=ada.t[:, sc0:sc0 + 8, :], scalar1=1.0, scalar2=None, op0=ALU.add),
                         reads=[ada.b], writes=[mod.b])
                    S.op("dve", lambda e: e.tensor_tensor(out=mod.t[:], in0=mod.t[:],
                                                          in1=gg.t[:, gofs:gofs + 8].unsqueeze(2).to_broadcast([128, 8, 2]), op=ALU.mult),
                         reads=[mod.b, gg.b], writes=[mod.b])
            S.barrier()

        def norm_mod(ph, xt, n, modA, shofs, si_ctx, sq, pss, rs, hT):
            s = 1 if si_ctx else 0
            S.op("act", lambda e: e.activation(out=sq.t[:, :, 0:n], in_=xt.t[:, :, 0:n], func=AF.Square), reads=[xt.b], writes=[sq.b])
            for kc in range(8):
                S.op("pe", lambda e: e.matmul(pss.t[:, 0:n], lhsT=ones32.t[:], rhs=sq.t[:, kc, 0:n], start=(kc == 0), stop=(kc == 7)),
                     reads=[ones32.b, sq.b], writes=[pss.b])
            rsqrt_from(None, rs, pss.t[:, 0:n], pss.b, 1.0 / D, shape_ap=rs.t[:, 0:n])
            S.op("dve", lambda e: e.tensor_tensor(out=sq.t[:, :, 0:n], in0=xt.t[:, :, 0:n],
                                                  in1=rs.t[:, 0:n].unsqueeze(1).to_broadcast([128, 8, n]), op=ALU.mult),
                 reads=[xt.b, rs.b], writes=[sq.b])
            for kc in range(8):
                S.op("act", lambda e: e.activation(out=hT.t[:, kc, 0:n], in_=sq.t[:, kc, 0:n], func=AF.Identity,
                                                   bias=ada.t[:, shofs + kc, s:s + 1], scale=modA.t[:, kc, s:s + 1]),
                     reads=[sq.b, ada.b, modA.b], writes=[hT.b])

        def phase_A(l, hsrc, do_ctx_q):
            with contextlib.ExitStack() as ph:
                Win = sbt(ph, "Win", [128, 8, O_G], BF16)
                wsrc = w_in[l].rearrange("(kc p) n -> p kc n", p=128)
                load_cast(lambda a0, a1: (Win.t[:, a0:a1, :], Win.b), wsrc[:, :, 0:O_G], 8, O_G)
                qag = sbt(ph, "qag", [128, 5], F32)
                S.dma(qag.t[:, 0:3], qagc[l], writes=[qag.b], key="qag")
                S.dma(qag.t[:, 3:5], kvagc[l], writes=[qag.b], key="qag")
                Wuq = sbt(ph, "Wuq", [128, 3, 768], BF16)
                load_cast(lambda a0, a1: (Wuq.t[:, a0:a1, :], Wuq.b), w_uq[l].rearrange("(kc p) n -> p kc n", p=128), 3, 768,
                          scale_fn=lambda a: (qag.t[:, a:a + 1], qag.b))
                Wkp = sbt(ph, "Wkp", [128, 2, 8, 96], BF16)
                S.op("pool", lambda e: e.memset(Wkp.t[:], 0.0), writes=[Wkp.b])
                Wv = sbt(ph, "Wv", [128, 2, 8, 64], BF16)
                ukv = w_ukv[l].rearrange("(kc p) n -> p kc n", p=128)
                for kc in range(2):
                    s_ = stg[stg_i[0] % 2]
                    stg_i[0] += 1
                    S.dma(s_.t[:, 0:1024], ukv[:, kc, :], writes=[s_.b], key=s_.b.name)
                    v3 = s_.t[:, 0:1024].rearrange("p (h c) -> p h c", c=128)
                    S.op("pool", lambda e: e.tensor_scalar(out=Wkp.t[:, kc, :, 0:64], in0=v3[:, :, 0:64], scalar1=qag.t[:, 3 + kc:4 + kc],
                                                           scalar2=None, op0=ALU.mult), reads=[s_.b, qag.b], writes=[Wkp.b])
                    S.op("pool", lambda e: e.tensor_scalar(out=Wv.t[:, kc, :, :], in0=v3[:, :, 64:128], scalar1=qag.t[:, 3 + kc:4 + kc],
                                                           scalar2=None, op0=ALU.mult), reads=[s_.b, qag.b], writes=[Wv.b])
                Wkd = sbt(ph, "Wkd", [128, 8, 2, 128], BF16)
                for kh in range(2):
                    for hf in range(2):
                        S.op("pool", lambda e: e.tensor_copy(out=Wkd.t[:, :, kh, hf * 64:(hf + 1) * 64],
                                                             in_=Win.t[:, :, O_SK + kh * 64:O_SK + (kh + 1) * 64]),
                             reads=[Win.b], writes=[Wkd.b])
                gq = sbt(ph, "gq", [128, 4], F32)
                S.dma(gq.t[0:96, 0:2], mqkg[l], writes=[gq.b], key="gq")
                S.dma(gq.t[:, 2:4], swag[l], writes=[gq.b], key="gq")

                xt = sbt(ph, "xt", [128, 8, 512], F32)
                sq = sbt(ph, "sq", [128, 8, 512], F32)
                rs = sbt(ph, "rs", [128, 512], F32)
                hT = sbt(ph, "hT", [128, 8, 512], BF16)
                q32 = sbt(ph, "q32", [128, 3, 512], F32)
                sqq = sbt(ph, "sqq", [128, 3, 512], F32)
                rq = sbt(ph, "rq", [128, 512], F32)
                qn = sbt(ph, "qn", [128, 3, 512], BF16)
                kvn = sbt(ph, "kvn", [128, 2, 512], BF16)
                krT = sbt(ph, "krT", [32, 512], BF16)
                uT = sbt(ph, "uT", [128, 4, 512], BF16)
                NH = 4
                hq32 = [sbt(ph, "hq32_%d" % i, [128, 512], F32) for i in range(NH)]
                hsq = [sbt(ph, "hsq_%d" % i, [128, 512], F32) for i in range(NH)]
                hqn = [sbt(ph, "hqn_%d" % i, [128, 512], F32) for i in range(NH)]
                hqb = [sbt(ph, "hqb_%d" % i, [128, 512], BF16) for i in range(NH)]
                hout = [sbt(ph, "hout_%d" % i, [128, 512], BF16) for i in range(NH)]
                va = [sbt(ph, "va_%d" % i, [128, 8, 128], BF16) for i in range(2)]
                sva = [sbt(ph, "sva_%d" % i, [128, 2, 128], BF16) for i in range(2)]
                for v_ in va + sva:
                    S.op("pool", lambda e: e.memset(v_.t[:], 1.0), writes=[v_.b])
                rope = sbt(ph, "rope", [128, 4, 512], F32)
                pss = pst(ph, "pss")
                pp = [pst(ph, "pp%d" % i) for i in range(2)]
                hp = [pst(ph, "hp%d" % i) for i in range(NH)]
                pv = pst(ph, "pv")
                cnt = [0]

                def headnorm(i, mm_fn, rows, gcol, onesT, dim, use_rope, Pmat, rc, rs_, n, dst_ap, dst_bufs):
                    ps, a32, asq, aqn, aqb, ao = hp[i], hq32[i], hsq[i], hqn[i], hqb[i], hout[i]
                    mm_fn(ps)
                    yield
                    S.op("act", lambda e: e.copy(out=a32.t[0:rows, 0:n], in_=ps.t[0:rows, 0:n]), reads=[ps.b], writes=[a32.b])
                    S.op("act", lambda e: e.activation(out=asq.t[0:rows, 0:n], in_=ps.t[0:rows, 0:n], func=AF.Square), reads=[ps.b], writes=[asq.b])
                    yield
                    S.op("pe", lambda e: e.matmul(ps.t[0:rows, 0:n], lhsT=onesT.t[0:rows, 0:rows], rhs=asq.t[0:rows, 0:n], start=True, stop=True),
                         reads=[onesT.b, asq.b], writes=[ps.b])
                    yield
                    rsqrt_from(None, asq, ps.t[0:rows, 0:n], ps.b, 1.0 / dim, shape_ap=asq.t[0:rows, 0:n])
                    yield
                    if not use_rope:
                        S.op("dve", lambda e: e.scalar_tensor_tensor(out=ao.t[0:rows, 0:n], in0=a32.t[0:rows, 0:n], scalar=gcol,
                                                                     in1=asq.t[0:rows, 0:n], op0=ALU.mult, op1=ALU.mult),
                             reads=[a32.b, asq.b, gq.b], writes=[ao.b])
                    else:
                        S.op("dve", lambda e: e.scalar_tensor_tensor(out=aqn.t[0:rows, 0:n], in0=a32.t[0:rows, 0:n], scalar=gcol,
                                                                     in1=asq.t[0:rows, 0:n], op0=ALU.mult, op1=ALU.mult),
                             reads=[a32.b, asq.b, gq.b], writes=[aqn.b])
                        yield
                        S.op("act", lambda e: e.copy(out=aqb.t[0:rows, 0:n], in_=aqn.t[0:rows, 0:n]), reads=[aqn.b], writes=[aqb.b])
                        yield
                        S.op("pe", lambda e: e.matmul(ps.t[0:rows, 0:n], lhsT=Pmat.t[0:rows, 0:rows], rhs=aqb.t[0:rows, 0:n], start=True, stop=True),
                             reads=[Pmat.b, aqb.b], writes=[ps.b])
                        S.op("pool", lambda e: e.tensor_tensor(out=a32.t[0:rows, 0:n], in0=aqn.t[0:rows, 0:n], in1=rope.t[0:rows, rc, 0:n], op=ALU.mult),
                             reads=[aqn.b, rope.b], writes=[a32.b])
                        yield
                        S.op("dve", lambda e: e.tensor_tensor(out=asq.t[0:rows, 0:n], in0=ps.t[0:rows, 0:n], in1=rope.t[0:rows, rs_, 0:n], op=ALU.mult),
                             reads=[ps.b, rope.b], writes=[asq.b])
                        yield
                        S.op("dve", lambda e: e.tensor_tensor(out=ao.t[0:rows, 0:n], in0=a32.t[0:rows, 0:n], in1=asq.t[0:rows, 0:n], op=ALU.add),
                             reads=[a32.b, asq.b], writes=[ao.b])
                    yield
                    if isinstance(dst_ap, list):
                        for (d_ap, r0, r1) in dst_ap:
                            S.dma(d_ap, ao.t[r0:r1, 0:n], reads=[ao.b], writes=dst_bufs, key=ao.b.name)
                    else:
                        S.dma(dst_ap, ao.t[0:rows, 0:n], reads=[ao.b], writes=dst_bufs, key=ao.b.name)

                def run_chains(jobs):
                    for b0 in range(0, len(jobs), NH):
                        gens = [jobs[b0 + k](k) for k in range(min(NH, len(jobs) - b0))]
                        while gens:
                            for g_ in list(gens):
                                try:
                                    next(g_)
                                except StopIteration:
                                    gens.remove(g_)

                for (si, t0, n, isc) in STS:
                    s = 1 if isc else 0
                    src = hsrc.rearrange("(kc p) t -> p kc t", p=128)
                    S.dma(xt.t[:, :, 0:n], src[:, :, t0:t0 + n], reads=[db("hsrc%d" % id(hsrc), si)], writes=[xt.b], key="xt")
                    if not isc:
                        x0 = t0 - NC_
                        S.dma(rope.t[0:96, 0, 0:n], cMc[:, x0:x0 + n], writes=[rope.b], key="rope")
                        S.dma(rope.t[0:96, 1, 0:n], cMs[:, x0:x0 + n], writes=[rope.b], key="rope")
                        S.dma(rope.t[:, 2, 0:n], cSc[:, x0:x0 + n], writes=[rope.b], key="rope")
                        S.dma(rope.t[:, 3, 0:n], cSs[:, x0:x0 + n], writes=[rope.b], key="rope")
                    norm_mod(ph, xt, n, modA1, 0, isc, sq, pss, rs, hT)
                    S.dma(hTs[:, :, t0:t0 + n], hT.t[:, :, 0:n], reads=[hT.b], writes=[db("hTs", si)], key="hT_st")

                    def proj(ps, c0, ncols=128):
                        for kc in range(8):
                            S.op("pe", lambda e: e.matmul(ps.t[0:ncols, 0:n], lhsT=Win.t[:, kc, c0:c0 + ncols], rhs=hT.t[:, kc, 0:n],
                                                          start=(kc == 0), stop=(kc == 7)), reads=[Win.b, hT.b], writes=[ps.b])
                    pi = [0]

                    def nextp():
                        pi[0] += 1
                        return pp[pi[0] % 2]

                    for (nch, c0, dst, dim) in ((3, 0, qn, 384), (2, 384, kvn, 256)):
                        for c in range(nch):
                            ps = nextp()
                            proj(ps, c0 + c * 128)
                            S.op("act", lambda e: e.copy(out=q32.t[:, c, 0:n], in_=ps.t[:, 0:n]), reads=[ps.b], writes=[q32.b])
                        S.op("pool", lambda e: e.tensor_tensor(out=sqq.t[:, 0:nch, 0:n], in0=q32.t[:, 0:nch, 0:n], in1=q32.t[:, 0:nch, 0:n], op=ALU.mult),
                             reads=[q32.b], writes=[sqq.b])
                        for c in range(nch):
                            S.op("pe", lambda e: e.matmul(pss.t[:, 0:n], lhsT=ones32.t[:], rhs=sqq.t[:, c, 0:n], start=(c == 0), stop=(c == nch - 1)),
                                 reads=[ones32.b, sqq.b], writes=[pss.b])
                        rsqrt_from(None, rq, pss.t[:, 0:n], pss.b, 1.0 / dim, shape_ap=rq.t[:, 0:n])
                        S.op("dve", lambda e: e.tensor_tensor(out=dst.t[:, 0:nch, 0:n], in0=q32.t[:, 0:nch, 0:n],
                                                              in1=rq.t[:, 0:n].unsqueeze(1).to_broadcast([128, nch, n]), op=ALU.mult),
                             reads=[q32.b, rq.b], writes=[dst.b])
                    ps = nextp()
                    proj(ps, O_KR, 32)
                    S.op("act", lambda e: e.copy(out=krT.t[:, 0:n], in_=ps.t[0:32, 0:n]), reads=[ps.b], writes=[krT.b])
                    jobs = []
                    for h in range(8):
                        if (not isc) or do_ctx_q:
                            def mmq(ps, h=h):
                                for c in range(3):
                                    S.op("pe", lambda e: e.matmul(ps.t[0:96, 0:n], lhsT=Wuq.t[:, c, h * 96:(h + 1) * 96], rhs=qn.t[:, c, 0:n],
                                                                  start=(c == 0), stop=(c == 2)), reads=[Wuq.b, qn.b], writes=[ps.b])
                            jobs.append(lambda i, h=h, mmq=mmq: headnorm(i, mmq, 96, gq.t[0:96, 0:1], ones32, 96.0, not isc, Pm, 0, 1, n,
                                                                           QTs[h, :, t0:t0 + n], [db("QTs", (h, si))]))

                        def mmk(ps, h=h):
                            for c in range(2):
                                S.op("pe", lambda e: e.matmul(ps.t[0:96, 0:n], lhsT=Wkp.t[:, c, h, :], rhs=kvn.t[:, c, 0:n],
                                                              start=(c == 0), stop=False), reads=[Wkp.b, kvn.b], writes=[ps.b])
                            S.op("pe", lambda e: e.matmul(ps.t[0:96, 0:n], lhsT=shiftI.t[:, :], rhs=krT.t[:, 0:n], start=False, stop=True),
                                 reads=[shiftI.b, krT.b], writes=[ps.b])
                        jobs.append(lambda i, h=h, mmk=mmk: headnorm(i, mmk, 96, gq.t[0:96, 1:2], ones32, 96.0, not isc, Pm, 0, 1, n,
                                                                       KTs[h, :, t0:t0 + n], [db("KTs", (h, si))]))
                    for c in range(4):
                        def mmsq(ps, c=c):
                            proj(ps, O_SQ + c * 128)
                        jobs.append(lambda i, c=c, mmsq=mmsq: headnorm(i, mmsq, 128, gq.t[:, 2:3], bones32, 64.0, not isc, Ps, 2, 3, n,
                                                                         [(sQs[:, 2 * c, t0:t0 + n], 0, 64), (sQs[:, 2 * c + 1, t0:t0 + n], 64, 128)],
                                                                         [db("sQs", (c, si))]))
                    for kh in range(2):
                        def mmsk(ps, kh=kh):
                            for kc in range(8):
                                S.op("pe", lambda e: e.matmul(ps.t[:, 0:n], lhsT=Wkd.t[:, kc, kh, :], rhs=hT.t[:, kc, 0:n],
                                                              start=(kc == 0), stop=(kc == 7)), reads=[Wkd.b, hT.b], writes=[ps.b])
                        jobs.append(lambda i, kh=kh, mmsk=mmsk: headnorm(i, mmsk, 128, gq.t[:, 3:4], bones32, 64.0, not isc, Ps, 2, 3, n,
                                                                           [(sKs[:, kh, t0:t0 + n], 0, 64)], [db("sKs", (kh, si))]))
                    run_chains(jobs)
                    for j in range(n // 128):
                        tile_i = t0 // 128 + j
                        v_ = va[tile_i % 2]
                        for c in range(2):
                            S.op("pe", lambda e: e.matmul(pv.t[:, 0:512], lhsT=kvn.t[:, c, j * 128:(j + 1) * 128],
                                                          rhs=Wv.t[:, c, :, :].rearrange("p h c -> p (h c)"),
                                                          start=(c == 0), stop=(c == 1)), reads=[kvn.b, Wv.b], writes=[pv.b])
                        S.op("act", lambda e: e.copy(out=v_.t[:, :, 0:64], in_=pv.t[:, 0:512].rearrange("p (h c) -> p h c", c=64)),
                             reads=[pv.b], writes=[v_.b])
                        S.dma(VsM[tile_i], v_.t[:], reads=[v_.b], writes=[db("VsM", tile_i)], key=v_.b.name)
                    for c in range(4):
                        ps = nextp()
                        proj(ps, O_U + c * 128)
                        S.op("act", lambda e: e.copy(out=uT.t[:, c, 0:n], in_=ps.t[:, 0:n]), reads=[ps.b], writes=[uT.b])
                    S.dma(uTs[:, :, t0:t0 + n], uT.t[:, :, 0:n], reads=[uT.b], writes=[db("uTs", si)], key="uT_st")
                    for j in range(n // 128):
                        tile_i = t0 // 128 + j
                        v_ = sva[tile_i % 2]
                        for kc in range(8):
                            S.op("pe", lambda e: e.matmul(pv.t[:, 0:128], lhsT=hT.t[:, kc, j * 128:(j + 1) * 128], rhs=Win.t[:, kc, O_SV:O_SV + 128],
                                                          start=(kc == 0), stop=(kc == 7)), reads=[hT.b, Win.b], writes=[pv.b])
                        S.op("act", lambda e: e.copy(out=v_.t[:, :, 0:64], in_=pv.t[:, 0:128].rearrange("p (h c) -> p h c", c=64)),
                             reads=[pv.b], writes=[v_.b])
                        S.dma(sVs[tile_i], v_.t[:], reads=[v_.b], writes=[db("sVs", tile_i)], key=v_.b.name)
            S.barrier()

        def attn_core(ph, nkeys_tiles, score_fn, pv_lhsT_fn, pv_reads, n, pS, pO, Pt, scale, mask_fn=None):
            nk = len(nkeys_tiles)

            def do_s(i):
                score_fn(nkeys_tiles[i], pS[i % 3])

            def do_e(i):
                ps_, p_ = pS[i % 3], Pt[i % 3]
                S.op("act", lambda e: e.activation(out=p_.t[:, 0:n], in_=ps_.t[:, 0:n], func=AF.Exp, scale=scale), reads=[ps_.b], writes=[p_.b])
                if mask_fn is not None:
                    mask_fn(nkeys_tiles[i], p_)

            def do_pv(i):
                p_ = Pt[i % 3]
                S.op("pe", lambda e: e.matmul(pO.t[:, 0:n], lhsT=pv_lhsT_fn(nkeys_tiles[i]), rhs=p_.t[:, 0:n], start=(i == 0), stop=(i == nk - 1)),
                     reads=[p_.b] + pv_reads, writes=[pO.b])

            do_s(0)
            if nk > 1:
                do_s(1)
            for i in range(nk):
                do_e(i)
                if i + 2 < nk:
                    do_s(i + 2)
                do_pv(i)

        def phase_B(l, do_ctx):
            with contextlib.ExitStack() as ph:
                KT = [sbt(ph, "KT%d" % i, [96, T], BF16) for i in range(2)]
                QT = [sbt(ph, "QT%d" % i, [96, T], BF16) for i in range(2)]
                VH = [sbt(ph, "VH%d" % i, [128, 34, 128], BF16) for i in range(2)]
                Pt = [sbt(ph, "Pt%d" % i, [128, 512], BF16) for i in range(3)]
                rsum = [sbt(ph, "rsum%d" % i, [64, 512], F32) for i in range(2)]
                oT = [sbt(ph, "oT%d" % i, [64, 512], BF16) for i in range(2)]
                pS = [pst(ph, "pS%d" % i) for i in range(3)]
                pO = [pst(ph, "pO%d" % i) for i in range(2)]
                allv = [db("VsM", i) for i in range(34)]
                it = [0]
                for h in range(8):
                    allq = [db("QTs", (h, si)) for si in range(9)]
                    allk = [db("KTs", (h, si)) for si in range(9)]
                    kt_, qt_, vh_ = KT[h % 2], QT[h % 2], VH[h % 2]
                    S.dma(kt_.t[:], KTs[h], reads=allk, writes=[kt_.b], key=kt_.b.name)
                    S.dma(qt_.t[:], QTs[h], reads=allq, writes=[qt_.b], key=qt_.b.name)
                    S.dma(vh_.t[:], VsM.rearrange("t p h c -> p t h c")[:, :, h, :], reads=allv, writes=[vh_.b], key=vh_.b.name)
                    for (si, t0, n, isc) in STS:
                        if isc and not do_ctx:
                            continue
                        keys = [0, 1] if isc else list(range(34))
                        po = pO[it[0] % 2]
                        rs_ = rsum[it[0] % 2]
                        o_ = oT[it[0] % 2]
                        it[0] += 1

                        def score(kt, ps_):
                            S.op("pe", lambda e: e.matmul(ps_.t[:, 0:n], lhsT=kt_.t[:, kt * 128:(kt + 1) * 128], rhs=qt_.t[:, t0:t0 + n], start=True, stop=True),
                                 reads=[kt_.b, qt_.b], writes=[ps_.b])
                        attn_core(ph, keys, score, lambda kt: vh_.t[:, kt, :], [vh_.b], n, pS, po, Pt, 96.0 ** -0.5)
                        S.op("act", lambda e: e.copy(out=rs_.t[:, 0:n], in_=po.t[64:128, 0:n]), reads=[po.b], writes=[rs_.b])
                        S.op("dve", lambda e: e.reciprocal(out=rs_.t[:, 0:n], in_=rs_.t[:, 0:n]), reads=[rs_.b], writes=[rs_.b])
                        S.op("dve", lambda e: e.tensor_tensor(out=o_.t[:, 0:n], in0=po.t[0:64, 0:n], in1=rs_.t[:, 0:n], op=ALU.mult),
                             reads=[po.b, rs_.b], writes=[o_.b])
                        S.dma(mlaO[h * 64:(h + 1) * 64, t0:t0 + n], o_.t[:, 0:n], reads=[o_.b], writes=[db("mlaO", (h, si))], key=o_.b.name)
            S.barrier()

        def phase_C(l, do_ctx):
            with contextlib.ExitStack() as ph:
                sk = sbt(ph, "sk", [64, T], BF16)
                sq_ = sbt(ph, "sq_", [64, 4, T], BF16)
                sv = sbt(ph, "sv", [128, 34, 128], BF16)
                Pt = [sbt(ph, "Pt%d" % i, [128, 512], BF16) for i in range(3)]
                rsum = [sbt(ph, "rsum%d" % i, [64, 512], F32) for i in range(2)]
                oT = [sbt(ph, "oT%d" % i, [64, 512], BF16) for i in range(2)]
                esk = sbt(ph, "esk", [128, 8], F32)
                S.dma(esk.t[:], sinkb[l], writes=[esk.b], key="esk")
                S.op("act", lambda e: e.activation(out=esk.t[:], in_=esk.t[:], func=AF.Exp), reads=[esk.b], writes=[esk.b])
                pS = [pst(ph, "pS%d" % i) for i in range(3)]
                pO = [pst(ph, "pO%d" % i) for i in range(2)]
                allv = [db("sVs", i) for i in range(34)]
                it = [0]
                for kh in range(2):
                    allq = [db("sQs", (c_, si)) for si in range(9) for c_ in (2 * kh, 2 * kh + 1)]
                    allk = [db("sKs", (kh, si)) for si in range(9)]
                    S.dma(sk.t[:], sKs[:, kh, :], reads=allk, writes=[sk.b], key="sk")
                    S.dma(sq_.t[:], sQs[:, 4 * kh:4 * kh + 4, :], reads=allq, writes=[sq_.b], key="sq_")
                    S.dma(sv.t[:], sVs.rearrange("t p h c -> p t h c")[:, :, kh, :], reads=allv, writes=[sv.b], key="sv")
                    qtiles = ([0, 1] if do_ctx else []) + list(range(2, 34))
                    for qt in qtiles:
                        q0 = qt * 128
                        if qt < 2:
                            keys = [(0, None), (1, None)]
                        else:
                            keys = [(0, None), (1, None)]
                            if qt > 2:
                                keys.append((qt - 1, Mge))
                            keys.append((qt, None))
                            if qt < 33:
                                keys.append((qt + 1, Mle))
                        po = pO[it[0] % 2]
                        rs_ = rsum[it[0] % 2]
                        o_ = oT[it[0] % 2]
                        it[0] += 1

                        def score(km, ps_):
                            kt = km[0]
                            for hd in range(4):
                                S.op("pe", lambda e: e.matmul(ps_.t[:, hd * 128:(hd + 1) * 128], lhsT=sk.t[:, kt * 128:(kt + 1) * 128],
                                                              rhs=sq_.t[:, hd, q0:q0 + 128], start=True, stop=True),
                                     reads=[sk.b, sq_.b], writes=[ps_.b])

                        def maskf(km, p_):
                            if km[1] is not None:
                                m = km[1]
                                p3 = p_.t[:, :].rearrange("p (h q) -> p h q", q=128)
                                S.op("pool", lambda e: e.tensor_tensor(out=p3, in0=p3, in1=m.t[:, :].unsqueeze(1).to_broadcast([128, 4, 128]), op=ALU.mult),
                                     reads=[p_.b, m.b], writes=[p_.b])
                        attn_core(ph, keys, score, lambda km: sv.t[:, km[0], :], [sv.b], 512, pS, po, Pt, 0.125, mask_fn=maskf)
                        S.op("act", lambda e: e.copy(out=rs_.t[:, :], in_=po.t[64:128, :]), reads=[po.b], writes=[rs_.b])
                        r3 = rs_.t[:, :].rearrange("p (h q) -> p h q", q=128)
                        S.op("dve", lambda e: e.tensor_tensor(out=r3, in0=r3, in1=esk.t[0:64, 4 * kh:4 * kh + 4].unsqueeze(2).to_broadcast([64, 4, 128]), op=ALU.add),
                             reads=[rs_.b, esk.b], writes=[rs_.b])
                        S.op("dve", lambda e: e.reciprocal(out=rs_.t[:, :], in_=rs_.t[:, :]), reads=[rs_.b], writes=[rs_.b])
                        S.op("dve", lambda e: e.tensor_tensor(out=o_.t[:, :], in0=po.t[0:64, :], in1=rs_.t[:, :], op=ALU.mult),
                             reads=[po.b, rs_.b], writes=[o_.b])
                        dst = swaO.rearrange("(h d) t -> d h t", d=64)[:, 4 * kh:4 * kh + 4, q0:q0 + 128]
                        S.dma(dst, o_.t[:, :].rearrange("p (h q) -> p h q", q=128), reads=[o_.b], writes=[db("swaO", (kh, qt))], key=o_.b.name)
            S.barrier()

        def phase_D(l, do_ctx):
            with contextlib.ExitStack() as ph:
                uT = sbt(ph, "uT", [128, 4, T], BF16)
                S.dma(uT.t[:], uTs[:, :, :], reads=[db("uTs", si) for si in range(9)], writes=[uT.b], key="uT_ld")
                acc = sbt(ph, "acc", [128, T], F32)
                def small(name, w=32):
                    return sbt(ph, name, [128, w], F32)
                lre, lim, dtt = small("lre"), small("lim"), small("dtt")
                S.dma(lre.t[:], s5lre[l].rearrange("p d q -> p (d q)"), writes=[lre.b], key="s5p1")
                S.dma(lim.t[:], s5lim[l].rearrange("p d q -> p (d q)"), writes=[lim.b], key="s5p2")
                S.dma(dtt.t[:], s5ls[l].rearrange("p d q -> p (d q)"), writes=[dtt.b], key="s5p3")
                S.op("act", lambda e: e.activation(out=dtt.t[:], in_=dtt.t[:], func=AF.Exp), reads=[dtt.b], writes=[dtt.b])
                S.op("dve", lambda e: e.tensor_scalar(out=lre.t[:], in0=lre.t[:], scalar1=-1e-4, scalar2=None, op0=ALU.min), reads=[lre.b], writes=[lre.b])
                rr, th, fq = small("rr"), small("th"), small("fq")
                S.op("dve", lambda e: e.tensor_tensor(out=rr.t[:], in0=lre.t[:], in1=dtt.t[:], op=ALU.mult), reads=[lre.b, dtt.b], writes=[rr.b])
                S.op("act", lambda e: e.activation(out=rr.t[:], in_=rr.t[:], func=AF.Exp), reads=[rr.b], writes=[rr.b])
                S.op("dve", lambda e: e.tensor_tensor(out=th.t[:], in0=lim.t[:], in1=dtt.t[:], op=ALU.mult), reads=[lim.b, dtt.b], writes=[th.b])
                S.op("dve", lambda e: e.tensor_scalar(out=fq.t[:], in0=th.t[:], scalar1=1.0 / TWO_PI, scalar2=None, op0=ALU.mult), reads=[th.b], writes=[fq.b])
                tmpi = sbt(ph, "tmpi", [128, 512], I32)
                tmpf = sbt(ph, "tmpf", [128, 512], F32)

                def sincos(dst_s, dst_c, ph_ap, ph_buf, w):
                    for (dst, shift) in ((dst_s, 0.0), (dst_c, 0.25)):
                        S.op("dve", lambda e: e.tensor_scalar(out=tmpf.t[:, 0:w], in0=ph_ap, scalar1=shift, scalar2=None, op0=ALU.add), reads=[ph_buf], writes=[tmpf.b])
                        S.op("dve", lambda e: e.tensor_copy(out=tmpi.t[:, 0:w], in_=tmpf.t[:, 0:w]), reads=[tmpf.b], writes=[tmpi.b])
                        S.op("dve", lambda e: e.tensor_copy(out=dst[1], in_=tmpi.t[:, 0:w]), reads=[tmpi.b], writes=[dst[0]])
                        S.op("dve", lambda e: e.tensor_tensor(out=tmpf.t[:, 0:w], in0=tmpf.t[:, 0:w], in1=dst[1], op=ALU.subtract), reads=[tmpf.b, dst[0]], writes=[tmpf.b])
                        S.op("act", lambda e: e.activation(out=dst[1], in_=tmpf.t[:, 0:w], func=AF.Sin, scale=TWO_PI), reads=[tmpf.b], writes=[dst[0]])

                sn, cs = small("sn"), small("cs")
                sincos((sn.b, sn.t[:]), (cs.b, cs.t[:]), fq.t[:], fq.b, 32)
                are, aim = small("are"), small("aim")
                S.op("dve", lambda e: e.tensor_tensor(out=are.t[:], in0=rr.t[:], in1=cs.t[:], op=ALU.mult), reads=[rr.b, cs.b], writes=[are.b])
                S.op("dve", lambda e: e.tensor_tensor(out=aim.t[:], in0=rr.t[:], in1=sn.t[:], op=ALU.mult), reads=[rr.b, sn.b], writes=[aim.b])
                am1, den, t1_, t2_, cre, cim = small("am1"), small("den"), small("t1_"), small("t2_"), small("cre"), small("cim")
                S.op("dve", lambda e: e.tensor_scalar(out=am1.t[:], in0=are.t[:], scalar1=-1.0, scalar2=None, op0=ALU.add), reads=[are.b], writes=[am1.b])
                S.op("dve", lambda e: e.tensor_tensor(out=den.t[:], in0=lre.t[:], in1=lre.t[:], op=ALU.mult), reads=[lre.b], writes=[den.b])
                S.op("dve", lambda e: e.tensor_tensor(out=t1_.t[:], in0=lim.t[:], in1=lim.t[:], op=ALU.mult), reads=[lim.b], writes=[t1_.b])
                S.op("dve", lambda e: e.tensor_tensor(out=den.t[:], in0=den.t[:], in1=t1_.t[:], op=ALU.add), reads=[den.b, t1_.b], writes=[den.b])
                S.op("dve", lambda e: e.reciprocal(out=den.t[:], in_=den.t[:]), reads=[den.b], writes=[den.b])
                S.op("dve", lambda e: e.tensor_tensor(out=t1_.t[:], in0=am1.t[:], in1=lre.t[:], op=ALU.mult), reads=[am1.b, lre.b], writes=[t1_.b])
                S.op("dve", lambda e: e.tensor_tensor(out=t2_.t[:], in0=aim.t[:], in1=lim.t[:], op=ALU.mult), reads=[aim.b, lim.b], writes=[t2_.b])
                S.op("dve", lambda e: e.tensor_tensor(out=cre.t[:], in0=t1_.t[:], in1=t2_.t[:], op=ALU.add), reads=[t1_.b, t2_.b], writes=[cre.b])
                S.op("dve", lambda e: e.tensor_tensor(out=cre.t[:], in0=cre.t[:], in1=den.t[:], op=ALU.mult), reads=[cre.b, den.b], writes=[cre.b])
                S.op("dve", lambda e: e.tensor_tensor(out=t1_.t[:], in0=aim.t[:], in1=lre.t[:], op=ALU.mult), reads=[aim.b, lre.b], writes=[t1_.b])
                S.op("dve", lambda e: e.tensor_tensor(out=t2_.t[:], in0=am1.t[:], in1=lim.t[:], op=ALU.mult), reads=[am1.b, lim.b], writes=[t2_.b])
                S.op("dve", lambda e: e.tensor_tensor(out=cim.t[:], in0=t1_.t[:], in1=t2_.t[:], op=ALU.subtract), reads=[t1_.b, t2_.b], writes=[cim.b])
                S.op("dve", lambda e: e.tensor_tensor(out=cim.t[:], in0=cim.t[:], in1=den.t[:], op=ALU.mult), reads=[cim.b, den.b], writes=[cim.b])
                fb, snB, csB = small("fb", 64), small("snB", 64), small("csB", 64)
                S.op("dve", lambda e: e.tensor_scalar(out=fb.t[:, 0:32], in0=fq.t[:], scalar1=256.0, scalar2=None, op0=ALU.mult), reads=[fq.b], writes=[fb.b])
                S.op("dve", lambda e: e.tensor_scalar(out=fb.t[:, 32:64], in0=fq.t[:], scalar1=512.0, scalar2=None, op0=ALU.mult), reads=[fq.b], writes=[fb.b])
                sincos((snB.b, snB.t[:]), (csB.b, csB.t[:]), fb.t[:], fb.b, 64)
                WB = sbt(ph, "WB", [128, 2, 2, 8, 128], BF16)
                CW = sbt(ph, "CW", [128, 3, 2, 16, 128], BF16)
                S.op("pool", lambda e: e.memset(CW.t[:], 0.0), writes=[CW.b])
                prep = contextlib.ExitStack()
                braw = sbt(prep, "braw", [128, 2, 2, 16, 16], F32)
                craw = sbt(prep, "craw", [128, 2, 2, 16, 16], F32)
                S.dma(braw.t[:, 0], s5bre[l], writes=[braw.b], key="braw")
                S.dma(braw.t[:, 1], s5bim[l], writes=[braw.b], key="braw")
                S.dma(craw.t[:, 0], s5cre[l], writes=[craw.b], key="craw")
                S.dma(craw.t[:, 1], s5cim[l], writes=[craw.b], key="craw")
                bbar = sbt(prep, "bbar", [128, 2, 2, 16, 16], F32)
                tb = sbt(prep, "tb", [128, 2, 16, 16], F32)
                cre3 = cre.t[:].rearrange("p (d q) -> p d q", q=16).unsqueeze(3).to_broadcast([128, 2, 16, 16])
                cim3 = cim.t[:].rearrange("p (d q) -> p d q", q=16).unsqueeze(3).to_broadcast([128, 2, 16, 16])
                S.op("dve", lambda e: e.tensor_tensor(out=bbar.t[:, 0], in0=braw.t[:, 0], in1=cre3, op=ALU.mult), reads=[braw.b, cre.b], writes=[bbar.b])
                S.op("dve", lambda e: e.tensor_tensor(out=tb.t[:], in0=braw.t[:, 1], in1=cim3, op=ALU.mult), reads=[braw.b, cim.b], writes=[tb.b])
                S.op("dve", lambda e: e.tensor_tensor(out=bbar.t[:, 0], in0=bbar.t[:, 0], in1=tb.t[:], op=ALU.subtract), reads=[bbar.b, tb.b], writes=[bbar.b])
                S.op("dve", lambda e: e.tensor_tensor(out=bbar.t[:, 1], in0=braw.t[:, 1], in1=cre3, op=ALU.mult), reads=[braw.b, cre.b], writes=[bbar.b])
                S.op("dve", lambda e: e.tensor_tensor(out=tb.t[:], in0=braw.t[:, 0], in1=cim3, op=ALU.mult), reads=[braw.b, cim.b], writes=[tb.b])
                S.op("dve", lambda e: e.tensor_tensor(out=bbar.t[:, 1], in0=bbar.t[:, 1], in1=tb.t[:], op=ALU.add), reads=[bbar.b, tb.b], writes=[bbar.b])
                Z = [sbt(prep, "Z%d" % i, [128, 64], BF16) for i in range(2)]
                pz = [pst(prep, "pz%d" % i, [128, 128], BF16) for i in range(2)]
                zi = 0
                for d_ in range(2):
                    for q in range(16):
                        c, q4 = q // 4, q % 4
                        hf, q2 = q4 // 2, q4 % 2
                        for ri in range(2):
                            z, pzz = Z[zi % 2], pz[zi % 2]
                            zi += 1
                            S.op("pool", lambda e: e.memset(z.t[:], 0.0), writes=[z.b])
                            S.op("dve", lambda e: e.tensor_copy(out=z.t[0:64, q2 * 32:q2 * 32 + 16], in_=bbar.t[0:64, ri, d_, q, :]), reads=[bbar.b], writes=[z.b])
                            S.op("dve", lambda e: e.tensor_copy(out=z.t[64:128, q2 * 32 + 16:q2 * 32 + 32], in_=bbar.t[64:128, ri, d_, q, :]), reads=[bbar.b], writes=[z.b])
                            S.op("pe", lambda e: e.transpose(pzz.t[0:64, :], z.t[:, :], identb.t[:, :]), reads=[z.b, identb.b], writes=[pzz.b])
                            S.op("act", lambda e: e.copy(out=WB.t[hf * 64:(hf + 1) * 64, ri, d_, c * 2 + q2, :], in_=pzz.t[0:64, :]), reads=[pzz.b], writes=[WB.b])
                            sgn = 1.0 if ri == 0 else -1.0
                            for e_ in range(2):
                                S.op("pool", lambda e: e.tensor_scalar(out=CW.t[e_ * 64:(e_ + 1) * 64, ri, d_, q, q4 * 32 + e_ * 16:q4 * 32 + e_ * 16 + 16],
                                                                       in0=craw.t[e_ * 64:(e_ + 1) * 64, ri, d_, q, :], scalar1=sgn, scalar2=None, op0=ALU.mult),
                                     reads=[craw.b], writes=[CW.b])
                                if ri == 0:
                                    S.op("pool", lambda e: e.tensor_scalar(out=CW.t[e_ * 64:(e_ + 1) * 64, 2, d_, q, q4 * 32 + e_ * 16:q4 * 32 + e_ * 16 + 16],
                                                                           in0=craw.t[e_ * 64:(e_ + 1) * 64, 0, d_, q, :], scalar1=-1.0, scalar2=None, op0=ALU.mult),
                                         reads=[craw.b], writes=[CW.b])
                S.barrier()
                prep.close()
                Wg = sbt(ph, "Wg", [128, 4, 512], BF16)
                load_cast(lambda a0, a1: (Wg.t[:, a0:a1, :], Wg.b), w_glu[l].rearrange("(kc p) n -> p kc n", p=128), 4, 512, eng="act")
                dcol = sbt(ph, "dcol", [128, 8], F32)
                S.dma(dcol.t[:, 0:4], s5dc[l], writes=[dcol.b], key="dcol")
                S.dma(dcol.t[:, 4:8], s5bgc[l], writes=[dcol.b], key="dcol")
                iot = sbt(ph, "iot", [128, 512], F32)
                S.dma(iot.t[:], cIota[:, :], writes=[iot.b], key="iot")
                wk = contextlib.ExitStack()
                tph = sbt(wk, "tph", [128, 512], F32)
                tC4 = [sbt(wk, "tC4_%d" % i, [128, 512], F32) for i in range(4)]
                tS4 = [sbt(wk, "tS4_%d" % i, [128, 512], F32) for i in range(4)]
                NS = int(_os.environ.get("D_NS", "3"))
                W = {k: [sbt(wk, "w%s%d" % (k, i), [128, 512], F32) for i in range(NS)] for k in
                     ("t1", "t2", "t3", "t4", "xa", "xb", "ga", "gb")}
                Ub = {k: [sbt(wk, "b%s%d" % (k, i), [128, 512], BF16) for i in range(NS)] for k in ("u1", "u2", "u3", "u4")}
                ini4 = [sbt(wk, "ini4_%d" % i, [128, 4], F32) for i in range(4)]
                ygt = [sbt(wk, "ygt%d" % i, [128, 512], BF16) for i in range(2)]
                Xs = [[sbt(wk, "Xs%d%d" % (i, j), [128, 512], F32) for j in range(2)] for i in range(NS)] if D_EVAC else None
                pX = [[pst(wk, "pX%d%d" % (i, j)) for j in range(2)] for i in range(NS)]
                pY = [pst(wk, "pY%d" % i) for i in range(2)]
                UENG = D_UENG

                def tt(eng, o, a, b_, op, rd, wr):
                    S.op(eng, lambda e: e.tensor_tensor(out=o, in0=a, in1=b_, op=op), reads=rd, writes=wr)
                it = 0
                for c in range(4):
                    S.op("dve", lambda e: e.tensor_scalar(out=acc.t[:], in0=uT.t[:, c, :], scalar1=dcol.t[:, c:c + 1], scalar2=None, op0=ALU.mult),
                         reads=[uT.b, dcol.b], writes=[acc.b])
                    for d_ in range(2):
                        for q4 in range(4):
                            col = d_ * 16 + c * 4 + q4
                            S.op("dve", lambda e: e.tensor_scalar(out=tph.t[:], in0=iot.t[:], scalar1=fq.t[:, col:col + 1], scalar2=None, op0=ALU.mult),
                                 reads=[iot.b, fq.b], writes=[tph.b])
                            sincos((tS4[q4].b, tS4[q4].t[:]), (tC4[q4].b, tC4[q4].t[:]), tph.t[:], tph.b, 512)
                            S.op("pool", lambda e: e.memset(ini4[q4].t[:], 0.0), writes=[ini4[q4].b])
                        for fidx in range(9):
                            n = 256 if fidx == 0 else 512
                            if d_ == 0:
                                k0 = 0 if fidx == 0 else 256 + 512 * (fidx - 1)
                                cols = slice(k0, k0 + n)
                            else:
                                if fidx == 0:
                                    cols = slice(255, None, -1)
                                else:
                                    hi = NC_ + NX - 512 * (fidx - 1) - 1
                                    cols = slice(hi, hi - 512, -1)
                            py = pY[fidx % 2]

                            def chain(q4, i, py=py, n=n, cols=cols, fidx=fidx, d_=d_, c=c):
                                q = c * 4 + q4
                                hf, q2 = q4 // 2, q4 % 2
                                col = d_ * 16 + q
                                tC, tS, iv = tC4[q4], tS4[q4], ini4[q4]
                                px = pX[i]
                                for ri in range(2):
                                    S.op("pe", lambda e: e.matmul(px[ri].t[:, 0:n], lhsT=WB.t[hf * 64:(hf + 1) * 64, ri, d_, c * 2 + q2, :],
                                                                  rhs=uT.t[hf * 64:(hf + 1) * 64, c, cols], start=True, stop=True),
                                         reads=[WB.b, uT.b], writes=[px[ri].b])
                                yield
                                t1, t2, t3, t4 = W["t1"][i], W["t2"][i], W["t3"][i], W["t4"][i]
                                xa, xb, ga, gb = W["xa"][i], W["xb"][i], W["ga"][i], W["gb"][i]
                                u1, u2, u3, u4 = Ub["u1"][i], Ub["u2"][i], Ub["u3"][i], Ub["u4"][i]
                                tt("dve", t1.t[:, 0:n], px[0].t[:, 0:n], tC.t[:, 0:n], ALU.mult, [px[0].b, tC.b], [t1.b])
                                yield
                                tt("dve", t2.t[:, 0:n], px[1].t[:, 0:n], tS.t[:, 0:n], ALU.mult, [px[1].b, tS.b], [t2.b])
                                yield
                                tt("dve", t3.t[:, 0:n], px[1].t[:, 0:n], tC.t[:, 0:n], ALU.mult, [px[1].b, tC.b], [t3.b])
                                yield
                                tt("dve", t4.t[:, 0:n], px[0].t[:, 0:n], tS.t[:, 0:n], ALU.mult, [px[0].b, tS.b], [t4.b])
                                yield
                                tt(D_XENG, xa.t[:, 0:n], t1.t[:, 0:n], t2.t[:, 0:n], ALU.add, [t1.b, t2.b], [xa.b])
                                yield
                                tt(D_XENG, xb.t[:, 0:n], t3.t[:, 0:n], t4.t[:, 0:n], ALU.subtract, [t3.b, t4.b], [xb.b])
                                yield
                                rbc = rr.t[:, col:col + 1].to_broadcast([128, n])
                                S.op("dve", lambda e: e.tensor_tensor_scan(out=ga.t[:, 0:n], data0=rbc, data1=xa.t[:, 0:n], initial=iv.t[:, 0:1], op0=ALU.mult, op1=ALU.add),
                                     reads=[xa.b, rr.b, iv.b], writes=[ga.b])
                                yield
                                S.op("dve", lambda e: e.tensor_tensor_scan(out=gb.t[:, 0:n], data0=rbc, data1=xb.t[:, 0:n], initial=iv.t[:, 1:2], op0=ALU.mult, op1=ALU.add),
                                     reads=[xb.b, rr.b, iv.b], writes=[gb.b])
                                yield
                                if fidx < 8:
                                    bc = (0 if n == 256 else 32) + col
                                    cB, sB = csB.t[:, bc:bc + 1], snB.t[:, bc:bc + 1]
                                    S.op("pool", lambda e: e.tensor_scalar(out=iv.t[:, 2:3], in0=gb.t[:, n - 1:n], scalar1=sB, scalar2=None, op0=ALU.mult), reads=[gb.b, snB.b], writes=[iv.b])
                                    S.op("pool", lambda e: e.tensor_scalar(out=iv.t[:, 3:4], in0=ga.t[:, n - 1:n], scalar1=sB, scalar2=None, op0=ALU.mult), reads=[ga.b, snB.b], writes=[iv.b])
                                yield
                                tt(UENG[0], u1.t[:, 0:n], ga.t[:, 0:n], tC.t[:, 0:n], ALU.mult, [ga.b, tC.b], [u1.b])
                                yield
                                tt(UENG[1], u2.t[:, 0:n], gb.t[:, 0:n], tS.t[:, 0:n], ALU.mult, [gb.b, tS.b], [u2.b])
                                yield
                                tt(UENG[2], u3.t[:, 0:n], ga.t[:, 0:n], tS.t[:, 0:n], ALU.mult, [ga.b, tS.b], [u3.b])
                                yield
                                tt(UENG[3], u4.t[:, 0:n], gb.t[:, 0:n], tC.t[:, 0:n], ALU.mult, [gb.b, tC.b], [u4.b])
                                yield
                                if fidx < 8:
                                    bc = (0 if n == 256 else 32) + col
                                    cB = csB.t[:, bc:bc + 1]
                                    S.op("dve", lambda e: e.scalar_tensor_tensor(out=iv.t[:, 0:1], in0=ga.t[:, n - 1:n], scalar=cB, in1=iv.t[:, 2:3], op0=ALU.mult, op1=ALU.subtract),
                                         reads=[ga.b, csB.b, iv.b], writes=[iv.b])
                                    S.op("dve", lambda e: e.scalar_tensor_tensor(out=iv.t[:, 1:2], in0=gb.t[:, n - 1:n], scalar=cB, in1=iv.t[:, 3:4], op0=ALU.mult, op1=ALU.add),
                                         reads=[gb.b, csB.b, iv.b], writes=[iv.b])
                                yield
                                for j_, (ws, uu) in enumerate(((0, u1), (2, u2), (1, u3), (1, u4))):
                                    S.op("pe", lambda e: e.matmul(py.t[:, 0:n], lhsT=CW.t[:, ws, d_, q, :], rhs=uu.t[:, 0:n],
                                                                  start=(q4 == 0 and j_ == 0), stop=(q4 == 3 and j_ == 3)), reads=[CW.b, uu.b], writes=[py.b])
                                if q4 == 3:
                                    S.op("dve", lambda e: e.tensor_tensor(out=acc.t[:, cols], in0=acc.t[:, cols], in1=py.t[:, 0:n], op=ALU.add), reads=[acc.b, py.b], writes=[acc.b])

                            gens = []
                            for q4 in range(4):
                                gens.append(chain(q4, it % NS))
                                it += 1
                            for b0 in (0, 2):
                                live = gens[b0:b0 + 2]
                                while live:
                                    for g_ in list(live):
                                        try:
                                            next(g_)
                                        except StopIteration:
                                            live.remove(g_)
                    for a0 in range(0, T, 512):
                        w_ = min(512, T - a0)
                        i = (a0 // 512) % 2
                        t1, t2 = W["t1"][i], W["t2"][i]
                        yo = ygt[i]
                        S.op("pool", lambda e: e.tensor_tensor(out=t1.t[:, 0:w_], in0=acc.t[:, a0:a0 + w_], in1=acc.t[:, a0:a0 + w_], op=ALU.mult), reads=[acc.b], writes=[t1.b])
                        S.op("pool", lambda e: e.tensor_scalar(out=t1.t[:, 0:w_], in0=t1.t[:, 0:w_], scalar1=0.044715, scalar2=1.0, op0=ALU.mult, op1=ALU.add), reads=[t1.b], writes=[t1.b])
                        S.op("pool", lambda e: e.tensor_tensor(out=t1.t[:, 0:w_], in0=t1.t[:, 0:w_], in1=acc.t[:, a0:a0 + w_], op=ALU.mult), reads=[t1.b, acc.b], writes=[t1.b])
                        S.op("act", lambda e: e.activation(out=t2.t[:, 0:w_], in_=t1.t[:, 0:w_], func=AF.Sigmoid, scale=1.5957691216057308), reads=[t1.b], writes=[t2.b])
                        S.op("dve", lambda e: e.tensor_tensor(out=yo.t[:, 0:w_], in0=t2.t[:, 0:w_], in1=acc.t[:, a0:a0 + w_], op=ALU.mult), reads=[t2.b, acc.b], writes=[yo.b])
                        S.dma(ygs[:, c, a0:a0 + w_], yo.t[:, 0:w_], reads=[yo.b], writes=[db("ygs", (c, a0 // 512))], key=yo.b.name)
                S.barrier()
                wk.close()
                pY = [pst(ph, "pYg%d" % i) for i in range(2)]
                W = {"t2": [sbt(ph, "gsig%d" % i, [128, 512], F32) for i in range(2)]}
                so = [sbt(ph, "so%d" % i, [128, 4, 512], BF16) for i in range(2)]
                ygl = [sbt(ph, "ygl%d" % i, [128, 4, 512], BF16) for i in range(2)]
                for (si, t0, n, isc) in STS:
                    if isc and not do_ctx:
                        continue
                    o_ = so[si % 2]
                    yg = ygl[si % 2]
                    S.dma(yg.t[:, :, 0:n], ygs[:, :, t0:t0 + n], reads=[db("ygs", (c_, (t0 + j_) // 512)) for c_ in range(4) for j_ in (0, n - 1)],
                          writes=[yg.b], key=yg.b.name)
                    for oc in range(4):
                        py = pY[oc % 2]
                        for kc in range(4):
                            S.op("pe", lambda e: e.matmul(py.t[:, 0:n], lhsT=Wg.t[:, kc, oc * 128:(oc + 1) * 128], rhs=yg.t[:, kc, 0:n], start=(kc == 0), stop=(kc == 3)),
                                 reads=[Wg.b, yg.b], writes=[py.b])
                        t2 = W["t2"][oc % 2]
                        S.op("act", lambda e: e.activation(out=t2.t[:, 0:n], in_=py.t[:, 0:n], func=AF.Sigmoid, bias=dcol.t[:, 4 + oc:5 + oc], scale=1.0), reads=[py.b, dcol.b], writes=[t2.b])
                        S.op("dve", lambda e: e.tensor_tensor(out=o_.t[:, oc, 0:n], in0=t2.t[:, 0:n], in1=yg.t[:, oc, 0:n], op=ALU.mult), reads=[t2.b, yg.b], writes=[o_.b])
                    S.dma(s5O[:, :, t0:t0 + n], o_.t[:, :, 0:n], reads=[o_.b], writes=[db("s5O", si)], key=o_.b.name)
            S.barrier()

        def phase_E(l, hsrc, hdst, do_ctx):
            with contextlib.ExitStack() as ph:
                Wgt = sbt(ph, "Wgt", [128, 8, 3072], BF16)
                wsrc = w_in[l].rearrange("(kc p) n -> p kc n", p=128)
                for r in range(3):
                    load_cast(lambda a0, a1: (Wgt.t[:, a0:a1, r * 1024:(r + 1) * 1024], Wgt.b), wsrc[:, :, O_G + r * 1024:O_G + (r + 1) * 1024], 8, 1024,
                              eng=("pool" if r % 2 == 0 else "act"))
                Wb = sbt(ph, "Wb", [128, 3, 4, 1024], BF16)
                for r in range(3):
                    load_cast(lambda a0, a1: (Wb.t[:, r, a0:a1, :], Wb.b), w_br[l, r].rearrange("(kc p) n -> p kc n", p=128), 4, 1024,
                              eng=("act" if r % 2 == 0 else "pool"))
                Wo = sbt(ph, "Wo", [128, 8, 1024], BF16)
                load_cast(lambda a0, a1: (Wo.t[:, a0:a1, :], Wo.b), w_o[l].rearrange("(kc p) n -> p kc n", p=128), 8, 1024)
                hT = [sbt(ph, "hT%d" % i, [128, 8, 512], BF16) for i in range(2)]
                oB = [sbt(ph, "oB%d" % i, [128, 3, 4, 512], BF16) for i in range(2)]
                xt1 = sbt(ph, "xt", [128, 8, 512], F32)
                xt = [xt1, xt1]
                mT = sbt(ph, "mT", [128, 8, 512], BF16)
                sg = [sbt(ph, "sg%d" % i, [128, 512], BF16) for i in range(3)]
                gp = [sbt(ph, "gp%d" % i, [128, 512], F32) for i in range(3)]
                hn = xt1
                sq = sbt(ph, "sq", [128, 8, 512], F32)
                rs = sbt(ph, "rs", [128, 512], F32)
                fT = sbt(ph, "fT", [128, 8, 512], BF16)
                pG = [pst(ph, "pG%d" % i) for i in range(3)]
                pP = [pst(ph, "pP%d" % i) for i in range(3)]
                pM = pst(ph, "pM")
                pss = pst(ph, "pss")
                srcs = [(mlaO.rearrange("(kc p) t -> p kc t", p=128), "mlaO"), (s5O, "s5O"), (swaO.rearrange("(kc p) t -> p kc t", p=128), "swaO")]
                sts = [s for s in STS if (not s[3]) or do_ctx]

                def loads(k):
                    (si, t0, n, isc) = sts[k]
                    S.dma(hT[k % 2].t[:, :, 0:n], hTs[:, :, t0:t0 + n], reads=[db("hTs", si)], writes=[hT[k % 2].b], key=hT[k % 2].b.name)
                    for r in range(3):
                        if srcs[r][1] == "swaO":
                            rd = [db("swaO", (kh_, t0 // 128 + j)) for j in range(n // 128) for kh_ in range(2)]
                        elif srcs[r][1] == "mlaO":
                            rd = [db("mlaO", (h_i, si)) for h_i in range(8)]
                        else:
                            rd = [db(srcs[r][1], si)]
                        S.dma(oB[k % 2].t[:, r, :, 0:n], srcs[r][0][:, :, t0:t0 + n], reads=rd, writes=[oB[k % 2].b], key=oB[k % 2].b.name)

                def loadx(k):
                    (si, t0, n, isc) = sts[k]
                    src = hsrc.rearrange("(kc p) t -> p kc t", p=128)
                    S.dma(xt1.t[:, :, 0:n], src[:, :, t0:t0 + n], reads=[db("hsrc%d" % id(hsrc), si)], writes=[xt1.b], key=xt1.b.name)

                loads(0)
                for k, (si, t0, n, isc) in enumerate(sts):
                    loadx(k)
                    if k + 1 < len(sts):
                        loads(k + 1)
                    s = 1 if isc else 0
                    h_, o_, x_ = hT[k % 2], oB[k % 2], xt[k % 2]
                    for oc in range(8):
                        for r in range(3):
                            for kc in range(8):
                                S.op("pe", lambda e: e.matmul(pG[r].t[:, 0:n], lhsT=Wgt.t[:, kc, r * 1024 + oc * 128:r * 1024 + (oc + 1) * 128], rhs=h_.t[:, kc, 0:n],
                                                              start=(kc == 0), stop=(kc == 7)), reads=[Wgt.b, h_.b], writes=[pG[r].b])
                            for kc in range(4):
                                S.op("pe", lambda e: e.matmul(pP[r].t[:, 0:n], lhsT=Wb.t[:, r, kc, oc * 128:(oc + 1) * 128], rhs=o_.t[:, r, kc, 0:n],
                                                              start=(kc == 0), stop=(kc == 3)), reads=[Wb.b, o_.b], writes=[pP[r].b])
                        for r in range(3):
                            S.op("act", lambda e: e.activation(out=sg[r].t[:, 0:n], in_=pG[r].t[:, 0:n], func=AF.Sigmoid), reads=[pG[r].b], writes=[sg[r].b])
                            S.op("dve", lambda e: e.tensor_tensor(out=gp[r].t[:, 0:n], in0=pP[r].t[:, 0:n], in1=sg[r].t[:, 0:n], op=ALU.mult),
                                 reads=[pP[r].b, sg[r].b], writes=[gp[r].b])
                        S.op("pool", lambda e: e.tensor_tensor(out=gp[0].t[:, 0:n], in0=gp[0].t[:, 0:n], in1=gp[1].t[:, 0:n], op=ALU.add), reads=[gp[0].b, gp[1].b], writes=[gp[0].b])
                        S.op("pool", lambda e: e.tensor_tensor(out=mT.t[:, oc, 0:n], in0=gp[0].t[:, 0:n], in1=gp[2].t[:, 0:n], op=ALU.add), reads=[gp[0].b, gp[2].b], writes=[mT.b])
                    for oc in range(8):
                        for kc in range(8):
                            S.op("pe", lambda e: e.matmul(pM.t[:, 0:n], lhsT=Wo.t[:, kc, oc * 128:(oc + 1) * 128], rhs=mT.t[:, kc, 0:n], start=(kc == 0), stop=(kc == 7)),
                                 reads=[Wo.b, mT.b], writes=[pM.b])
                        S.op("dve", lambda e: e.scalar_tensor_tensor(out=hn.t[:, oc, 0:n], in0=pM.t[:, 0:n], scalar=ada.t[:, 16 + oc, s:s + 1], in1=x_.t[:, oc, 0:n],
                                                                     op0=ALU.mult, op1=ALU.add), reads=[pM.b, ada.b, x_.b], writes=[hn.b])
                    dst = hdst.rearrange("(kc p) t -> p kc t", p=128)
                    S.dma(dst[:, :, t0:t0 + n], hn.t[:, :, 0:n], reads=[hn.b], writes=[db("hsrc%d" % id(hdst), si)], key="hn_st")
                    norm_mod(ph, hn, n, modA2, 24, isc, sq, pss, rs, fT)
                    S.dma(fTs[:, :, t0:t0 + n], fT.t[:, :, 0:n], reads=[fT.b], writes=[db("fTs", si)], key="fT_st")
            S.barrier()

        def phase_F1(l, do_ctx):
            with contextlib.ExitStack() as ph:
                Wf = sbt(ph, "Wf", [128, 8, 2 * FH], BF16)
                wsrc = f_in[l].rearrange("(kc p) n -> p kc n", p=128)
                for cb in range(11):
                    load_cast(lambda a0, a1: (Wf.t[:, a0:a1, cb * 512:(cb + 1) * 512], Wf.b), wsrc[:, :, cb * 512:(cb + 1) * 512], 8, 512,
                              eng=("pool" if cb % 2 == 0 else "act"))
                fT = [sbt(ph, "fT%d" % i, [128, 8, 512], BF16) for i in range(2)]
                aT = [sbt(ph, "aT%d" % i, [128, 22, 512], BF16) for i in range(2)]
                sa = [sbt(ph, "sa%d" % i, [128, 512], F32) for i in range(2)]
                pA = [pst(ph, "pA%d" % i) for i in range(3)]
                pGt = [pst(ph, "pGt%d" % i) for i in range(3)]
                sts = [s for s in STS if (not s[3]) or do_ctx]
                S.dma(fT[0].t[:, :, 0:sts[0][2]], fTs[:, :, sts[0][1]:sts[0][1] + sts[0][2]], reads=[db("fTs", sts[0][0])], writes=[fT[0].b], key=fT[0].b.name)
                for k, (si, t0, n, isc) in enumerate(sts):
                    if k + 1 < len(sts):
                        (si2, t02, n2, _) = sts[k + 1]
                        S.dma(fT[(k + 1) % 2].t[:, :, 0:n2], fTs[:, :, t02:t02 + n2], reads=[db("fTs", si2)], writes=[fT[(k + 1) % 2].b], key=fT[(k + 1) % 2].b.name)
                    f_, a_ = fT[k % 2], aT[k % 2]
                    for hc in range(22):
                        pa_, pg_ = pA[hc % 3], pGt[hc % 3]
                        for kc in range(8):
                            S.op("pe", lambda e: e.matmul(pa_.t[:, 0:n], lhsT=Wf.t[:, kc, hc * 128:(hc + 1) * 128], rhs=f_.t[:, kc, 0:n], start=(kc == 0), stop=(kc == 7)),
                                 reads=[Wf.b, f_.b], writes=[pa_.b])
                        for kc in range(8):
                            S.op("pe", lambda e: e.matmul(pg_.t[:, 0:n], lhsT=Wf.t[:, kc, FH + hc * 128:FH + (hc + 1) * 128], rhs=f_.t[:, kc, 0:n], start=(kc == 0), stop=(kc == 7)),
                                 reads=[Wf.b, f_.b], writes=[pg_.b])
                        s_ = sa[hc % 2]
                        S.op("act", lambda e: e.activation(out=s_.t[:, 0:n], in_=pa_.t[:, 0:n], func=AF.Silu), reads=[pa_.b], writes=[s_.b])
                        S.op("dve", lambda e: e.tensor_tensor(out=a_.t[:, hc, 0:n], in0=pg_.t[:, 0:n], in1=s_.t[:, 0:n], op=ALU.mult), reads=[pg_.b, s_.b], writes=[a_.b])
                    S.dma(actTs[:, :, t0:t0 + n], a_.t[:, :, 0:n], reads=[a_.b], writes=[db("actTs", si)], key=a_.b.name)
            S.barrier()

        def phase_F2(l, hsrc, hdst, do_ctx, final):
            with contextlib.ExitStack() as ph:
                Wfo = sbt(ph, "Wfo", [128, 22, 1024], BF16)
                load_cast(lambda a0, a1: (Wfo.t[:, a0:a1, :], Wfo.b), f_out[l].rearrange("(kc p) n -> p kc n", p=128), 22, 1024)
                aT = [sbt(ph, "aT%d" % i, [128, 22, 512], BF16) for i in range(2)]
                xt = [sbt(ph, "xt%d" % i, [128, 8, 512], F32) for i in range(2)]
                ho = [sbt(ph, "ho%d" % i, [128, 8, 512], F32) for i in range(2)]
                pO = [pst(ph, "pO%d" % i) for i in range(3)]
                sts = [s for s in STS if (not s[3]) or do_ctx]
                src = hsrc.rearrange("(kc p) t -> p kc t", p=128)

                def loads(k):
                    (si, t0, n, isc) = sts[k]
                    S.dma(aT[k % 2].t[:, :, 0:n], actTs[:, :, t0:t0 + n], reads=[db("actTs", si)], writes=[aT[k % 2].b], key=aT[k % 2].b.name)
                    S.dma(xt[k % 2].t[:, :, 0:n], src[:, :, t0:t0 + n], reads=[db("hsrc%d" % id(hsrc), si)], writes=[xt[k % 2].b], key=xt[k % 2].b.name)
                loads(0)
                for k, (si, t0, n, isc) in enumerate(sts):
                    if k + 1 < len(sts):
                        loads(k + 1)
                    s = 1 if isc else 0
                    a_, x_, h_ = aT[k % 2], xt[k % 2], ho[k % 2]
                    for oc in range(8):
                        po = pO[oc % 3]
                        for hc in range(22):
                            S.op("pe", lambda e: e.matmul(po.t[:, 0:n], lhsT=Wfo.t[:, hc, oc * 128:(oc + 1) * 128], rhs=a_.t[:, hc, 0:n], start=(hc == 0), stop=(hc == 21)),
                                 reads=[Wfo.b, a_.b], writes=[po.b])
                        S.op("dve", lambda e: e.scalar_tensor_tensor(out=h_.t[:, oc, 0:n], in0=po.t[:, 0:n], scalar=ada.t[:, 40 + oc, s:s + 1], in1=x_.t[:, oc, 0:n],
                                                                     op0=ALU.mult, op1=ALU.add), reads=[po.b, ada.b, x_.b], writes=[h_.b])
                    if final:
                        dst = outT.rearrange("(kc p) t -> p kc t", p=128)
                        S.dma(dst[:, :, t0 - NC_:t0 - NC_ + n], h_.t[:, :, 0:n], reads=[h_.b], writes=[db("outT", si)], key=h_.b.name)
                    else:
                        dst = hdst.rearrange("(kc p) t -> p kc t", p=128)
                        S.dma(dst[:, :, t0:t0 + n], h_.t[:, :, 0:n], reads=[h_.b], writes=[db("hsrc%d" % id(hdst), si)], key=h_.b.name)
            S.barrier()

        cur = xT
        for l in range(nlayers):
            last = (l == L - 1)
            do_ctx = not last
            if "a" in phases:
                phase_ada(l)
            if "A" in phases:
                phase_A(l, cur, do_ctx)
            if stop_after == "A":
                break
            if "B" in phases:
                phase_B(l, do_ctx)
            if stop_after == "B":
                break
            if "C" in phases:
                phase_C(l, do_ctx)
            if stop_after == "C":
                break
            if "D" in phases:
                phase_D(l, do_ctx)
            if stop_after == "D":
                break
            if "E" in phases:
                phase_E(l, cur, hres[0], do_ctx)
            if stop_after == "E":
                break
            if "F" in phases:
                phase_F1(l, do_ctx)
                phase_F2(l, hres[0], hres[1], do_ctx, last)
            cur = hres[1]
        S.barrier()
        g.stats = (S.nins, S.nwait, len(S.semh))
    return nc, dbg_names, g


def _consts():
    n_axis = 8
    freqs = (10000.0 ** (-np.arange(n_axis, dtype=np.float32) / n_axis)).astype(np.float32)
    rows = NX // 64
    row = np.repeat(np.arange(rows, dtype=np.float32), 64)
    col = np.tile(np.arange(64, dtype=np.float32), rows)
    angM = np.concatenate([row[:, None] * freqs, col[:, None] * freqs], axis=-1).astype(np.float32)
    n_axis_s = 16
    freqs_s = (10000.0 ** (-np.arange(n_axis_s, dtype=np.float32) / n_axis_s)).astype(np.float32)
    angS = np.concatenate([row[:, None] * freqs_s, col[:, None] * freqs_s], axis=-1).astype(np.float32)
    cMc = np.ones((96, NX), np.float32); cMs = np.zeros((96, NX), np.float32)
    cMc[64:80] = np.cos(angM).T; cMc[80:96] = np.cos(angM).T
    cMs[64:80] = np.sin(angM).T; cMs[80:96] = np.sin(angM).T
    cSc = np.zeros((128, NX), np.float32); cSs = np.zeros((128, NX), np.float32)
    for hh in range(2):
        for half in range(2):
            r0 = hh * 64 + half * 32
            cSc[r0:r0 + 32] = np.cos(angS).T
            cSs[r0:r0 + 32] = np.sin(angS).T
    cPm = np.zeros((96, 96), np.float32)
    for i in range(16):
        cPm[80 + i, 64 + i] = -1.0
        cPm[64 + i, 80 + i] = 1.0
    cPs = np.zeros((128, 128), np.float32)
    for hh in range(2):
        for i in range(32):
            cPs[hh * 64 + 32 + i, hh * 64 + i] = -1.0
            cPs[hh * 64 + i, hh * 64 + 32 + i] = 1.0
    cShift = np.zeros((32, 96), np.float32)
    for i in range(32):
        cShift[i, 64 + i] = 1.0
    j = np.arange(128)[:, None]; i = np.arange(128)[None, :]
    cMge = (j >= i).astype(np.float32)
    cMle = (j <= i).astype(np.float32)
    cBones = np.zeros((128, 128), np.float32)
    cBones[0:64, 0:64] = 1.0; cBones[64:128, 64:128] = 1.0
    cIota = np.tile(np.arange(512, dtype=np.float32)[None, :], (128, 1))
    cId = np.eye(128, dtype=np.float32)
    return dict(cMc=cMc, cMs=cMs, cSc=cSc, cSs=cSs, cPm=cPm, cPs=cPs, cShift=cShift, cMge=cMge, cMle=cMle,
                cBones=cBones, cIota=cIota, cId=cId)


def _colv(v, nch):
    return np.ascontiguousarray(np.transpose(v.reshape(v.shape[0], nch, 128), (0, 2, 1)))


def _shared_inputs(inp):
    f = lambda a: np.ascontiguousarray(np.asarray(a, dtype=np.float32))
    sh = {}
    sh["w_ada"] = f(inp["w_ada"]); sh["badac"] = _colv(f(inp["b_ada"]), 48)
    sh["g1c"] = _colv(f(inp["norm1_g"]), 8); sh["g2c"] = _colv(f(inp["norm2_g"]), 8)
    sh["w_in"] = f(inp["w_in"])
    sh["qagc"] = _colv(f(inp["mla_qa_g"]), 3); sh["kvagc"] = _colv(f(inp["mla_kva_g"]), 2)
    sh["w_uq"] = f(inp["mla_w_uq"]); sh["w_ukv"] = f(inp["mla_w_ukv"])
    sh["mqkg"] = np.ascontiguousarray(np.transpose(f(inp["mla_qk_g"]), (0, 2, 1)))
    sw = np.transpose(f(inp["swa_qk_g"]), (0, 2, 1))
    sh["swag"] = np.ascontiguousarray(np.concatenate([sw, sw], axis=1))
    sh["sinkb"] = np.ascontiguousarray(np.broadcast_to(f(inp["swa_sink"])[:, None, :], (L, 128, 8)))

    def pairlay(a):
        return np.ascontiguousarray(np.transpose(a.reshape(L, 2, 16, 2, 64), (0, 3, 4, 1, 2)).reshape(L, 128, 2, 16))
    sh["s5lre"] = pairlay(f(inp["s5_lambda_re"])); sh["s5lim"] = pairlay(f(inp["s5_lambda_im"]))
    ls = np.broadcast_to(f(inp["s5_log_step"])[:, :, :, None], (L, 2, 32, 64))
    sh["s5ls"] = pairlay(np.ascontiguousarray(ls))

    def pairlay_b(a):
        return np.ascontiguousarray(np.transpose(a.reshape(L, 2, 16, 2, 64, 16), (0, 3, 4, 1, 2, 5)).reshape(L, 128, 2, 16, 16))
    sh["s5bre"] = pairlay_b(f(inp["s5_b_re"])); sh["s5bim"] = pairlay_b(f(inp["s5_b_im"]))
    sh["s5cre"] = pairlay_b(np.ascontiguousarray(np.transpose(f(inp["s5_c_re"]), (0, 1, 2, 4, 3))))
    sh["s5cim"] = pairlay_b(np.ascontiguousarray(np.transpose(f(inp["s5_c_im"]), (0, 1, 2, 4, 3))))
    sh["s5dc"] = _colv(f(inp["s5_d"]), 4); sh["s5bgc"] = _colv(f(inp["s5_b_glu"]), 4)
    sh["w_glu"] = f(inp["s5_w_glu"]); sh["w_br"] = f(inp["w_branch"]); sh["w_o"] = f(inp["w_out"])
    sh["f_in"] = f(inp["ffn_w_in"]); sh["f_out"] = f(inp["ffn_w_out"])
    sh.update(_consts())
    return sh


def _core_inputs(inp, b, sh):
    x = np.asarray(inp["x"][b], np.float32); ctx = np.asarray(inp["ctx"][b], np.float32)
    m = dict(sh)
    m["xT"] = np.ascontiguousarray(np.concatenate([ctx, x], axis=0).T)
    cc = np.stack([np.asarray(inp["c"][b], np.float32), np.asarray(inp["c_ctx"], np.float32)], axis=-1)
    m["ccol"] = np.ascontiguousarray(np.transpose(cc.reshape(8, 128, 2), (1, 0, 2)))
    return m


_CACHE = {}


def kernel(**inputs):
    if "nc" not in _CACHE:
        _CACHE["nc"] = build()[0]
    nc = _CACHE["nc"]
    sh = _shared_inputs(inputs)
    in_maps = [_core_inputs(inputs, b, sh) for b in range(8)]
    res = run_bass_kernel_spmd(nc, in_maps, core_ids=list(range(8)))
    out = np.stack([np.ascontiguousarray(r["outT"].T) for r in res.results], axis=0)
    return out.astype(np.float32)
```

```python
import contextlib
import math
import numpy as np
import concourse.bass as bass
import concourse.mybir as mybir
from concourse.bass_utils import run_bass_kernel_spmd

F32 = mybir.dt.float32
BF16 = mybir.dt.bfloat16
I32 = mybir.dt.int32
AF = mybir.ActivationFunctionType
ALU = mybir.AluOpType

D = 1024
NX = 4096
NC_ = 256
T = NX + NC_
L = 2
EPS = 1e-6
FH = 2816
DIN = 5024
O_KR = 640
O_U = 672
O_SQ = 1184
O_SK = 1696
O_SV = 1824
O_G = 1952
import os as _os
D_UENG = _os.environ.get("UENG", "dve,dve,dve,dve").split(",")
D_XENG = _os.environ.get("XENG", "pool")
D_TENG = _os.environ.get("TENG", "dve,dve,dve,dve").split(",")
D_EVAC = _os.environ.get("D_EVAC", "0") == "1"
TWO_PI = 2.0 * math.pi

STS = [(0, 0, 256, True)] + [(1 + k, 256 + 512 * k, 512, False) for k in range(8)]


class Buf:
    __slots__ = ("name", "w", "r")

    def __init__(self, name):
        self.name = name
        self.w = {}
        self.r = {}


def _merge(d, s):
    for k, v in s.items():
        if d.get(k, 0) < v:
            d[k] = v


class Sched:
    def __init__(self, nc, stack):
        self.nc = nc
        self.stack = stack
        self.E = {"pe": nc.tensor, "act": nc.scalar, "dve": nc.vector,
                  "pool": nc.gpsimd, "sp": nc.sync}
        self.semh = {}
        self.cnt = {}
        self.seen = {k: {} for k in self.E}
        for k in self.E:
            self.semh[k] = stack.enter_context(nc.semaphore("s_" + k))
            self.cnt[k] = 0
        self.nins = 0
        self.nwait = 0
        self.alias = {}
        self.free = []
        self.dkeys = []

    def _sem(self, key):
        if key in self.alias:
            return self.alias[key]
        if self.free:
            ck = self.free.pop()
        else:
            ck = "dq%d" % len(self.dkeys)
            self.dkeys.append(ck)
            self.semh[ck] = self.stack.enter_context(self.nc.semaphore(ck))
            self.cnt[ck] = 0
        self.alias[key] = ck
        return ck

    def _wait(self, eng, deps):
        seen = self.seen[eng]
        for key, val in deps.items():
            if val <= 0 or seen.get(key, 0) >= val:
                continue
            self.E[eng].wait_ge(self.semh[key], val)
            seen[key] = val
            self.nwait += 1

    def _deps(self, reads, writes):
        deps = {}
        for b in reads:
            _merge(deps, b.w)
        for b in writes:
            _merge(deps, b.w)
            _merge(deps, b.r)
        return deps

    def op(self, eng, fn, reads=(), writes=()):
        deps = self._deps(reads, writes)
        if eng == "pe":
            deps.pop("pe", None)
        self._wait(eng, deps)
        ins = fn(self.E[eng])
        self.cnt[eng] += 1
        v = self.cnt[eng]
        ins.then_inc(self.semh[eng], 1)
        for b in reads:
            if b.r.get(eng, 0) < v:
                b.r[eng] = v
        for b in writes:
            b.w = {eng: v}
            b.r = {}
        self.nins += 1
        return ins

    def dma(self, out, in_, reads=(), writes=(), key=None, eng="sp"):
        key = self._sem(key)
        deps = self._deps(reads, writes)
        deps.pop(key, None)
        self._wait(eng, deps)
        ins = self.E[eng].dma_start(out=out, in_=in_)
        self.cnt[key] += 16
        v = self.cnt[key]
        ins.then_inc(self.semh[key], 16)
        for b in reads:
            if b.r.get(key, 0) < v:
                b.r[key] = v
        for b in writes:
            b.w = {key: v}
            b.r = {}
        self.nins += 1
        return ins

    def barrier(self):
        deps = {k: v for k, v in self.cnt.items() if v > 0}
        for e in self.E:
            self._wait(e, deps)
        self.alias = {}
        self.free = list(self.dkeys)


class TT:
    __slots__ = ("t", "b")

    def __init__(self, t, b):
        self.t = t
        self.b = b


class Ctx:
    pass


def build(dbg=False, nlayers=L, stop_after=None, phases="aABCDEF"):
    nc = bass.Bass("TRN2", target_bir_lowering=False)
    g = Ctx()
    g.nc = nc

    def din(name, shape, dt=F32):
        return nc.dram_tensor(name, list(shape), dt, kind="ExternalInput").ap()

    dbg_names = []

    def dscr(name, shape, dt):
        if dbg:
            dbg_names.append(name)
            return nc.dram_tensor(name, list(shape), dt, kind="ExternalOutput").ap()
        return nc.dram_tensor(name, list(shape), dt).ap()

    xT = din("xT", [D, T])
    ccol = din("ccol", [128, 8, 2])
    w_ada = din("w_ada", [L, D, 6 * D])
    badac = din("badac", [L, 128, 48])
    g1c = din("g1c", [L, 128, 8])
    g2c = din("g2c", [L, 128, 8])
    w_in = din("w_in", [L, D, DIN])
    qagc = din("qagc", [L, 128, 3])
    kvagc = din("kvagc", [L, 128, 2])
    w_uq = din("w_uq", [L, 384, 768])
    w_ukv = din("w_ukv", [L, 256, 1024])
    mqkg = din("mqkg", [L, 96, 2])
    swag = din("swag", [L, 128, 2])
    sinkb = din("sinkb", [L, 128, 8])
    s5lre = din("s5lre", [L, 128, 2, 16])
    s5lim = din("s5lim", [L, 128, 2, 16])
    s5ls = din("s5ls", [L, 128, 2, 16])
    s5bre = din("s5bre", [L, 128, 2, 16, 16])
    s5bim = din("s5bim", [L, 128, 2, 16, 16])
    s5cre = din("s5cre", [L, 128, 2, 16, 16])
    s5cim = din("s5cim", [L, 128, 2, 16, 16])
    s5dc = din("s5dc", [L, 128, 4])
    s5bgc = din("s5bgc", [L, 128, 4])
    w_glu = din("w_glu", [L, 512, 512])
    w_br = din("w_br", [L, 3, 512, D])
    w_o = din("w_o", [L, D, D])
    f_in = din("f_in", [L, D, 2 * FH])
    f_out = din("f_out", [L, FH, D])
    cMc = din("cMc", [96, NX]); cMs = din("cMs", [96, NX])
    cSc = din("cSc", [128, NX]); cSs = din("cSs", [128, NX])
    cPm = din("cPm", [96, 96]); cPs = din("cPs", [128, 128])
    cShift = din("cShift", [32, 96])
    cMge = din("cMge", [128, 128]); cMle = din("cMle", [128, 128])
    cBones = din("cBones", [128, 128])
    cIota = din("cIota", [128, 512])
    cId = din("cId", [128, 128])

    outT = nc.dram_tensor("outT", [D, NX], F32, kind="ExternalOutput").ap()

    hres = [dscr("hres%d" % i, [D, T], F32) for i in range(2)]
    hTs = dscr("hTs", [128, 8, T], BF16)
    QTs = dscr("QTs", [8, 96, T], BF16)
    KTs = dscr("KTs", [8, 96, T], BF16)
    VsM = dscr("VsM", [34, 128, 8, 128], BF16)
    uTs = dscr("uTs", [128, 4, T], BF16)
    sQs = dscr("sQs", [64, 8, T], BF16)
    sKs = dscr("sKs", [64, 2, T], BF16)
    sVs = dscr("sVs", [34, 128, 2, 128], BF16)
    mlaO = dscr("mlaO", [512, T], BF16)
    swaO = dscr("swaO", [512, T], BF16)
    s5O = dscr("s5O", [128, 4, T], BF16)
    ygs = dscr("ygs", [128, 4, T], BF16)
    fTs = dscr("fTs", [128, 8, T], BF16)
    actTs = dscr("actTs", [128, 22, T], BF16)

    dbufs = {}

    def db(name, idx=0):
        k = (name, idx)
        if k not in dbufs:
            dbufs[k] = Buf("%s_%s" % (name, idx))
        return dbufs[k]

    with contextlib.ExitStack() as top:
        S = Sched(nc, top)
        uid = [0]

        def sbt(ctx, name, shape, dt):
            uid[0] += 1
            nm = "%s_%d" % (name, uid[0])
            t = ctx.enter_context(nc.sbuf_tensor(nm, list(shape), dt))
            return TT(t, Buf(nm))

        def pst(ctx, name, shape=(128, 512), dt=F32):
            uid[0] += 1
            nm = "%s_%d" % (name, uid[0])
            t = ctx.enter_context(nc.psum_tensor(nm, list(shape), dt))
            return TT(t, Buf(nm))

        ones32 = sbt(top, "ones32", [128, 128], F32)
        S.op("pool", lambda e: e.memset(ones32.t[:], 1.0), writes=[ones32.b])
        bones32 = sbt(top, "bones32", [128, 128], F32)
        S.dma(bones32.t[:], cBones[:, :], writes=[bones32.b], key="c_bones")
        stgc = sbt(top, "stgc", [128, 128], F32)

        def const_bf(name, src, rows, cols):
            t = sbt(top, name, [rows, cols], BF16)
            S.dma(stgc.t[0:rows, 0:cols], src[:, :], writes=[stgc.b], key="c_stg")
            S.op("dve", lambda e: e.tensor_copy(out=t.t[:], in_=stgc.t[0:rows, 0:cols]),
                 reads=[stgc.b], writes=[t.b])
            return t

        Pm = const_bf("Pm", cPm, 96, 96)
        Ps = const_bf("Ps", cPs, 128, 128)
        shiftI = const_bf("shiftI", cShift, 32, 96)
        Mge = const_bf("Mge", cMge, 128, 128)
        Mle = const_bf("Mle", cMle, 128, 128)
        identb = const_bf("identb", cId, 128, 128)

        stg = [sbt(top, "stg%d" % i, [128, 2048], F32) for i in range(2)]
        stg_i = [0]

        def load_cast(dst_ap_fn, src3, A, B, scale_fn=None, eng="pool"):
            step = 1 if scale_fn is not None else max(1, 2048 // B)
            a0 = 0
            while a0 < A:
                a1 = min(A, a0 + step)
                s_ = stg[stg_i[0] % 2]
                stg_i[0] += 1
                na = a1 - a0
                view = s_.t[:, 0:na * B].rearrange("p (a b) -> p a b", b=B)
                S.dma(view, src3[:, a0:a1, :], writes=[s_.b], key=s_.b.name)
                dst, dbuf = dst_ap_fn(a0, a1)
                if scale_fn is None:
                    if eng == "act":
                        S.op(eng, lambda e: e.copy(out=dst, in_=view), reads=[s_.b], writes=[dbuf])
                    else:
                        S.op(eng, lambda e: e.tensor_copy(out=dst, in_=view), reads=[s_.b], writes=[dbuf])
                else:
                    assert na == 1
                    sc, scb = scale_fn(a0)
                    S.op(eng, lambda e: e.tensor_scalar(out=dst, in0=view, scalar1=sc, scalar2=None, op0=ALU.mult),
                         reads=[s_.b, scb], writes=[dbuf])
                a0 = a1

        def rsqrt_from(ctx_eng_out, out_t, in_ap, in_buf, scale, shape_ap=None):
            o = out_t.t[:] if shape_ap is None else shape_ap
            S.op("act", lambda e: e.activation(out=o, in_=in_ap, func=AF.Ln, bias=epsc.t[0:o.shape[0], 0:1], scale=scale),
                 reads=[in_buf, epsc.b], writes=[out_t.b])
            S.op("act", lambda e: e.activation(out=o, in_=o, func=AF.Exp, scale=-0.5), reads=[out_t.b], writes=[out_t.b])

        epsc = sbt(top, "epsc", [128, 1], F32)
        S.op("pool", lambda e: e.memset(epsc.t[:], EPS), writes=[epsc.b])

        modA1 = sbt(top, "modA1", [128, 8, 2], F32)
        modA2 = sbt(top, "modA2", [128, 8, 2], F32)
        ada = sbt(top, "ada", [128, 48, 2], F32)

        def phase_ada(l):
            with contextlib.ExitStack() as ph:
                cc = sbt(ph, "cc", [128, 8, 2], F32)
                S.dma(cc.t[:], ccol[:, :, :], writes=[cc.b], key="cc")
                sc = sbt(ph, "sc", [128, 8, 2], F32)
                S.op("act", lambda e: e.activation(out=sc.t[:], in_=cc.t[:], func=AF.Silu), reads=[cc.b], writes=[sc.b])
                bad = sbt(ph, "bad", [128, 48], F32)
                S.dma(bad.t[:], badac[l], writes=[bad.b], key="bad")
                gg = sbt(ph, "gg", [128, 16], F32)
                S.dma(gg.t[:, 0:8], g1c[l], writes=[gg.b], key="gg")
                S.dma(gg.t[:, 8:16], g2c[l], writes=[gg.b], key="gg")
                wa = [sbt(ph, "wa%d" % i, [128, 8, 512], F32) for i in range(2)]
                pa = pst(ph, "pa", [128, 96], F32)
                wsrc = w_ada[l].rearrange("(kc p) n -> p kc n", p=128)
                for cb in range(12):
                    w = wa[cb % 2]
                    S.dma(w.t[:], wsrc[:, :, cb * 512:(cb + 1) * 512], writes=[w.b], key=w.b.name)
                    for f4 in range(4):
                        fc = cb * 4 + f4
                        for kc in range(8):
                            S.op("pe", lambda e: e.matmul(pa.t[:, fc * 2:fc * 2 + 2], lhsT=w.t[:, kc, f4 * 128:(f4 + 1) * 128],
                                                          rhs=sc.t[:, kc, :], start=(kc == 0), stop=(kc == 7)),
                                 reads=[w.b, sc.b], writes=[pa.b])
                S.op("dve", lambda e: e.tensor_tensor(out=ada.t[:], in0=pa.t[:].rearrange("p (c s) -> p c s", s=2),
                                                      in1=bad.t[:].unsqueeze(2).to_broadcast([128, 48, 2]), op=ALU.add),
                     reads=[pa.b, bad.b], writes=[ada.b])
                for (mod, sc0, gofs) in ((modA1, 8, 0), (modA2, 32, 8)):
                    S.op("dve", lambda e: e.tensor_scalar(out=mod.t[:], in0=ada.t[:, sc0:sc0 + 8, :], scalar1=1.0, scalar2=None, op0=ALU.add),
                         reads=[ada.b], writes=[mod.b])
                    S.op("dve", lambda e: e.tensor_tensor(out=mod.t[:], in0=mod.t[:],
                                                          in1=gg.t[:, gofs:gofs + 8].unsqueeze(2).to_broadcast([128, 8, 2]), op=ALU.mult),
                         reads=[mod.b, gg.b], writes=[mod.b])
            S.barrier()

        def norm_mod(ph, xt, n, modA, shofs, si_ctx, sq, pss, rs, hT):
            s = 1 if si_ctx else 0
            S.op("act", lambda e: e.activation(out=sq.t[:, :, 0:n], in_=xt.t[:, :, 0:n], func=AF.Square), reads=[xt.b], writes=[sq.b])
            for kc in range(8):
                S.op("pe", lambda e: e.matmul(pss.t[:, 0:n], lhsT=ones32.t[:], rhs=sq.t[:, kc, 0:n], start=(kc == 0), stop=(kc == 7)),
                     reads=[ones32.b, sq.b], writes=[pss.b])
            rsqrt_from(None, rs, pss.t[:, 0:n], pss.b, 1.0 / D, shape_ap=rs.t[:, 0:n])
            S.op("dve", lambda e: e.tensor_tensor(out=sq.t[:, :, 0:n], in0=xt.t[:, :, 0:n],
                                                  in1=rs.t[:, 0:n].unsqueeze(1).to_broadcast([128, 8, n]), op=ALU.mult),
                 reads=[xt.b, rs.b], writes=[sq.b])
            for kc in range(8):
                S.op("act", lambda e: e.activation(out=hT.t[:, kc, 0:n], in_=sq.t[:, kc, 0:n], func=AF.Identity,
                                                   bias=ada.t[:, shofs + kc, s:s + 1], scale=modA.t[:, kc, s:s + 1]),
                     reads=[sq.b, ada.b, modA.b], writes=[hT.b])

        def phase_A(l, hsrc, do_ctx_q):
            with contextlib.ExitStack() as ph:
                Win = sbt(ph, "Win", [128, 8, O_G], BF16)
                wsrc = w_in[l].rearrange("(kc p) n -> p kc n", p=128)
                load_cast(lambda a0, a1: (Win.t[:, a0:a1, :], Win.b), wsrc[:, :, 0:O_G], 8, O_G)
                qag = sbt(ph, "qag", [128, 5], F32)
                S.dma(qag.t[:, 0:3], qagc[l], writes=[qag.b], key="qag")
                S.dma(qag.t[:, 3:5], kvagc[l], writes=[qag.b], key="qag")
                Wuq = sbt(ph, "Wuq", [128, 3, 768], BF16)
                load_cast(lambda a0, a1: (Wuq.t[:, a0:a1, :], Wuq.b), w_uq[l].rearrange("(kc p) n -> p kc n", p=128), 3, 768,
                          scale_fn=lambda a: (qag.t[:, a:a + 1], qag.b))
                Wkp = sbt(ph, "Wkp", [128, 2, 8, 96], BF16)
                S.op("pool", lambda e: e.memset(Wkp.t[:], 0.0), writes=[Wkp.b])
                Wv = sbt(ph, "Wv", [128, 2, 8, 64], BF16)
                ukv = w_ukv[l].rearrange("(kc p) n -> p kc n", p=128)
                for kc in range(2):
                    s_ = stg[stg_i[0] % 2]
                    stg_i[0] += 1
                    S.dma(s_.t[:, 0:1024], ukv[:, kc, :], writes=[s_.b], key=s_.b.name)
                    v3 = s_.t[:, 0:1024].rearrange("p (h c) -> p h c", c=128)
                    S.op("pool", lambda e: e.tensor_scalar(out=Wkp.t[:, kc, :, 0:64], in0=v3[:, :, 0:64], scalar1=qag.t[:, 3 + kc:4 + kc],
                                                           scalar2=None, op0=ALU.mult), reads=[s_.b, qag.b], writes=[Wkp.b])
                    S.op("pool", lambda e: e.tensor_scalar(out=Wv.t[:, kc, :, :], in0=v3[:, :, 64:128], scalar1=qag.t[:, 3 + kc:4 + kc],
                                                           scalar2=None, op0=ALU.mult), reads=[s_.b, qag.b], writes=[Wv.b])
                Wkd = sbt(ph, "Wkd", [128, 8, 2, 128], BF16)
                for kh in range(2):
                    for hf in range(2):
                        S.op("pool", lambda e: e.tensor_copy(out=Wkd.t[:, :, kh, hf * 64:(hf + 1) * 64],
                                                             in_=Win.t[:, :, O_SK + kh * 64:O_SK + (kh + 1) * 64]),
                             reads=[Win.b], writes=[Wkd.b])
                gq = sbt(ph, "gq", [128, 4], F32)
                S.dma(gq.t[0:96, 0:2], mqkg[l], writes=[gq.b], key="gq")
                S.dma(gq.t[:, 2:4], swag[l], writes=[gq.b], key="gq")

                xt = sbt(ph, "xt", [128, 8, 512], F32)
                sq = sbt(ph, "sq", [128, 8, 512], F32)
                rs = sbt(ph, "rs", [128, 512], F32)
                hT = sbt(ph, "hT", [128, 8, 512], BF16)
                q32 = sbt(ph, "q32", [128, 3, 512], F32)
                sqq = sbt(ph, "sqq", [128, 3, 512], F32)
                rq = sbt(ph, "rq", [128, 512], F32)
                qn = sbt(ph, "qn", [128, 3, 512], BF16)
                kvn = sbt(ph, "kvn", [128, 2, 512], BF16)
                krT = sbt(ph, "krT", [32, 512], BF16)
                uT = sbt(ph, "uT", [128, 4, 512], BF16)
                NH = 4
                hq32 = [sbt(ph, "hq32_%d" % i, [128, 512], F32) for i in range(NH)]
                hsq = [sbt(ph, "hsq_%d" % i, [128, 512], F32) for i in range(NH)]
                hqn = [sbt(ph, "hqn_%d" % i, [128, 512], F32) for i in range(NH)]
                hqb = [sbt(ph, "hqb_%d" % i, [128, 512], BF16) for i in range(NH)]
                hout = [sbt(ph, "hout_%d" % i, [128, 512], BF16) for i in range(NH)]
                va = [sbt(ph, "va_%d" % i, [128, 8, 128], BF16) for i in range(2)]
                sva = [sbt(ph, "sva_%d" % i, [128, 2, 128], BF16) for i in range(2)]
                for v_ in va + sva:
                    S.op("pool", lambda e: e.memset(v_.t[:], 1.0), writes=[v_.b])
                rope = sbt(ph, "rope", [128, 4, 512], F32)
                pss = pst(ph, "pss")
                pp = [pst(ph, "pp%d" % i) for i in range(2)]
                hp = [pst(ph, "hp%d" % i) for i in range(NH)]
                pv = pst(ph, "pv")
                cnt = [0]

                def headnorm(i, mm_fn, rows, gcol, onesT, dim, use_rope, Pmat, rc, rs_, n, dst_ap, dst_bufs):
                    ps, a32, asq, aqn, aqb, ao = hp[i], hq32[i], hsq[i], hqn[i], hqb[i], hout[i]
                    mm_fn(ps)
                    yield
                    S.op("act", lambda e: e.copy(out=a32.t[0:rows, 0:n], in_=ps.t[0:rows, 0:n]), reads=[ps.b], writes=[a32.b])
                    S.op("act", lambda e: e.activation(out=asq.t[0:rows, 0:n], in_=ps.t[0:rows, 0:n], func=AF.Square), reads=[ps.b], writes=[asq.b])
                    yield
                    S.op("pe", lambda e: e.matmul(ps.t[0:rows, 0:n], lhsT=onesT.t[0:rows, 0:rows], rhs=asq.t[0:rows, 0:n], start=True, stop=True),
                         reads=[onesT.b, asq.b], writes=[ps.b])
                    yield
                    rsqrt_from(None, asq, ps.t[0:rows, 0:n], ps.b, 1.0 / dim, shape_ap=asq.t[0:rows, 0:n])
                    yield
                    if not use_rope:
                        S.op("dve", lambda e: e.scalar_tensor_tensor(out=ao.t[0:rows, 0:n], in0=a32.t[0:rows, 0:n], scalar=gcol,
                                                                     in1=asq.t[0:rows, 0:n], op0=ALU.mult, op1=ALU.mult),
                             reads=[a32.b, asq.b, gq.b], writes=[ao.b])
                    else:
                        S.op("dve", lambda e: e.scalar_tensor_tensor(out=aqn.t[0:rows, 0:n], in0=a32.t[0:rows, 0:n], scalar=gcol,
                                                                     in1=asq.t[0:rows, 0:n], op0=ALU.mult, op1=ALU.mult),
                             reads=[a32.b, asq.b, gq.b], writes=[aqn.b])
                        yield
                        S.op("act", lambda e: e.copy(out=aqb.t[0:rows, 0:n], in_=aqn.t[0:rows, 0:n]), reads=[aqn.b], writes=[aqb.b])
                        yield
                        S.op("pe", lambda e: e.matmul(ps.t[0:rows, 0:n], lhsT=Pmat.t[0:rows, 0:rows], rhs=aqb.t[0:rows, 0:n], start=True, stop=True),
                             reads=[Pmat.b, aqb.b], writes=[ps.b])
                        S.op("pool", lambda e: e.tensor_tensor(out=a32.t[0:rows, 0:n], in0=aqn.t[0:rows, 0:n], in1=rope.t[0:rows, rc, 0:n], op=ALU.mult),
                             reads=[aqn.b, rope.b], writes=[a32.b])
                        yield
                        S.op("dve", lambda e: e.tensor_tensor(out=asq.t[0:rows, 0:n], in0=ps.t[0:rows, 0:n], in1=rope.t[0:rows, rs_, 0:n], op=ALU.mult),
                             reads=[ps.b, rope.b], writes=[asq.b])
                        yield
                        S.op("dve", lambda e: e.tensor_tensor(out=ao.t[0:rows, 0:n], in0=a32.t[0:rows, 0:n], in1=asq.t[0:rows, 0:n], op=ALU.add),
                             reads=[a32.b, asq.b], writes=[ao.b])
                    yield
                    if isinstance(dst_ap, list):
                        for (d_ap, r0, r1) in dst_ap:
                            S.dma(d_ap, ao.t[r0:r1, 0:n], reads=[ao.b], writes=dst_bufs, key=ao.b.name)
                    else:
                        S.dma(dst_ap, ao.t[0:rows, 0:n], reads=[ao.b], writes=dst_bufs, key=ao.b.name)

                def run_chains(jobs):
                    live = []
                    free_slots = list(range(NH))
                    nj = 0
                    while live or nj < len(jobs):
                        while len(live) < NH and nj < len(jobs):
                            sl = free_slots.pop(0)
                            live.append((jobs[nj](sl), sl))
                            nj += 1
                        for (g_, sl) in list(live):
                            try:
                                next(g_)
                            except StopIteration:
                                live.remove((g_, sl))
                                free_slots.append(sl)

                for (si, t0, n, isc) in STS:
                    s = 1 if isc else 0
                    src = hsrc.rearrange("(kc p) t -> p kc t", p=128)
                    S.dma(xt.t[:, :, 0:n], src[:, :, t0:t0 + n], reads=[db("hsrc%d" % id(hsrc), si)], writes=[xt.b], key="xt")
                    if not isc:
                        x0 = t0 - NC_
                        S.dma(rope.t[0:96, 0, 0:n], cMc[:, x0:x0 + n], writes=[rope.b], key="rope")
                        S.dma(rope.t[0:96, 1, 0:n], cMs[:, x0:x0 + n], writes=[rope.b], key="rope")
                        S.dma(rope.t[:, 2, 0:n], cSc[:, x0:x0 + n], writes=[rope.b], key="rope")
                        S.dma(rope.t[:, 3, 0:n], cSs[:, x0:x0 + n], writes=[rope.b], key="rope")
                    norm_mod(ph, xt, n, modA1, 0, isc, sq, pss, rs, hT)
                    S.dma(hTs[:, :, t0:t0 + n], hT.t[:, :, 0:n], reads=[hT.b], writes=[db("hTs", si)], key="hT_st")

                    def proj(ps, c0, ncols=128):
                        for kc in range(8):
                            S.op("pe", lambda e: e.matmul(ps.t[0:ncols, 0:n], lhsT=Win.t[:, kc, c0:c0 + ncols], rhs=hT.t[:, kc, 0:n],
                                                          start=(kc == 0), stop=(kc == 7)), reads=[Win.b, hT.b], writes=[ps.b])
                    pi = [0]

                    def nextp():
                        pi[0] += 1
                        return pp[pi[0] % 2]

                    for (nch, c0, dst, dim) in ((3, 0, qn, 384), (2, 384, kvn, 256)):
                        for c in range(nch):
                            ps = nextp()
                            proj(ps, c0 + c * 128)
                            S.op("act", lambda e: e.copy(out=q32.t[:, c, 0:n], in_=ps.t[:, 0:n]), reads=[ps.b], writes=[q32.b])
                        S.op("pool", lambda e: e.tensor_tensor(out=sqq.t[:, 0:nch, 0:n], in0=q32.t[:, 0:nch, 0:n], in1=q32.t[:, 0:nch, 0:n], op=ALU.mult),
                             reads=[q32.b], writes=[sqq.b])
                        for c in range(nch):
                            S.op("pe", lambda e: e.matmul(pss.t[:, 0:n], lhsT=ones32.t[:], rhs=sqq.t[:, c, 0:n], start=(c == 0), stop=(c == nch - 1)),
                                 reads=[ones32.b, sqq.b], writes=[pss.b])
                        rsqrt_from(None, rq, pss.t[:, 0:n], pss.b, 1.0 / dim, shape_ap=rq.t[:, 0:n])
                        S.op("dve", lambda e: e.tensor_tensor(out=dst.t[:, 0:nch, 0:n], in0=q32.t[:, 0:nch, 0:n],
                                                              in1=rq.t[:, 0:n].unsqueeze(1).to_broadcast([128, nch, n]), op=ALU.mult),
                             reads=[q32.b, rq.b], writes=[dst.b])
                    ps = nextp()
                    proj(ps, O_KR, 32)
                    S.op("act", lambda e: e.copy(out=krT.t[:, 0:n], in_=ps.t[0:32, 0:n]), reads=[ps.b], writes=[krT.b])
                    jobs = []
                    for h in range(8):
                        if (not isc) or do_ctx_q:
                            def mmq(ps, h=h):
                                for c in range(3):
                                    S.op("pe", lambda e: e.matmul(ps.t[0:96, 0:n], lhsT=Wuq.t[:, c, h * 96:(h + 1) * 96], rhs=qn.t[:, c, 0:n],
                                                                  start=(c == 0), stop=(c == 2)), reads=[Wuq.b, qn.b], writes=[ps.b])
                            jobs.append(lambda i, h=h, mmq=mmq: headnorm(i, mmq, 96, gq.t[0:96, 0:1], ones32, 96.0, not isc, Pm, 0, 1, n,
                                                                           QTs[h, :, t0:t0 + n], [db("QTs", (h, si))]))

                        def mmk(ps, h=h):
                            for c in range(2):
                                S.op("pe", lambda e: e.matmul(ps.t[0:96, 0:n], lhsT=Wkp.t[:, c, h, :], rhs=kvn.t[:, c, 0:n],
                                                              start=(c == 0), stop=False), reads=[Wkp.b, kvn.b], writes=[ps.b])
                            S.op("pe", lambda e: e.matmul(ps.t[0:96, 0:n], lhsT=shiftI.t[:, :], rhs=krT.t[:, 0:n], start=False, stop=True),
                                 reads=[shiftI.b, krT.b], writes=[ps.b])
                        jobs.append(lambda i, h=h, mmk=mmk: headnorm(i, mmk, 96, gq.t[0:96, 1:2], ones32, 96.0, not isc, Pm, 0, 1, n,
                                                                       KTs[h, :, t0:t0 + n], [db("KTs", (h, si))]))
                    for c in range(4):
                        def mmsq(ps, c=c):
                            proj(ps, O_SQ + c * 128)
                        jobs.append(lambda i, c=c, mmsq=mmsq: headnorm(i, mmsq, 128, gq.t[:, 2:3], bones32, 64.0, not isc, Ps, 2, 3, n,
                                                                         [(sQs[:, 2 * c, t0:t0 + n], 0, 64), (sQs[:, 2 * c + 1, t0:t0 + n], 64, 128)],
                                                                         [db("sQs", (c, si))]))
                    for kh in range(2):
                        def mmsk(ps, kh=kh):
                            for kc in range(8):
                                S.op("pe", lambda e: e.matmul(ps.t[:, 0:n], lhsT=Wkd.t[:, kc, kh, :], rhs=hT.t[:, kc, 0:n],
                                                              start=(kc == 0), stop=(kc == 7)), reads=[Wkd.b, hT.b], writes=[ps.b])
                        jobs.append(lambda i, kh=kh, mmsk=mmsk: headnorm(i, mmsk, 128, gq.t[:, 3:4], bones32, 64.0, not isc, Ps, 2, 3, n,
                                                                           [(sKs[:, kh, t0:t0 + n], 0, 64)], [db("sKs", (kh, si))]))
                    run_chains(jobs)
                    for j in range(n // 128):
                        tile_i = t0 // 128 + j
                        v_ = va[tile_i % 2]
                        for c in range(2):
                            S.op("pe", lambda e: e.matmul(pv.t[:, 0:512], lhsT=kvn.t[:, c, j * 128:(j + 1) * 128],
                                                          rhs=Wv.t[:, c, :, :].rearrange("p h c -> p (h c)"),
                                                          start=(c == 0), stop=(c == 1)), reads=[kvn.b, Wv.b], writes=[pv.b])
                        S.op("act", lambda e: e.copy(out=v_.t[:, :, 0:64], in_=pv.t[:, 0:512].rearrange("p (h c) -> p h c", c=64)),
                             reads=[pv.b], writes=[v_.b])
                        S.dma(VsM[tile_i], v_.t[:], reads=[v_.b], writes=[db("VsM", tile_i)], key=v_.b.name)
                    for c in range(4):
                        ps = nextp()
                        proj(ps, O_U + c * 128)
                        S.op("act", lambda e: e.copy(out=uT.t[:, c, 0:n], in_=ps.t[:, 0:n]), reads=[ps.b], writes=[uT.b])
                    S.dma(uTs[:, :, t0:t0 + n], uT.t[:, :, 0:n], reads=[uT.b], writes=[db("uTs", si)], key="uT_st")
                    for j in range(n // 128):
                        tile_i = t0 // 128 + j
                        v_ = sva[tile_i % 2]
                        for kc in range(8):
                            S.op("pe", lambda e: e.matmul(pv.t[:, 0:128], lhsT=hT.t[:, kc, j * 128:(j + 1) * 128], rhs=Win.t[:, kc, O_SV:O_SV + 128],
                                                          start=(kc == 0), stop=(kc == 7)), reads=[hT.b, Win.b], writes=[pv.b])
                        S.op("act", lambda e: e.copy(out=v_.t[:, :, 0:64], in_=pv.t[:, 0:128].rearrange("p (h c) -> p h c", c=64)),
                             reads=[pv.b], writes=[v_.b])
                        S.dma(sVs[tile_i], v_.t[:], reads=[v_.b], writes=[db("sVs", tile_i)], key=v_.b.name)
            S.barrier()

        def attn_core(ph, nkeys_tiles, score_fn, pv_lhsT_fn, pv_reads, n, pS, pO, Pt, scale, mask_fn=None):
            nk = len(nkeys_tiles)

            def do_s(i):
                score_fn(nkeys_tiles[i], pS[i % 3])

            def do_e(i):
                ps_, p_ = pS[i % 3], Pt[i % 3]
                S.op("act", lambda e: e.activation(out=p_.t[:, 0:n], in_=ps_.t[:, 0:n], func=AF.Exp, scale=scale), reads=[ps_.b], writes=[p_.b])
                if mask_fn is not None:
                    mask_fn(nkeys_tiles[i], p_)

            def do_pv(i):
                p_ = Pt[i % 3]
                S.op("pe", lambda e: e.matmul(pO.t[:, 0:n], lhsT=pv_lhsT_fn(nkeys_tiles[i]), rhs=p_.t[:, 0:n], start=(i == 0), stop=(i == nk - 1)),
                     reads=[p_.b] + pv_reads, writes=[pO.b])

            do_s(0)
            if nk > 1:
                do_s(1)
            for i in range(nk):
                do_e(i)
                if i + 2 < nk:
                    do_s(i + 2)
                do_pv(i)

        def phase_B(l, do_ctx):
            with contextlib.ExitStack() as ph:
                KT = [sbt(ph, "KT%d" % i, [96, T], BF16) for i in range(2)]
                QT = [sbt(ph, "QT%d" % i, [96, T], BF16) for i in range(2)]
                VH = [sbt(ph, "VH%d" % i, [128, 34, 128], BF16) for i in range(2)]
                Pt = [sbt(ph, "Pt%d" % i, [128, 512], BF16) for i in range(3)]
                rsum = [sbt(ph, "rsum%d" % i, [64, 512], F32) for i in range(2)]
                oT = [sbt(ph, "oT%d" % i, [64, 512], BF16) for i in range(2)]
                pS = [pst(ph, "pS%d" % i) for i in range(3)]
                pO = [pst(ph, "pO%d" % i) for i in range(2)]
                allv = [db("VsM", i) for i in range(34)]
                it = [0]
                for h in range(8):
                    allq = [db("QTs", (h, si)) for si in range(9)]
                    allk = [db("KTs", (h, si)) for si in range(9)]
                    kt_, qt_, vh_ = KT[h % 2], QT[h % 2], VH[h % 2]
                    S.dma(kt_.t[:], KTs[h], reads=allk, writes=[kt_.b], key=kt_.b.name)
                    S.dma(qt_.t[:], QTs[h], reads=allq, writes=[qt_.b], key=qt_.b.name)
                    S.dma(vh_.t[:], VsM.rearrange("t p h c -> p t h c")[:, :, h, :], reads=allv, writes=[vh_.b], key=vh_.b.name)
                    for (si, t0, n, isc) in STS:
                        if isc and not do_ctx:
                            continue
                        keys = [0, 1] if isc else list(range(34))
                        po = pO[it[0] % 2]
                        rs_ = rsum[it[0] % 2]
                        o_ = oT[it[0] % 2]
                        it[0] += 1

                        def score(kt, ps_):
                            S.op("pe", lambda e: e.matmul(ps_.t[:, 0:n], lhsT=kt_.t[:, kt * 128:(kt + 1) * 128], rhs=qt_.t[:, t0:t0 + n], start=True, stop=True),
                                 reads=[kt_.b, qt_.b], writes=[ps_.b])
                        attn_core(ph, keys, score, lambda kt: vh_.t[:, kt, :], [vh_.b], n, pS, po, Pt, 96.0 ** -0.5)
                        S.op("act", lambda e: e.copy(out=rs_.t[:, 0:n], in_=po.t[64:128, 0:n]), reads=[po.b], writes=[rs_.b])
                        S.op("dve", lambda e: e.reciprocal(out=rs_.t[:, 0:n], in_=rs_.t[:, 0:n]), reads=[rs_.b], writes=[rs_.b])
                        S.op("dve", lambda e: e.tensor_tensor(out=o_.t[:, 0:n], in0=po.t[0:64, 0:n], in1=rs_.t[:, 0:n], op=ALU.mult),
                             reads=[po.b, rs_.b], writes=[o_.b])
                        S.dma(mlaO[h * 64:(h + 1) * 64, t0:t0 + n], o_.t[:, 0:n], reads=[o_.b], writes=[db("mlaO", (h, si))], key=o_.b.name)
            S.barrier()

        def phase_C(l, do_ctx):
            with contextlib.ExitStack() as ph:
                sk = sbt(ph, "sk", [64, T], BF16)
                sq_ = sbt(ph, "sq_", [64, 4, T], BF16)
                sv = sbt(ph, "sv", [128, 34, 128], BF16)
                Pt = [sbt(ph, "Pt%d" % i, [128, 512], BF16) for i in range(3)]
                rsum = [sbt(ph, "rsum%d" % i, [64, 512], F32) for i in range(2)]
                oT = [sbt(ph, "oT%d" % i, [64, 512], BF16) for i in range(2)]
                esk = sbt(ph, "esk", [128, 8], F32)
                S.dma(esk.t[:], sinkb[l], writes=[esk.b], key="esk")
                S.op("act", lambda e: e.activation(out=esk.t[:], in_=esk.t[:], func=AF.Exp), reads=[esk.b], writes=[esk.b])
                pS = [pst(ph, "pS%d" % i) for i in range(3)]
                pO = [pst(ph, "pO%d" % i) for i in range(2)]
                allv = [db("sVs", i) for i in range(34)]
                it = [0]
                for kh in range(2):
                    allq = [db("sQs", (c_, si)) for si in range(9) for c_ in (2 * kh, 2 * kh + 1)]
                    allk = [db("sKs", (kh, si)) for si in range(9)]
                    S.dma(sk.t[:], sKs[:, kh, :], reads=allk, writes=[sk.b], key="sk")
                    S.dma(sq_.t[:], sQs[:, 4 * kh:4 * kh + 4, :], reads=allq, writes=[sq_.b], key="sq_")
                    S.dma(sv.t[:], sVs.rearrange("t p h c -> p t h c")[:, :, kh, :], reads=allv, writes=[sv.b], key="sv")
                    qtiles = ([0, 1] if do_ctx else []) + list(range(2, 34))
                    for qt in qtiles:
                        q0 = qt * 128
                        if qt < 2:
                            keys = [(0, None), (1, None)]
                        else:
                            keys = [(0, None), (1, None)]
                            if qt > 2:
                                keys.append((qt - 1, Mge))
                            keys.append((qt, None))
                            if qt < 33:
                                keys.append((qt + 1, Mle))
                        po = pO[it[0] % 2]
                        rs_ = rsum[it[0] % 2]
                        o_ = oT[it[0] % 2]
                        it[0] += 1

                        def score(km, ps_):
                            kt = km[0]
                            for hd in range(4):
                                S.op("pe", lambda e: e.matmul(ps_.t[:, hd * 128:(hd + 1) * 128], lhsT=sk.t[:, kt * 128:(kt + 1) * 128],
                                                              rhs=sq_.t[:, hd, q0:q0 + 128], start=True, stop=True),
                                     reads=[sk.b, sq_.b], writes=[ps_.b])

                        def maskf(km, p_):
                            if km[1] is not None:
                                m = km[1]
                                p3 = p_.t[:, :].rearrange("p (h q) -> p h q", q=128)
                                S.op("pool", lambda e: e.tensor_tensor(out=p3, in0=p3, in1=m.t[:, :].unsqueeze(1).to_broadcast([128, 4, 128]), op=ALU.mult),
                                     reads=[p_.b, m.b], writes=[p_.b])
                        attn_core(ph, keys, score, lambda km: sv.t[:, km[0], :], [sv.b], 512, pS, po, Pt, 0.125, mask_fn=maskf)
                        S.op("act", lambda e: e.copy(out=rs_.t[:, :], in_=po.t[64:128, :]), reads=[po.b], writes=[rs_.b])
                        r3 = rs_.t[:, :].rearrange("p (h q) -> p h q", q=128)
                        S.op("dve", lambda e: e.tensor_tensor(out=r3, in0=r3, in1=esk.t[0:64, 4 * kh:4 * kh + 4].unsqueeze(2).to_broadcast([64, 4, 128]), op=ALU.add),
                             reads=[rs_.b, esk.b], writes=[rs_.b])
                        S.op("dve", lambda e: e.reciprocal(out=rs_.t[:, :], in_=rs_.t[:, :]), reads=[rs_.b], writes=[rs_.b])
                        S.op("dve", lambda e: e.tensor_tensor(out=o_.t[:, :], in0=po.t[0:64, :], in1=rs_.t[:, :], op=ALU.mult),
                             reads=[po.b, rs_.b], writes=[o_.b])
                        dst = swaO.rearrange("(h d) t -> d h t", d=64)[:, 4 * kh:4 * kh + 4, q0:q0 + 128]
                        S.dma(dst, o_.t[:, :].rearrange("p (h q) -> p h q", q=128), reads=[o_.b], writes=[db("swaO", (kh, qt))], key=o_.b.name)
            S.barrier()

        def phase_D(l, do_ctx):
            with contextlib.ExitStack() as ph:
                uT = sbt(ph, "uT", [128, 4, T], BF16)
                S.dma(uT.t[:], uTs[:, :, :], reads=[db("uTs", si) for si in range(9)], writes=[uT.b], key="uT_ld")
                acc = sbt(ph, "acc", [128, T], F32)
                def small(name, w=32):
                    return sbt(ph, name, [128, w], F32)
                lre, lim, dtt = small("lre"), small("lim"), small("dtt")
                S.dma(lre.t[:], s5lre[l].rearrange("p d q -> p (d q)"), writes=[lre.b], key="s5p1")
                S.dma(lim.t[:], s5lim[l].rearrange("p d q -> p (d q)"), writes=[lim.b], key="s5p2")
                S.dma(dtt.t[:], s5ls[l].rearrange("p d q -> p (d q)"), writes=[dtt.b], key="s5p3")
                S.op("act", lambda e: e.activation(out=dtt.t[:], in_=dtt.t[:], func=AF.Exp), reads=[dtt.b], writes=[dtt.b])
                S.op("dve", lambda e: e.tensor_scalar(out=lre.t[:], in0=lre.t[:], scalar1=-1e-4, scalar2=None, op0=ALU.min), reads=[lre.b], writes=[lre.b])
                rr, th, fq = small("rr"), small("th"), small("fq")
                S.op("dve", lambda e: e.tensor_tensor(out=rr.t[:], in0=lre.t[:], in1=dtt.t[:], op=ALU.mult), reads=[lre.b, dtt.b], writes=[rr.b])
                S.op("act", lambda e: e.activation(out=rr.t[:], in_=rr.t[:], func=AF.Exp), reads=[rr.b], writes=[rr.b])
                S.op("dve", lambda e: e.tensor_tensor(out=th.t[:], in0=lim.t[:], in1=dtt.t[:], op=ALU.mult), reads=[lim.b, dtt.b], writes=[th.b])
                S.op("dve", lambda e: e.tensor_scalar(out=fq.t[:], in0=th.t[:], scalar1=1.0 / TWO_PI, scalar2=None, op0=ALU.mult), reads=[th.b], writes=[fq.b])
                tmpi = sbt(ph, "tmpi", [128, 512], I32)
                tmpf = sbt(ph, "tmpf", [128, 512], F32)

                def sincos(dst_s, dst_c, ph_ap, ph_buf, w):
                    for (dst, shift) in ((dst_s, 0.0), (dst_c, 0.25)):
                        S.op("dve", lambda e: e.tensor_scalar(out=tmpf.t[:, 0:w], in0=ph_ap, scalar1=shift, scalar2=None, op0=ALU.add), reads=[ph_buf], writes=[tmpf.b])
                        S.op("dve", lambda e: e.tensor_copy(out=tmpi.t[:, 0:w], in_=tmpf.t[:, 0:w]), reads=[tmpf.b], writes=[tmpi.b])
                        S.op("dve", lambda e: e.tensor_copy(out=dst[1], in_=tmpi.t[:, 0:w]), reads=[tmpi.b], writes=[dst[0]])
                        S.op("dve", lambda e: e.tensor_tensor(out=tmpf.t[:, 0:w], in0=tmpf.t[:, 0:w], in1=dst[1], op=ALU.subtract), reads=[tmpf.b, dst[0]], writes=[tmpf.b])
                        S.op("act", lambda e: e.activation(out=dst[1], in_=tmpf.t[:, 0:w], func=AF.Sin, scale=TWO_PI), reads=[tmpf.b], writes=[dst[0]])

                sn, cs = small("sn"), small("cs")
                sincos((sn.b, sn.t[:]), (cs.b, cs.t[:]), fq.t[:], fq.b, 32)
                are, aim = small("are"), small("aim")
                S.op("dve", lambda e: e.tensor_tensor(out=are.t[:], in0=rr.t[:], in1=cs.t[:], op=ALU.mult), reads=[rr.b, cs.b], writes=[are.b])
                S.op("dve", lambda e: e.tensor_tensor(out=aim.t[:], in0=rr.t[:], in1=sn.t[:], op=ALU.mult), reads=[rr.b, sn.b], writes=[aim.b])
                am1, den, t1_, t2_, cre, cim = small("am1"), small("den"), small("t1_"), small("t2_"), small("cre"), small("cim")
                S.op("dve", lambda e: e.tensor_scalar(out=am1.t[:], in0=are.t[:], scalar1=-1.0, scalar2=None, op0=ALU.add), reads=[are.b], writes=[am1.b])
                S.op("dve", lambda e: e.tensor_tensor(out=den.t[:], in0=lre.t[:], in1=lre.t[:], op=ALU.mult), reads=[lre.b], writes=[den.b])
                S.op("dve", lambda e: e.tensor_tensor(out=t1_.t[:], in0=lim.t[:], in1=lim.t[:], op=ALU.mult), reads=[lim.b], writes=[t1_.b])
                S.op("dve", lambda e: e.tensor_tensor(out=den.t[:], in0=den.t[:], in1=t1_.t[:], op=ALU.add), reads=[den.b, t1_.b], writes=[den.b])
                S.op("dve", lambda e: e.reciprocal(out=den.t[:], in_=den.t[:]), reads=[den.b], writes=[den.b])
                S.op("dve", lambda e: e.tensor_tensor(out=t1_.t[:], in0=am1.t[:], in1=lre.t[:], op=ALU.mult), reads=[am1.b, lre.b], writes=[t1_.b])
                S.op("dve", lambda e: e.tensor_tensor(out=t2_.t[:], in0=aim.t[:], in1=lim.t[:], op=ALU.mult), reads=[aim.b, lim.b], writes=[t2_.b])
                S.op("dve", lambda e: e.tensor_tensor(out=cre.t[:], in0=t1_.t[:], in1=t2_.t[:], op=ALU.add), reads=[t1_.b, t2_.b], writes=[cre.b])
                S.op("dve", lambda e: e.tensor_tensor(out=cre.t[:], in0=cre.t[:], in1=den.t[:], op=ALU.mult), reads=[cre.b, den.b], writes=[cre.b])
                S.op("dve", lambda e: e.tensor_tensor(out=t1_.t[:], in0=aim.t[:], in1=lre.t[:], op=ALU.mult), reads=[aim.b, lre.b], writes=[t1_.b])
                S.op("dve", lambda e: e.tensor_tensor(out=t2_.t[:], in0=am1.t[:], in1=lim.t[:], op=ALU.mult), reads=[am1.b, lim.b], writes=[t2_.b])
                S.op("dve", lambda e: e.tensor_tensor(out=cim.t[:], in0=t1_.t[:], in1=t2_.t[:], op=ALU.subtract), reads=[t1_.b, t2_.b], writes=[cim.b])
                S.op("dve", lambda e: e.tensor_tensor(out=cim.t[:], in0=cim.t[:], in1=den.t[:], op=ALU.mult), reads=[cim.b, den.b], writes=[cim.b])
                fb, snB, csB = small("fb", 64), small("snB", 64), small("csB", 64)
                S.op("dve", lambda e: e.tensor_scalar(out=fb.t[:, 0:32], in0=fq.t[:], scalar1=256.0, scalar2=None, op0=ALU.mult), reads=[fq.b], writes=[fb.b])
                S.op("dve", lambda e: e.tensor_scalar(out=fb.t[:, 32:64], in0=fq.t[:], scalar1=512.0, scalar2=None, op0=ALU.mult), reads=[fq.b], writes=[fb.b])
                sincos((snB.b, snB.t[:]), (csB.b, csB.t[:]), fb.t[:], fb.b, 64)
                WB = sbt(ph, "WB", [128, 2, 2, 8, 128], BF16)
                CW = sbt(ph, "CW", [128, 3, 2, 16, 128], BF16)
                S.op("pool", lambda e: e.memset(CW.t[:], 0.0), writes=[CW.b])
                prep = contextlib.ExitStack()
                braw = sbt(prep, "braw", [128, 2, 2, 16, 16], F32)
                craw = sbt(prep, "craw", [128, 2, 2, 16, 16], F32)
                S.dma(braw.t[:, 0], s5bre[l], writes=[braw.b], key="braw")
                S.dma(braw.t[:, 1], s5bim[l], writes=[braw.b], key="braw")
                S.dma(craw.t[:, 0], s5cre[l], writes=[craw.b], key="craw")
                S.dma(craw.t[:, 1], s5cim[l], writes=[craw.b], key="craw")
                bbar = sbt(prep, "bbar", [128, 2, 2, 16, 16], F32)
                tb = sbt(prep, "tb", [128, 2, 16, 16], F32)
                cre3 = cre.t[:].rearrange("p (d q) -> p d q", q=16).unsqueeze(3).to_broadcast([128, 2, 16, 16])
                cim3 = cim.t[:].rearrange("p (d q) -> p d q", q=16).unsqueeze(3).to_broadcast([128, 2, 16, 16])
                S.op("dve", lambda e: e.tensor_tensor(out=bbar.t[:, 0], in0=braw.t[:, 0], in1=cre3, op=ALU.mult), reads=[braw.b, cre.b], writes=[bbar.b])
                S.op("dve", lambda e: e.tensor_tensor(out=tb.t[:], in0=braw.t[:, 1], in1=cim3, op=ALU.mult), reads=[braw.b, cim.b], writes=[tb.b])
                S.op("dve", lambda e: e.tensor_tensor(out=bbar.t[:, 0], in0=bbar.t[:, 0], in1=tb.t[:], op=ALU.subtract), reads=[bbar.b, tb.b], writes=[bbar.b])
                S.op("dve", lambda e: e.tensor_tensor(out=bbar.t[:, 1], in0=braw.t[:, 1], in1=cre3, op=ALU.mult), reads=[braw.b, cre.b], writes=[bbar.b])
                S.op("dve", lambda e: e.tensor_tensor(out=tb.t[:], in0=braw.t[:, 0], in1=cim3, op=ALU.mult), reads=[braw.b, cim.b], writes=[tb.b])
                S.op("dve", lambda e: e.tensor_tensor(out=bbar.t[:, 1], in0=bbar.t[:, 1], in1=tb.t[:], op=ALU.add), reads=[bbar.b, tb.b], writes=[bbar.b])
                Z = [sbt(prep, "Z%d" % i, [128, 64], BF16) for i in range(2)]
                pz = [pst(prep, "pz%d" % i, [128, 128], BF16) for i in range(2)]
                zi = 0
                for d_ in range(2):
                    for q in range(16):
                        c, q4 = q // 4, q % 4
                        hf, q2 = q4 // 2, q4 % 2
                        for ri in range(2):
                            z, pzz = Z[zi % 2], pz[zi % 2]
                            zi += 1
                            S.op("pool", lambda e: e.memset(z.t[:], 0.0), writes=[z.b])
                            S.op("dve", lambda e: e.tensor_copy(out=z.t[0:64, q2 * 32:q2 * 32 + 16], in_=bbar.t[0:64, ri, d_, q, :]), reads=[bbar.b], writes=[z.b])
                            S.op("dve", lambda e: e.tensor_copy(out=z.t[64:128, q2 * 32 + 16:q2 * 32 + 32], in_=bbar.t[64:128, ri, d_, q, :]), reads=[bbar.b], writes=[z.b])
                            S.op("pe", lambda e: e.transpose(pzz.t[0:64, :], z.t[:, :], identb.t[:, :]), reads=[z.b, identb.b], writes=[pzz.b])
                            S.op("act", lambda e: e.copy(out=WB.t[hf * 64:(hf + 1) * 64, ri, d_, c * 2 + q2, :], in_=pzz.t[0:64, :]), reads=[pzz.b], writes=[WB.b])
                            sgn = 1.0 if ri == 0 else -1.0
                            for e_ in range(2):
                                S.op("pool", lambda e: e.tensor_scalar(out=CW.t[e_ * 64:(e_ + 1) * 64, ri, d_, q, q4 * 32 + e_ * 16:q4 * 32 + e_ * 16 + 16],
                                                                       in0=craw.t[e_ * 64:(e_ + 1) * 64, ri, d_, q, :], scalar1=sgn, scalar2=None, op0=ALU.mult),
                                     reads=[craw.b], writes=[CW.b])
                                if ri == 0:
                                    S.op("pool", lambda e: e.tensor_scalar(out=CW.t[e_ * 64:(e_ + 1) * 64, 2, d_, q, q4 * 32 + e_ * 16:q4 * 32 + e_ * 16 + 16],
                                                                           in0=craw.t[e_ * 64:(e_ + 1) * 64, 0, d_, q, :], scalar1=-1.0, scalar2=None, op0=ALU.mult),
                                         reads=[craw.b], writes=[CW.b])
                S.barrier()
                prep.close()
                Wg = sbt(ph, "Wg", [128, 4, 512], BF16)
                load_cast(lambda a0, a1: (Wg.t[:, a0:a1, :], Wg.b), w_glu[l].rearrange("(kc p) n -> p kc n", p=128), 4, 512, eng="act")
                dcol = sbt(ph, "dcol", [128, 8], F32)
                S.dma(dcol.t[:, 0:4], s5dc[l], writes=[dcol.b], key="dcol")
                S.dma(dcol.t[:, 4:8], s5bgc[l], writes=[dcol.b], key="dcol")
                iot = sbt(ph, "iot", [128, 512], F32)
                S.dma(iot.t[:], cIota[:, :], writes=[iot.b], key="iot")
                wk = contextlib.ExitStack()
                tph = sbt(wk, "tph", [128, 512], F32)
                tC4 = [sbt(wk, "tC4_%d" % i, [128, 512], F32) for i in range(4)]
                tS4 = [sbt(wk, "tS4_%d" % i, [128, 512], F32) for i in range(4)]
                NS = int(_os.environ.get("D_NS", "3"))
                W = {k: [sbt(wk, "w%s%d" % (k, i), [128, 512], F32) for i in range(NS)] for k in
                     ("t1", "t2", "t3", "t4", "xa", "xb", "ga", "gb")}
                Ub = {k: [sbt(wk, "b%s%d" % (k, i), [128, 512], BF16) for i in range(NS)] for k in ("u1", "u2", "u3", "u4")}
                ini4 = [sbt(wk, "ini4_%d" % i, [128, 4], F32) for i in range(4)]
                ygt = [sbt(wk, "ygt%d" % i, [128, 512], BF16) for i in range(2)]
                Xs = [[sbt(wk, "Xs%d%d" % (i, j), [128, 512], F32) for j in range(2)] for i in range(NS)] if D_EVAC else None
                pX = [[pst(wk, "pX%d%d" % (i, j)) for j in range(2)] for i in range(NS)]
                pY = [pst(wk, "pY%d" % i) for i in range(2)]
                UENG = D_UENG

                def tt(eng, o, a, b_, op, rd, wr):
                    S.op(eng, lambda e: e.tensor_tensor(out=o, in0=a, in1=b_, op=op), reads=rd, writes=wr)
                it = 0
                for c in range(4):
                    S.op("dve", lambda e: e.tensor_scalar(out=acc.t[:], in0=uT.t[:, c, :], scalar1=dcol.t[:, c:c + 1], scalar2=None, op0=ALU.mult),
                         reads=[uT.b, dcol.b], writes=[acc.b])
                    for d_ in range(2):
                        for q4 in range(4):
                            col = d_ * 16 + c * 4 + q4
                            S.op("dve", lambda e: e.tensor_scalar(out=tph.t[:], in0=iot.t[:], scalar1=fq.t[:, col:col + 1], scalar2=None, op0=ALU.mult),
                                 reads=[iot.b, fq.b], writes=[tph.b])
                            sincos((tS4[q4].b, tS4[q4].t[:]), (tC4[q4].b, tC4[q4].t[:]), tph.t[:], tph.b, 512)
                            S.op("pool", lambda e: e.memset(ini4[q4].t[:], 0.0), writes=[ini4[q4].b])
                        def frame_geom(fidx, d_=d_):
                            n = 256 if fidx == 0 else 512
                            if d_ == 0:
                                k0 = 0 if fidx == 0 else 256 + 512 * (fidx - 1)
                                cols = slice(k0, k0 + n)
                            else:
                                if fidx == 0:
                                    cols = slice(255, None, -1)
                                else:
                                    hi = NC_ + NX - 512 * (fidx - 1) - 1
                                    cols = slice(hi, hi - 512, -1)
                            return n, cols

                        if True:
                            if True:
                                pass
                            def chain(q4, i, fidx, d_=d_, c=c):
                                n, cols = frame_geom(fidx)
                                py = pY[fidx % 2]
                                q = c * 4 + q4
                                hf, q2 = q4 // 2, q4 % 2
                                col = d_ * 16 + q
                                tC, tS, iv = tC4[q4], tS4[q4], ini4[q4]
                                px = pX[i]
                                for ri in range(2):
                                    S.op("pe", lambda e: e.matmul(px[ri].t[:, 0:n], lhsT=WB.t[hf * 64:(hf + 1) * 64, ri, d_, c * 2 + q2, :],
                                                                  rhs=uT.t[hf * 64:(hf + 1) * 64, c, cols], start=True, stop=True),
                                         reads=[WB.b, uT.b], writes=[px[ri].b])
                                yield
                                t1, t2, t3, t4 = W["t1"][i], W["t2"][i], W["t3"][i], W["t4"][i]
                                xa, xb, ga, gb = W["xa"][i], W["xb"][i], W["ga"][i], W["gb"][i]
                                u1, u2, u3, u4 = Ub["u1"][i], Ub["u2"][i], Ub["u3"][i], Ub["u4"][i]
                                tt("dve", t1.t[:, 0:n], px[0].t[:, 0:n], tC.t[:, 0:n], ALU.mult, [px[0].b, tC.b], [t1.b])
                                yield
                                tt("dve", t2.t[:, 0:n], px[1].t[:, 0:n], tS.t[:, 0:n], ALU.mult, [px[1].b, tS.b], [t2.b])
                                yield
                                tt("dve", t3.t[:, 0:n], px[1].t[:, 0:n], tC.t[:, 0:n], ALU.mult, [px[1].b, tC.b], [t3.b])
                                yield
                                tt("dve", t4.t[:, 0:n], px[0].t[:, 0:n], tS.t[:, 0:n], ALU.mult, [px[0].b, tS.b], [t4.b])
                                yield
                                tt(D_XENG, xa.t[:, 0:n], t1.t[:, 0:n], t2.t[:, 0:n], ALU.add, [t1.b, t2.b], [xa.b])
                                yield
                                tt(D_XENG, xb.t[:, 0:n], t3.t[:, 0:n], t4.t[:, 0:n], ALU.subtract, [t3.b, t4.b], [xb.b])
                                yield
                                rbc = rr.t[:, col:col + 1].to_broadcast([128, n])
                                S.op("dve", lambda e: e.tensor_tensor_scan(out=ga.t[:, 0:n], data0=rbc, data1=xa.t[:, 0:n], initial=iv.t[:, 0:1], op0=ALU.mult, op1=ALU.add),
                                     reads=[xa.b, rr.b, iv.b], writes=[ga.b])
                                yield
                                S.op("dve", lambda e: e.tensor_tensor_scan(out=gb.t[:, 0:n], data0=rbc, data1=xb.t[:, 0:n], initial=iv.t[:, 1:2], op0=ALU.mult, op1=ALU.add),
                                     reads=[xb.b, rr.b, iv.b], writes=[gb.b])
                                yield
                                if fidx < 8:
                                    bc = (0 if n == 256 else 32) + col
                                    cB, sB = csB.t[:, bc:bc + 1], snB.t[:, bc:bc + 1]
                                    S.op("pool", lambda e: e.tensor_scalar(out=iv.t[:, 2:3], in0=gb.t[:, n - 1:n], scalar1=sB, scalar2=None, op0=ALU.mult), reads=[gb.b, snB.b], writes=[iv.b])
                                    S.op("pool", lambda e: e.tensor_scalar(out=iv.t[:, 3:4], in0=ga.t[:, n - 1:n], scalar1=sB, scalar2=None, op0=ALU.mult), reads=[ga.b, snB.b], writes=[iv.b])
                                yield
                                tt(UENG[0], u1.t[:, 0:n], ga.t[:, 0:n], tC.t[:, 0:n], ALU.mult, [ga.b, tC.b], [u1.b])
                                yield
                                tt(UENG[1], u2.t[:, 0:n], gb.t[:, 0:n], tS.t[:, 0:n], ALU.mult, [gb.b, tS.b], [u2.b])
                                yield
                                tt(UENG[2], u3.t[:, 0:n], ga.t[:, 0:n], tS.t[:, 0:n], ALU.mult, [ga.b, tS.b], [u3.b])
                                yield
                                tt(UENG[3], u4.t[:, 0:n], gb.t[:, 0:n], tC.t[:, 0:n], ALU.mult, [gb.b, tC.b], [u4.b])
                                yield
                                if fidx < 8:
                                    bc = (0 if n == 256 else 32) + col
                                    cB = csB.t[:, bc:bc + 1]
                                    S.op("dve", lambda e: e.scalar_tensor_tensor(out=iv.t[:, 0:1], in0=ga.t[:, n - 1:n], scalar=cB, in1=iv.t[:, 2:3], op0=ALU.mult, op1=ALU.subtract),
                                         reads=[ga.b, csB.b, iv.b], writes=[iv.b])
                                    S.op("dve", lambda e: e.scalar_tensor_tensor(out=iv.t[:, 1:2], in0=gb.t[:, n - 1:n], scalar=cB, in1=iv.t[:, 3:4], op0=ALU.mult, op1=ALU.add),
                                         reads=[gb.b, csB.b, iv.b], writes=[iv.b])
                                yield
                                for j_, (ws, uu) in enumerate(((0, u1), (2, u2), (1, u3), (1, u4))):
                                    S.op("pe", lambda e: e.matmul(py.t[:, 0:n], lhsT=CW.t[:, ws, d_, q, :], rhs=uu.t[:, 0:n],
                                                                  start=(q4 == 0 and j_ == 0), stop=(q4 == 3 and j_ == 3)), reads=[CW.b, uu.b], writes=[py.b])
                                if q4 == 3:
                                    S.op("dve", lambda e: e.tensor_tensor(out=acc.t[:, cols], in0=acc.t[:, cols], in1=py.t[:, 0:n], op=ALU.add), reads=[acc.b, py.b], writes=[acc.b])

                            jobs = [(fidx, q4) for fidx in range(9) for q4 in range(4)]
                            live = []
                            free_slots = list(range(NS))
                            nj = 0
                            while live or nj < len(jobs):
                                while len(live) < NS and nj < len(jobs):
                                    sl = free_slots.pop(0)
                                    live.append((chain(jobs[nj][1], sl, jobs[nj][0]), sl))
                                    nj += 1
                                for (g_, sl) in list(live):
                                    try:
                                        next(g_)
                                    except StopIteration:
                                        live.remove((g_, sl))
                                        free_slots.append(sl)
                    for a0 in range(0, T, 512):
                        w_ = min(512, T - a0)
                        i = (a0 // 512) % 2
                        t1, t2 = W["t1"][i], W["t2"][i]
                        yo = ygt[i]
                        S.op("pool", lambda e: e.tensor_tensor(out=t1.t[:, 0:w_], in0=acc.t[:, a0:a0 + w_], in1=acc.t[:, a0:a0 + w_], op=ALU.mult), reads=[acc.b], writes=[t1.b])
                        S.op("pool", lambda e: e.tensor_scalar(out=t1.t[:, 0:w_], in0=t1.t[:, 0:w_], scalar1=0.044715, scalar2=1.0, op0=ALU.mult, op1=ALU.add), reads=[t1.b], writes=[t1.b])
                        S.op("pool", lambda e: e.tensor_tensor(out=t1.t[:, 0:w_], in0=t1.t[:, 0:w_], in1=acc.t[:, a0:a0 + w_], op=ALU.mult), reads=[t1.b, acc.b], writes=[t1.b])
                        S.op("act", lambda e: e.activation(out=t2.t[:, 0:w_], in_=t1.t[:, 0:w_], func=AF.Sigmoid, scale=1.5957691216057308), reads=[t1.b], writes=[t2.b])
                        S.op("dve", lambda e: e.tensor_tensor(out=yo.t[:, 0:w_], in0=t2.t[:, 0:w_], in1=acc.t[:, a0:a0 + w_], op=ALU.mult), reads=[t2.b, acc.b], writes=[yo.b])
                        S.dma(ygs[:, c, a0:a0 + w_], yo.t[:, 0:w_], reads=[yo.b], writes=[db("ygs", (c, a0 // 512))], key=yo.b.name)
                S.barrier()
                wk.close()
                pY = [pst(ph, "pYg%d" % i) for i in range(2)]
                W = {"t2": [sbt(ph, "gsig%d" % i, [128, 512], F32) for i in range(2)]}
                so = [sbt(ph, "so%d" % i, [128, 4, 512], BF16) for i in range(2)]
                ygl = [sbt(ph, "ygl%d" % i, [128, 4, 512], BF16) for i in range(2)]
                for (si, t0, n, isc) in STS:
                    if isc and not do_ctx:
                        continue
                    o_ = so[si % 2]
                    yg = ygl[si % 2]
                    S.dma(yg.t[:, :, 0:n], ygs[:, :, t0:t0 + n], reads=[db("ygs", (c_, (t0 + j_) // 512)) for c_ in range(4) for j_ in (0, n - 1)],
                          writes=[yg.b], key=yg.b.name)
                    for oc in range(4):
                        py = pY[oc % 2]
                        for kc in range(4):
                            S.op("pe", lambda e: e.matmul(py.t[:, 0:n], lhsT=Wg.t[:, kc, oc * 128:(oc + 1) * 128], rhs=yg.t[:, kc, 0:n], start=(kc == 0), stop=(kc == 3)),
                                 reads=[Wg.b, yg.b], writes=[py.b])
                        t2 = W["t2"][oc % 2]
                        S.op("act", lambda e: e.activation(out=t2.t[:, 0:n], in_=py.t[:, 0:n], func=AF.Sigmoid, bias=dcol.t[:, 4 + oc:5 + oc], scale=1.0), reads=[py.b, dcol.b], writes=[t2.b])
                        S.op("dve", lambda e: e.tensor_tensor(out=o_.t[:, oc, 0:n], in0=t2.t[:, 0:n], in1=yg.t[:, oc, 0:n], op=ALU.mult), reads=[t2.b, yg.b], writes=[o_.b])
                    S.dma(s5O[:, :, t0:t0 + n], o_.t[:, :, 0:n], reads=[o_.b], writes=[db("s5O", si)], key=o_.b.name)
            S.barrier()

        def phase_E(l, hsrc, hdst, do_ctx):
            with contextlib.ExitStack() as ph:
                Wgt = sbt(ph, "Wgt", [128, 8, 3072], BF16)
                wsrc = w_in[l].rearrange("(kc p) n -> p kc n", p=128)
                for r in range(3):
                    load_cast(lambda a0, a1: (Wgt.t[:, a0:a1, r * 1024:(r + 1) * 1024], Wgt.b), wsrc[:, :, O_G + r * 1024:O_G + (r + 1) * 1024], 8, 1024,
                              eng=("pool" if r % 2 == 0 else "act"))
                Wb = sbt(ph, "Wb", [128, 3, 4, 1024], BF16)
                for r in range(3):
                    load_cast(lambda a0, a1: (Wb.t[:, r, a0:a1, :], Wb.b), w_br[l, r].rearrange("(kc p) n -> p kc n", p=128), 4, 1024,
                              eng=("act" if r % 2 == 0 else "pool"))
                Wo = sbt(ph, "Wo", [128, 8, 1024], BF16)
                load_cast(lambda a0, a1: (Wo.t[:, a0:a1, :], Wo.b), w_o[l].rearrange("(kc p) n -> p kc n", p=128), 8, 1024)
                hT = [sbt(ph, "hT%d" % i, [128, 8, 512], BF16) for i in range(2)]
                oB = [sbt(ph, "oB%d" % i, [128, 3, 4, 512], BF16) for i in range(2)]
                xt1 = sbt(ph, "xt", [128, 8, 512], F32)
                xt = [xt1, xt1]
                mT = sbt(ph, "mT", [128, 8, 512], BF16)
                sg = [sbt(ph, "sg%d" % i, [128, 512], BF16) for i in range(3)]
                gp = [sbt(ph, "gp%d" % i, [128, 512], F32) for i in range(3)]
                hn = xt1
                sq = sbt(ph, "sq", [128, 8, 512], F32)
                rs = sbt(ph, "rs", [128, 512], F32)
                fT = sbt(ph, "fT", [128, 8, 512], BF16)
                pG = [pst(ph, "pG%d" % i) for i in range(3)]
                pP = [pst(ph, "pP%d" % i) for i in range(3)]
                pM = pst(ph, "pM")
                pss = pst(ph, "pss")
                srcs = [(mlaO.rearrange("(kc p) t -> p kc t", p=128), "mlaO"), (s5O, "s5O"), (swaO.rearrange("(kc p) t -> p kc t", p=128), "swaO")]
                sts = [s for s in STS if (not s[3]) or do_ctx]

                def loads(k):
                    (si, t0, n, isc) = sts[k]
                    S.dma(hT[k % 2].t[:, :, 0:n], hTs[:, :, t0:t0 + n], reads=[db("hTs", si)], writes=[hT[k % 2].b], key=hT[k % 2].b.name)
                    for r in range(3):
                        if srcs[r][1] == "swaO":
                            rd = [db("swaO", (kh_, t0 // 128 + j)) for j in range(n // 128) for kh_ in range(2)]
                        elif srcs[r][1] == "mlaO":
                            rd = [db("mlaO", (h_i, si)) for h_i in range(8)]
                        else:
                            rd = [db(srcs[r][1], si)]
                        S.dma(oB[k % 2].t[:, r, :, 0:n], srcs[r][0][:, :, t0:t0 + n], reads=rd, writes=[oB[k % 2].b], key=oB[k % 2].b.name)

                def loadx(k):
                    (si, t0, n, isc) = sts[k]
                    src = hsrc.rearrange("(kc p) t -> p kc t", p=128)
                    S.dma(xt1.t[:, :, 0:n], src[:, :, t0:t0 + n], reads=[db("hsrc%d" % id(hsrc), si)], writes=[xt1.b], key=xt1.b.name)

                loads(0)
                for k, (si, t0, n, isc) in enumerate(sts):
                    loadx(k)
                    if k + 1 < len(sts):
                        loads(k + 1)
                    s = 1 if isc else 0
                    h_, o_, x_ = hT[k % 2], oB[k % 2], xt[k % 2]
                    for oc in range(8):
                        for r in range(3):
                            for kc in range(8):
                                S.op("pe", lambda e: e.matmul(pG[r].t[:, 0:n], lhsT=Wgt.t[:, kc, r * 1024 + oc * 128:r * 1024 + (oc + 1) * 128], rhs=h_.t[:, kc, 0:n],
                                                              start=(kc == 0), stop=(kc == 7)), reads=[Wgt.b, h_.b], writes=[pG[r].b])
                            for kc in range(4):
                                S.op("pe", lambda e: e.matmul(pP[r].t[:, 0:n], lhsT=Wb.t[:, r, kc, oc * 128:(oc + 1) * 128], rhs=o_.t[:, r, kc, 0:n],
                                                              start=(kc == 0), stop=(kc == 3)), reads=[Wb.b, o_.b], writes=[pP[r].b])
                        for r in range(3):
                            S.op("act", lambda e: e.activation(out=sg[r].t[:, 0:n], in_=pG[r].t[:, 0:n], func=AF.Sigmoid), reads=[pG[r].b], writes=[sg[r].b])
                            S.op("dve", lambda e: e.tensor_tensor(out=gp[r].t[:, 0:n], in0=pP[r].t[:, 0:n], in1=sg[r].t[:, 0:n], op=ALU.mult),
                                 reads=[pP[r].b, sg[r].b], writes=[gp[r].b])
                        S.op("pool", lambda e: e.tensor_tensor(out=gp[0].t[:, 0:n], in0=gp[0].t[:, 0:n], in1=gp[1].t[:, 0:n], op=ALU.add), reads=[gp[0].b, gp[1].b], writes=[gp[0].b])
                        S.op("pool", lambda e: e.tensor_tensor(out=mT.t[:, oc, 0:n], in0=gp[0].t[:, 0:n], in1=gp[2].t[:, 0:n], op=ALU.add), reads=[gp[0].b, gp[2].b], writes=[mT.b])
                    for oc in range(8):
                        for kc in range(8):
                            S.op("pe", lambda e: e.matmul(pM.t[:, 0:n], lhsT=Wo.t[:, kc, oc * 128:(oc + 1) * 128], rhs=mT.t[:, kc, 0:n], start=(kc == 0), stop=(kc == 7)),
                                 reads=[Wo.b, mT.b], writes=[pM.b])
                        S.op("dve", lambda e: e.scalar_tensor_tensor(out=hn.t[:, oc, 0:n], in0=pM.t[:, 0:n], scalar=ada.t[:, 16 + oc, s:s + 1], in1=x_.t[:, oc, 0:n],
                                                                     op0=ALU.mult, op1=ALU.add), reads=[pM.b, ada.b, x_.b], writes=[hn.b])
                    dst = hdst.rearrange("(kc p) t -> p kc t", p=128)
                    S.dma(dst[:, :, t0:t0 + n], hn.t[:, :, 0:n], reads=[hn.b], writes=[db("hsrc%d" % id(hdst), si)], key="hn_st")
                    norm_mod(ph, hn, n, modA2, 24, isc, sq, pss, rs, fT)
                    S.dma(fTs[:, :, t0:t0 + n], fT.t[:, :, 0:n], reads=[fT.b], writes=[db("fTs", si)], key="fT_st")
            S.barrier()

        def phase_F1(l, do_ctx):
            with contextlib.ExitStack() as ph:
                Wf = sbt(ph, "Wf", [128, 8, 2 * FH], BF16)
                wsrc = f_in[l].rearrange("(kc p) n -> p kc n", p=128)
                for cb in range(11):
                    load_cast(lambda a0, a1: (Wf.t[:, a0:a1, cb * 512:(cb + 1) * 512], Wf.b), wsrc[:, :, cb * 512:(cb + 1) * 512], 8, 512,
                              eng=("pool" if cb % 2 == 0 else "act"))
                fT = [sbt(ph, "fT%d" % i, [128, 8, 512], BF16) for i in range(2)]
                aT = [sbt(ph, "aT%d" % i, [128, 22, 512], BF16) for i in range(2)]
                sa = [sbt(ph, "sa%d" % i, [128, 512], F32) for i in range(2)]
                pA = [pst(ph, "pA%d" % i) for i in range(3)]
                pGt = [pst(ph, "pGt%d" % i) for i in range(3)]
                sts = [s for s in STS if (not s[3]) or do_ctx]
                S.dma(fT[0].t[:, :, 0:sts[0][2]], fTs[:, :, sts[0][1]:sts[0][1] + sts[0][2]], reads=[db("fTs", sts[0][0])], writes=[fT[0].b], key=fT[0].b.name)
                for k, (si, t0, n, isc) in enumerate(sts):
                    if k + 1 < len(sts):
                        (si2, t02, n2, _) = sts[k + 1]
                        S.dma(fT[(k + 1) % 2].t[:, :, 0:n2], fTs[:, :, t02:t02 + n2], reads=[db("fTs", si2)], writes=[fT[(k + 1) % 2].b], key=fT[(k + 1) % 2].b.name)
                    f_, a_ = fT[k % 2], aT[k % 2]
                    for hc in range(22):
                        pa_, pg_ = pA[hc % 3], pGt[hc % 3]
                        for kc in range(8):
                            S.op("pe", lambda e: e.matmul(pa_.t[:, 0:n], lhsT=Wf.t[:, kc, hc * 128:(hc + 1) * 128], rhs=f_.t[:, kc, 0:n], start=(kc == 0), stop=(kc == 7)),
                                 reads=[Wf.b, f_.b], writes=[pa_.b])
                        for kc in range(8):
                            S.op("pe", lambda e: e.matmul(pg_.t[:, 0:n], lhsT=Wf.t[:, kc, FH + hc * 128:FH + (hc + 1) * 128], rhs=f_.t[:, kc, 0:n], start=(kc == 0), stop=(kc == 7)),
                                 reads=[Wf.b, f_.b], writes=[pg_.b])
                        s_ = sa[hc % 2]
                        S.op("act", lambda e: e.activation(out=s_.t[:, 0:n], in_=pa_.t[:, 0:n], func=AF.Silu), reads=[pa_.b], writes=[s_.b])
                        S.op("dve", lambda e: e.tensor_tensor(out=a_.t[:, hc, 0:n], in0=pg_.t[:, 0:n], in1=s_.t[:, 0:n], op=ALU.mult), reads=[pg_.b, s_.b], writes=[a_.b])
                    S.dma(actTs[:, :, t0:t0 + n], a_.t[:, :, 0:n], reads=[a_.b], writes=[db("actTs", si)], key=a_.b.name)
            S.barrier()

        def phase_F2(l, hsrc, hdst, do_ctx, final):
            with contextlib.ExitStack() as ph:
                Wfo = sbt(ph, "Wfo", [128, 22, 1024], BF16)
                load_cast(lambda a0, a1: (Wfo.t[:, a0:a1, :], Wfo.b), f_out[l].rearrange("(kc p) n -> p kc n", p=128), 22, 1024)
                aT = [sbt(ph, "aT%d" % i, [128, 22, 512], BF16) for i in range(2)]
                xt = [sbt(ph, "xt%d" % i, [128, 8, 512], F32) for i in range(2)]
                ho = [sbt(ph, "ho%d" % i, [128, 8, 512], F32) for i in range(2)]
                pO = [pst(ph, "pO%d" % i) for i in range(3)]
                sts = [s for s in STS if (not s[3]) or do_ctx]
                src = hsrc.rearrange("(kc p) t -> p kc t", p=128)

                def loads(k):
                    (si, t0, n, isc) = sts[k]
                    S.dma(aT[k % 2].t[:, :, 0:n], actTs[:, :, t0:t0 + n], reads=[db("actTs", si)], writes=[aT[k % 2].b], key=aT[k % 2].b.name)
                    S.dma(xt[k % 2].t[:, :, 0:n], src[:, :, t0:t0 + n], reads=[db("hsrc%d" % id(hsrc), si)], writes=[xt[k % 2].b], key=xt[k % 2].b.name)
                loads(0)
                for k, (si, t0, n, isc) in enumerate(sts):
                    if k + 1 < len(sts):
                        loads(k + 1)
                    s = 1 if isc else 0
                    a_, x_, h_ = aT[k % 2], xt[k % 2], ho[k % 2]
                    for oc in range(8):
                        po = pO[oc % 3]
                        for hc in range(22):
                            S.op("pe", lambda e: e.matmul(po.t[:, 0:n], lhsT=Wfo.t[:, hc, oc * 128:(oc + 1) * 128], rhs=a_.t[:, hc, 0:n], start=(hc == 0), stop=(hc == 21)),
                                 reads=[Wfo.b, a_.b], writes=[po.b])
                        S.op("dve", lambda e: e.scalar_tensor_tensor(out=h_.t[:, oc, 0:n], in0=po.t[:, 0:n], scalar=ada.t[:, 40 + oc, s:s + 1], in1=x_.t[:, oc, 0:n],
                                                                     op0=ALU.mult, op1=ALU.add), reads=[po.b, ada.b, x_.b], writes=[h_.b])
                    if final:
                        dst = outT.rearrange("(kc p) t -> p kc t", p=128)
                        S.dma(dst[:, :, t0 - NC_:t0 - NC_ + n], h_.t[:, :, 0:n], reads=[h_.b], writes=[db("outT", si)], key=h_.b.name)
                    else:
                        dst = hdst.rearrange("(kc p) t -> p kc t", p=128)
                        S.dma(dst[:, :, t0:t0 + n], h_.t[:, :, 0:n], reads=[h_.b], writes=[db("hsrc%d" % id(hdst), si)], key=h_.b.name)
            S.barrier()

        cur = xT
        for l in range(nlayers):
            last = (l == L - 1)
            do_ctx = not last
            if "a" in phases:
                phase_ada(l)
            if "A" in phases:
                phase_A(l, cur, do_ctx)
            if stop_after == "A":
                break
            if "B" in phases:
                phase_B(l, do_ctx)
            if stop_after == "B":
                break
            if "C" in phases:
                phase_C(l, do_ctx)
            if stop_after == "C":
                break
            if "D" in phases:
                phase_D(l, do_ctx)
            if stop_after == "D":
                break
            if "E" in phases:
                phase_E(l, cur, hres[0], do_ctx)
            if stop_after == "E":
                break
            if "F" in phases:
                phase_F1(l, do_ctx)
                phase_F2(l, hres[0], hres[1], do_ctx, last)
            cur = hres[1]
        S.barrier()
        g.stats = (S.nins, S.nwait, len(S.semh))
    return nc, dbg_names, g


def _consts():
    n_axis = 8
    freqs = (10000.0 ** (-np.arange(n_axis, dtype=np.float32) / n_axis)).astype(np.float32)
    rows = NX // 64
    row = np.repeat(np.arange(rows, dtype=np.float32), 64)
    col = np.tile(np.arange(64, dtype=np.float32), rows)
    angM = np.concatenate([row[:, None] * freqs, col[:, None] * freqs], axis=-1).astype(np.float32)
    n_axis_s = 16
    freqs_s = (10000.0 ** (-np.arange(n_axis_s, dtype=np.float32) / n_axis_s)).astype(np.float32)
    angS = np.concatenate([row[:, None] * freqs_s, col[:, None] * freqs_s], axis=-1).astype(np.float32)
    cMc = np.ones((96, NX), np.float32); cMs = np.zeros((96, NX), np.float32)
    cMc[64:80] = np.cos(angM).T; cMc[80:96] = np.cos(angM).T
    cMs[64:80] = np.sin(angM).T; cMs[80:96] = np.sin(angM).T
    cSc = np.zeros((128, NX), np.float32); cSs = np.zeros((128, NX), np.float32)
    for hh in range(2):
        for half in range(2):
            r0 = hh * 64 + half * 32
            cSc[r0:r0 + 32] = np.cos(angS).T
            cSs[r0:r0 + 32] = np.sin(angS).T
    cPm = np.zeros((96, 96), np.float32)
    for i in range(16):
        cPm[80 + i, 64 + i] = -1.0
        cPm[64 + i, 80 + i] = 1.0
    cPs = np.zeros((128, 128), np.float32)
    for hh in range(2):
        for i in range(32):
            cPs[hh * 64 + 32 + i, hh * 64 + i] = -1.0
            cPs[hh * 64 + i, hh * 64 + 32 + i] = 1.0
    cShift = np.zeros((32, 96), np.float32)
    for i in range(32):
        cShift[i, 64 + i] = 1.0
    j = np.arange(128)[:, None]; i = np.arange(128)[None, :]
    cMge = (j >= i).astype(np.float32)
    cMle = (j <= i).astype(np.float32)
    cBones = np.zeros((128, 128), np.float32)
    cBones[0:64, 0:64] = 1.0; cBones[64:128, 64:128] = 1.0
    cIota = np.tile(np.arange(512, dtype=np.float32)[None, :], (128, 1))
    cId = np.eye(128, dtype=np.float32)
    return dict(cMc=cMc, cMs=cMs, cSc=cSc, cSs=cSs, cPm=cPm, cPs=cPs, cShift=cShift, cMge=cMge, cMle=cMle,
                cBones=cBones, cIota=cIota, cId=cId)


def _colv(v, nch):
    return np.ascontiguousarray(np.transpose(v.reshape(v.shape[0], nch, 128), (0, 2, 1)))


def _shared_inputs(inp):
    f = lambda a: np.ascontiguousarray(np.asarray(a, dtype=np.float32))
    sh = {}
    sh["w_ada"] = f(inp["w_ada"]); sh["badac"] = _colv(f(inp["b_ada"]), 48)
    sh["g1c"] = _colv(f(inp["norm1_g"]), 8); sh["g2c"] = _colv(f(inp["norm2_g"]), 8)
    sh["w_in"] = f(inp["w_in"])
    sh["qagc"] = _colv(f(inp["mla_qa_g"]), 3); sh["kvagc"] = _colv(f(inp["mla_kva_g"]), 2)
    sh["w_uq"] = f(inp["mla_w_uq"]); sh["w_ukv"] = f(inp["mla_w_ukv"])
    sh["mqkg"] = np.ascontiguousarray(np.transpose(f(inp["mla_qk_g"]), (0, 2, 1)))
    sw = np.transpose(f(inp["swa_qk_g"]), (0, 2, 1))
    sh["swag"] = np.ascontiguousarray(np.concatenate([sw, sw], axis=1))
    sh["sinkb"] = np.ascontiguousarray(np.broadcast_to(f(inp["swa_sink"])[:, None, :], (L, 128, 8)))

    def pairlay(a):
        return np.ascontiguousarray(np.transpose(a.reshape(L, 2, 16, 2, 64), (0, 3, 4, 1, 2)).reshape(L, 128, 2, 16))
    sh["s5lre"] = pairlay(f(inp["s5_lambda_re"])); sh["s5lim"] = pairlay(f(inp["s5_lambda_im"]))
    ls = np.broadcast_to(f(inp["s5_log_step"])[:, :, :, None], (L, 2, 32, 64))
    sh["s5ls"] = pairlay(np.ascontiguousarray(ls))

    def pairlay_b(a):
        return np.ascontiguousarray(np.transpose(a.reshape(L, 2, 16, 2, 64, 16), (0, 3, 4, 1, 2, 5)).reshape(L, 128, 2, 16, 16))
    sh["s5bre"] = pairlay_b(f(inp["s5_b_re"])); sh["s5bim"] = pairlay_b(f(inp["s5_b_im"]))
    sh["s5cre"] = pairlay_b(np.ascontiguousarray(np.transpose(f(inp["s5_c_re"]), (0, 1, 2, 4, 3))))
    sh["s5cim"] = pairlay_b(np.ascontiguousarray(np.transpose(f(inp["s5_c_im"]), (0, 1, 2, 4, 3))))
    sh["s5dc"] = _colv(f(inp["s5_d"]), 4); sh["s5bgc"] = _colv(f(inp["s5_b_glu"]), 4)
    sh["w_glu"] = f(inp["s5_w_glu"]); sh["w_br"] = f(inp["w_branch"]); sh["w_o"] = f(inp["w_out"])
    sh["f_in"] = f(inp["ffn_w_in"]); sh["f_out"] = f(inp["ffn_w_out"])
    sh.update(_consts())
    return sh


def _core_inputs(inp, b, sh):
    x = np.asarray(inp["x"][b], np.float32); ctx = np.asarray(inp["ctx"][b], np.float32)
    m = dict(sh)
    m["xT"] = np.ascontiguousarray(np.concatenate([ctx, x], axis=0).T)
    cc = np.stack([np.asarray(inp["c"][b], np.float32), np.asarray(inp["c_ctx"], np.float32)], axis=-1)
    m["ccol"] = np.ascontiguousarray(np.transpose(cc.reshape(8, 128, 2), (1, 0, 2)))
    return m


_CACHE = {}


def kernel(**inputs):
    if "nc" not in _CACHE:
        _CACHE["nc"] = build()[0]
    nc = _CACHE["nc"]
    sh = _shared_inputs(inputs)
    in_maps = [_core_inputs(inputs, b, sh) for b in range(8)]
    res = run_bass_kernel_spmd(nc, in_maps, core_ids=list(range(8)))
    out = np.stack([np.ascontiguousarray(r["outT"].T) for r in res.results], axis=0)
    return out.astype(np.float32)
```

```python
import contextlib
import math
import numpy as np
import concourse.bass as bass
import concourse.mybir as mybir
from concourse.bass_utils import run_bass_kernel_spmd

F32 = mybir.dt.float32
BF16 = mybir.dt.bfloat16
I32 = mybir.dt.int32
AF = mybir.ActivationFunctionType
ALU = mybir.AluOpType

D = 1024
NX = 4096
NC_ = 256
T = NX + NC_
L = 2
EPS = 1e-6
FH = 2816
DIN = 5024
O_KR = 640
O_U = 672
O_SQ = 1184
O_SK = 1696
O_SV = 1824
O_G = 1952
import os as _os
D_UENG = _os.environ.get("UENG", "dve,dve,dve,dve").split(",")
D_XENG = _os.environ.get("XENG", "pool")
D_TENG = _os.environ.get("TENG", "dve,dve,dve,dve").split(",")
D_EVAC = _os.environ.get("D_EVAC", "0") == "1"
PREFETCH = _os.environ.get("PREFETCH", "1") == "1"
TWO_PI = 2.0 * math.pi

STS = [(0, 0, 256, True)] + [(1 + k, 256 + 512 * k, 512, False) for k in range(8)]


class Buf:
    __slots__ = ("name", "w", "r")

    def __init__(self, name):
        self.name = name
        self.w = {}
        self.r = {}


def _merge(d, s):
    for k, v in s.items():
        if d.get(k, 0) < v:
            d[k] = v


class Sched:
    def __init__(self, nc, stack):
        self.nc = nc
        self.stack = stack
        self.E = {"pe": nc.tensor, "act": nc.scalar, "dve": nc.vector,
                  "pool": nc.gpsimd, "sp": nc.sync}
        self.semh = {}
        self.cnt = {}
        self.seen = {k: {} for k in self.E}
        for k in self.E:
            self.semh[k] = stack.enter_context(nc.semaphore("s_" + k))
            self.cnt[k] = 0
        self.nins = 0
        self.nwait = 0
        self.alias = {}
        self.free = []
        self.dkeys = []

    def _sem(self, key):
        if key in self.alias:
            return self.alias[key]
        if self.free:
            ck = self.free.pop()
        else:
            ck = "dq%d" % len(self.dkeys)
            self.dkeys.append(ck)
            self.semh[ck] = self.stack.enter_context(self.nc.semaphore(ck))
            self.cnt[ck] = 0
        self.alias[key] = ck
        return ck

    def _wait(self, eng, deps):
        seen = self.seen[eng]
        for key, val in deps.items():
            if val <= 0 or seen.get(key, 0) >= val:
                continue
            self.E[eng].wait_ge(self.semh[key], val)
            seen[key] = val
            self.nwait += 1

    def _deps(self, reads, writes):
        deps = {}
        for b in reads:
            _merge(deps, b.w)
        for b in writes:
            _merge(deps, b.w)
            _merge(deps, b.r)
        return deps

    def op(self, eng, fn, reads=(), writes=()):
        deps = self._deps(reads, writes)
        if eng == "pe":
            deps.pop("pe", None)
        self._wait(eng, deps)
        ins = fn(self.E[eng])
        self.cnt[eng] += 1
        v = self.cnt[eng]
        ins.then_inc(self.semh[eng], 1)
        for b in reads:
            if b.r.get(eng, 0) < v:
                b.r[eng] = v
        for b in writes:
            b.w = {eng: v}
            b.r = {}
        self.nins += 1
        return ins

    def dma(self, out, in_, reads=(), writes=(), key=None, eng="sp"):
        key = self._sem(key)
        deps = self._deps(reads, writes)
        deps.pop(key, None)
        self._wait(eng, deps)
        ins = self.E[eng].dma_start(out=out, in_=in_)
        self.cnt[key] += 16
        v = self.cnt[key]
        ins.then_inc(self.semh[key], 16)
        for b in reads:
            if b.r.get(key, 0) < v:
                b.r[key] = v
        for b in writes:
            b.w = {key: v}
            b.r = {}
        self.nins += 1
        return ins

    def barrier(self):
        deps = {k: v for k, v in self.cnt.items() if v > 0}
        for e in self.E:
            self._wait(e, deps)
        self.alias = {}
        self.free = list(self.dkeys)


class TT:
    __slots__ = ("t", "b")

    def __init__(self, t, b):
        self.t = t
        self.b = b


class Ctx:
    pass


def build(dbg=False, nlayers=L, stop_after=None, phases="aABCDEF"):
    nc = bass.Bass("TRN2", target_bir_lowering=False)
    g = Ctx()
    g.nc = nc

    def din(name, shape, dt=F32):
        return nc.dram_tensor(name, list(shape), dt, kind="ExternalInput").ap()

    dbg_names = []

    def dscr(name, shape, dt):
        if dbg:
            dbg_names.append(name)
            return nc.dram_tensor(name, list(shape), dt, kind="ExternalOutput").ap()
        return nc.dram_tensor(name, list(shape), dt).ap()

    xT = din("xT", [D, T])
    ccol = din("ccol", [128, 8, 2])
    w_ada = din("w_ada", [L, D, 6 * D])
    badac = din("badac", [L, 128, 48])
    g1c = din("g1c", [L, 128, 8])
    g2c = din("g2c", [L, 128, 8])
    w_in = din("w_in", [L, D, DIN])
    qagc = din("qagc", [L, 128, 3])
    kvagc = din("kvagc", [L, 128, 2])
    w_uq = din("w_uq", [L, 384, 768])
    w_ukv = din("w_ukv", [L, 256, 1024])
    mqkg = din("mqkg", [L, 96, 2])
    swag = din("swag", [L, 128, 2])
    sinkb = din("sinkb", [L, 128, 8])
    s5lre = din("s5lre", [L, 128, 2, 16])
    s5lim = din("s5lim", [L, 128, 2, 16])
    s5ls = din("s5ls", [L, 128, 2, 16])
    s5bre = din("s5bre", [L, 128, 2, 16, 16])
    s5bim = din("s5bim", [L, 128, 2, 16, 16])
    s5cre = din("s5cre", [L, 128, 2, 16, 16])
    s5cim = din("s5cim", [L, 128, 2, 16, 16])
    s5dc = din("s5dc", [L, 128, 4])
    s5bgc = din("s5bgc", [L, 128, 4])
    w_glu = din("w_glu", [L, 512, 512])
    w_br = din("w_br", [L, 3, 512, D])
    w_o = din("w_o", [L, D, D])
    f_in = din("f_in", [L, D, 2 * FH])
    f_out = din("f_out", [L, FH, D])
    cMc = din("cMc", [96, NX]); cMs = din("cMs", [96, NX])
    cSc = din("cSc", [128, NX]); cSs = din("cSs", [128, NX])
    cPm = din("cPm", [96, 96]); cPs = din("cPs", [128, 128])
    cShift = din("cShift", [32, 96])
    cMge = din("cMge", [128, 128]); cMle = din("cMle", [128, 128])
    cBones = din("cBones", [128, 128])
    cIota = din("cIota", [128, 512])
    cId = din("cId", [128, 128])

    outT = nc.dram_tensor("outT", [D, NX], F32, kind="ExternalOutput").ap()

    hres = [dscr("hres%d" % i, [D, T], F32) for i in range(2)]
    hTs = dscr("hTs", [128, 8, T], BF16)
    QTs = dscr("QTs", [8, 96, T], BF16)
    KTs = dscr("KTs", [8, 96, T], BF16)
    VsM = dscr("VsM", [34, 128, 8, 128], BF16)
    uTs = dscr("uTs", [128, 4, T], BF16)
    sQs = dscr("sQs", [64, 8, T], BF16)
    sKs = dscr("sKs", [64, 2, T], BF16)
    sVs = dscr("sVs", [34, 128, 2, 128], BF16)
    mlaO = dscr("mlaO", [512, T], BF16)
    swaO = dscr("swaO", [512, T], BF16)
    s5O = dscr("s5O", [128, 4, T], BF16)
    ygs = dscr("ygs", [128, 4, T], BF16)
    fTs = dscr("fTs", [128, 8, T], BF16)
    actTs = dscr("actTs", [128, 22, T], BF16)

    dbufs = {}

    def db(name, idx=0):
        k = (name, idx)
        if k not in dbufs:
            dbufs[k] = Buf("%s_%s" % (name, idx))
        return dbufs[k]

    with contextlib.ExitStack() as top:
        S = Sched(nc, top)
        uid = [0]

        def sbt(ctx, name, shape, dt):
            uid[0] += 1
            nm = "%s_%d" % (name, uid[0])
            t = ctx.enter_context(nc.sbuf_tensor(nm, list(shape), dt))
            return TT(t, Buf(nm))

        def pst(ctx, name, shape=(128, 512), dt=F32):
            uid[0] += 1
            nm = "%s_%d" % (name, uid[0])
            t = ctx.enter_context(nc.psum_tensor(nm, list(shape), dt))
            return TT(t, Buf(nm))

        ones32 = sbt(top, "ones32", [128, 128], F32)
        S.op("pool", lambda e: e.memset(ones32.t[:], 1.0), writes=[ones32.b])
        bones32 = sbt(top, "bones32", [128, 128], F32)
        S.dma(bones32.t[:], cBones[:, :], writes=[bones32.b], key="c_bones")
        stgc = sbt(top, "stgc", [128, 128], F32)

        def const_bf(name, src, rows, cols):
            t = sbt(top, name, [rows, cols], BF16)
            S.dma(stgc.t[0:rows, 0:cols], src[:, :], writes=[stgc.b], key="c_stg")
            S.op("dve", lambda e: e.tensor_copy(out=t.t[:], in_=stgc.t[0:rows, 0:cols]),
                 reads=[stgc.b], writes=[t.b])
            return t

        Pm = const_bf("Pm", cPm, 96, 96)
        Ps = const_bf("Ps", cPs, 128, 128)
        shiftI = const_bf("shiftI", cShift, 32, 96)
        Mge = const_bf("Mge", cMge, 128, 128)
        Mle = const_bf("Mle", cMle, 128, 128)
        identb = const_bf("identb", cId, 128, 128)

        stg = [sbt(top, "stg%d" % i, [128, 2048], F32) for i in range(2)]
        stg_i = [0]

        def load_cast(dst_ap_fn, src3, A, B, scale_fn=None, eng="pool", dma_eng="sp"):
            step = 1 if scale_fn is not None else max(1, 2048 // B)
            a0 = 0
            while a0 < A:
                a1 = min(A, a0 + step)
                s_ = stg[stg_i[0] % 2]
                stg_i[0] += 1
                na = a1 - a0
                view = s_.t[:, 0:na * B].rearrange("p (a b) -> p a b", b=B)
                S.dma(view, src3[:, a0:a1, :], writes=[s_.b], key=s_.b.name, eng=dma_eng)
                dst, dbuf = dst_ap_fn(a0, a1)
                if scale_fn is None:
                    if eng == "act":
                        S.op(eng, lambda e: e.copy(out=dst, in_=view), reads=[s_.b], writes=[dbuf])
                    else:
                        S.op(eng, lambda e: e.tensor_copy(out=dst, in_=view), reads=[s_.b], writes=[dbuf])
                else:
                    assert na == 1
                    sc, scb = scale_fn(a0)
                    S.op(eng, lambda e: e.tensor_scalar(out=dst, in0=view, scalar1=sc, scalar2=None, op0=ALU.mult),
                         reads=[s_.b, scb], writes=[dbuf])
                a0 = a1

        def rsqrt_from(ctx_eng_out, out_t, in_ap, in_buf, scale, shape_ap=None):
            o = out_t.t[:] if shape_ap is None else shape_ap
            S.op("act", lambda e: e.activation(out=o, in_=in_ap, func=AF.Ln, bias=epsc.t[0:o.shape[0], 0:1], scale=scale),
                 reads=[in_buf, epsc.b], writes=[out_t.b])
            S.op("act", lambda e: e.activation(out=o, in_=o, func=AF.Exp, scale=-0.5), reads=[out_t.b], writes=[out_t.b])

        epsc = sbt(top, "epsc", [128, 1], F32)
        S.op("pool", lambda e: e.memset(epsc.t[:], EPS), writes=[epsc.b])

        modA1 = sbt(top, "modA1", [128, 8, 2], F32)
        modA2 = sbt(top, "modA2", [128, 8, 2], F32)
        ada = sbt(top, "ada", [128, 48, 2], F32)

        def phase_ada(l):
            with contextlib.ExitStack() as ph:
                cc = sbt(ph, "cc", [128, 8, 2], F32)
                S.dma(cc.t[:], ccol[:, :, :], writes=[cc.b], key="cc")
                sc = sbt(ph, "sc", [128, 8, 2], F32)
                S.op("act", lambda e: e.activation(out=sc.t[:], in_=cc.t[:], func=AF.Silu), reads=[cc.b], writes=[sc.b])
                bad = sbt(ph, "bad", [128, 48], F32)
                S.dma(bad.t[:], badac[l], writes=[bad.b], key="bad")
                gg = sbt(ph, "gg", [128, 16], F32)
                S.dma(gg.t[:, 0:8], g1c[l], writes=[gg.b], key="gg")
                S.dma(gg.t[:, 8:16], g2c[l], writes=[gg.b], key="gg")
                wa = [sbt(ph, "wa%d" % i, [128, 8, 512], F32) for i in range(2)]
                pa = pst(ph, "pa", [128, 96], F32)
                wsrc = w_ada[l].rearrange("(kc p) n -> p kc n", p=128)
                for cb in range(12):
                    w = wa[cb % 2]
                    S.dma(w.t[:], wsrc[:, :, cb * 512:(cb + 1) * 512], writes=[w.b], key=w.b.name)
                    for f4 in range(4):
                        fc = cb * 4 + f4
                        for kc in range(8):
                            S.op("pe", lambda e: e.matmul(pa.t[:, fc * 2:fc * 2 + 2], lhsT=w.t[:, kc, f4 * 128:(f4 + 1) * 128],
                                                          rhs=sc.t[:, kc, :], start=(kc == 0), stop=(kc == 7)),
                                 reads=[w.b, sc.b], writes=[pa.b])
                S.op("dve", lambda e: e.tensor_tensor(out=ada.t[:], in0=pa.t[:].rearrange("p (c s) -> p c s", s=2),
                                                      in1=bad.t[:].unsqueeze(2).to_broadcast([128, 48, 2]), op=ALU.add),
                     reads=[pa.b, bad.b], writes=[ada.b])
                for (mod, sc0, gofs) in ((modA1, 8, 0), (modA2, 32, 8)):
                    S.op("dve", lambda e: e.tensor_scalar(out=mod.t[:], in0=ada.t[:, sc0:sc0 + 8, :], scalar1=1.0, scalar2=None, op0=ALU.add),
                         reads=[ada.b], writes=[mod.b])
                    S.op("dve", lambda e: e.tensor_tensor(out=mod.t[:], in0=mod.t[:],
                                                          in1=gg.t[:, gofs:gofs + 8].unsqueeze(2).to_broadcast([128, 8, 2]), op=ALU.mult),
                         reads=[mod.b, gg.b], writes=[mod.b])
            S.barrier()

        def norm_mod(ph, xt, n, modA, shofs, si_ctx, sq, pss, rs, hT):
            s = 1 if si_ctx else 0
            S.op("act", lambda e: e.activation(out=sq.t[:, :, 0:n], in_=xt.t[:, :, 0:n], func=AF.Square), reads=[xt.b], writes=[sq.b])
            for kc in range(8):
                S.op("pe", lambda e: e.matmul(pss.t[:, 0:n], lhsT=ones32.t[:], rhs=sq.t[:, kc, 0:n], start=(kc == 0), stop=(kc == 7)),
                     reads=[ones32.b, sq.b], writes=[pss.b])
            rsqrt_from(None, rs, pss.t[:, 0:n], pss.b, 1.0 / D, shape_ap=rs.t[:, 0:n])
            S.op("dve", lambda e: e.tensor_tensor(out=sq.t[:, :, 0:n], in0=xt.t[:, :, 0:n],
                                                  in1=rs.t[:, 0:n].unsqueeze(1).to_broadcast([128, 8, n]), op=ALU.mult),
                 reads=[xt.b, rs.b], writes=[sq.b])
            for kc in range(8):
                S.op("act", lambda e: e.activation(out=hT.t[:, kc, 0:n], in_=sq.t[:, kc, 0:n], func=AF.Identity,
                                                   bias=ada.t[:, shofs + kc, s:s + 1], scale=modA.t[:, kc, s:s + 1]),
                     reads=[sq.b, ada.b, modA.b], writes=[hT.b])

        def phase_A(l, hsrc, do_ctx_q):
            with contextlib.ExitStack() as ph:
                Win = sbt(ph, "Win", [128, 8, O_G], BF16)
                wsrc = w_in[l].rearrange("(kc p) n -> p kc n", p=128)
                load_cast(lambda a0, a1: (Win.t[:, a0:a1, :], Win.b), wsrc[:, :, 0:O_G], 8, O_G)
                qag = sbt(ph, "qag", [128, 5], F32)
                S.dma(qag.t[:, 0:3], qagc[l], writes=[qag.b], key="qag")
                S.dma(qag.t[:, 3:5], kvagc[l], writes=[qag.b], key="qag")
                Wuq = sbt(ph, "Wuq", [128, 3, 768], BF16)
                load_cast(lambda a0, a1: (Wuq.t[:, a0:a1, :], Wuq.b), w_uq[l].rearrange("(kc p) n -> p kc n", p=128), 3, 768,
                          scale_fn=lambda a: (qag.t[:, a:a + 1], qag.b))
                Wkp = sbt(ph, "Wkp", [128, 2, 8, 96], BF16)
                S.op("pool", lambda e: e.memset(Wkp.t[:], 0.0), writes=[Wkp.b])
                Wv = sbt(ph, "Wv", [128, 2, 8, 64], BF16)
                ukv = w_ukv[l].rearrange("(kc p) n -> p kc n", p=128)
                for kc in range(2):
                    s_ = stg[stg_i[0] % 2]
                    stg_i[0] += 1
                    S.dma(s_.t[:, 0:1024], ukv[:, kc, :], writes=[s_.b], key=s_.b.name)
                    v3 = s_.t[:, 0:1024].rearrange("p (h c) -> p h c", c=128)
                    S.op("pool", lambda e: e.tensor_scalar(out=Wkp.t[:, kc, :, 0:64], in0=v3[:, :, 0:64], scalar1=qag.t[:, 3 + kc:4 + kc],
                                                           scalar2=None, op0=ALU.mult), reads=[s_.b, qag.b], writes=[Wkp.b])
                    S.op("pool", lambda e: e.tensor_scalar(out=Wv.t[:, kc, :, :], in0=v3[:, :, 64:128], scalar1=qag.t[:, 3 + kc:4 + kc],
                                                           scalar2=None, op0=ALU.mult), reads=[s_.b, qag.b], writes=[Wv.b])
                Wkd = sbt(ph, "Wkd", [128, 8, 2, 128], BF16)
                for kh in range(2):
                    for hf in range(2):
                        S.op("pool", lambda e: e.tensor_copy(out=Wkd.t[:, :, kh, hf * 64:(hf + 1) * 64],
                                                             in_=Win.t[:, :, O_SK + kh * 64:O_SK + (kh + 1) * 64]),
                             reads=[Win.b], writes=[Wkd.b])
                gq = sbt(ph, "gq", [128, 4], F32)
                S.dma(gq.t[0:96, 0:2], mqkg[l], writes=[gq.b], key="gq")
                S.dma(gq.t[:, 2:4], swag[l], writes=[gq.b], key="gq")

                xt = sbt(ph, "xt", [128, 8, 512], F32)
                sq = sbt(ph, "sq", [128, 8, 512], F32)
                rs = sbt(ph, "rs", [128, 512], F32)
                hT = sbt(ph, "hT", [128, 8, 512], BF16)
                q32 = sbt(ph, "q32", [128, 3, 512], F32)
                sqq = sbt(ph, "sqq", [128, 3, 512], F32)
                rq = sbt(ph, "rq", [128, 512], F32)
                qn = sbt(ph, "qn", [128, 3, 512], BF16)
                kvn = sbt(ph, "kvn", [128, 2, 512], BF16)
                krT = sbt(ph, "krT", [32, 512], BF16)
                uT = sbt(ph, "uT", [128, 4, 512], BF16)
                NH = 4
                hq32 = [sbt(ph, "hq32_%d" % i, [128, 512], F32) for i in range(NH)]
                hsq = [sbt(ph, "hsq_%d" % i, [128, 512], F32) for i in range(NH)]
                hqn = [sbt(ph, "hqn_%d" % i, [128, 512], F32) for i in range(NH)]
                hqb = [sbt(ph, "hqb_%d" % i, [128, 512], BF16) for i in range(NH)]
                hout = [sbt(ph, "hout_%d" % i, [128, 512], BF16) for i in range(NH)]
                va = [sbt(ph, "va_%d" % i, [128, 8, 128], BF16) for i in range(2)]
                sva = [sbt(ph, "sva_%d" % i, [128, 2, 128], BF16) for i in range(2)]
                for v_ in va + sva:
                    S.op("pool", lambda e: e.memset(v_.t[:], 1.0), writes=[v_.b])
                rope = sbt(ph, "rope", [128, 4, 512], F32)
                pss = pst(ph, "pss")
                pp = [pst(ph, "pp%d" % i) for i in range(2)]
                hp = [pst(ph, "hp%d" % i) for i in range(NH)]
                pv = pst(ph, "pv")
                cnt = [0]

                def headnorm(i, mm_fn, rows, gcol, onesT, dim, use_rope, Pmat, rc, rs_, n, dst_ap, dst_bufs):
                    ps, a32, asq, aqn, aqb, ao = hp[i], hq32[i], hsq[i], hqn[i], hqb[i], hout[i]
                    mm_fn(ps)
                    yield
                    S.op("act", lambda e: e.copy(out=a32.t[0:rows, 0:n], in_=ps.t[0:rows, 0:n]), reads=[ps.b], writes=[a32.b])
                    S.op("act", lambda e: e.activation(out=asq.t[0:rows, 0:n], in_=ps.t[0:rows, 0:n], func=AF.Square), reads=[ps.b], writes=[asq.b])
                    yield
                    S.op("pe", lambda e: e.matmul(ps.t[0:rows, 0:n], lhsT=onesT.t[0:rows, 0:rows], rhs=asq.t[0:rows, 0:n], start=True, stop=True),
                         reads=[onesT.b, asq.b], writes=[ps.b])
                    yield
                    rsqrt_from(None, asq, ps.t[0:rows, 0:n], ps.b, 1.0 / dim, shape_ap=asq.t[0:rows, 0:n])
                    yield
                    if not use_rope:
                        S.op("dve", lambda e: e.scalar_tensor_tensor(out=ao.t[0:rows, 0:n], in0=a32.t[0:rows, 0:n], scalar=gcol,
                                                                     in1=asq.t[0:rows, 0:n], op0=ALU.mult, op1=ALU.mult),
                             reads=[a32.b, asq.b, gq.b], writes=[ao.b])
                    else:
                        S.op("dve", lambda e: e.scalar_tensor_tensor(out=aqn.t[0:rows, 0:n], in0=a32.t[0:rows, 0:n], scalar=gcol,
                                                                     in1=asq.t[0:rows, 0:n], op0=ALU.mult, op1=ALU.mult),
                             reads=[a32.b, asq.b, gq.b], writes=[aqn.b])
                        yield
                        S.op("act", lambda e: e.copy(out=aqb.t[0:rows, 0:n], in_=aqn.t[0:rows, 0:n]), reads=[aqn.b], writes=[aqb.b])
                        yield
                        S.op("pe", lambda e: e.matmul(ps.t[0:rows, 0:n], lhsT=Pmat.t[0:rows, 0:rows], rhs=aqb.t[0:rows, 0:n], start=True, stop=True),
                             reads=[Pmat.b, aqb.b], writes=[ps.b])
                        S.op("pool", lambda e: e.tensor_tensor(out=a32.t[0:rows, 0:n], in0=aqn.t[0:rows, 0:n], in1=rope.t[0:rows, rc, 0:n], op=ALU.mult),
                             reads=[aqn.b, rope.b], writes=[a32.b])
                        yield
                        S.op("dve", lambda e: e.tensor_tensor(out=asq.t[0:rows, 0:n], in0=ps.t[0:rows, 0:n], in1=rope.t[0:rows, rs_, 0:n], op=ALU.mult),
                             reads=[ps.b, rope.b], writes=[asq.b])
                        yield
                        S.op("dve", lambda e: e.tensor_tensor(out=ao.t[0:rows, 0:n], in0=a32.t[0:rows, 0:n], in1=asq.t[0:rows, 0:n], op=ALU.add),
                             reads=[a32.b, asq.b], writes=[ao.b])
                    yield
                    if isinstance(dst_ap, list):
                        for (d_ap, r0, r1) in dst_ap:
                            S.dma(d_ap, ao.t[r0:r1, 0:n], reads=[ao.b], writes=dst_bufs, key=ao.b.name)
                    else:
                        S.dma(dst_ap, ao.t[0:rows, 0:n], reads=[ao.b], writes=dst_bufs, key=ao.b.name)

                def run_chains(jobs):
                    live = []
                    free_slots = list(range(NH))
                    nj = 0
                    while live or nj < len(jobs):
                        while len(live) < NH and nj < len(jobs):
                            sl = free_slots.pop(0)
                            live.append((jobs[nj](sl), sl))
                            nj += 1
                        for (g_, sl) in list(live):
                            try:
                                next(g_)
                            except StopIteration:
                                live.remove((g_, sl))
                                free_slots.append(sl)

                for (si, t0, n, isc) in STS:
                    s = 1 if isc else 0
                    src = hsrc.rearrange("(kc p) t -> p kc t", p=128)
                    S.dma(xt.t[:, :, 0:n], src[:, :, t0:t0 + n], reads=[db("hsrc%d" % id(hsrc), si)], writes=[xt.b], key="xt")
                    if not isc:
                        x0 = t0 - NC_
                        S.dma(rope.t[0:96, 0, 0:n], cMc[:, x0:x0 + n], writes=[rope.b], key="rope")
                        S.dma(rope.t[0:96, 1, 0:n], cMs[:, x0:x0 + n], writes=[rope.b], key="rope")
                        S.dma(rope.t[:, 2, 0:n], cSc[:, x0:x0 + n], writes=[rope.b], key="rope")
                        S.dma(rope.t[:, 3, 0:n], cSs[:, x0:x0 + n], writes=[rope.b], key="rope")
                    norm_mod(ph, xt, n, modA1, 0, isc, sq, pss, rs, hT)
                    S.dma(hTs[:, :, t0:t0 + n], hT.t[:, :, 0:n], reads=[hT.b], writes=[db("hTs", si)], key="hT_st")

                    def proj(ps, c0, ncols=128):
                        for kc in range(8):
                            S.op("pe", lambda e: e.matmul(ps.t[0:ncols, 0:n], lhsT=Win.t[:, kc, c0:c0 + ncols], rhs=hT.t[:, kc, 0:n],
                                                          start=(kc == 0), stop=(kc == 7)), reads=[Win.b, hT.b], writes=[ps.b])
                    pi = [0]

                    def nextp():
                        pi[0] += 1
                        return pp[pi[0] % 2]

                    for (nch, c0, dst, dim) in ((3, 0, qn, 384), (2, 384, kvn, 256)):
                        for c in range(nch):
                            ps = nextp()
                            proj(ps, c0 + c * 128)
                            S.op("act", lambda e: e.copy(out=q32.t[:, c, 0:n], in_=ps.t[:, 0:n]), reads=[ps.b], writes=[q32.b])
                        S.op("pool", lambda e: e.tensor_tensor(out=sqq.t[:, 0:nch, 0:n], in0=q32.t[:, 0:nch, 0:n], in1=q32.t[:, 0:nch, 0:n], op=ALU.mult),
                             reads=[q32.b], writes=[sqq.b])
                        for c in range(nch):
                            S.op("pe", lambda e: e.matmul(pss.t[:, 0:n], lhsT=ones32.t[:], rhs=sqq.t[:, c, 0:n], start=(c == 0), stop=(c == nch - 1)),
                                 reads=[ones32.b, sqq.b], writes=[pss.b])
                        rsqrt_from(None, rq, pss.t[:, 0:n], pss.b, 1.0 / dim, shape_ap=rq.t[:, 0:n])
                        S.op("dve", lambda e: e.tensor_tensor(out=dst.t[:, 0:nch, 0:n], in0=q32.t[:, 0:nch, 0:n],
                                                              in1=rq.t[:, 0:n].unsqueeze(1).to_broadcast([128, nch, n]), op=ALU.mult),
                             reads=[q32.b, rq.b], writes=[dst.b])
                    ps = nextp()
                    proj(ps, O_KR, 32)
                    S.op("act", lambda e: e.copy(out=krT.t[:, 0:n], in_=ps.t[0:32, 0:n]), reads=[ps.b], writes=[krT.b])
                    jobs = []
                    for h in range(8):
                        if (not isc) or do_ctx_q:
                            def mmq(ps, h=h):
                                for c in range(3):
                                    S.op("pe", lambda e: e.matmul(ps.t[0:96, 0:n], lhsT=Wuq.t[:, c, h * 96:(h + 1) * 96], rhs=qn.t[:, c, 0:n],
                                                                  start=(c == 0), stop=(c == 2)), reads=[Wuq.b, qn.b], writes=[ps.b])
                            jobs.append(lambda i, h=h, mmq=mmq: headnorm(i, mmq, 96, gq.t[0:96, 0:1], ones32, 96.0, not isc, Pm, 0, 1, n,
                                                                           QTs[h, :, t0:t0 + n], [db("QTs", (h, si))]))

                        def mmk(ps, h=h):
                            for c in range(2):
                                S.op("pe", lambda e: e.matmul(ps.t[0:96, 0:n], lhsT=Wkp.t[:, c, h, :], rhs=kvn.t[:, c, 0:n],
                                                              start=(c == 0), stop=False), reads=[Wkp.b, kvn.b], writes=[ps.b])
                            S.op("pe", lambda e: e.matmul(ps.t[0:96, 0:n], lhsT=shiftI.t[:, :], rhs=krT.t[:, 0:n], start=False, stop=True),
                                 reads=[shiftI.b, krT.b], writes=[ps.b])
                        jobs.append(lambda i, h=h, mmk=mmk: headnorm(i, mmk, 96, gq.t[0:96, 1:2], ones32, 96.0, not isc, Pm, 0, 1, n,
                                                                       KTs[h, :, t0:t0 + n], [db("KTs", (h, si))]))
                    for c in range(4):
                        def mmsq(ps, c=c):
                            proj(ps, O_SQ + c * 128)
                        jobs.append(lambda i, c=c, mmsq=mmsq: headnorm(i, mmsq, 128, gq.t[:, 2:3], bones32, 64.0, not isc, Ps, 2, 3, n,
                                                                         [(sQs[:, 2 * c, t0:t0 + n], 0, 64), (sQs[:, 2 * c + 1, t0:t0 + n], 64, 128)],
                                                                         [db("sQs", (c, si))]))
                    for kh in range(2):
                        def mmsk(ps, kh=kh):
                            for kc in range(8):
                                S.op("pe", lambda e: e.matmul(ps.t[:, 0:n], lhsT=Wkd.t[:, kc, kh, :], rhs=hT.t[:, kc, 0:n],
                                                              start=(kc == 0), stop=(kc == 7)), reads=[Wkd.b, hT.b], writes=[ps.b])
                        jobs.append(lambda i, kh=kh, mmsk=mmsk: headnorm(i, mmsk, 128, gq.t[:, 3:4], bones32, 64.0, not isc, Ps, 2, 3, n,
                                                                           [(sKs[:, kh, t0:t0 + n], 0, 64)], [db("sKs", (kh, si))]))
                    run_chains(jobs)
                    for j in range(n // 128):
                        tile_i = t0 // 128 + j
                        v_ = va[tile_i % 2]
                        for c in range(2):
                            S.op("pe", lambda e: e.matmul(pv.t[:, 0:512], lhsT=kvn.t[:, c, j * 128:(j + 1) * 128],
                                                          rhs=Wv.t[:, c, :, :].rearrange("p h c -> p (h c)"),
                                                          start=(c == 0), stop=(c == 1)), reads=[kvn.b, Wv.b], writes=[pv.b])
                        S.op("act", lambda e: e.copy(out=v_.t[:, :, 0:64], in_=pv.t[:, 0:512].rearrange("p (h c) -> p h c", c=64)),
                             reads=[pv.b], writes=[v_.b])
                        S.dma(VsM[tile_i], v_.t[:], reads=[v_.b], writes=[db("VsM", tile_i)], key=v_.b.name)
                    for c in range(4):
                        ps = nextp()
                        proj(ps, O_U + c * 128)
                        S.op("act", lambda e: e.copy(out=uT.t[:, c, 0:n], in_=ps.t[:, 0:n]), reads=[ps.b], writes=[uT.b])
                    S.dma(uTs[:, :, t0:t0 + n], uT.t[:, :, 0:n], reads=[uT.b], writes=[db("uTs", si)], key="uT_st")
                    for j in range(n // 128):
                        tile_i = t0 // 128 + j
                        v_ = sva[tile_i % 2]
                        for kc in range(8):
                            S.op("pe", lambda e: e.matmul(pv.t[:, 0:128], lhsT=hT.t[:, kc, j * 128:(j + 1) * 128], rhs=Win.t[:, kc, O_SV:O_SV + 128],
                                                          start=(kc == 0), stop=(kc == 7)), reads=[hT.b, Win.b], writes=[pv.b])
                        S.op("act", lambda e: e.copy(out=v_.t[:, :, 0:64], in_=pv.t[:, 0:128].rearrange("p (h c) -> p h c", c=64)),
                             reads=[pv.b], writes=[v_.b])
                        S.dma(sVs[tile_i], v_.t[:], reads=[v_.b], writes=[db("sVs", tile_i)], key=v_.b.name)
            S.barrier()

        def attn_core(ph, nkeys_tiles, score_fn, pv_lhsT_fn, pv_reads, n, pS, pO, Pt, scale, mask_fn=None):
            nk = len(nkeys_tiles)

            def do_s(i):
                score_fn(nkeys_tiles[i], pS[i % 3])

            def do_e(i):
                ps_, p_ = pS[i % 3], Pt[i % 3]
                S.op("act", lambda e: e.activation(out=p_.t[:, 0:n], in_=ps_.t[:, 0:n], func=AF.Exp, scale=scale), reads=[ps_.b], writes=[p_.b])
                if mask_fn is not None:
                    mask_fn(nkeys_tiles[i], p_)

            def do_pv(i):
                p_ = Pt[i % 3]
                S.op("pe", lambda e: e.matmul(pO.t[:, 0:n], lhsT=pv_lhsT_fn(nkeys_tiles[i]), rhs=p_.t[:, 0:n], start=(i == 0), stop=(i == nk - 1)),
                     reads=[p_.b] + pv_reads, writes=[pO.b])

            do_s(0)
            if nk > 1:
                do_s(1)
            for i in range(nk):
                do_e(i)
                if i + 2 < nk:
                    do_s(i + 2)
                do_pv(i)

        def phase_B(l, do_ctx):
            with contextlib.ExitStack() as ph:
                KT = [sbt(ph, "KT%d" % i, [96, T], BF16) for i in range(2)]
                QT = [sbt(ph, "QT%d" % i, [96, T], BF16) for i in range(2)]
                VH = [sbt(ph, "VH%d" % i, [128, 34, 128], BF16) for i in range(2)]
                Pt = [sbt(ph, "Pt%d" % i, [128, 512], BF16) for i in range(3)]
                rsum = [sbt(ph, "rsum%d" % i, [64, 512], F32) for i in range(2)]
                oT = [sbt(ph, "oT%d" % i, [64, 512], BF16) for i in range(2)]
                pS = [pst(ph, "pS%d" % i) for i in range(3)]
                pO = [pst(ph, "pO%d" % i) for i in range(2)]
                allv = [db("VsM", i) for i in range(34)]
                it = [0]
                for h in range(8):
                    allq = [db("QTs", (h, si)) for si in range(9)]
                    allk = [db("KTs", (h, si)) for si in range(9)]
                    kt_, qt_, vh_ = KT[h % 2], QT[h % 2], VH[h % 2]
                    S.dma(kt_.t[:], KTs[h], reads=allk, writes=[kt_.b], key=kt_.b.name)
                    S.dma(qt_.t[:], QTs[h], reads=allq, writes=[qt_.b], key=qt_.b.name)
                    S.dma(vh_.t[:], VsM.rearrange("t p h c -> p t h c")[:, :, h, :], reads=allv, writes=[vh_.b], key=vh_.b.name)
                    for (si, t0, n, isc) in STS:
                        if isc and not do_ctx:
                            continue
                        keys = [0, 1] if isc else list(range(34))
                        po = pO[it[0] % 2]
                        rs_ = rsum[it[0] % 2]
                        o_ = oT[it[0] % 2]
                        it[0] += 1

                        def score(kt, ps_):
                            S.op("pe", lambda e: e.matmul(ps_.t[:, 0:n], lhsT=kt_.t[:, kt * 128:(kt + 1) * 128], rhs=qt_.t[:, t0:t0 + n], start=True, stop=True),
                                 reads=[kt_.b, qt_.b], writes=[ps_.b])
                        attn_core(ph, keys, score, lambda kt: vh_.t[:, kt, :], [vh_.b], n, pS, po, Pt, 96.0 ** -0.5)
                        S.op("act", lambda e: e.copy(out=rs_.t[:, 0:n], in_=po.t[64:128, 0:n]), reads=[po.b], writes=[rs_.b])
                        S.op("dve", lambda e: e.reciprocal(out=rs_.t[:, 0:n], in_=rs_.t[:, 0:n]), reads=[rs_.b], writes=[rs_.b])
                        S.op("dve", lambda e: e.tensor_tensor(out=o_.t[:, 0:n], in0=po.t[0:64, 0:n], in1=rs_.t[:, 0:n], op=ALU.mult),
                             reads=[po.b, rs_.b], writes=[o_.b])
                        S.dma(mlaO[h * 64:(h + 1) * 64, t0:t0 + n], o_.t[:, 0:n], reads=[o_.b], writes=[db("mlaO", (h, si))], key=o_.b.name)
            S.barrier()

        def phase_C(l, do_ctx):
            with contextlib.ExitStack() as ph:
                sk = sbt(ph, "sk", [64, T], BF16)
                sq_ = sbt(ph, "sq_", [64, 4, T], BF16)
                sv = sbt(ph, "sv", [128, 34, 128], BF16)
                Pt = [sbt(ph, "Pt%d" % i, [128, 512], BF16) for i in range(3)]
                rsum = [sbt(ph, "rsum%d" % i, [64, 512], F32) for i in range(2)]
                oT = [sbt(ph, "oT%d" % i, [64, 512], BF16) for i in range(2)]
                esk = sbt(ph, "esk", [128, 8], F32)
                S.dma(esk.t[:], sinkb[l], writes=[esk.b], key="esk")
                S.op("act", lambda e: e.activation(out=esk.t[:], in_=esk.t[:], func=AF.Exp), reads=[esk.b], writes=[esk.b])
                pS = [pst(ph, "pS%d" % i) for i in range(3)]
                pO = [pst(ph, "pO%d" % i) for i in range(2)]
                allv = [db("sVs", i) for i in range(34)]
                it = [0]
                for kh in range(2):
                    allq = [db("sQs", (c_, si)) for si in range(9) for c_ in (2 * kh, 2 * kh + 1)]
                    allk = [db("sKs", (kh, si)) for si in range(9)]
                    S.dma(sk.t[:], sKs[:, kh, :], reads=allk, writes=[sk.b], key="sk")
                    S.dma(sq_.t[:], sQs[:, 4 * kh:4 * kh + 4, :], reads=allq, writes=[sq_.b], key="sq_")
                    S.dma(sv.t[:], sVs.rearrange("t p h c -> p t h c")[:, :, kh, :], reads=allv, writes=[sv.b], key="sv")
                    qtiles = ([0, 1] if do_ctx else []) + list(range(2, 34))
                    for qt in qtiles:
                        q0 = qt * 128
                        if qt < 2:
                            keys = [(0, None), (1, None)]
                        else:
                            keys = [(0, None), (1, None)]
                            if qt > 2:
                                keys.append((qt - 1, Mge))
                            keys.append((qt, None))
                            if qt < 33:
                                keys.append((qt + 1, Mle))
                        po = pO[it[0] % 2]
                        rs_ = rsum[it[0] % 2]
                        o_ = oT[it[0] % 2]
                        it[0] += 1

                        def score(km, ps_):
                            kt = km[0]
                            for hd in range(4):
                                S.op("pe", lambda e: e.matmul(ps_.t[:, hd * 128:(hd + 1) * 128], lhsT=sk.t[:, kt * 128:(kt + 1) * 128],
                                                              rhs=sq_.t[:, hd, q0:q0 + 128], start=True, stop=True),
                                     reads=[sk.b, sq_.b], writes=[ps_.b])

                        def maskf(km, p_):
                            if km[1] is not None:
                                m = km[1]
                                p3 = p_.t[:, :].rearrange("p (h q) -> p h q", q=128)
                                S.op("pool", lambda e: e.tensor_tensor(out=p3, in0=p3, in1=m.t[:, :].unsqueeze(1).to_broadcast([128, 4, 128]), op=ALU.mult),
                                     reads=[p_.b, m.b], writes=[p_.b])
                        attn_core(ph, keys, score, lambda km: sv.t[:, km[0], :], [sv.b], 512, pS, po, Pt, 0.125, mask_fn=maskf)
                        S.op("act", lambda e: e.copy(out=rs_.t[:, :], in_=po.t[64:128, :]), reads=[po.b], writes=[rs_.b])
                        r3 = rs_.t[:, :].rearrange("p (h q) -> p h q", q=128)
                        S.op("dve", lambda e: e.tensor_tensor(out=r3, in0=r3, in1=esk.t[0:64, 4 * kh:4 * kh + 4].unsqueeze(2).to_broadcast([64, 4, 128]), op=ALU.add),
                             reads=[rs_.b, esk.b], writes=[rs_.b])
                        S.op("dve", lambda e: e.reciprocal(out=rs_.t[:, :], in_=rs_.t[:, :]), reads=[rs_.b], writes=[rs_.b])
                        S.op("dve", lambda e: e.tensor_tensor(out=o_.t[:, :], in0=po.t[0:64, :], in1=rs_.t[:, :], op=ALU.mult),
                             reads=[po.b, rs_.b], writes=[o_.b])
                        dst = swaO.rearrange("(h d) t -> d h t", d=64)[:, 4 * kh:4 * kh + 4, q0:q0 + 128]
                        S.dma(dst, o_.t[:, :].rearrange("p (h q) -> p h q", q=128), reads=[o_.b], writes=[db("swaO", (kh, qt))], key=o_.b.name)
            S.barrier()

        def phase_D(l, do_ctx):
            with contextlib.ExitStack() as ph:
                uT = sbt(ph, "uT", [128, 4, T], BF16)
                S.dma(uT.t[:], uTs[:, :, :], reads=[db("uTs", si) for si in range(9)], writes=[uT.b], key="uT_ld")
                acc = sbt(ph, "acc", [128, T], F32)
                def small(name, w=32):
                    return sbt(ph, name, [128, w], F32)
                lre, lim, dtt = small("lre"), small("lim"), small("dtt")
                S.dma(lre.t[:], s5lre[l].rearrange("p d q -> p (d q)"), writes=[lre.b], key="s5p1")
                S.dma(lim.t[:], s5lim[l].rearrange("p d q -> p (d q)"), writes=[lim.b], key="s5p2")
                S.dma(dtt.t[:], s5ls[l].rearrange("p d q -> p (d q)"), writes=[dtt.b], key="s5p3")
                S.op("act", lambda e: e.activation(out=dtt.t[:], in_=dtt.t[:], func=AF.Exp), reads=[dtt.b], writes=[dtt.b])
                S.op("dve", lambda e: e.tensor_scalar(out=lre.t[:], in0=lre.t[:], scalar1=-1e-4, scalar2=None, op0=ALU.min), reads=[lre.b], writes=[lre.b])
                rr, th, fq = small("rr"), small("th"), small("fq")
                S.op("dve", lambda e: e.tensor_tensor(out=rr.t[:], in0=lre.t[:], in1=dtt.t[:], op=ALU.mult), reads=[lre.b, dtt.b], writes=[rr.b])
                S.op("act", lambda e: e.activation(out=rr.t[:], in_=rr.t[:], func=AF.Exp), reads=[rr.b], writes=[rr.b])
                S.op("dve", lambda e: e.tensor_tensor(out=th.t[:], in0=lim.t[:], in1=dtt.t[:], op=ALU.mult), reads=[lim.b, dtt.b], writes=[th.b])
                S.op("dve", lambda e: e.tensor_scalar(out=fq.t[:], in0=th.t[:], scalar1=1.0 / TWO_PI, scalar2=None, op0=ALU.mult), reads=[th.b], writes=[fq.b])
                tmpi = sbt(ph, "tmpi", [128, 512], I32)
                tmpf = sbt(ph, "tmpf", [128, 512], F32)

                def sincos(dst_s, dst_c, ph_ap, ph_buf, w):
                    for (dst, shift) in ((dst_s, 0.0), (dst_c, 0.25)):
                        S.op("dve", lambda e: e.tensor_scalar(out=tmpf.t[:, 0:w], in0=ph_ap, scalar1=shift, scalar2=None, op0=ALU.add), reads=[ph_buf], writes=[tmpf.b])
                        S.op("dve", lambda e: e.tensor_copy(out=tmpi.t[:, 0:w], in_=tmpf.t[:, 0:w]), reads=[tmpf.b], writes=[tmpi.b])
                        S.op("dve", lambda e: e.tensor_copy(out=dst[1], in_=tmpi.t[:, 0:w]), reads=[tmpi.b], writes=[dst[0]])
                        S.op("dve", lambda e: e.tensor_tensor(out=tmpf.t[:, 0:w], in0=tmpf.t[:, 0:w], in1=dst[1], op=ALU.subtract), reads=[tmpf.b, dst[0]], writes=[tmpf.b])
                        S.op("act", lambda e: e.activation(out=dst[1], in_=tmpf.t[:, 0:w], func=AF.Sin, scale=TWO_PI), reads=[tmpf.b], writes=[dst[0]])

                sn, cs = small("sn"), small("cs")
                sincos((sn.b, sn.t[:]), (cs.b, cs.t[:]), fq.t[:], fq.b, 32)
                are, aim = small("are"), small("aim")
                S.op("dve", lambda e: e.tensor_tensor(out=are.t[:], in0=rr.t[:], in1=cs.t[:], op=ALU.mult), reads=[rr.b, cs.b], writes=[are.b])
                S.op("dve", lambda e: e.tensor_tensor(out=aim.t[:], in0=rr.t[:], in1=sn.t[:], op=ALU.mult), reads=[rr.b, sn.b], writes=[aim.b])
                am1, den, t1_, t2_, cre, cim = small("am1"), small("den"), small("t1_"), small("t2_"), small("cre"), small("cim")
                S.op("dve", lambda e: e.tensor_scalar(out=am1.t[:], in0=are.t[:], scalar1=-1.0, scalar2=None, op0=ALU.add), reads=[are.b], writes=[am1.b])
                S.op("dve", lambda e: e.tensor_tensor(out=den.t[:], in0=lre.t[:], in1=lre.t[:], op=ALU.mult), reads=[lre.b], writes=[den.b])
                S.op("dve", lambda e: e.tensor_tensor(out=t1_.t[:], in0=lim.t[:], in1=lim.t[:], op=ALU.mult), reads=[lim.b], writes=[t1_.b])
                S.op("dve", lambda e: e.tensor_tensor(out=den.t[:], in0=den.t[:], in1=t1_.t[:], op=ALU.add), reads=[den.b, t1_.b], writes=[den.b])
                S.op("dve", lambda e: e.reciprocal(out=den.t[:], in_=den.t[:]), reads=[den.b], writes=[den.b])
                S.op("dve", lambda e: e.tensor_tensor(out=t1_.t[:], in0=am1.t[:], in1=lre.t[:], op=ALU.mult), reads=[am1.b, lre.b], writes=[t1_.b])
                S.op("dve", lambda e: e.tensor_tensor(out=t2_.t[:], in0=aim.t[:], in1=lim.t[:], op=ALU.mult), reads=[aim.b, lim.b], writes=[t2_.b])
                S.op("dve", lambda e: e.tensor_tensor(out=cre.t[:], in0=t1_.t[:], in1=t2_.t[:], op=ALU.add), reads=[t1_.b, t2_.b], writes=[cre.b])
                S.op("dve", lambda e: e.tensor_tensor(out=cre.t[:], in0=cre.t[:], in1=den.t[:], op=ALU.mult), reads=[cre.b, den.b], writes=[cre.b])
                S.op("dve", lambda e: e.tensor_tensor(out=t1_.t[:], in0=aim.t[:], in1=lre.t[:], op=ALU.mult), reads=[aim.b, lre.b], writes=[t1_.b])
                S.op("dve", lambda e: e.tensor_tensor(out=t2_.t[:], in0=am1.t[:], in1=lim.t[:], op=ALU.mult), reads=[am1.b, lim.b], writes=[t2_.b])
                S.op("dve", lambda e: e.tensor_tensor(out=cim.t[:], in0=t1_.t[:], in1=t2_.t[:], op=ALU.subtract), reads=[t1_.b, t2_.b], writes=[cim.b])
                S.op("dve", lambda e: e.tensor_tensor(out=cim.t[:], in0=cim.t[:], in1=den.t[:], op=ALU.mult), reads=[cim.b, den.b], writes=[cim.b])
                fb, snB, csB = small("fb", 64), small("snB", 64), small("csB", 64)
                S.op("dve", lambda e: e.tensor_scalar(out=fb.t[:, 0:32], in0=fq.t[:], scalar1=256.0, scalar2=None, op0=ALU.mult), reads=[fq.b], writes=[fb.b])
                S.op("dve", lambda e: e.tensor_scalar(out=fb.t[:, 32:64], in0=fq.t[:], scalar1=512.0, scalar2=None, op0=ALU.mult), reads=[fq.b], writes=[fb.b])
                sincos((snB.b, snB.t[:]), (csB.b, csB.t[:]), fb.t[:], fb.b, 64)
                WB = sbt(ph, "WB", [128, 2, 2, 8, 128], BF16)
                CW = sbt(ph, "CW", [128, 3, 2, 16, 128], BF16)
                S.op("pool", lambda e: e.memset(CW.t[:], 0.0), writes=[CW.b])
                prep = contextlib.ExitStack()
                braw = sbt(prep, "braw", [128, 2, 2, 16, 16], F32)
                craw = sbt(prep, "craw", [128, 2, 2, 16, 16], F32)
                S.dma(braw.t[:, 0], s5bre[l], writes=[braw.b], key="braw")
                S.dma(braw.t[:, 1], s5bim[l], writes=[braw.b], key="braw")
                S.dma(craw.t[:, 0], s5cre[l], writes=[craw.b], key="craw")
                S.dma(craw.t[:, 1], s5cim[l], writes=[craw.b], key="craw")
                bbar = sbt(prep, "bbar", [128, 2, 2, 16, 16], F32)
                tb = sbt(prep, "tb", [128, 2, 16, 16], F32)
                cre3 = cre.t[:].rearrange("p (d q) -> p d q", q=16).unsqueeze(3).to_broadcast([128, 2, 16, 16])
                cim3 = cim.t[:].rearrange("p (d q) -> p d q", q=16).unsqueeze(3).to_broadcast([128, 2, 16, 16])
                S.op("dve", lambda e: e.tensor_tensor(out=bbar.t[:, 0], in0=braw.t[:, 0], in1=cre3, op=ALU.mult), reads=[braw.b, cre.b], writes=[bbar.b])
                S.op("dve", lambda e: e.tensor_tensor(out=tb.t[:], in0=braw.t[:, 1], in1=cim3, op=ALU.mult), reads=[braw.b, cim.b], writes=[tb.b])
                S.op("dve", lambda e: e.tensor_tensor(out=bbar.t[:, 0], in0=bbar.t[:, 0], in1=tb.t[:], op=ALU.subtract), reads=[bbar.b, tb.b], writes=[bbar.b])
                S.op("dve", lambda e: e.tensor_tensor(out=bbar.t[:, 1], in0=braw.t[:, 1], in1=cre3, op=ALU.mult), reads=[braw.b, cre.b], writes=[bbar.b])
                S.op("dve", lambda e: e.tensor_tensor(out=tb.t[:], in0=braw.t[:, 0], in1=cim3, op=ALU.mult), reads=[braw.b, cim.b], writes=[tb.b])
                S.op("dve", lambda e: e.tensor_tensor(out=bbar.t[:, 1], in0=bbar.t[:, 1], in1=tb.t[:], op=ALU.add), reads=[bbar.b, tb.b], writes=[bbar.b])
                Z = [sbt(prep, "Z%d" % i, [128, 64], BF16) for i in range(2)]
                pz = [pst(prep, "pz%d" % i, [128, 128], BF16) for i in range(2)]
                zi = 0
                for d_ in range(2):
                    for q in range(16):
                        c, q4 = q // 4, q % 4
                        hf, q2 = q4 // 2, q4 % 2
                        for ri in range(2):
                            z, pzz = Z[zi % 2], pz[zi % 2]
                            zi += 1
                            S.op("pool", lambda e: e.memset(z.t[:], 0.0), writes=[z.b])
                            S.op("dve", lambda e: e.tensor_copy(out=z.t[0:64, q2 * 32:q2 * 32 + 16], in_=bbar.t[0:64, ri, d_, q, :]), reads=[bbar.b], writes=[z.b])
                            S.op("dve", lambda e: e.tensor_copy(out=z.t[64:128, q2 * 32 + 16:q2 * 32 + 32], in_=bbar.t[64:128, ri, d_, q, :]), reads=[bbar.b], writes=[z.b])
                            S.op("pe", lambda e: e.transpose(pzz.t[0:64, :], z.t[:, :], identb.t[:, :]), reads=[z.b, identb.b], writes=[pzz.b])
                            S.op("act", lambda e: e.copy(out=WB.t[hf * 64:(hf + 1) * 64, ri, d_, c * 2 + q2, :], in_=pzz.t[0:64, :]), reads=[pzz.b], writes=[WB.b])
                            sgn = 1.0 if ri == 0 else -1.0
                            for e_ in range(2):
                                S.op("pool", lambda e: e.tensor_scalar(out=CW.t[e_ * 64:(e_ + 1) * 64, ri, d_, q, q4 * 32 + e_ * 16:q4 * 32 + e_ * 16 + 16],
                                                                       in0=craw.t[e_ * 64:(e_ + 1) * 64, ri, d_, q, :], scalar1=sgn, scalar2=None, op0=ALU.mult),
                                     reads=[craw.b], writes=[CW.b])
                                if ri == 0:
                                    S.op("pool", lambda e: e.tensor_scalar(out=CW.t[e_ * 64:(e_ + 1) * 64, 2, d_, q, q4 * 32 + e_ * 16:q4 * 32 + e_ * 16 + 16],
                                                                           in0=craw.t[e_ * 64:(e_ + 1) * 64, 0, d_, q, :], scalar1=-1.0, scalar2=None, op0=ALU.mult),
                                         reads=[craw.b], writes=[CW.b])
                S.barrier()
                prep.close()
                Wg = sbt(ph, "Wg", [128, 4, 512], BF16)
                load_cast(lambda a0, a1: (Wg.t[:, a0:a1, :], Wg.b), w_glu[l].rearrange("(kc p) n -> p kc n", p=128), 4, 512, eng="act")
                dcol = sbt(ph, "dcol", [128, 8], F32)
                S.dma(dcol.t[:, 0:4], s5dc[l], writes=[dcol.b], key="dcol")
                S.dma(dcol.t[:, 4:8], s5bgc[l], writes=[dcol.b], key="dcol")
                iot = sbt(ph, "iot", [128, 512], F32)
                S.dma(iot.t[:], cIota[:, :], writes=[iot.b], key="iot")
                wk = contextlib.ExitStack()
                tph = sbt(wk, "tph", [128, 512], F32)
                tC4 = [sbt(wk, "tC4_%d" % i, [128, 512], F32) for i in range(4)]
                tS4 = [sbt(wk, "tS4_%d" % i, [128, 512], F32) for i in range(4)]
                NS = int(_os.environ.get("D_NS", "3"))
                W = {k: [sbt(wk, "w%s%d" % (k, i), [128, 512], F32) for i in range(NS)] for k in
                     ("t1", "t2", "t3", "t4", "xa", "xb", "ga", "gb")}
                Ub = {k: [sbt(wk, "b%s%d" % (k, i), [128, 512], BF16) for i in range(NS)] for k in ("u1", "u2", "u3", "u4")}
                ini4 = [sbt(wk, "ini4_%d" % i, [128, 4], F32) for i in range(4)]
                ygt = [sbt(wk, "ygt%d" % i, [128, 512], BF16) for i in range(2)]
                Xs = [[sbt(wk, "Xs%d%d" % (i, j), [128, 512], F32) for j in range(2)] for i in range(NS)] if D_EVAC else None
                pX = [[pst(wk, "pX%d%d" % (i, j)) for j in range(2)] for i in range(NS)]
                pY = [pst(wk, "pY%d" % i) for i in range(2)]
                UENG = D_UENG

                def tt(eng, o, a, b_, op, rd, wr):
                    S.op(eng, lambda e: e.tensor_tensor(out=o, in0=a, in1=b_, op=op), reads=rd, writes=wr)
                it = 0
                for c in range(4):
                    S.op("dve", lambda e: e.tensor_scalar(out=acc.t[:], in0=uT.t[:, c, :], scalar1=dcol.t[:, c:c + 1], scalar2=None, op0=ALU.mult),
                         reads=[uT.b, dcol.b], writes=[acc.b])
                    for d_ in range(2):
                        for q4 in range(4):
                            col = d_ * 16 + c * 4 + q4
                            S.op("dve", lambda e: e.tensor_scalar(out=tph.t[:], in0=iot.t[:], scalar1=fq.t[:, col:col + 1], scalar2=None, op0=ALU.mult),
                                 reads=[iot.b, fq.b], writes=[tph.b])
                            sincos((tS4[q4].b, tS4[q4].t[:]), (tC4[q4].b, tC4[q4].t[:]), tph.t[:], tph.b, 512)
                            S.op("pool", lambda e: e.memset(ini4[q4].t[:], 0.0), writes=[ini4[q4].b])
                        def frame_geom(fidx, d_=d_):
                            n = 256 if fidx == 0 else 512
                            if d_ == 0:
                                k0 = 0 if fidx == 0 else 256 + 512 * (fidx - 1)
                                cols = slice(k0, k0 + n)
                            else:
                                if fidx == 0:
                                    cols = slice(255, None, -1)
                                else:
                                    hi = NC_ + NX - 512 * (fidx - 1) - 1
                                    cols = slice(hi, hi - 512, -1)
                            return n, cols

                        if True:
                            if True:
                                pass
                            def chain(q4, i, fidx, d_=d_, c=c):
                                n, cols = frame_geom(fidx)
                                py = pY[fidx % 2]
                                q = c * 4 + q4
                                hf, q2 = q4 // 2, q4 % 2
                                col = d_ * 16 + q
                                tC, tS, iv = tC4[q4], tS4[q4], ini4[q4]
                                px = pX[i]
                                for ri in range(2):
                                    S.op("pe", lambda e: e.matmul(px[ri].t[:, 0:n], lhsT=WB.t[hf * 64:(hf + 1) * 64, ri, d_, c * 2 + q2, :],
                                                                  rhs=uT.t[hf * 64:(hf + 1) * 64, c, cols], start=True, stop=True),
                                         reads=[WB.b, uT.b], writes=[px[ri].b])
                                yield
                                t1, t2, t3, t4 = W["t1"][i], W["t2"][i], W["t3"][i], W["t4"][i]
                                xa, xb, ga, gb = W["xa"][i], W["xb"][i], W["ga"][i], W["gb"][i]
                                u1, u2, u3, u4 = Ub["u1"][i], Ub["u2"][i], Ub["u3"][i], Ub["u4"][i]
                                if D_EVAC:
                                    x0_, x1_ = Xs[i][0], Xs[i][1]
                                    S.op("act", lambda e: e.copy(out=x0_.t[:, 0:n], in_=px[0].t[:, 0:n]), reads=[px[0].b], writes=[x0_.b])
                                    S.op("act", lambda e: e.copy(out=x1_.t[:, 0:n], in_=px[1].t[:, 0:n]), reads=[px[1].b], writes=[x1_.b])
                                    yield
                                else:
                                    x0_, x1_ = px[0], px[1]
                                tt("dve", t1.t[:, 0:n], x0_.t[:, 0:n], tC.t[:, 0:n], ALU.mult, [x0_.b, tC.b], [t1.b])
                                yield
                                tt("dve", t2.t[:, 0:n], x1_.t[:, 0:n], tS.t[:, 0:n], ALU.mult, [x1_.b, tS.b], [t2.b])
                                yield
                                tt("dve", t3.t[:, 0:n], x1_.t[:, 0:n], tC.t[:, 0:n], ALU.mult, [x1_.b, tC.b], [t3.b])
                                yield
                                tt("dve", t4.t[:, 0:n], x0_.t[:, 0:n], tS.t[:, 0:n], ALU.mult, [x0_.b, tS.b], [t4.b])
                                yield
                                tt(D_XENG, xa.t[:, 0:n], t1.t[:, 0:n], t2.t[:, 0:n], ALU.add, [t1.b, t2.b], [xa.b])
                                yield
                                tt(D_XENG, xb.t[:, 0:n], t3.t[:, 0:n], t4.t[:, 0:n], ALU.subtract, [t3.b, t4.b], [xb.b])
                                yield
                                rbc = rr.t[:, col:col + 1].to_broadcast([128, n])
                                S.op("dve", lambda e: e.tensor_tensor_scan(out=ga.t[:, 0:n], data0=rbc, data1=xa.t[:, 0:n], initial=iv.t[:, 0:1], op0=ALU.mult, op1=ALU.add),
                                     reads=[xa.b, rr.b, iv.b], writes=[ga.b])
                                yield
                                S.op("dve", lambda e: e.tensor_tensor_scan(out=gb.t[:, 0:n], data0=rbc, data1=xb.t[:, 0:n], initial=iv.t[:, 1:2], op0=ALU.mult, op1=ALU.add),
                                     reads=[xb.b, rr.b, iv.b], writes=[gb.b])
                                yield
                                if fidx < 8:
                                    bc = (0 if n == 256 else 32) + col
                                    cB, sB = csB.t[:, bc:bc + 1], snB.t[:, bc:bc + 1]
                                    S.op("pool", lambda e: e.tensor_scalar(out=iv.t[:, 2:3], in0=gb.t[:, n - 1:n], scalar1=sB, scalar2=None, op0=ALU.mult), reads=[gb.b, snB.b], writes=[iv.b])
                                    S.op("pool", lambda e: e.tensor_scalar(out=iv.t[:, 3:4], in0=ga.t[:, n - 1:n], scalar1=sB, scalar2=None, op0=ALU.mult), reads=[ga.b, snB.b], writes=[iv.b])
                                yield
                                tt(UENG[0], u1.t[:, 0:n], ga.t[:, 0:n], tC.t[:, 0:n], ALU.mult, [ga.b, tC.b], [u1.b])
                                yield
                                tt(UENG[1], u2.t[:, 0:n], gb.t[:, 0:n], tS.t[:, 0:n], ALU.mult, [gb.b, tS.b], [u2.b])
                                yield
                                tt(UENG[2], u3.t[:, 0:n], ga.t[:, 0:n], tS.t[:, 0:n], ALU.mult, [ga.b, tS.b], [u3.b])
                                yield
                                tt(UENG[3], u4.t[:, 0:n], gb.t[:, 0:n], tC.t[:, 0:n], ALU.mult, [gb.b, tC.b], [u4.b])
                                yield
                                if fidx < 8:
                                    bc = (0 if n == 256 else 32) + col
                                    cB = csB.t[:, bc:bc + 1]
                                    S.op("dve", lambda e: e.scalar_tensor_tensor(out=iv.t[:, 0:1], in0=ga.t[:, n - 1:n], scalar=cB, in1=iv.t[:, 2:3], op0=ALU.mult, op1=ALU.subtract),
                                         reads=[ga.b, csB.b, iv.b], writes=[iv.b])
                                    S.op("dve", lambda e: e.scalar_tensor_tensor(out=iv.t[:, 1:2], in0=gb.t[:, n - 1:n], scalar=cB, in1=iv.t[:, 3:4], op0=ALU.mult, op1=ALU.add),
                                         reads=[gb.b, csB.b, iv.b], writes=[iv.b])
                                yield
                                for j_, (ws, uu) in enumerate(((0, u1), (2, u2), (1, u3), (1, u4))):
                                    S.op("pe", lambda e: e.matmul(py.t[:, 0:n], lhsT=CW.t[:, ws, d_, q, :], rhs=uu.t[:, 0:n],
                                                                  start=(q4 == 0 and j_ == 0), stop=(q4 == 3 and j_ == 3)), reads=[CW.b, uu.b], writes=[py.b])
                                if q4 == 3:
                                    S.op("dve", lambda e: e.tensor_tensor(out=acc.t[:, cols], in0=acc.t[:, cols], in1=py.t[:, 0:n], op=ALU.add), reads=[acc.b, py.b], writes=[acc.b])

                            jobs = [(fidx, q4) for fidx in range(9) for q4 in range(4)]
                            live = []
                            free_slots = list(range(NS))
                            nj = 0
                            while live or nj < len(jobs):
                                while len(live) < NS and nj < len(jobs):
                                    sl = free_slots.pop(0)
                                    live.append((chain(jobs[nj][1], sl, jobs[nj][0]), sl))
                                    nj += 1
                                for (g_, sl) in list(live):
                                    try:
                                        next(g_)
                                    except StopIteration:
                                        live.remove((g_, sl))
                                        free_slots.append(sl)
                    for a0 in range(0, T, 512):
                        w_ = min(512, T - a0)
                        i = (a0 // 512) % 2
                        t1, t2 = W["t1"][i], W["t2"][i]
                        yo = ygt[i]
                        S.op("pool", lambda e: e.tensor_tensor(out=t1.t[:, 0:w_], in0=acc.t[:, a0:a0 + w_], in1=acc.t[:, a0:a0 + w_], op=ALU.mult), reads=[acc.b], writes=[t1.b])
                        S.op("pool", lambda e: e.tensor_scalar(out=t1.t[:, 0:w_], in0=t1.t[:, 0:w_], scalar1=0.044715, scalar2=1.0, op0=ALU.mult, op1=ALU.add), reads=[t1.b], writes=[t1.b])
                        S.op("pool", lambda e: e.tensor_tensor(out=t1.t[:, 0:w_], in0=t1.t[:, 0:w_], in1=acc.t[:, a0:a0 + w_], op=ALU.mult), reads=[t1.b, acc.b], writes=[t1.b])
                        S.op("act", lambda e: e.activation(out=t2.t[:, 0:w_], in_=t1.t[:, 0:w_], func=AF.Sigmoid, scale=1.5957691216057308), reads=[t1.b], writes=[t2.b])
                        S.op("dve", lambda e: e.tensor_tensor(out=yo.t[:, 0:w_], in0=t2.t[:, 0:w_], in1=acc.t[:, a0:a0 + w_], op=ALU.mult), reads=[t2.b, acc.b], writes=[yo.b])
                        S.dma(ygs[:, c, a0:a0 + w_], yo.t[:, 0:w_], reads=[yo.b], writes=[db("ygs", (c, a0 // 512))], key=yo.b.name)
                S.barrier()
                wk.close()
                pY = [pst(ph, "pYg%d" % i) for i in range(2)]
                W = {"t2": [sbt(ph, "gsig%d" % i, [128, 512], F32) for i in range(2)]}
                so = [sbt(ph, "so%d" % i, [128, 4, 512], BF16) for i in range(2)]
                ygl = [sbt(ph, "ygl%d" % i, [128, 4, 512], BF16) for i in range(2)]
                for (si, t0, n, isc) in STS:
                    if isc and not do_ctx:
                        continue
                    o_ = so[si % 2]
                    yg = ygl[si % 2]
                    S.dma(yg.t[:, :, 0:n], ygs[:, :, t0:t0 + n], reads=[db("ygs", (c_, (t0 + j_) // 512)) for c_ in range(4) for j_ in (0, n - 1)],
                          writes=[yg.b], key=yg.b.name)
                    for oc in range(4):
                        py = pY[oc % 2]
                        for kc in range(4):
                            S.op("pe", lambda e: e.matmul(py.t[:, 0:n], lhsT=Wg.t[:, kc, oc * 128:(oc + 1) * 128], rhs=yg.t[:, kc, 0:n], start=(kc == 0), stop=(kc == 3)),
                                 reads=[Wg.b, yg.b], writes=[py.b])
                        t2 = W["t2"][oc % 2]
                        S.op("act", lambda e: e.activation(out=t2.t[:, 0:n], in_=py.t[:, 0:n], func=AF.Sigmoid, bias=dcol.t[:, 4 + oc:5 + oc], scale=1.0), reads=[py.b, dcol.b], writes=[t2.b])
                        S.op("dve", lambda e: e.tensor_tensor(out=o_.t[:, oc, 0:n], in0=t2.t[:, 0:n], in1=yg.t[:, oc, 0:n], op=ALU.mult), reads=[t2.b, yg.b], writes=[o_.b])
                    S.dma(s5O[:, :, t0:t0 + n], o_.t[:, :, 0:n], reads=[o_.b], writes=[db("s5O", si)], key=o_.b.name)
            S.barrier()

        def load_E_weights(ph, l, prefetch):
            de = "pool" if prefetch else "sp"
            Wgt = sbt(ph, "Wgt", [128, 8, 3072], BF16)
            wsrc = w_in[l].rearrange("(kc p) n -> p kc n", p=128)
            for r in range(3):
                load_cast(lambda a0, a1: (Wgt.t[:, a0:a1, r * 1024:(r + 1) * 1024], Wgt.b), wsrc[:, :, O_G + r * 1024:O_G + (r + 1) * 1024], 8, 1024,
                          eng=("pool" if (prefetch or r % 2 == 0) else "act"), dma_eng=de)
            Wb = sbt(ph, "Wb", [128, 3, 4, 1024], BF16)
            for r in range(3):
                load_cast(lambda a0, a1: (Wb.t[:, r, a0:a1, :], Wb.b), w_br[l, r].rearrange("(kc p) n -> p kc n", p=128), 4, 1024,
                          eng=("pool" if (prefetch or r % 2 == 1) else "act"), dma_eng=de)
            Wo = sbt(ph, "Wo", [128, 8, 1024], BF16)
            load_cast(lambda a0, a1: (Wo.t[:, a0:a1, :], Wo.b), w_o[l].rearrange("(kc p) n -> p kc n", p=128), 8, 1024, dma_eng=de)
            return Wgt, Wb, Wo

        def load_F2_weights(ph, l, prefetch, emit=True):
            Wfo = sbt(ph, "Wfo", [128, 22, 1024], BF16)

            def emit_fn():
                load_cast(lambda a0, a1: (Wfo.t[:, a0:a1, :], Wfo.b), f_out[l].rearrange("(kc p) n -> p kc n", p=128), 22, 1024,
                          dma_eng=("pool" if prefetch else "sp"))
            if emit:
                emit_fn()
                return Wfo
            return Wfo, emit_fn

        def phase_E(l, hsrc, hdst, do_ctx, pre=None):
            with contextlib.ExitStack() as ph:
                Wgt, Wb, Wo = pre if pre is not None else load_E_weights(ph, l, False)
                hT = [sbt(ph, "hT%d" % i, [128, 8, 512], BF16) for i in range(2)]
                oB = [sbt(ph, "oB%d" % i, [128, 3, 4, 512], BF16) for i in range(2)]
                xt1 = sbt(ph, "xt", [128, 8, 512], F32)
                xt = [xt1, xt1]
                mT = sbt(ph, "mT", [128, 8, 512], BF16)
                sg = [sbt(ph, "sg%d" % i, [128, 512], BF16) for i in range(3)]
                gp = [sbt(ph, "gp%d" % i, [128, 512], F32) for i in range(3)]
                hn = xt1
                sq = sbt(ph, "sq", [128, 8, 512], F32)
                rs = sbt(ph, "rs", [128, 512], F32)
                fT = sbt(ph, "fT", [128, 8, 512], BF16)
                pG = [pst(ph, "pG%d" % i) for i in range(3)]
                pP = [pst(ph, "pP%d" % i) for i in range(3)]
                pM = pst(ph, "pM")
                pss = pst(ph, "pss")
                srcs = [(mlaO.rearrange("(kc p) t -> p kc t", p=128), "mlaO"), (s5O, "s5O"), (swaO.rearrange("(kc p) t -> p kc t", p=128), "swaO")]
                sts = [s for s in STS if (not s[3]) or do_ctx]

                def loads(k):
                    (si, t0, n, isc) = sts[k]
                    S.dma(hT[k % 2].t[:, :, 0:n], hTs[:, :, t0:t0 + n], reads=[db("hTs", si)], writes=[hT[k % 2].b], key=hT[k % 2].b.name)
                    for r in range(3):
                        if srcs[r][1] == "swaO":
                            rd = [db("swaO", (kh_, t0 // 128 + j)) for j in range(n // 128) for kh_ in range(2)]
                        elif srcs[r][1] == "mlaO":
                            rd = [db("mlaO", (h_i, si)) for h_i in range(8)]
                        else:
                            rd = [db(srcs[r][1], si)]
                        S.dma(oB[k % 2].t[:, r, :, 0:n], srcs[r][0][:, :, t0:t0 + n], reads=rd, writes=[oB[k % 2].b], key=oB[k % 2].b.name)

                def loadx(k):
                    (si, t0, n, isc) = sts[k]
                    src = hsrc.rearrange("(kc p) t -> p kc t", p=128)
                    S.dma(xt1.t[:, :, 0:n], src[:, :, t0:t0 + n], reads=[db("hsrc%d" % id(hsrc), si)], writes=[xt1.b], key=xt1.b.name)

                loads(0)
                for k, (si, t0, n, isc) in enumerate(sts):
                    loadx(k)
                    if k + 1 < len(sts):
                        loads(k + 1)
                    s = 1 if isc else 0
                    h_, o_, x_ = hT[k % 2], oB[k % 2], xt[k % 2]
                    for oc in range(8):
                        for r in range(3):
                            for kc in range(8):
                                S.op("pe", lambda e: e.matmul(pG[r].t[:, 0:n], lhsT=Wgt.t[:, kc, r * 1024 + oc * 128:r * 1024 + (oc + 1) * 128], rhs=h_.t[:, kc, 0:n],
                                                              start=(kc == 0), stop=(kc == 7)), reads=[Wgt.b, h_.b], writes=[pG[r].b])
                            for kc in range(4):
                                S.op("pe", lambda e: e.matmul(pP[r].t[:, 0:n], lhsT=Wb.t[:, r, kc, oc * 128:(oc + 1) * 128], rhs=o_.t[:, r, kc, 0:n],
                                                              start=(kc == 0), stop=(kc == 3)), reads=[Wb.b, o_.b], writes=[pP[r].b])
                        for r in range(3):
                            S.op("act", lambda e: e.activation(out=sg[r].t[:, 0:n], in_=pG[r].t[:, 0:n], func=AF.Sigmoid), reads=[pG[r].b], writes=[sg[r].b])
                            S.op("dve", lambda e: e.tensor_tensor(out=gp[r].t[:, 0:n], in0=pP[r].t[:, 0:n], in1=sg[r].t[:, 0:n], op=ALU.mult),
                                 reads=[pP[r].b, sg[r].b], writes=[gp[r].b])
                        S.op("pool", lambda e: e.tensor_tensor(out=gp[0].t[:, 0:n], in0=gp[0].t[:, 0:n], in1=gp[1].t[:, 0:n], op=ALU.add), reads=[gp[0].b, gp[1].b], writes=[gp[0].b])
                        S.op("pool", lambda e: e.tensor_tensor(out=mT.t[:, oc, 0:n], in0=gp[0].t[:, 0:n], in1=gp[2].t[:, 0:n], op=ALU.add), reads=[gp[0].b, gp[2].b], writes=[mT.b])
                    for oc in range(8):
                        for kc in range(8):
                            S.op("pe", lambda e: e.matmul(pM.t[:, 0:n], lhsT=Wo.t[:, kc, oc * 128:(oc + 1) * 128], rhs=mT.t[:, kc, 0:n], start=(kc == 0), stop=(kc == 7)),
                                 reads=[Wo.b, mT.b], writes=[pM.b])
                        S.op("dve", lambda e: e.scalar_tensor_tensor(out=hn.t[:, oc, 0:n], in0=pM.t[:, 0:n], scalar=ada.t[:, 16 + oc, s:s + 1], in1=x_.t[:, oc, 0:n],
                                                                     op0=ALU.mult, op1=ALU.add), reads=[pM.b, ada.b, x_.b], writes=[hn.b])
                    dst = hdst.rearrange("(kc p) t -> p kc t", p=128)
                    S.dma(dst[:, :, t0:t0 + n], hn.t[:, :, 0:n], reads=[hn.b], writes=[db("hsrc%d" % id(hdst), si)], key="hn_st")
                    norm_mod(ph, hn, n, modA2, 24, isc, sq, pss, rs, fT)
                    S.dma(fTs[:, :, t0:t0 + n], fT.t[:, :, 0:n], reads=[fT.b], writes=[db("fTs", si)], key="fT_st")
            S.barrier()

        def phase_F1(l, do_ctx, after_weights=None):
            with contextlib.ExitStack() as ph:
                Wf = sbt(ph, "Wf", [128, 8, 2 * FH], BF16)
                wsrc = f_in[l].rearrange("(kc p) n -> p kc n", p=128)
                for cb in range(11):
                    load_cast(lambda a0, a1: (Wf.t[:, a0:a1, cb * 512:(cb + 1) * 512], Wf.b), wsrc[:, :, cb * 512:(cb + 1) * 512], 8, 512,
                              eng=("pool" if cb % 2 == 0 else "act"))
                if after_weights is not None:
                    after_weights()
                fT = [sbt(ph, "fT%d" % i, [128, 8, 512], BF16) for i in range(2)]
                aT = [sbt(ph, "aT%d" % i, [128, 22, 512], BF16) for i in range(2)]
                sa = [sbt(ph, "sa%d" % i, [128, 512], F32) for i in range(2)]
                pA = [pst(ph, "pA%d" % i) for i in range(3)]
                pGt = [pst(ph, "pGt%d" % i) for i in range(3)]
                sts = [s for s in STS if (not s[3]) or do_ctx]
                S.dma(fT[0].t[:, :, 0:sts[0][2]], fTs[:, :, sts[0][1]:sts[0][1] + sts[0][2]], reads=[db("fTs", sts[0][0])], writes=[fT[0].b], key=fT[0].b.name)
                for k, (si, t0, n, isc) in enumerate(sts):
                    if k + 1 < len(sts):
                        (si2, t02, n2, _) = sts[k + 1]
                        S.dma(fT[(k + 1) % 2].t[:, :, 0:n2], fTs[:, :, t02:t02 + n2], reads=[db("fTs", si2)], writes=[fT[(k + 1) % 2].b], key=fT[(k + 1) % 2].b.name)
                    f_, a_ = fT[k % 2], aT[k % 2]
                    for hc in range(22):
                        pa_, pg_ = pA[hc % 3], pGt[hc % 3]
                        for kc in range(8):
                            S.op("pe", lambda e: e.matmul(pa_.t[:, 0:n], lhsT=Wf.t[:, kc, hc * 128:(hc + 1) * 128], rhs=f_.t[:, kc, 0:n], start=(kc == 0), stop=(kc == 7)),
                                 reads=[Wf.b, f_.b], writes=[pa_.b])
                        for kc in range(8):
                            S.op("pe", lambda e: e.matmul(pg_.t[:, 0:n], lhsT=Wf.t[:, kc, FH + hc * 128:FH + (hc + 1) * 128], rhs=f_.t[:, kc, 0:n], start=(kc == 0), stop=(kc == 7)),
                                 reads=[Wf.b, f_.b], writes=[pg_.b])
                        s_ = sa[hc % 2]
                        S.op("act", lambda e: e.activation(out=s_.t[:, 0:n], in_=pa_.t[:, 0:n], func=AF.Silu), reads=[pa_.b], writes=[s_.b])
                        S.op("dve", lambda e: e.tensor_tensor(out=a_.t[:, hc, 0:n], in0=pg_.t[:, 0:n], in1=s_.t[:, 0:n], op=ALU.mult), reads=[pg_.b, s_.b], writes=[a_.b])
                    S.dma(actTs[:, :, t0:t0 + n], a_.t[:, :, 0:n], reads=[a_.b], writes=[db("actTs", si)], key=a_.b.name)
            S.barrier()

        def phase_F2(l, hsrc, hdst, do_ctx, final, pre=None):
            with contextlib.ExitStack() as ph:
                Wfo = pre if pre is not None else load_F2_weights(ph, l, False)
                aT = [sbt(ph, "aT%d" % i, [128, 22, 512], BF16) for i in range(2)]
                xt = [sbt(ph, "xt%d" % i, [128, 8, 512], F32) for i in range(2)]
                ho = [sbt(ph, "ho%d" % i, [128, 8, 512], F32) for i in range(2)]
                pO = [pst(ph, "pO%d" % i) for i in range(3)]
                sts = [s for s in STS if (not s[3]) or do_ctx]
                src = hsrc.rearrange("(kc p) t -> p kc t", p=128)

                def loads(k):
                    (si, t0, n, isc) = sts[k]
                    S.dma(aT[k % 2].t[:, :, 0:n], actTs[:, :, t0:t0 + n], reads=[db("actTs", si)], writes=[aT[k % 2].b], key=aT[k % 2].b.name)
                    S.dma(xt[k % 2].t[:, :, 0:n], src[:, :, t0:t0 + n], reads=[db("hsrc%d" % id(hsrc), si)], writes=[xt[k % 2].b], key=xt[k % 2].b.name)
                loads(0)
                for k, (si, t0, n, isc) in enumerate(sts):
                    if k + 1 < len(sts):
                        loads(k + 1)
                    s = 1 if isc else 0
                    a_, x_, h_ = aT[k % 2], xt[k % 2], ho[k % 2]
                    for oc in range(8):
                        po = pO[oc % 3]
                        for hc in range(22):
                            S.op("pe", lambda e: e.matmul(po.t[:, 0:n], lhsT=Wfo.t[:, hc, oc * 128:(oc + 1) * 128], rhs=a_.t[:, hc, 0:n], start=(hc == 0), stop=(hc == 21)),
                                 reads=[Wfo.b, a_.b], writes=[po.b])
                        S.op("dve", lambda e: e.scalar_tensor_tensor(out=h_.t[:, oc, 0:n], in0=po.t[:, 0:n], scalar=ada.t[:, 40 + oc, s:s + 1], in1=x_.t[:, oc, 0:n],
                                                                     op0=ALU.mult, op1=ALU.add), reads=[po.b, ada.b, x_.b], writes=[h_.b])
                    if final:
                        dst = outT.rearrange("(kc p) t -> p kc t", p=128)
                        S.dma(dst[:, :, t0 - NC_:t0 - NC_ + n], h_.t[:, :, 0:n], reads=[h_.b], writes=[db("outT", si)], key=h_.b.name)
                    else:
                        dst = hdst.rearrange("(kc p) t -> p kc t", p=128)
                        S.dma(dst[:, :, t0:t0 + n], h_.t[:, :, 0:n], reads=[h_.b], writes=[db("hsrc%d" % id(hdst), si)], key=h_.b.name)
            S.barrier()

        cur = xT
        for l in range(nlayers):
            last = (l == L - 1)
            do_ctx = not last
            if "a" in phases:
                phase_ada(l)
            if "A" in phases:
                phase_A(l, cur, do_ctx)
            if stop_after == "A":
                break
            full = all(x in phases for x in "BCDEF") and stop_after is None and PREFETCH
            if full:
                phase_D(l, do_ctx)
                wE = contextlib.ExitStack()
                preE = load_E_weights(wE, l, True)
                phase_B(l, do_ctx)
                phase_C(l, do_ctx)
                phase_E(l, cur, hres[0], do_ctx, pre=preE)
                wE.close()
                phase_F1(l, do_ctx)
                phase_F2(l, hres[0], hres[1], do_ctx, last)
            else:
                if "B" in phases:
                    phase_B(l, do_ctx)
                if stop_after == "B":
                    break
                if "C" in phases:
                    phase_C(l, do_ctx)
                if stop_after == "C":
                    break
                if "D" in phases:
                    phase_D(l, do_ctx)
                if stop_after == "D":
                    break
                if "E" in phases:
                    phase_E(l, cur, hres[0], do_ctx)
                if stop_after == "E":
                    break
                if "F" in phases:
                    phase_F1(l, do_ctx)
                    phase_F2(l, hres[0], hres[1], do_ctx, last)
            cur = hres[1]
        S.barrier()
        g.stats = (S.nins, S.nwait, len(S.semh))
    return nc, dbg_names, g


def _consts():
    n_axis = 8
    freqs = (10000.0 ** (-np.arange(n_axis, dtype=np.float32) / n_axis)).astype(np.float32)
    rows = NX // 64
    row = np.repeat(np.arange(rows, dtype=np.float32), 64)
    col = np.tile(np.arange(64, dtype=np.float32), rows)
    angM = np.concatenate([row[:, None] * freqs, col[:, None] * freqs], axis=-1).astype(np.float32)
    n_axis_s = 16
    freqs_s = (10000.0 ** (-np.arange(n_axis_s, dtype=np.float32) / n_axis_s)).astype(np.float32)
    angS = np.concatenate([row[:, None] * freqs_s, col[:, None] * freqs_s], axis=-1).astype(np.float32)
    cMc = np.ones((96, NX), np.float32); cMs = np.zeros((96, NX), np.float32)
    cMc[64:80] = np.cos(angM).T; cMc[80:96] = np.cos(angM).T
    cMs[64:80] = np.sin(angM).T; cMs[80:96] = np.sin(angM).T
    cSc = np.zeros((128, NX), np.float32); cSs = np.zeros((128, NX), np.float32)
    for hh in range(2):
        for half in range(2):
            r0 = hh * 64 + half * 32
            cSc[r0:r0 + 32] = np.cos(angS).T
            cSs[r0:r0 + 32] = np.sin(angS).T
    cPm = np.zeros((96, 96), np.float32)
    for i in range(16):
        cPm[80 + i, 64 + i] = -1.0
        cPm[64 + i, 80 + i] = 1.0
    cPs = np.zeros((128, 128), np.float32)
    for hh in range(2):
        for i in range(32):
            cPs[hh * 64 + 32 + i, hh * 64 + i] = -1.0
            cPs[hh * 64 + i, hh * 64 + 32 + i] = 1.0
    cShift = np.zeros((32, 96), np.float32)
    for i in range(32):
        cShift[i, 64 + i] = 1.0
    j = np.arange(128)[:, None]; i = np.arange(128)[None, :]
    cMge = (j >= i).astype(np.float32)
    cMle = (j <= i).astype(np.float32)
    cBones = np.zeros((128, 128), np.float32)
    cBones[0:64, 0:64] = 1.0; cBones[64:128, 64:128] = 1.0
    cIota = np.tile(np.arange(512, dtype=np.float32)[None, :], (128, 1))
    cId = np.eye(128, dtype=np.float32)
    return dict(cMc=cMc, cMs=cMs, cSc=cSc, cSs=cSs, cPm=cPm, cPs=cPs, cShift=cShift, cMge=cMge, cMle=cMle,
                cBones=cBones, cIota=cIota, cId=cId)


def _colv(v, nch):
    return np.ascontiguousarray(np.transpose(v.reshape(v.shape[0], nch, 128), (0, 2, 1)))


def _shared_inputs(inp):
    f = lambda a: np.ascontiguousarray(np.asarray(a, dtype=np.float32))
    sh = {}
    sh["w_ada"] = f(inp["w_ada"]); sh["badac"] = _colv(f(inp["b_ada"]), 48)
    sh["g1c"] = _colv(f(inp["norm1_g"]), 8); sh["g2c"] = _colv(f(inp["norm2_g"]), 8)
    sh["w_in"] = f(inp["w_in"])
    sh["qagc"] = _colv(f(inp["mla_qa_g"]), 3); sh["kvagc"] = _colv(f(inp["mla_kva_g"]), 2)
    sh["w_uq"] = f(inp["mla_w_uq"]); sh["w_ukv"] = f(inp["mla_w_ukv"])
    sh["mqkg"] = np.ascontiguousarray(np.transpose(f(inp["mla_qk_g"]), (0, 2, 1)))
    sw = np.transpose(f(inp["swa_qk_g"]), (0, 2, 1))
    sh["swag"] = np.ascontiguousarray(np.concatenate([sw, sw], axis=1))
    sh["sinkb"] = np.ascontiguousarray(np.broadcast_to(f(inp["swa_sink"])[:, None, :], (L, 128, 8)))

    def pairlay(a):
        return np.ascontiguousarray(np.transpose(a.reshape(L, 2, 16, 2, 64), (0, 3, 4, 1, 2)).reshape(L, 128, 2, 16))
    sh["s5lre"] = pairlay(f(inp["s5_lambda_re"])); sh["s5lim"] = pairlay(f(inp["s5_lambda_im"]))
    ls = np.broadcast_to(f(inp["s5_log_step"])[:, :, :, None], (L, 2, 32, 64))
    sh["s5ls"] = pairlay(np.ascontiguousarray(ls))

    def pairlay_b(a):
        return np.ascontiguousarray(np.transpose(a.reshape(L, 2, 16, 2, 64, 16), (0, 3, 4, 1, 2, 5)).reshape(L, 128, 2, 16, 16))
    sh["s5bre"] = pairlay_b(f(inp["s5_b_re"])); sh["s5bim"] = pairlay_b(f(inp["s5_b_im"]))
    sh["s5cre"] = pairlay_b(np.ascontiguousarray(np.transpose(f(inp["s5_c_re"]), (0, 1, 2, 4, 3))))
    sh["s5cim"] = pairlay_b(np.ascontiguousarray(np.transpose(f(inp["s5_c_im"]), (0, 1, 2, 4, 3))))
    sh["s5dc"] = _colv(f(inp["s5_d"]), 4); sh["s5bgc"] = _colv(f(inp["s5_b_glu"]), 4)
    sh["w_glu"] = f(inp["s5_w_glu"]); sh["w_br"] = f(inp["w_branch"]); sh["w_o"] = f(inp["w_out"])
    sh["f_in"] = f(inp["ffn_w_in"]); sh["f_out"] = f(inp["ffn_w_out"])
    sh.update(_consts())
    return sh


def _core_inputs(inp, b, sh):
    x = np.asarray(inp["x"][b], np.float32); ctx = np.asarray(inp["ctx"][b], np.float32)
    m = dict(sh)
    m["xT"] = np.ascontiguousarray(np.concatenate([ctx, x], axis=0).T)
    cc = np.stack([np.asarray(inp["c"][b], np.float32), np.asarray(inp["c_ctx"], np.float32)], axis=-1)
    m["ccol"] = np.ascontiguousarray(np.transpose(cc.reshape(8, 128, 2), (1, 0, 2)))
    return m


_CACHE = {}


def kernel(**inputs):
    if "nc" not in _CACHE:
        _CACHE["nc"] = build()[0]
    nc = _CACHE["nc"]
    sh = _shared_inputs(inputs)
    in_maps = [_core_inputs(inputs, b, sh) for b in range(8)]
    res = run_bass_kernel_spmd(nc, in_maps, core_ids=list(range(8)))
    out = np.stack([np.ascontiguousarray(r["outT"].T) for r in res.results], axis=0)
    return out.astype(np.float32)
```

```python
import contextlib
import math
import numpy as np
import concourse.bass as bass
import concourse.mybir as mybir
from concourse.bass_utils import run_bass_kernel_spmd

F32 = mybir.dt.float32
BF16 = mybir.dt.bfloat16
I32 = mybir.dt.int32
AF = mybir.ActivationFunctionType
ALU = mybir.AluOpType

D = 1024
NX = 4096
NC_ = 256
T = NX + NC_
L = 2
EPS = 1e-6
FH = 2816
DIN = 5024
O_KR = 640
O_U = 672
O_SQ = 1184
O_SK = 1696
O_SV = 1824
O_G = 1952
import os as _os
D_UENG = _os.environ.get("UENG", "dve,dve,dve,dve").split(",")
D_XENG = _os.environ.get("XENG", "pool")
D_TENG = _os.environ.get("TENG", "dve,dve,dve,dve").split(",")
D_EVAC = _os.environ.get("D_EVAC", "0") == "1"
PREFETCH = _os.environ.get("PREFETCH", "1") == "1"
TWO_PI = 2.0 * math.pi

STS = [(0, 0, 256, True)] + [(1 + k, 256 + 512 * k, 512, False) for k in range(8)]


class Buf:
    __slots__ = ("name", "w", "r")

    def __init__(self, name):
        self.name = name
        self.w = {}
        self.r = {}


def _merge(d, s):
    for k, v in s.items():
        if d.get(k, 0) < v:
            d[k] = v


class Sched:
    def __init__(self, nc, stack):
        self.nc = nc
        self.stack = stack
        self.E = {"pe": nc.tensor, "act": nc.scalar, "dve": nc.vector,
                  "pool": nc.gpsimd, "sp": nc.sync}
        self.semh = {}
        self.cnt = {}
        self.seen = {k: {} for k in self.E}
        for k in self.E:
            self.semh[k] = stack.enter_context(nc.semaphore("s_" + k))
            self.cnt[k] = 0
        self.nins = 0
        self.nwait = 0
        self.alias = {}
        self.free = []
        self.dkeys = []

    def _sem(self, key):
        if key in self.alias:
            return self.alias[key]
        if self.free:
            ck = self.free.pop()
        else:
            ck = "dq%d" % len(self.dkeys)
            self.dkeys.append(ck)
            self.semh[ck] = self.stack.enter_context(self.nc.semaphore(ck))
            self.cnt[ck] = 0
        self.alias[key] = ck
        return ck

    def _wait(self, eng, deps):
        seen = self.seen[eng]
        for key, val in deps.items():
            if val <= 0 or seen.get(key, 0) >= val:
                continue
            self.E[eng].wait_ge(self.semh[key], val)
            seen[key] = val
            self.nwait += 1

    def _deps(self, reads, writes):
        deps = {}
        for b in reads:
            _merge(deps, b.w)
        for b in writes:
            _merge(deps, b.w)
            _merge(deps, b.r)
        return deps

    def op(self, eng, fn, reads=(), writes=()):
        deps = self._deps(reads, writes)
        if eng == "pe":
            deps.pop("pe", None)
        self._wait(eng, deps)
        ins = fn(self.E[eng])
        self.cnt[eng] += 1
        v = self.cnt[eng]
        ins.then_inc(self.semh[eng], 1)
        for b in reads:
            if b.r.get(eng, 0) < v:
                b.r[eng] = v
        for b in writes:
            b.w = {eng: v}
            b.r = {}
        self.nins += 1
        return ins

    def dma(self, out, in_, reads=(), writes=(), key=None, eng="sp"):
        key = self._sem(key)
        deps = self._deps(reads, writes)
        deps.pop(key, None)
        self._wait(eng, deps)
        ins = self.E[eng].dma_start(out=out, in_=in_)
        self.cnt[key] += 16
        v = self.cnt[key]
        ins.then_inc(self.semh[key], 16)
        for b in reads:
            if b.r.get(key, 0) < v:
                b.r[key] = v
        for b in writes:
            b.w = {key: v}
            b.r = {}
        self.nins += 1
        return ins

    def barrier(self):
        deps = {k: v for k, v in self.cnt.items() if v > 0}
        for e in self.E:
            self._wait(e, deps)
        self.alias = {}
        self.free = list(self.dkeys)


class TT:
    __slots__ = ("t", "b")

    def __init__(self, t, b):
        self.t = t
        self.b = b


class Ctx:
    pass


def build(dbg=False, nlayers=L, stop_after=None, phases="aABCDEF"):
    nc = bass.Bass("TRN2", target_bir_lowering=False)
    g = Ctx()
    g.nc = nc

    def din(name, shape, dt=F32):
        return nc.dram_tensor(name, list(shape), dt, kind="ExternalInput").ap()

    dbg_names = []

    def dscr(name, shape, dt):
        if dbg:
            dbg_names.append(name)
            return nc.dram_tensor(name, list(shape), dt, kind="ExternalOutput").ap()
        return nc.dram_tensor(name, list(shape), dt).ap()

    xT = din("xT", [D, T])
    ccol = din("ccol", [128, 8, 2])
    w_ada = din("w_ada", [L, D, 6 * D])
    badac = din("badac", [L, 128, 48])
    g1c = din("g1c", [L, 128, 8])
    g2c = din("g2c", [L, 128, 8])
    w_in = din("w_in", [L, D, DIN])
    qagc = din("qagc", [L, 128, 3])
    kvagc = din("kvagc", [L, 128, 2])
    w_uq = din("w_uq", [L, 384, 768])
    w_ukv = din("w_ukv", [L, 256, 1024])
    mqkg = din("mqkg", [L, 96, 2])
    swag = din("swag", [L, 128, 2])
    sinkb = din("sinkb", [L, 128, 8])
    s5lre = din("s5lre", [L, 128, 2, 16])
    s5lim = din("s5lim", [L, 128, 2, 16])
    s5ls = din("s5ls", [L, 128, 2, 16])
    s5bre = din("s5bre", [L, 128, 2, 16, 16])
    s5bim = din("s5bim", [L, 128, 2, 16, 16])
    s5cre = din("s5cre", [L, 128, 2, 16, 16])
    s5cim = din("s5cim", [L, 128, 2, 16, 16])
    s5dc = din("s5dc", [L, 128, 4])
    s5bgc = din("s5bgc", [L, 128, 4])
    w_glu = din("w_glu", [L, 512, 512])
    w_br = din("w_br", [L, 3, 512, D])
    w_o = din("w_o", [L, D, D])
    f_in = din("f_in", [L, D, 2 * FH])
    f_out = din("f_out", [L, FH, D])
    cMc = din("cMc", [96, NX]); cMs = din("cMs", [96, NX])
    cSc = din("cSc", [128, NX]); cSs = din("cSs", [128, NX])
    cPm = din("cPm", [96, 96]); cPs = din("cPs", [128, 128])
    cShift = din("cShift", [32, 96])
    cMge = din("cMge", [128, 128]); cMle = din("cMle", [128, 128])
    cBones = din("cBones", [128, 128])
    cIota = din("cIota", [128, 512])
    cId = din("cId", [128, 128])

    outT = nc.dram_tensor("outT", [D, NX], F32, kind="ExternalOutput").ap()

    hres = [dscr("hres%d" % i, [D, T], F32) for i in range(2)]
    hTs = dscr("hTs", [128, 8, T], BF16)
    QTs = dscr("QTs", [8, 96, T], BF16)
    KTs = dscr("KTs", [8, 96, T], BF16)
    VsM = dscr("VsM", [34, 128, 8, 128], BF16)
    uTs = dscr("uTs", [128, 4, T], BF16)
    sQs = dscr("sQs", [64, 8, T], BF16)
    sKs = dscr("sKs", [64, 2, T], BF16)
    sVs = dscr("sVs", [34, 128, 2, 128], BF16)
    mlaO = dscr("mlaO", [512, T], BF16)
    swaO = dscr("swaO", [512, T], BF16)
    s5O = dscr("s5O", [128, 4, T], BF16)
    ygs = dscr("ygs", [128, 4, T], BF16)
    fTs = dscr("fTs", [128, 8, T], BF16)
    actTs = dscr("actTs", [128, 22, T], BF16)

    dbufs = {}

    def db(name, idx=0):
        k = (name, idx)
        if k not in dbufs:
            dbufs[k] = Buf("%s_%s" % (name, idx))
        return dbufs[k]

    with contextlib.ExitStack() as top:
        S = Sched(nc, top)
        uid = [0]

        def sbt(ctx, name, shape, dt):
            uid[0] += 1
            nm = "%s_%d" % (name, uid[0])
            t = ctx.enter_context(nc.sbuf_tensor(nm, list(shape), dt))
            return TT(t, Buf(nm))

        def pst(ctx, name, shape=(128, 512), dt=F32):
            uid[0] += 1
            nm = "%s_%d" % (name, uid[0])
            t = ctx.enter_context(nc.psum_tensor(nm, list(shape), dt))
            return TT(t, Buf(nm))

        ones32 = sbt(top, "ones32", [128, 128], F32)
        S.op("pool", lambda e: e.memset(ones32.t[:], 1.0), writes=[ones32.b])
        bones32 = sbt(top, "bones32", [128, 128], F32)
        S.dma(bones32.t[:], cBones[:, :], writes=[bones32.b], key="c_bones")
        stgc = sbt(top, "stgc", [128, 128], F32)

        def const_bf(name, src, rows, cols):
            t = sbt(top, name, [rows, cols], BF16)
            S.dma(stgc.t[0:rows, 0:cols], src[:, :], writes=[stgc.b], key="c_stg")
            S.op("dve", lambda e: e.tensor_copy(out=t.t[:], in_=stgc.t[0:rows, 0:cols]),
                 reads=[stgc.b], writes=[t.b])
            return t

        Pm = const_bf("Pm", cPm, 96, 96)
        Ps = const_bf("Ps", cPs, 128, 128)
        shiftI = const_bf("shiftI", cShift, 32, 96)
        Mge = const_bf("Mge", cMge, 128, 128)
        Mle = const_bf("Mle", cMle, 128, 128)
        identb = const_bf("identb", cId, 128, 128)

        stg = [sbt(top, "stg%d" % i, [128, 2048], F32) for i in range(2)]
        stg_i = [0]

        def load_cast(dst_ap_fn, src3, A, B, scale_fn=None, eng="pool", dma_eng="sp"):
            step = 1 if scale_fn is not None else max(1, 2048 // B)
            a0 = 0
            while a0 < A:
                a1 = min(A, a0 + step)
                s_ = stg[stg_i[0] % 2]
                stg_i[0] += 1
                na = a1 - a0
                view = s_.t[:, 0:na * B].rearrange("p (a b) -> p a b", b=B)
                S.dma(view, src3[:, a0:a1, :], writes=[s_.b], key=s_.b.name, eng=dma_eng)
                dst, dbuf = dst_ap_fn(a0, a1)
                if scale_fn is None:
                    if eng == "act":
                        S.op(eng, lambda e: e.copy(out=dst, in_=view), reads=[s_.b], writes=[dbuf])
                    else:
                        S.op(eng, lambda e: e.tensor_copy(out=dst, in_=view), reads=[s_.b], writes=[dbuf])
                else:
                    assert na == 1
                    sc, scb = scale_fn(a0)
                    S.op(eng, lambda e: e.tensor_scalar(out=dst, in0=view, scalar1=sc, scalar2=None, op0=ALU.mult),
                         reads=[s_.b, scb], writes=[dbuf])
                a0 = a1

        def rsqrt_from(ctx_eng_out, out_t, in_ap, in_buf, scale, shape_ap=None):
            o = out_t.t[:] if shape_ap is None else shape_ap
            S.op("act", lambda e: e.activation(out=o, in_=in_ap, func=AF.Ln, bias=epsc.t[0:o.shape[0], 0:1], scale=scale),
                 reads=[in_buf, epsc.b], writes=[out_t.b])
            S.op("act", lambda e: e.activation(out=o, in_=o, func=AF.Exp, scale=-0.5), reads=[out_t.b], writes=[out_t.b])

        epsc = sbt(top, "epsc", [128, 1], F32)
        S.op("pool", lambda e: e.memset(epsc.t[:], EPS), writes=[epsc.b])

        modA1 = sbt(top, "modA1", [128, 8, 2], F32)
        modA2 = sbt(top, "modA2", [128, 8, 2], F32)
        ada = sbt(top, "ada", [128, 48, 2], F32)

        def phase_ada(l):
            with contextlib.ExitStack() as ph:
                cc = sbt(ph, "cc", [128, 8, 2], F32)
                S.dma(cc.t[:], ccol[:, :, :], writes=[cc.b], key="cc")
                sc = sbt(ph, "sc", [128, 8, 2], F32)
                S.op("act", lambda e: e.activation(out=sc.t[:], in_=cc.t[:], func=AF.Silu), reads=[cc.b], writes=[sc.b])
                bad = sbt(ph, "bad", [128, 48], F32)
                S.dma(bad.t[:], badac[l], writes=[bad.b], key="bad")
                gg = sbt(ph, "gg", [128, 16], F32)
                S.dma(gg.t[:, 0:8], g1c[l], writes=[gg.b], key="gg")
                S.dma(gg.t[:, 8:16], g2c[l], writes=[gg.b], key="gg")
                wa = [sbt(ph, "wa%d" % i, [128, 8, 512], F32) for i in range(2)]
                pa = pst(ph, "pa", [128, 96], F32)
                wsrc = w_ada[l].rearrange("(kc p) n -> p kc n", p=128)
                for cb in range(12):
                    w = wa[cb % 2]
                    S.dma(w.t[:], wsrc[:, :, cb * 512:(cb + 1) * 512], writes=[w.b], key=w.b.name)
                    for f4 in range(4):
                        fc = cb * 4 + f4
                        for kc in range(8):
                            S.op("pe", lambda e: e.matmul(pa.t[:, fc * 2:fc * 2 + 2], lhsT=w.t[:, kc, f4 * 128:(f4 + 1) * 128],
                                                          rhs=sc.t[:, kc, :], start=(kc == 0), stop=(kc == 7)),
                                 reads=[w.b, sc.b], writes=[pa.b])
                S.op("dve", lambda e: e.tensor_tensor(out=ada.t[:], in0=pa.t[:].rearrange("p (c s) -> p c s", s=2),
                                                      in1=bad.t[:].unsqueeze(2).to_broadcast([128, 48, 2]), op=ALU.add),
                     reads=[pa.b, bad.b], writes=[ada.b])
                for (mod, sc0, gofs) in ((modA1, 8, 0), (modA2, 32, 8)):
                    S.op("dve", lambda e: e.tensor_scalar(out=mod.t[:], in0=ada.t[:, sc0:sc0 + 8, :], scalar1=1.0, scalar2=None, op0=ALU.add),
                         reads=[ada.b], writes=[mod.b])
                    S.op("dve", lambda e: e.tensor_tensor(out=mod.t[:], in0=mod.t[:],
                                                          in1=gg.t[:, gofs:gofs + 8].unsqueeze(2).to_broadcast([128, 8, 2]), op=ALU.mult),
                         reads=[mod.b, gg.b], writes=[mod.b])
            S.barrier()

        def norm_mod(ph, xt, n, modA, shofs, si_ctx, sq, pss, rs, hT):
            s = 1 if si_ctx else 0
            S.op("act", lambda e: e.activation(out=sq.t[:, :, 0:n], in_=xt.t[:, :, 0:n], func=AF.Square), reads=[xt.b], writes=[sq.b])
            for kc in range(8):
                S.op("pe", lambda e: e.matmul(pss.t[:, 0:n], lhsT=ones32.t[:], rhs=sq.t[:, kc, 0:n], start=(kc == 0), stop=(kc == 7)),
                     reads=[ones32.b, sq.b], writes=[pss.b])
            rsqrt_from(None, rs, pss.t[:, 0:n], pss.b, 1.0 / D, shape_ap=rs.t[:, 0:n])
            S.op("dve", lambda e: e.tensor_tensor(out=sq.t[:, :, 0:n], in0=xt.t[:, :, 0:n],
                                                  in1=rs.t[:, 0:n].unsqueeze(1).to_broadcast([128, 8, n]), op=ALU.mult),
                 reads=[xt.b, rs.b], writes=[sq.b])
            for kc in range(8):
                S.op("act", lambda e: e.activation(out=hT.t[:, kc, 0:n], in_=sq.t[:, kc, 0:n], func=AF.Identity,
                                                   bias=ada.t[:, shofs + kc, s:s + 1], scale=modA.t[:, kc, s:s + 1]),
                     reads=[sq.b, ada.b, modA.b], writes=[hT.b])

        def phase_A(l, hsrc, do_ctx_q):
            with contextlib.ExitStack() as ph:
                Win = sbt(ph, "Win", [128, 8, O_G], BF16)
                wsrc = w_in[l].rearrange("(kc p) n -> p kc n", p=128)
                load_cast(lambda a0, a1: (Win.t[:, a0:a1, :], Win.b), wsrc[:, :, 0:O_G], 8, O_G)
                qag = sbt(ph, "qag", [128, 5], F32)
                S.dma(qag.t[:, 0:3], qagc[l], writes=[qag.b], key="qag")
                S.dma(qag.t[:, 3:5], kvagc[l], writes=[qag.b], key="qag")
                Wuq = sbt(ph, "Wuq", [128, 3, 768], BF16)
                load_cast(lambda a0, a1: (Wuq.t[:, a0:a1, :], Wuq.b), w_uq[l].rearrange("(kc p) n -> p kc n", p=128), 3, 768,
                          scale_fn=lambda a: (qag.t[:, a:a + 1], qag.b))
                Wkp = sbt(ph, "Wkp", [128, 2, 8, 96], BF16)
                S.op("pool", lambda e: e.memset(Wkp.t[:], 0.0), writes=[Wkp.b])
                Wv = sbt(ph, "Wv", [128, 2, 8, 64], BF16)
                ukv = w_ukv[l].rearrange("(kc p) n -> p kc n", p=128)
                for kc in range(2):
                    s_ = stg[stg_i[0] % 2]
                    stg_i[0] += 1
                    S.dma(s_.t[:, 0:1024], ukv[:, kc, :], writes=[s_.b], key=s_.b.name)
                    v3 = s_.t[:, 0:1024].rearrange("p (h c) -> p h c", c=128)
                    S.op("pool", lambda e: e.tensor_scalar(out=Wkp.t[:, kc, :, 0:64], in0=v3[:, :, 0:64], scalar1=qag.t[:, 3 + kc:4 + kc],
                                                           scalar2=None, op0=ALU.mult), reads=[s_.b, qag.b], writes=[Wkp.b])
                    S.op("pool", lambda e: e.tensor_scalar(out=Wv.t[:, kc, :, :], in0=v3[:, :, 64:128], scalar1=qag.t[:, 3 + kc:4 + kc],
                                                           scalar2=None, op0=ALU.mult), reads=[s_.b, qag.b], writes=[Wv.b])
                Wkd = sbt(ph, "Wkd", [128, 8, 2, 128], BF16)
                for kh in range(2):
                    for hf in range(2):
                        S.op("pool", lambda e: e.tensor_copy(out=Wkd.t[:, :, kh, hf * 64:(hf + 1) * 64],
                                                             in_=Win.t[:, :, O_SK + kh * 64:O_SK + (kh + 1) * 64]),
                             reads=[Win.b], writes=[Wkd.b])
                gq = sbt(ph, "gq", [128, 4], F32)
                S.dma(gq.t[0:96, 0:2], mqkg[l], writes=[gq.b], key="gq")
                S.dma(gq.t[:, 2:4], swag[l], writes=[gq.b], key="gq")

                xt = sbt(ph, "xt", [128, 8, 512], F32)
                sq = sbt(ph, "sq", [128, 8, 512], F32)
                rs = sbt(ph, "rs", [128, 512], F32)
                hT = sbt(ph, "hT", [128, 8, 512], BF16)
                q32 = sbt(ph, "q32", [128, 3, 512], F32)
                sqq = sbt(ph, "sqq", [128, 3, 512], F32)
                rq = sbt(ph, "rq", [128, 512], F32)
                qn = sbt(ph, "qn", [128, 3, 512], BF16)
                kvn = sbt(ph, "kvn", [128, 2, 512], BF16)
                krT = sbt(ph, "krT", [32, 512], BF16)
                uT = sbt(ph, "uT", [128, 4, 512], BF16)
                NH = 4
                hq32 = [sbt(ph, "hq32_%d" % i, [128, 512], F32) for i in range(NH)]
                hsq = [sbt(ph, "hsq_%d" % i, [128, 512], F32) for i in range(NH)]
                hqn = [sbt(ph, "hqn_%d" % i, [128, 512], F32) for i in range(NH)]
                hqb = [sbt(ph, "hqb_%d" % i, [128, 512], BF16) for i in range(NH)]
                hout = [sbt(ph, "hout_%d" % i, [128, 512], BF16) for i in range(NH)]
                va = [sbt(ph, "va_%d" % i, [128, 8, 128], BF16) for i in range(2)]
                sva = [sbt(ph, "sva_%d" % i, [128, 2, 128], BF16) for i in range(2)]
                for v_ in va + sva:
                    S.op("pool", lambda e: e.memset(v_.t[:], 1.0), writes=[v_.b])
                rope = sbt(ph, "rope", [128, 4, 512], F32)
                pss = pst(ph, "pss")
                pp = [pst(ph, "pp%d" % i) for i in range(2)]
                hp = [pst(ph, "hp%d" % i) for i in range(NH)]
                pv = pst(ph, "pv")
                cnt = [0]

                def headnorm(i, mm_fn, rows, gcol, onesT, dim, use_rope, Pmat, rc, rs_, n, dst_ap, dst_bufs):
                    ps, a32, asq, aqn, aqb, ao = hp[i], hq32[i], hsq[i], hqn[i], hqb[i], hout[i]
                    mm_fn(ps)
                    yield
                    S.op("act", lambda e: e.copy(out=a32.t[0:rows, 0:n], in_=ps.t[0:rows, 0:n]), reads=[ps.b], writes=[a32.b])
                    S.op("act", lambda e: e.activation(out=asq.t[0:rows, 0:n], in_=ps.t[0:rows, 0:n], func=AF.Square), reads=[ps.b], writes=[asq.b])
                    yield
                    S.op("pe", lambda e: e.matmul(ps.t[0:rows, 0:n], lhsT=onesT.t[0:rows, 0:rows], rhs=asq.t[0:rows, 0:n], start=True, stop=True),
                         reads=[onesT.b, asq.b], writes=[ps.b])
                    yield
                    rsqrt_from(None, asq, ps.t[0:rows, 0:n], ps.b, 1.0 / dim, shape_ap=asq.t[0:rows, 0:n])
                    yield
                    if not use_rope:
                        S.op("dve", lambda e: e.scalar_tensor_tensor(out=ao.t[0:rows, 0:n], in0=a32.t[0:rows, 0:n], scalar=gcol,
                                                                     in1=asq.t[0:rows, 0:n], op0=ALU.mult, op1=ALU.mult),
                             reads=[a32.b, asq.b, gq.b], writes=[ao.b])
                    else:
                        S.op("dve", lambda e: e.scalar_tensor_tensor(out=aqn.t[0:rows, 0:n], in0=a32.t[0:rows, 0:n], scalar=gcol,
                                                                     in1=asq.t[0:rows, 0:n], op0=ALU.mult, op1=ALU.mult),
                             reads=[a32.b, asq.b, gq.b], writes=[aqn.b])
                        yield
                        S.op("act", lambda e: e.copy(out=aqb.t[0:rows, 0:n], in_=aqn.t[0:rows, 0:n]), reads=[aqn.b], writes=[aqb.b])
                        yield
                        S.op("pe", lambda e: e.matmul(ps.t[0:rows, 0:n], lhsT=Pmat.t[0:rows, 0:rows], rhs=aqb.t[0:rows, 0:n], start=True, stop=True),
                             reads=[Pmat.b, aqb.b], writes=[ps.b])
                        S.op("pool", lambda e: e.tensor_tensor(out=a32.t[0:rows, 0:n], in0=aqn.t[0:rows, 0:n], in1=rope.t[0:rows, rc, 0:n], op=ALU.mult),
                             reads=[aqn.b, rope.b], writes=[a32.b])
                        yield
                        S.op("dve", lambda e: e.tensor_tensor(out=asq.t[0:rows, 0:n], in0=ps.t[0:rows, 0:n], in1=rope.t[0:rows, rs_, 0:n], op=ALU.mult),
                             reads=[ps.b, rope.b], writes=[asq.b])
                        yield
                        S.op("dve", lambda e: e.tensor_tensor(out=ao.t[0:rows, 0:n], in0=a32.t[0:rows, 0:n], in1=asq.t[0:rows, 0:n], op=ALU.add),
                             reads=[a32.b, asq.b], writes=[ao.b])
                    yield
                    if isinstance(dst_ap, list):
                        for (d_ap, r0, r1) in dst_ap:
                            S.dma(d_ap, ao.t[r0:r1, 0:n], reads=[ao.b], writes=dst_bufs, key=ao.b.name)
                    else:
                        S.dma(dst_ap, ao.t[0:rows, 0:n], reads=[ao.b], writes=dst_bufs, key=ao.b.name)

                def run_chains(jobs):
                    live = []
                    free_slots = list(range(NH))
                    nj = 0
                    while live or nj < len(jobs):
                        while len(live) < NH and nj < len(jobs):
                            sl = free_slots.pop(0)
                            live.append((jobs[nj](sl), sl))
                            nj += 1
                        for (g_, sl) in list(live):
                            try:
                                next(g_)
                            except StopIteration:
                                live.remove((g_, sl))
                                free_slots.append(sl)

                for (si, t0, n, isc) in STS:
                    s = 1 if isc else 0
                    src = hsrc.rearrange("(kc p) t -> p kc t", p=128)
                    S.dma(xt.t[:, :, 0:n], src[:, :, t0:t0 + n], reads=[db("hsrc%d" % id(hsrc), si)], writes=[xt.b], key="xt")
                    if not isc:
                        x0 = t0 - NC_
                        S.dma(rope.t[0:96, 0, 0:n], cMc[:, x0:x0 + n], writes=[rope.b], key="rope")
                        S.dma(rope.t[0:96, 1, 0:n], cMs[:, x0:x0 + n], writes=[rope.b], key="rope")
                        S.dma(rope.t[:, 2, 0:n], cSc[:, x0:x0 + n], writes=[rope.b], key="rope")
                        S.dma(rope.t[:, 3, 0:n], cSs[:, x0:x0 + n], writes=[rope.b], key="rope")
                    norm_mod(ph, xt, n, modA1, 0, isc, sq, pss, rs, hT)
                    S.dma(hTs[:, :, t0:t0 + n], hT.t[:, :, 0:n], reads=[hT.b], writes=[db("hTs", si)], key="hT_st")

                    def proj(ps, c0, ncols=128):
                        for kc in range(8):
                            S.op("pe", lambda e: e.matmul(ps.t[0:ncols, 0:n], lhsT=Win.t[:, kc, c0:c0 + ncols], rhs=hT.t[:, kc, 0:n],
                                                          start=(kc == 0), stop=(kc == 7)), reads=[Win.b, hT.b], writes=[ps.b])
                    pi = [0]

                    def nextp():
                        pi[0] += 1
                        return pp[pi[0] % 2]

                    for (nch, c0, dst, dim) in ((3, 0, qn, 384), (2, 384, kvn, 256)):
                        for c in range(nch):
                            ps = nextp()
                            proj(ps, c0 + c * 128)
                            S.op("act", lambda e: e.copy(out=q32.t[:, c, 0:n], in_=ps.t[:, 0:n]), reads=[ps.b], writes=[q32.b])
                        S.op("pool", lambda e: e.tensor_tensor(out=sqq.t[:, 0:nch, 0:n], in0=q32.t[:, 0:nch, 0:n], in1=q32.t[:, 0:nch, 0:n], op=ALU.mult),
                             reads=[q32.b], writes=[sqq.b])
                        for c in range(nch):
                            S.op("pe", lambda e: e.matmul(pss.t[:, 0:n], lhsT=ones32.t[:], rhs=sqq.t[:, c, 0:n], start=(c == 0), stop=(c == nch - 1)),
                                 reads=[ones32.b, sqq.b], writes=[pss.b])
                        rsqrt_from(None, rq, pss.t[:, 0:n], pss.b, 1.0 / dim, shape_ap=rq.t[:, 0:n])
                        S.op("dve", lambda e: e.tensor_tensor(out=dst.t[:, 0:nch, 0:n], in0=q32.t[:, 0:nch, 0:n],
                                                              in1=rq.t[:, 0:n].unsqueeze(1).to_broadcast([128, nch, n]), op=ALU.mult),
                             reads=[q32.b, rq.b], writes=[dst.b])
                    ps = nextp()
                    proj(ps, O_KR, 32)
                    S.op("act", lambda e: e.copy(out=krT.t[:, 0:n], in_=ps.t[0:32, 0:n]), reads=[ps.b], writes=[krT.b])
                    jobs = []
                    for h in range(8):
                        if (not isc) or do_ctx_q:
                            def mmq(ps, h=h):
                                for c in range(3):
                                    S.op("pe", lambda e: e.matmul(ps.t[0:96, 0:n], lhsT=Wuq.t[:, c, h * 96:(h + 1) * 96], rhs=qn.t[:, c, 0:n],
                                                                  start=(c == 0), stop=(c == 2)), reads=[Wuq.b, qn.b], writes=[ps.b])
                            jobs.append(lambda i, h=h, mmq=mmq: headnorm(i, mmq, 96, gq.t[0:96, 0:1], ones32, 96.0, not isc, Pm, 0, 1, n,
                                                                           QTs[h, :, t0:t0 + n], [db("QTs", (h, si))]))

                        def mmk(ps, h=h):
                            for c in range(2):
                                S.op("pe", lambda e: e.matmul(ps.t[0:96, 0:n], lhsT=Wkp.t[:, c, h, :], rhs=kvn.t[:, c, 0:n],
                                                              start=(c == 0), stop=False), reads=[Wkp.b, kvn.b], writes=[ps.b])
                            S.op("pe", lambda e: e.matmul(ps.t[0:96, 0:n], lhsT=shiftI.t[:, :], rhs=krT.t[:, 0:n], start=False, stop=True),
                                 reads=[shiftI.b, krT.b], writes=[ps.b])
                        jobs.append(lambda i, h=h, mmk=mmk: headnorm(i, mmk, 96, gq.t[0:96, 1:2], ones32, 96.0, not isc, Pm, 0, 1, n,
                                                                       KTs[h, :, t0:t0 + n], [db("KTs", (h, si))]))
                    for c in range(4):
                        def mmsq(ps, c=c):
                            proj(ps, O_SQ + c * 128)
                        jobs.append(lambda i, c=c, mmsq=mmsq: headnorm(i, mmsq, 128, gq.t[:, 2:3], bones32, 64.0, not isc, Ps, 2, 3, n,
                                                                         [(sQs[:, 2 * c, t0:t0 + n], 0, 64), (sQs[:, 2 * c + 1, t0:t0 + n], 64, 128)],
                                                                         [db("sQs", (c, si))]))
                    for kh in range(2):
                        def mmsk(ps, kh=kh):
                            for kc in range(8):
                                S.op("pe", lambda e: e.matmul(ps.t[:, 0:n], lhsT=Wkd.t[:, kc, kh, :], rhs=hT.t[:, kc, 0:n],
                                                              start=(kc == 0), stop=(kc == 7)), reads=[Wkd.b, hT.b], writes=[ps.b])
                        jobs.append(lambda i, kh=kh, mmsk=mmsk: headnorm(i, mmsk, 128, gq.t[:, 3:4], bones32, 64.0, not isc, Ps, 2, 3, n,
                                                                           [(sKs[:, kh, t0:t0 + n], 0, 64)], [db("sKs", (kh, si))]))
                    run_chains(jobs)
                    for j in range(n // 128):
                        tile_i = t0 // 128 + j
                        v_ = va[tile_i % 2]
                        for c in range(2):
                            S.op("pe", lambda e: e.matmul(pv.t[:, 0:512], lhsT=kvn.t[:, c, j * 128:(j + 1) * 128],
                                                          rhs=Wv.t[:, c, :, :].rearrange("p h c -> p (h c)"),
                                                          start=(c == 0), stop=(c == 1)), reads=[kvn.b, Wv.b], writes=[pv.b])
                        S.op("act", lambda e: e.copy(out=v_.t[:, :, 0:64], in_=pv.t[:, 0:512].rearrange("p (h c) -> p h c", c=64)),
                             reads=[pv.b], writes=[v_.b])
                        S.dma(VsM[tile_i], v_.t[:], reads=[v_.b], writes=[db("VsM", tile_i)], key=v_.b.name)
                    for c in range(4):
                        ps = nextp()
                        proj(ps, O_U + c * 128)
                        S.op("act", lambda e: e.copy(out=uT.t[:, c, 0:n], in_=ps.t[:, 0:n]), reads=[ps.b], writes=[uT.b])
                    S.dma(uTs[:, :, t0:t0 + n], uT.t[:, :, 0:n], reads=[uT.b], writes=[db("uTs", si)], key="uT_st")
                    for j in range(n // 128):
                        tile_i = t0 // 128 + j
                        v_ = sva[tile_i % 2]
                        for kc in range(8):
                            S.op("pe", lambda e: e.matmul(pv.t[:, 0:128], lhsT=hT.t[:, kc, j * 128:(j + 1) * 128], rhs=Win.t[:, kc, O_SV:O_SV + 128],
                                                          start=(kc == 0), stop=(kc == 7)), reads=[hT.b, Win.b], writes=[pv.b])
                        S.op("act", lambda e: e.copy(out=v_.t[:, :, 0:64], in_=pv.t[:, 0:128].rearrange("p (h c) -> p h c", c=64)),
                             reads=[pv.b], writes=[v_.b])
                        S.dma(sVs[tile_i], v_.t[:], reads=[v_.b], writes=[db("sVs", tile_i)], key=v_.b.name)
            S.barrier()

        def attn_core(ph, nkeys_tiles, score_fn, pv_lhsT_fn, pv_reads, n, pS, pO, Pt, scale, mask_fn=None):
            nk = len(nkeys_tiles)
            groups = [list(range(i, min(i + 2, nk))) for i in range(0, nk, 2)]
            ng = len(groups)

            def do_s(gi):
                ps_ = pS[gi % 3]
                for j, i in enumerate(groups[gi]):
                    score_fn(nkeys_tiles[i], ps_, j)

            def do_e(gi):
                ps_, p_ = pS[gi % 3], Pt[gi % 3]
                w = len(groups[gi])
                pin = ps_.t[:, :].rearrange("p (j q) -> p j q", q=512)[:, 0:w, 0:n]
                pout = p_.t[:, :].rearrange("p (j q) -> p j q", q=512)[:, 0:w, 0:n]
                S.op("act", lambda e: e.activation(out=pout, in_=pin, func=AF.Exp, scale=scale), reads=[ps_.b], writes=[p_.b])
                if mask_fn is not None:
                    for j, i in enumerate(groups[gi]):
                        mask_fn(nkeys_tiles[i], p_, j)

            def do_pv(gi):
                p_ = Pt[gi % 3]
                for j, i in enumerate(groups[gi]):
                    S.op("pe", lambda e: e.matmul(pO.t[:, 0:n], lhsT=pv_lhsT_fn(nkeys_tiles[i]), rhs=p_.t[:, j * 512:j * 512 + n], start=(i == 0), stop=(i == nk - 1)),
                         reads=[p_.b] + pv_reads, writes=[pO.b])

            do_s(0)
            if ng > 1:
                do_s(1)
            for gi in range(ng):
                do_e(gi)
                if gi + 2 < ng:
                    do_s(gi + 2)
                do_pv(gi)

        def phase_B(l, do_ctx):
            with contextlib.ExitStack() as ph:
                KT = [sbt(ph, "KT%d" % i, [96, T], BF16) for i in range(2)]
                QT = [sbt(ph, "QT%d" % i, [96, T], BF16) for i in range(2)]
                VH = [sbt(ph, "VH%d" % i, [128, 34, 128], BF16) for i in range(2)]
                Pt = [sbt(ph, "Pt%d" % i, [128, 1024], BF16) for i in range(3)]
                rsum = [sbt(ph, "rsum%d" % i, [64, 512], F32) for i in range(2)]
                oT = [sbt(ph, "oT%d" % i, [64, 512], BF16) for i in range(2)]
                pS = [pst(ph, "pS%d" % i, (128, 1024)) for i in range(3)]
                pO = [pst(ph, "pO%d" % i) for i in range(2)]
                allv = [db("VsM", i) for i in range(34)]
                it = [0]
                for h in range(8):
                    allq = [db("QTs", (h, si)) for si in range(9)]
                    allk = [db("KTs", (h, si)) for si in range(9)]
                    kt_, qt_, vh_ = KT[h % 2], QT[h % 2], VH[h % 2]
                    S.dma(kt_.t[:], KTs[h], reads=allk, writes=[kt_.b], key=kt_.b.name)
                    S.dma(qt_.t[:], QTs[h], reads=allq, writes=[qt_.b], key=qt_.b.name)
                    S.dma(vh_.t[:], VsM.rearrange("t p h c -> p t h c")[:, :, h, :], reads=allv, writes=[vh_.b], key=vh_.b.name)
                    for (si, t0, n, isc) in STS:
                        if isc and not do_ctx:
                            continue
                        keys = [0, 1] if isc else list(range(34))
                        po = pO[it[0] % 2]
                        rs_ = rsum[it[0] % 2]
                        o_ = oT[it[0] % 2]
                        it[0] += 1

                        def score(kt, ps_, j):
                            S.op("pe", lambda e: e.matmul(ps_.t[:, j * 512:j * 512 + n], lhsT=kt_.t[:, kt * 128:(kt + 1) * 128], rhs=qt_.t[:, t0:t0 + n], start=True, stop=True),
                                 reads=[kt_.b, qt_.b], writes=[ps_.b])
                        attn_core(ph, keys, score, lambda kt: vh_.t[:, kt, :], [vh_.b], n, pS, po, Pt, 96.0 ** -0.5)
                        S.op("act", lambda e: e.copy(out=rs_.t[:, 0:n], in_=po.t[64:128, 0:n]), reads=[po.b], writes=[rs_.b])
                        S.op("dve", lambda e: e.reciprocal(out=rs_.t[:, 0:n], in_=rs_.t[:, 0:n]), reads=[rs_.b], writes=[rs_.b])
                        S.op("dve", lambda e: e.tensor_tensor(out=o_.t[:, 0:n], in0=po.t[0:64, 0:n], in1=rs_.t[:, 0:n], op=ALU.mult),
                             reads=[po.b, rs_.b], writes=[o_.b])
                        S.dma(mlaO[h * 64:(h + 1) * 64, t0:t0 + n], o_.t[:, 0:n], reads=[o_.b], writes=[db("mlaO", (h, si))], key=o_.b.name)
            S.barrier()

        def phase_C(l, do_ctx):
            with contextlib.ExitStack() as ph:
                sk = sbt(ph, "sk", [64, T], BF16)
                sq_ = sbt(ph, "sq_", [64, 4, T], BF16)
                sv = sbt(ph, "sv", [128, 34, 128], BF16)
                Pt = [sbt(ph, "Pt%d" % i, [128, 1024], BF16) for i in range(3)]
                rsum = [sbt(ph, "rsum%d" % i, [64, 512], F32) for i in range(2)]
                oT = [sbt(ph, "oT%d" % i, [64, 512], BF16) for i in range(2)]
                esk = sbt(ph, "esk", [128, 8], F32)
                S.dma(esk.t[:], sinkb[l], writes=[esk.b], key="esk")
                S.op("act", lambda e: e.activation(out=esk.t[:], in_=esk.t[:], func=AF.Exp), reads=[esk.b], writes=[esk.b])
                pS = [pst(ph, "pS%d" % i, (128, 1024)) for i in range(3)]
                pO = [pst(ph, "pO%d" % i) for i in range(2)]
                allv = [db("sVs", i) for i in range(34)]
                it = [0]
                for kh in range(2):
                    allq = [db("sQs", (c_, si)) for si in range(9) for c_ in (2 * kh, 2 * kh + 1)]
                    allk = [db("sKs", (kh, si)) for si in range(9)]
                    S.dma(sk.t[:], sKs[:, kh, :], reads=allk, writes=[sk.b], key="sk")
                    S.dma(sq_.t[:], sQs[:, 4 * kh:4 * kh + 4, :], reads=allq, writes=[sq_.b], key="sq_")
                    S.dma(sv.t[:], sVs.rearrange("t p h c -> p t h c")[:, :, kh, :], reads=allv, writes=[sv.b], key="sv")
                    qtiles = ([0, 1] if do_ctx else []) + list(range(2, 34))
                    for qt in qtiles:
                        q0 = qt * 128
                        if qt < 2:
                            keys = [(0, None), (1, None)]
                        else:
                            keys = [(0, None), (1, None)]
                            if qt > 2:
                                keys.append((qt - 1, Mge))
                            keys.append((qt, None))
                            if qt < 33:
                                keys.append((qt + 1, Mle))
                        po = pO[it[0] % 2]
                        rs_ = rsum[it[0] % 2]
                        o_ = oT[it[0] % 2]
                        it[0] += 1

                        def score(km, ps_, j):
                            kt = km[0]
                            for hd in range(4):
                                S.op("pe", lambda e: e.matmul(ps_.t[:, j * 512 + hd * 128:j * 512 + (hd + 1) * 128], lhsT=sk.t[:, kt * 128:(kt + 1) * 128],
                                                              rhs=sq_.t[:, hd, q0:q0 + 128], start=True, stop=True),
                                     reads=[sk.b, sq_.b], writes=[ps_.b])

                        def maskf(km, p_, j):
                            if km[1] is not None:
                                m = km[1]
                                p3 = p_.t[:, j * 512:(j + 1) * 512].rearrange("p (h q) -> p h q", q=128)
                                S.op("pool", lambda e: e.tensor_tensor(out=p3, in0=p3, in1=m.t[:, :].unsqueeze(1).to_broadcast([128, 4, 128]), op=ALU.mult),
                                     reads=[p_.b, m.b], writes=[p_.b])
                        attn_core(ph, keys, score, lambda km: sv.t[:, km[0], :], [sv.b], 512, pS, po, Pt, 0.125, mask_fn=maskf)
                        S.op("act", lambda e: e.copy(out=rs_.t[:, :], in_=po.t[64:128, :]), reads=[po.b], writes=[rs_.b])
                        r3 = rs_.t[:, :].rearrange("p (h q) -> p h q", q=128)
                        S.op("dve", lambda e: e.tensor_tensor(out=r3, in0=r3, in1=esk.t[0:64, 4 * kh:4 * kh + 4].unsqueeze(2).to_broadcast([64, 4, 128]), op=ALU.add),
                             reads=[rs_.b, esk.b], writes=[rs_.b])
                        S.op("dve", lambda e: e.reciprocal(out=rs_.t[:, :], in_=rs_.t[:, :]), reads=[rs_.b], writes=[rs_.b])
                        S.op("dve", lambda e: e.tensor_tensor(out=o_.t[:, :], in0=po.t[0:64, :], in1=rs_.t[:, :], op=ALU.mult),
                             reads=[po.b, rs_.b], writes=[o_.b])
                        dst = swaO.rearrange("(h d) t -> d h t", d=64)[:, 4 * kh:4 * kh + 4, q0:q0 + 128]
                        S.dma(dst, o_.t[:, :].rearrange("p (h q) -> p h q", q=128), reads=[o_.b], writes=[db("swaO", (kh, qt))], key=o_.b.name)
            S.barrier()

        def phase_D(l, do_ctx):
            with contextlib.ExitStack() as ph:
                uT = sbt(ph, "uT", [128, 4, T], BF16)
                S.dma(uT.t[:], uTs[:, :, :], reads=[db("uTs", si) for si in range(9)], writes=[uT.b], key="uT_ld")
                acc = sbt(ph, "acc", [128, T], F32)
                def small(name, w=32):
                    return sbt(ph, name, [128, w], F32)
                lre, lim, dtt = small("lre"), small("lim"), small("dtt")
                S.dma(lre.t[:], s5lre[l].rearrange("p d q -> p (d q)"), writes=[lre.b], key="s5p1")
                S.dma(lim.t[:], s5lim[l].rearrange("p d q -> p (d q)"), writes=[lim.b], key="s5p2")
                S.dma(dtt.t[:], s5ls[l].rearrange("p d q -> p (d q)"), writes=[dtt.b], key="s5p3")
                S.op("act", lambda e: e.activation(out=dtt.t[:], in_=dtt.t[:], func=AF.Exp), reads=[dtt.b], writes=[dtt.b])
                S.op("dve", lambda e: e.tensor_scalar(out=lre.t[:], in0=lre.t[:], scalar1=-1e-4, scalar2=None, op0=ALU.min), reads=[lre.b], writes=[lre.b])
                rr, th, fq = small("rr"), small("th"), small("fq")
                S.op("dve", lambda e: e.tensor_tensor(out=rr.t[:], in0=lre.t[:], in1=dtt.t[:], op=ALU.mult), reads=[lre.b, dtt.b], writes=[rr.b])
                S.op("act", lambda e: e.activation(out=rr.t[:], in_=rr.t[:], func=AF.Exp), reads=[rr.b], writes=[rr.b])
                S.op("dve", lambda e: e.tensor_tensor(out=th.t[:], in0=lim.t[:], in1=dtt.t[:], op=ALU.mult), reads=[lim.b, dtt.b], writes=[th.b])
                S.op("dve", lambda e: e.tensor_scalar(out=fq.t[:], in0=th.t[:], scalar1=1.0 / TWO_PI, scalar2=None, op0=ALU.mult), reads=[th.b], writes=[fq.b])
                tmpi = sbt(ph, "tmpi", [128, 512], I32)
                tmpf = sbt(ph, "tmpf", [128, 512], F32)

                def sincos(dst_s, dst_c, ph_ap, ph_buf, w):
                    for (dst, shift) in ((dst_s, 0.0), (dst_c, 0.25)):
                        S.op("dve", lambda e: e.tensor_scalar(out=tmpf.t[:, 0:w], in0=ph_ap, scalar1=shift, scalar2=None, op0=ALU.add), reads=[ph_buf], writes=[tmpf.b])
                        S.op("dve", lambda e: e.tensor_copy(out=tmpi.t[:, 0:w], in_=tmpf.t[:, 0:w]), reads=[tmpf.b], writes=[tmpi.b])
                        S.op("dve", lambda e: e.tensor_copy(out=dst[1], in_=tmpi.t[:, 0:w]), reads=[tmpi.b], writes=[dst[0]])
                        S.op("dve", lambda e: e.tensor_tensor(out=tmpf.t[:, 0:w], in0=tmpf.t[:, 0:w], in1=dst[1], op=ALU.subtract), reads=[tmpf.b, dst[0]], writes=[tmpf.b])
                        S.op("act", lambda e: e.activation(out=dst[1], in_=tmpf.t[:, 0:w], func=AF.Sin, scale=TWO_PI), reads=[tmpf.b], writes=[dst[0]])

                sn, cs = small("sn"), small("cs")
                sincos((sn.b, sn.t[:]), (cs.b, cs.t[:]), fq.t[:], fq.b, 32)
                are, aim = small("are"), small("aim")
                S.op("dve", lambda e: e.tensor_tensor(out=are.t[:], in0=rr.t[:], in1=cs.t[:], op=ALU.mult), reads=[rr.b, cs.b], writes=[are.b])
                S.op("dve", lambda e: e.tensor_tensor(out=aim.t[:], in0=rr.t[:], in1=sn.t[:], op=ALU.mult), reads=[rr.b, sn.b], writes=[aim.b])
                am1, den, t1_, t2_, cre, cim = small("am1"), small("den"), small("t1_"), small("t2_"), small("cre"), small("cim")
                S.op("dve", lambda e: e.tensor_scalar(out=am1.t[:], in0=are.t[:], scalar1=-1.0, scalar2=None, op0=ALU.add), reads=[are.b], writes=[am1.b])
                S.op("dve", lambda e: e.tensor_tensor(out=den.t[:], in0=lre.t[:], in1=lre.t[:], op=ALU.mult), reads=[lre.b], writes=[den.b])
                S.op("dve", lambda e: e.tensor_tensor(out=t1_.t[:], in0=lim.t[:], in1=lim.t[:], op=ALU.mult), reads=[lim.b], writes=[t1_.b])
                S.op("dve", lambda e: e.tensor_tensor(out=den.t[:], in0=den.t[:], in1=t1_.t[:], op=ALU.add), reads=[den.b, t1_.b], writes=[den.b])
                S.op("dve", lambda e: e.reciprocal(out=den.t[:], in_=den.t[:]), reads=[den.b], writes=[den.b])
                S.op("dve", lambda e: e.tensor_tensor(out=t1_.t[:], in0=am1.t[:], in1=lre.t[:], op=ALU.mult), reads=[am1.b, lre.b], writes=[t1_.b])
                S.op("dve", lambda e: e.tensor_tensor(out=t2_.t[:], in0=aim.t[:], in1=lim.t[:], op=ALU.mult), reads=[aim.b, lim.b], writes=[t2_.b])
                S.op("dve", lambda e: e.tensor_tensor(out=cre.t[:], in0=t1_.t[:], in1=t2_.t[:], op=ALU.add), reads=[t1_.b, t2_.b], writes=[cre.b])
                S.op("dve", lambda e: e.tensor_tensor(out=cre.t[:], in0=cre.t[:], in1=den.t[:], op=ALU.mult), reads=[cre.b, den.b], writes=[cre.b])
                S.op("dve", lambda e: e.tensor_tensor(out=t1_.t[:], in0=aim.t[:], in1=lre.t[:], op=ALU.mult), reads=[aim.b, lre.b], writes=[t1_.b])
                S.op("dve", lambda e: e.tensor_tensor(out=t2_.t[:], in0=am1.t[:], in1=lim.t[:], op=ALU.mult), reads=[am1.b, lim.b], writes=[t2_.b])
                S.op("dve", lambda e: e.tensor_tensor(out=cim.t[:], in0=t1_.t[:], in1=t2_.t[:], op=ALU.subtract), reads=[t1_.b, t2_.b], writes=[cim.b])
                S.op("dve", lambda e: e.tensor_tensor(out=cim.t[:], in0=cim.t[:], in1=den.t[:], op=ALU.mult), reads=[cim.b, den.b], writes=[cim.b])
                fb, snB, csB = small("fb", 64), small("snB", 64), small("csB", 64)
                S.op("dve", lambda e: e.tensor_scalar(out=fb.t[:, 0:32], in0=fq.t[:], scalar1=256.0, scalar2=None, op0=ALU.mult), reads=[fq.b], writes=[fb.b])
                S.op("dve", lambda e: e.tensor_scalar(out=fb.t[:, 32:64], in0=fq.t[:], scalar1=512.0, scalar2=None, op0=ALU.mult), reads=[fq.b], writes=[fb.b])
                sincos((snB.b, snB.t[:]), (csB.b, csB.t[:]), fb.t[:], fb.b, 64)
                WB = sbt(ph, "WB", [128, 2, 2, 8, 128], BF16)
                CW = sbt(ph, "CW", [128, 3, 2, 16, 128], BF16)
                S.op("pool", lambda e: e.memset(CW.t[:], 0.0), writes=[CW.b])
                prep = contextlib.ExitStack()
                braw = sbt(prep, "braw", [128, 2, 2, 16, 16], F32)
                craw = sbt(prep, "craw", [128, 2, 2, 16, 16], F32)
                S.dma(braw.t[:, 0], s5bre[l], writes=[braw.b], key="braw")
                S.dma(braw.t[:, 1], s5bim[l], writes=[braw.b], key="braw")
                S.dma(craw.t[:, 0], s5cre[l], writes=[craw.b], key="craw")
                S.dma(craw.t[:, 1], s5cim[l], writes=[craw.b], key="craw")
                bbar = sbt(prep, "bbar", [128, 2, 2, 16, 16], F32)
                tb = sbt(prep, "tb", [128, 2, 16, 16], F32)
                cre3 = cre.t[:].rearrange("p (d q) -> p d q", q=16).unsqueeze(3).to_broadcast([128, 2, 16, 16])
                cim3 = cim.t[:].rearrange("p (d q) -> p d q", q=16).unsqueeze(3).to_broadcast([128, 2, 16, 16])
                S.op("dve", lambda e: e.tensor_tensor(out=bbar.t[:, 0], in0=braw.t[:, 0], in1=cre3, op=ALU.mult), reads=[braw.b, cre.b], writes=[bbar.b])
                S.op("dve", lambda e: e.tensor_tensor(out=tb.t[:], in0=braw.t[:, 1], in1=cim3, op=ALU.mult), reads=[braw.b, cim.b], writes=[tb.b])
                S.op("dve", lambda e: e.tensor_tensor(out=bbar.t[:, 0], in0=bbar.t[:, 0], in1=tb.t[:], op=ALU.subtract), reads=[bbar.b, tb.b], writes=[bbar.b])
                S.op("dve", lambda e: e.tensor_tensor(out=bbar.t[:, 1], in0=braw.t[:, 1], in1=cre3, op=ALU.mult), reads=[braw.b, cre.b], writes=[bbar.b])
                S.op("dve", lambda e: e.tensor_tensor(out=tb.t[:], in0=braw.t[:, 0], in1=cim3, op=ALU.mult), reads=[braw.b, cim.b], writes=[tb.b])
                S.op("dve", lambda e: e.tensor_tensor(out=bbar.t[:, 1], in0=bbar.t[:, 1], in1=tb.t[:], op=ALU.add), reads=[bbar.b, tb.b], writes=[bbar.b])
                Z = [sbt(prep, "Z%d" % i, [128, 64], BF16) for i in range(2)]
                pz = [pst(prep, "pz%d" % i, [128, 128], BF16) for i in range(2)]
                zi = 0
                for d_ in range(2):
                    for q in range(16):
                        c, q4 = q // 4, q % 4
                        hf, q2 = q4 // 2, q4 % 2
                        for ri in range(2):
                            z, pzz = Z[zi % 2], pz[zi % 2]
                            zi += 1
                            S.op("pool", lambda e: e.memset(z.t[:], 0.0), writes=[z.b])
                            S.op("dve", lambda e: e.tensor_copy(out=z.t[0:64, q2 * 32:q2 * 32 + 16], in_=bbar.t[0:64, ri, d_, q, :]), reads=[bbar.b], writes=[z.b])
                            S.op("dve", lambda e: e.tensor_copy(out=z.t[64:128, q2 * 32 + 16:q2 * 32 + 32], in_=bbar.t[64:128, ri, d_, q, :]), reads=[bbar.b], writes=[z.b])
                            S.op("pe", lambda e: e.transpose(pzz.t[0:64, :], z.t[:, :], identb.t[:, :]), reads=[z.b, identb.b], writes=[pzz.b])
                            S.op("act", lambda e: e.copy(out=WB.t[hf * 64:(hf + 1) * 64, ri, d_, c * 2 + q2, :], in_=pzz.t[0:64, :]), reads=[pzz.b], writes=[WB.b])
                            sgn = 1.0 if ri == 0 else -1.0
                            for e_ in range(2):
                                S.op("pool", lambda e: e.tensor_scalar(out=CW.t[e_ * 64:(e_ + 1) * 64, ri, d_, q, q4 * 32 + e_ * 16:q4 * 32 + e_ * 16 + 16],
                                                                       in0=craw.t[e_ * 64:(e_ + 1) * 64, ri, d_, q, :], scalar1=sgn, scalar2=None, op0=ALU.mult),
                                     reads=[craw.b], writes=[CW.b])
                                if ri == 0:
                                    S.op("pool", lambda e: e.tensor_scalar(out=CW.t[e_ * 64:(e_ + 1) * 64, 2, d_, q, q4 * 32 + e_ * 16:q4 * 32 + e_ * 16 + 16],
                                                                           in0=craw.t[e_ * 64:(e_ + 1) * 64, 0, d_, q, :], scalar1=-1.0, scalar2=None, op0=ALU.mult),
                                         reads=[craw.b], writes=[CW.b])
                S.barrier()
                prep.close()
                Wg = sbt(ph, "Wg", [128, 4, 512], BF16)
                load_cast(lambda a0, a1: (Wg.t[:, a0:a1, :], Wg.b), w_glu[l].rearrange("(kc p) n -> p kc n", p=128), 4, 512, eng="act")
                dcol = sbt(ph, "dcol", [128, 8], F32)
                S.dma(dcol.t[:, 0:4], s5dc[l], writes=[dcol.b], key="dcol")
                S.dma(dcol.t[:, 4:8], s5bgc[l], writes=[dcol.b], key="dcol")
                iot = sbt(ph, "iot", [128, 512], F32)
                S.dma(iot.t[:], cIota[:, :], writes=[iot.b], key="iot")
                wk = contextlib.ExitStack()
                tph = sbt(wk, "tph", [128, 512], F32)
                tC4 = [sbt(wk, "tC4_%d" % i, [128, 512], F32) for i in range(4)]
                tS4 = [sbt(wk, "tS4_%d" % i, [128, 512], F32) for i in range(4)]
                NS = int(_os.environ.get("D_NS", "3"))
                W = {k: [sbt(wk, "w%s%d" % (k, i), [128, 512], F32) for i in range(NS)] for k in
                     ("t1", "t2", "t3", "t4", "xa", "xb", "ga", "gb")}
                Ub = {k: [sbt(wk, "b%s%d" % (k, i), [128, 512], BF16) for i in range(NS)] for k in ("u1", "u2", "u3", "u4")}
                ini4 = [sbt(wk, "ini4_%d" % i, [128, 4], F32) for i in range(4)]
                ygt = [sbt(wk, "ygt%d" % i, [128, 512], BF16) for i in range(2)]
                Xs = [[sbt(wk, "Xs%d%d" % (i, j), [128, 512], F32) for j in range(2)] for i in range(NS)] if D_EVAC else None
                pX = [[pst(wk, "pX%d%d" % (i, j)) for j in range(2)] for i in range(NS)]
                pY = [pst(wk, "pY%d" % i) for i in range(2)]
                UENG = D_UENG

                def tt(eng, o, a, b_, op, rd, wr):
                    S.op(eng, lambda e: e.tensor_tensor(out=o, in0=a, in1=b_, op=op), reads=rd, writes=wr)
                it = 0
                for c in range(4):
                    S.op("dve", lambda e: e.tensor_scalar(out=acc.t[:], in0=uT.t[:, c, :], scalar1=dcol.t[:, c:c + 1], scalar2=None, op0=ALU.mult),
                         reads=[uT.b, dcol.b], writes=[acc.b])
                    for d_ in range(2):
                        for q4 in range(4):
                            col = d_ * 16 + c * 4 + q4
                            S.op("dve", lambda e: e.tensor_scalar(out=tph.t[:], in0=iot.t[:], scalar1=fq.t[:, col:col + 1], scalar2=None, op0=ALU.mult),
                                 reads=[iot.b, fq.b], writes=[tph.b])
                            sincos((tS4[q4].b, tS4[q4].t[:]), (tC4[q4].b, tC4[q4].t[:]), tph.t[:], tph.b, 512)
                            S.op("pool", lambda e: e.memset(ini4[q4].t[:], 0.0), writes=[ini4[q4].b])
                        def frame_geom(fidx, d_=d_):
                            n = 256 if fidx == 0 else 512
                            if d_ == 0:
                                k0 = 0 if fidx == 0 else 256 + 512 * (fidx - 1)
                                cols = slice(k0, k0 + n)
                            else:
                                if fidx == 0:
                                    cols = slice(255, None, -1)
                                else:
                                    hi = NC_ + NX - 512 * (fidx - 1) - 1
                                    cols = slice(hi, hi - 512, -1)
                            return n, cols

                        if True:
                            if True:
                                pass
                            def chain(q4, i, fidx, d_=d_, c=c):
                                n, cols = frame_geom(fidx)
                                py = pY[fidx % 2]
                                q = c * 4 + q4
                                hf, q2 = q4 // 2, q4 % 2
                                col = d_ * 16 + q
                                tC, tS, iv = tC4[q4], tS4[q4], ini4[q4]
                                px = pX[i]
                                for ri in range(2):
                                    S.op("pe", lambda e: e.matmul(px[ri].t[:, 0:n], lhsT=WB.t[hf * 64:(hf + 1) * 64, ri, d_, c * 2 + q2, :],
                                                                  rhs=uT.t[hf * 64:(hf + 1) * 64, c, cols], start=True, stop=True),
                                         reads=[WB.b, uT.b], writes=[px[ri].b])
                                yield
                                t1, t2, t3, t4 = W["t1"][i], W["t2"][i], W["t3"][i], W["t4"][i]
                                xa, xb, ga, gb = W["xa"][i], W["xb"][i], W["ga"][i], W["gb"][i]
                                u1, u2, u3, u4 = Ub["u1"][i], Ub["u2"][i], Ub["u3"][i], Ub["u4"][i]
                                if D_EVAC:
                                    x0_, x1_ = Xs[i][0], Xs[i][1]
                                    S.op("act", lambda e: e.copy(out=x0_.t[:, 0:n], in_=px[0].t[:, 0:n]), reads=[px[0].b], writes=[x0_.b])
                                    S.op("act", lambda e: e.copy(out=x1_.t[:, 0:n], in_=px[1].t[:, 0:n]), reads=[px[1].b], writes=[x1_.b])
                                    yield
                                else:
                                    x0_, x1_ = px[0], px[1]
                                tt("dve", t1.t[:, 0:n], x0_.t[:, 0:n], tC.t[:, 0:n], ALU.mult, [x0_.b, tC.b], [t1.b])
                                yield
                                tt("dve", t2.t[:, 0:n], x1_.t[:, 0:n], tS.t[:, 0:n], ALU.mult, [x1_.b, tS.b], [t2.b])
                                yield
                                tt("dve", t3.t[:, 0:n], x1_.t[:, 0:n], tC.t[:, 0:n], ALU.mult, [x1_.b, tC.b], [t3.b])
                                yield
                                tt("dve", t4.t[:, 0:n], x0_.t[:, 0:n], tS.t[:, 0:n], ALU.mult, [x0_.b, tS.b], [t4.b])
                                yield
                                tt(D_XENG, xa.t[:, 0:n], t1.t[:, 0:n], t2.t[:, 0:n], ALU.add, [t1.b, t2.b], [xa.b])
                                yield
                                tt(D_XENG, xb.t[:, 0:n], t3.t[:, 0:n], t4.t[:, 0:n], ALU.subtract, [t3.b, t4.b], [xb.b])
                                yield
                                rbc = rr.t[:, col:col + 1].to_broadcast([128, n])
                                S.op("dve", lambda e: e.tensor_tensor_scan(out=ga.t[:, 0:n], data0=rbc, data1=xa.t[:, 0:n], initial=iv.t[:, 0:1], op0=ALU.mult, op1=ALU.add),
                                     reads=[xa.b, rr.b, iv.b], writes=[ga.b])
                                yield
                                S.op("dve", lambda e: e.tensor_tensor_scan(out=gb.t[:, 0:n], data0=rbc, data1=xb.t[:, 0:n], initial=iv.t[:, 1:2], op0=ALU.mult, op1=ALU.add),
                                     reads=[xb.b, rr.b, iv.b], writes=[gb.b])
                                yield
                                if fidx < 8:
                                    bc = (0 if n == 256 else 32) + col
                                    cB, sB = csB.t[:, bc:bc + 1], snB.t[:, bc:bc + 1]
                                    S.op("pool", lambda e: e.tensor_scalar(out=iv.t[:, 2:3], in0=gb.t[:, n - 1:n], scalar1=sB, scalar2=None, op0=ALU.mult), reads=[gb.b, snB.b], writes=[iv.b])
                                    S.op("pool", lambda e: e.tensor_scalar(out=iv.t[:, 3:4], in0=ga.t[:, n - 1:n], scalar1=sB, scalar2=None, op0=ALU.mult), reads=[ga.b, snB.b], writes=[iv.b])
                                yield
                                tt(UENG[0], u1.t[:, 0:n], ga.t[:, 0:n], tC.t[:, 0:n], ALU.mult, [ga.b, tC.b], [u1.b])
                                yield
                                tt(UENG[1], u2.t[:, 0:n], gb.t[:, 0:n], tS.t[:, 0:n], ALU.mult, [gb.b, tS.b], [u2.b])
                                yield
                                tt(UENG[2], u3.t[:, 0:n], ga.t[:, 0:n], tS.t[:, 0:n], ALU.mult, [ga.b, tS.b], [u3.b])
                                yield
                                tt(UENG[3], u4.t[:, 0:n], gb.t[:, 0:n], tC.t[:, 0:n], ALU.mult, [gb.b, tC.b], [u4.b])
                                yield
                                if fidx < 8:
                                    bc = (0 if n == 256 else 32) + col
                                    cB = csB.t[:, bc:bc + 1]
                                    S.op("dve", lambda e: e.scalar_tensor_tensor(out=iv.t[:, 0:1], in0=ga.t[:, n - 1:n], scalar=cB, in1=iv.t[:, 2:3], op0=ALU.mult, op1=ALU.subtract),
                                         reads=[ga.b, csB.b, iv.b], writes=[iv.b])
                                    S.op("dve", lambda e: e.scalar_tensor_tensor(out=iv.t[:, 1:2], in0=gb.t[:, n - 1:n], scalar=cB, in1=iv.t[:, 3:4], op0=ALU.mult, op1=ALU.add),
                                         reads=[gb.b, csB.b, iv.b], writes=[iv.b])
                                yield
                                for j_, (ws, uu) in enumerate(((0, u1), (2, u2), (1, u3), (1, u4))):
                                    S.op("pe", lambda e: e.matmul(py.t[:, 0:n], lhsT=CW.t[:, ws, d_, q, :], rhs=uu.t[:, 0:n],
                                                                  start=(q4 == 0 and j_ == 0), stop=(q4 == 3 and j_ == 3)), reads=[CW.b, uu.b], writes=[py.b])
                                if q4 == 3:
                                    S.op("dve", lambda e: e.tensor_tensor(out=acc.t[:, cols], in0=acc.t[:, cols], in1=py.t[:, 0:n], op=ALU.add), reads=[acc.b, py.b], writes=[acc.b])

                            jobs = [(fidx, q4) for fidx in range(9) for q4 in range(4)]
                            live = []
                            free_slots = list(range(NS))
                            nj = 0
                            while live or nj < len(jobs):
                                while len(live) < NS and nj < len(jobs):
                                    sl = free_slots.pop(0)
                                    live.append((chain(jobs[nj][1], sl, jobs[nj][0]), sl))
                                    nj += 1
                                for (g_, sl) in list(live):
                                    try:
                                        next(g_)
                                    except StopIteration:
                                        live.remove((g_, sl))
                                        free_slots.append(sl)
                    for a0 in range(0, T, 512):
                        w_ = min(512, T - a0)
                        i = (a0 // 512) % 2
                        t1, t2 = W["t1"][i], W["t2"][i]
                        yo = ygt[i]
                        S.op("pool", lambda e: e.tensor_tensor(out=t1.t[:, 0:w_], in0=acc.t[:, a0:a0 + w_], in1=acc.t[:, a0:a0 + w_], op=ALU.mult), reads=[acc.b], writes=[t1.b])
                        S.op("pool", lambda e: e.tensor_scalar(out=t1.t[:, 0:w_], in0=t1.t[:, 0:w_], scalar1=0.044715, scalar2=1.0, op0=ALU.mult, op1=ALU.add), reads=[t1.b], writes=[t1.b])
                        S.op("pool", lambda e: e.tensor_tensor(out=t1.t[:, 0:w_], in0=t1.t[:, 0:w_], in1=acc.t[:, a0:a0 + w_], op=ALU.mult), reads=[t1.b, acc.b], writes=[t1.b])
                        S.op("act", lambda e: e.activation(out=t2.t[:, 0:w_], in_=t1.t[:, 0:w_], func=AF.Sigmoid, scale=1.5957691216057308), reads=[t1.b], writes=[t2.b])
                        S.op("dve", lambda e: e.tensor_tensor(out=yo.t[:, 0:w_], in0=t2.t[:, 0:w_], in1=acc.t[:, a0:a0 + w_], op=ALU.mult), reads=[t2.b, acc.b], writes=[yo.b])
                        S.dma(ygs[:, c, a0:a0 + w_], yo.t[:, 0:w_], reads=[yo.b], writes=[db("ygs", (c, a0 // 512))], key=yo.b.name)
                S.barrier()
                wk.close()
                pY = [pst(ph, "pYg%d" % i) for i in range(2)]
                W = {"t2": [sbt(ph, "gsig%d" % i, [128, 512], F32) for i in range(2)]}
                so = [sbt(ph, "so%d" % i, [128, 4, 512], BF16) for i in range(2)]
                ygl = [sbt(ph, "ygl%d" % i, [128, 4, 512], BF16) for i in range(2)]
                for (si, t0, n, isc) in STS:
                    if isc and not do_ctx:
                        continue
                    o_ = so[si % 2]
                    yg = ygl[si % 2]
                    S.dma(yg.t[:, :, 0:n], ygs[:, :, t0:t0 + n], reads=[db("ygs", (c_, (t0 + j_) // 512)) for c_ in range(4) for j_ in (0, n - 1)],
                          writes=[yg.b], key=yg.b.name)
                    for oc in range(4):
                        py = pY[oc % 2]
                        for kc in range(4):
                            S.op("pe", lambda e: e.matmul(py.t[:, 0:n], lhsT=Wg.t[:, kc, oc * 128:(oc + 1) * 128], rhs=yg.t[:, kc, 0:n], start=(kc == 0), stop=(kc == 3)),
                                 reads=[Wg.b, yg.b], writes=[py.b])
                        t2 = W["t2"][oc % 2]
                        S.op("act", lambda e: e.activation(out=t2.t[:, 0:n], in_=py.t[:, 0:n], func=AF.Sigmoid, bias=dcol.t[:, 4 + oc:5 + oc], scale=1.0), reads=[py.b, dcol.b], writes=[t2.b])
                        S.op("dve", lambda e: e.tensor_tensor(out=o_.t[:, oc, 0:n], in0=t2.t[:, 0:n], in1=yg.t[:, oc, 0:n], op=ALU.mult), reads=[t2.b, yg.b], writes=[o_.b])
                    S.dma(s5O[:, :, t0:t0 + n], o_.t[:, :, 0:n], reads=[o_.b], writes=[db("s5O", si)], key=o_.b.name)
            S.barrier()

        def load_E_weights(ph, l, prefetch):
            de = "pool" if prefetch else "sp"
            Wgt = sbt(ph, "Wgt", [128, 8, 3072], BF16)
            wsrc = w_in[l].rearrange("(kc p) n -> p kc n", p=128)
            for r in range(3):
                load_cast(lambda a0, a1: (Wgt.t[:, a0:a1, r * 1024:(r + 1) * 1024], Wgt.b), wsrc[:, :, O_G + r * 1024:O_G + (r + 1) * 1024], 8, 1024,
                          eng=("pool" if (prefetch or r % 2 == 0) else "act"), dma_eng=de)
            Wb = sbt(ph, "Wb", [128, 3, 4, 1024], BF16)
            for r in range(3):
                load_cast(lambda a0, a1: (Wb.t[:, r, a0:a1, :], Wb.b), w_br[l, r].rearrange("(kc p) n -> p kc n", p=128), 4, 1024,
                          eng=("pool" if (prefetch or r % 2 == 1) else "act"), dma_eng=de)
            Wo = sbt(ph, "Wo", [128, 8, 1024], BF16)
            load_cast(lambda a0, a1: (Wo.t[:, a0:a1, :], Wo.b), w_o[l].rearrange("(kc p) n -> p kc n", p=128), 8, 1024, dma_eng=de)
            return Wgt, Wb, Wo

        def load_F2_weights(ph, l, prefetch, emit=True):
            Wfo = sbt(ph, "Wfo", [128, 22, 1024], BF16)

            def emit_fn():
                load_cast(lambda a0, a1: (Wfo.t[:, a0:a1, :], Wfo.b), f_out[l].rearrange("(kc p) n -> p kc n", p=128), 22, 1024,
                          dma_eng=("pool" if prefetch else "sp"))
            if emit:
                emit_fn()
                return Wfo
            return Wfo, emit_fn

        def phase_E(l, hsrc, hdst, do_ctx, pre=None):
            with contextlib.ExitStack() as ph:
                Wgt, Wb, Wo = pre if pre is not None else load_E_weights(ph, l, False)
                hT = [sbt(ph, "hT%d" % i, [128, 8, 512], BF16) for i in range(2)]
                oB = [sbt(ph, "oB%d" % i, [128, 3, 4, 512], BF16) for i in range(2)]
                xt1 = sbt(ph, "xt", [128, 8, 512], F32)
                xt = [xt1, xt1]
                mT = sbt(ph, "mT", [128, 8, 512], BF16)
                sg = [sbt(ph, "sg%d" % i, [128, 512], BF16) for i in range(3)]
                gp = [sbt(ph, "gp%d" % i, [128, 512], F32) for i in range(3)]
                hn = xt1
                sq = sbt(ph, "sq", [128, 8, 512], F32)
                rs = sbt(ph, "rs", [128, 512], F32)
                fT = sbt(ph, "fT", [128, 8, 512], BF16)
                pG = [pst(ph, "pG%d" % i) for i in range(3)]
                pP = [pst(ph, "pP%d" % i) for i in range(3)]
                pM = pst(ph, "pM")
                pss = pst(ph, "pss")
                srcs = [(mlaO.rearrange("(kc p) t -> p kc t", p=128), "mlaO"), (s5O, "s5O"), (swaO.rearrange("(kc p) t -> p kc t", p=128), "swaO")]
                sts = [s for s in STS if (not s[3]) or do_ctx]

                def loads(k):
                    (si, t0, n, isc) = sts[k]
                    S.dma(hT[k % 2].t[:, :, 0:n], hTs[:, :, t0:t0 + n], reads=[db("hTs", si)], writes=[hT[k % 2].b], key=hT[k % 2].b.name)
                    for r in range(3):
                        if srcs[r][1] == "swaO":
                            rd = [db("swaO", (kh_, t0 // 128 + j)) for j in range(n // 128) for kh_ in range(2)]
                        elif srcs[r][1] == "mlaO":
                            rd = [db("mlaO", (h_i, si)) for h_i in range(8)]
                        else:
                            rd = [db(srcs[r][1], si)]
                        S.dma(oB[k % 2].t[:, r, :, 0:n], srcs[r][0][:, :, t0:t0 + n], reads=rd, writes=[oB[k % 2].b], key=oB[k % 2].b.name)

                def loadx(k):
                    (si, t0, n, isc) = sts[k]
                    src = hsrc.rearrange("(kc p) t -> p kc t", p=128)
                    S.dma(xt1.t[:, :, 0:n], src[:, :, t0:t0 + n], reads=[db("hsrc%d" % id(hsrc), si)], writes=[xt1.b], key=xt1.b.name)

                loads(0)
                for k, (si, t0, n, isc) in enumerate(sts):
                    loadx(k)
                    if k + 1 < len(sts):
                        loads(k + 1)
                    s = 1 if isc else 0
                    h_, o_, x_ = hT[k % 2], oB[k % 2], xt[k % 2]
                    for oc in range(8):
                        for r in range(3):
                            for kc in range(8):
                                S.op("pe", lambda e: e.matmul(pG[r].t[:, 0:n], lhsT=Wgt.t[:, kc, r * 1024 + oc * 128:r * 1024 + (oc + 1) * 128], rhs=h_.t[:, kc, 0:n],
                                                              start=(kc == 0), stop=(kc == 7)), reads=[Wgt.b, h_.b], writes=[pG[r].b])
                            for kc in range(4):
                                S.op("pe", lambda e: e.matmul(pP[r].t[:, 0:n], lhsT=Wb.t[:, r, kc, oc * 128:(oc + 1) * 128], rhs=o_.t[:, r, kc, 0:n],
                                                              start=(kc == 0), stop=(kc == 3)), reads=[Wb.b, o_.b], writes=[pP[r].b])
                        for r in range(3):
                            S.op("act", lambda e: e.activation(out=sg[r].t[:, 0:n], in_=pG[r].t[:, 0:n], func=AF.Sigmoid), reads=[pG[r].b], writes=[sg[r].b])
                            S.op("dve", lambda e: e.tensor_tensor(out=gp[r].t[:, 0:n], in0=pP[r].t[:, 0:n], in1=sg[r].t[:, 0:n], op=ALU.mult),
                                 reads=[pP[r].b, sg[r].b], writes=[gp[r].b])
                        S.op("pool", lambda e: e.tensor_tensor(out=gp[0].t[:, 0:n], in0=gp[0].t[:, 0:n], in1=gp[1].t[:, 0:n], op=ALU.add), reads=[gp[0].b, gp[1].b], writes=[gp[0].b])
                        S.op("pool", lambda e: e.tensor_tensor(out=mT.t[:, oc, 0:n], in0=gp[0].t[:, 0:n], in1=gp[2].t[:, 0:n], op=ALU.add), reads=[gp[0].b, gp[2].b], writes=[mT.b])
                    for oc in range(8):
                        for kc in range(8):
                            S.op("pe", lambda e: e.matmul(pM.t[:, 0:n], lhsT=Wo.t[:, kc, oc * 128:(oc + 1) * 128], rhs=mT.t[:, kc, 0:n], start=(kc == 0), stop=(kc == 7)),
                                 reads=[Wo.b, mT.b], writes=[pM.b])
                        S.op("dve", lambda e: e.scalar_tensor_tensor(out=hn.t[:, oc, 0:n], in0=pM.t[:, 0:n], scalar=ada.t[:, 16 + oc, s:s + 1], in1=x_.t[:, oc, 0:n],
                                                                     op0=ALU.mult, op1=ALU.add), reads=[pM.b, ada.b, x_.b], writes=[hn.b])
                    dst = hdst.rearrange("(kc p) t -> p kc t", p=128)
                    S.dma(dst[:, :, t0:t0 + n], hn.t[:, :, 0:n], reads=[hn.b], writes=[db("hsrc%d" % id(hdst), si)], key="hn_st")
                    norm_mod(ph, hn, n, modA2, 24, isc, sq, pss, rs, fT)
                    S.dma(fTs[:, :, t0:t0 + n], fT.t[:, :, 0:n], reads=[fT.b], writes=[db("fTs", si)], key="fT_st")
            S.barrier()

        def phase_F1(l, do_ctx, after_weights=None):
            with contextlib.ExitStack() as ph:
                Wf = sbt(ph, "Wf", [128, 8, 2 * FH], BF16)
                wsrc = f_in[l].rearrange("(kc p) n -> p kc n", p=128)
                for cb in range(11):
                    load_cast(lambda a0, a1: (Wf.t[:, a0:a1, cb * 512:(cb + 1) * 512], Wf.b), wsrc[:, :, cb * 512:(cb + 1) * 512], 8, 512,
                              eng=("pool" if cb % 2 == 0 else "act"))
                if after_weights is not None:
                    after_weights()
                fT = [sbt(ph, "fT%d" % i, [128, 8, 512], BF16) for i in range(2)]
                aT = [sbt(ph, "aT%d" % i, [128, 22, 512], BF16) for i in range(2)]
                sa = [sbt(ph, "sa%d" % i, [128, 512], F32) for i in range(2)]
                pA = [pst(ph, "pA%d" % i) for i in range(3)]
                pGt = [pst(ph, "pGt%d" % i) for i in range(3)]
                sts = [s for s in STS if (not s[3]) or do_ctx]
                S.dma(fT[0].t[:, :, 0:sts[0][2]], fTs[:, :, sts[0][1]:sts[0][1] + sts[0][2]], reads=[db("fTs", sts[0][0])], writes=[fT[0].b], key=fT[0].b.name)
                for k, (si, t0, n, isc) in enumerate(sts):
                    if k + 1 < len(sts):
                        (si2, t02, n2, _) = sts[k + 1]
                        S.dma(fT[(k + 1) % 2].t[:, :, 0:n2], fTs[:, :, t02:t02 + n2], reads=[db("fTs", si2)], writes=[fT[(k + 1) % 2].b], key=fT[(k + 1) % 2].b.name)
                    f_, a_ = fT[k % 2], aT[k % 2]
                    for hc in range(22):
                        pa_, pg_ = pA[hc % 3], pGt[hc % 3]
                        for kc in range(8):
                            S.op("pe", lambda e: e.matmul(pa_.t[:, 0:n], lhsT=Wf.t[:, kc, hc * 128:(hc + 1) * 128], rhs=f_.t[:, kc, 0:n], start=(kc == 0), stop=(kc == 7)),
                                 reads=[Wf.b, f_.b], writes=[pa_.b])
                        for kc in range(8):
                            S.op("pe", lambda e: e.matmul(pg_.t[:, 0:n], lhsT=Wf.t[:, kc, FH + hc * 128:FH + (hc + 1) * 128], rhs=f_.t[:, kc, 0:n], start=(kc == 0), stop=(kc == 7)),
                                 reads=[Wf.b, f_.b], writes=[pg_.b])
                        s_ = sa[hc % 2]
                        S.op("act", lambda e: e.activation(out=s_.t[:, 0:n], in_=pa_.t[:, 0:n], func=AF.Silu), reads=[pa_.b], writes=[s_.b])
                        S.op("dve", lambda e: e.tensor_tensor(out=a_.t[:, hc, 0:n], in0=pg_.t[:, 0:n], in1=s_.t[:, 0:n], op=ALU.mult), reads=[pg_.b, s_.b], writes=[a_.b])
                    S.dma(actTs[:, :, t0:t0 + n], a_.t[:, :, 0:n], reads=[a_.b], writes=[db("actTs", si)], key=a_.b.name)
            S.barrier()

        def phase_F2(l, hsrc, hdst, do_ctx, final, pre=None):
            with contextlib.ExitStack() as ph:
                Wfo = pre if pre is not None else load_F2_weights(ph, l, False)
                aT = [sbt(ph, "aT%d" % i, [128, 22, 512], BF16) for i in range(2)]
                xt = [sbt(ph, "xt%d" % i, [128, 8, 512], F32) for i in range(2)]
                ho = [sbt(ph, "ho%d" % i, [128, 8, 512], F32) for i in range(2)]
                pO = [pst(ph, "pO%d" % i) for i in range(3)]
                sts = [s for s in STS if (not s[3]) or do_ctx]
                src = hsrc.rearrange("(kc p) t -> p kc t", p=128)

                def loads(k):
                    (si, t0, n, isc) = sts[k]
                    S.dma(aT[k % 2].t[:, :, 0:n], actTs[:, :, t0:t0 + n], reads=[db("actTs", si)], writes=[aT[k % 2].b], key=aT[k % 2].b.name)
                    S.dma(xt[k % 2].t[:, :, 0:n], src[:, :, t0:t0 + n], reads=[db("hsrc%d" % id(hsrc), si)], writes=[xt[k % 2].b], key=xt[k % 2].b.name)
                loads(0)
                for k, (si, t0, n, isc) in enumerate(sts):
                    if k + 1 < len(sts):
                        loads(k + 1)
                    s = 1 if isc else 0
                    a_, x_, h_ = aT[k % 2], xt[k % 2], ho[k % 2]
                    for oc in range(8):
                        po = pO[oc % 3]
                        for hc in range(22):
                            S.op("pe", lambda e: e.matmul(po.t[:, 0:n], lhsT=Wfo.t[:, hc, oc * 128:(oc + 1) * 128], rhs=a_.t[:, hc, 0:n], start=(hc == 0), stop=(hc == 21)),
                                 reads=[Wfo.b, a_.b], writes=[po.b])
                        S.op("dve", lambda e: e.scalar_tensor_tensor(out=h_.t[:, oc, 0:n], in0=po.t[:, 0:n], scalar=ada.t[:, 40 + oc, s:s + 1], in1=x_.t[:, oc, 0:n],
                                                                     op0=ALU.mult, op1=ALU.add), reads=[po.b, ada.b, x_.b], writes=[h_.b])
                    if final:
                        dst = outT.rearrange("(kc p) t -> p kc t", p=128)
                        S.dma(dst[:, :, t0 - NC_:t0 - NC_ + n], h_.t[:, :, 0:n], reads=[h_.b], writes=[db("outT", si)], key=h_.b.name)
                    else:
                        dst = hdst.rearrange("(kc p) t -> p kc t", p=128)
                        S.dma(dst[:, :, t0:t0 + n], h_.t[:, :, 0:n], reads=[h_.b], writes=[db("hsrc%d" % id(hdst), si)], key=h_.b.name)
            S.barrier()

        cur = xT
        for l in range(nlayers):
            last = (l == L - 1)
            do_ctx = not last
            if "a" in phases:
                phase_ada(l)
            if "A" in phases:
                phase_A(l, cur, do_ctx)
            if stop_after == "A":
                break
            full = all(x in phases for x in "BCDEF") and stop_after is None and PREFETCH
            if full:
                phase_D(l, do_ctx)
                wE = contextlib.ExitStack()
                preE = load_E_weights(wE, l, True)
                phase_B(l, do_ctx)
                phase_C(l, do_ctx)
                phase_E(l, cur, hres[0], do_ctx, pre=preE)
                wE.close()
                phase_F1(l, do_ctx)
                phase_F2(l, hres[0], hres[1], do_ctx, last)
            else:
                if "B" in phases:
                    phase_B(l, do_ctx)
                if stop_after == "B":
                    break
                if "C" in phases:
                    phase_C(l, do_ctx)
                if stop_after == "C":
                    break
                if "D" in phases:
                    phase_D(l, do_ctx)
                if stop_after == "D":
                    break
                if "E" in phases:
                    phase_E(l, cur, hres[0], do_ctx)
                if stop_after == "E":
                    break
                if "F" in phases:
                    phase_F1(l, do_ctx)
                    phase_F2(l, hres[0], hres[1], do_ctx, last)
            cur = hres[1]
        S.barrier()
        g.stats = (S.nins, S.nwait, len(S.semh))
    return nc, dbg_names, g


def _consts():
    n_axis = 8
    freqs = (10000.0 ** (-np.arange(n_axis, dtype=np.float32) / n_axis)).astype(np.float32)
    rows = NX // 64
    row = np.repeat(np.arange(rows, dtype=np.float32), 64)
    col = np.tile(np.arange(64, dtype=np.float32), rows)
    angM = np.concatenate([row[:, None] * freqs, col[:, None] * freqs], axis=-1).astype(np.float32)
    n_axis_s = 16
    freqs_s = (10000.0 ** (-np.arange(n_axis_s, dtype=np.float32) / n_axis_s)).astype(np.float32)
    angS = np.concatenate([row[:, None] * freqs_s, col[:, None] * freqs_s], axis=-1).astype(np.float32)
    cMc = np.ones((96, NX), np.float32); cMs = np.zeros((96, NX), np.float32)
    cMc[64:80] = np.cos(angM).T; cMc[80:96] = np.cos(angM).T
    cMs[64:80] = np.sin(angM).T; cMs[80:96] = np.sin(angM).T
    cSc = np.zeros((128, NX), np.float32); cSs = np.zeros((128, NX), np.float32)
    for hh in range(2):
        for half in range(2):
            r0 = hh * 64 + half * 32
            cSc[r0:r0 + 32] = np.cos(angS).T
            cSs[r0:r0 + 32] = np.sin(angS).T
    cPm = np.zeros((96, 96), np.float32)
    for i in range(16):
        cPm[80 + i, 64 + i] = -1.0
        cPm[64 + i, 80 + i] = 1.0
    cPs = np.zeros((128, 128), np.float32)
    for hh in range(2):
        for i in range(32):
            cPs[hh * 64 + 32 + i, hh * 64 + i] = -1.0
            cPs[hh * 64 + i, hh * 64 + 32 + i] = 1.0
    cShift = np.zeros((32, 96), np.float32)
    for i in range(32):
        cShift[i, 64 + i] = 1.0
    j = np.arange(128)[:, None]; i = np.arange(128)[None, :]
    cMge = (j >= i).astype(np.float32)
    cMle = (j <= i).astype(np.float32)
    cBones = np.zeros((128, 128), np.float32)
    cBones[0:64, 0:64] = 1.0; cBones[64:128, 64:128] = 1.0
    cIota = np.tile(np.arange(512, dtype=np.float32)[None, :], (128, 1))
    cId = np.eye(128, dtype=np.float32)
    return dict(cMc=cMc, cMs=cMs, cSc=cSc, cSs=cSs, cPm=cPm, cPs=cPs, cShift=cShift, cMge=cMge, cMle=cMle,
                cBones=cBones, cIota=cIota, cId=cId)


def _colv(v, nch):
    return np.ascontiguousarray(np.transpose(v.reshape(v.shape[0], nch, 128), (0, 2, 1)))


def _shared_inputs(inp):
    f = lambda a: np.ascontiguousarray(np.asarray(a, dtype=np.float32))
    sh = {}
    sh["w_ada"] = f(inp["w_ada"]); sh["badac"] = _colv(f(inp["b_ada"]), 48)
    sh["g1c"] = _colv(f(inp["norm1_g"]), 8); sh["g2c"] = _colv(f(inp["norm2_g"]), 8)
    sh["w_in"] = f(inp["w_in"])
    sh["qagc"] = _colv(f(inp["mla_qa_g"]), 3); sh["kvagc"] = _colv(f(inp["mla_kva_g"]), 2)
    sh["w_uq"] = f(inp["mla_w_uq"]); sh["w_ukv"] = f(inp["mla_w_ukv"])
    sh["mqkg"] = np.ascontiguousarray(np.transpose(f(inp["mla_qk_g"]), (0, 2, 1)))
    sw = np.transpose(f(inp["swa_qk_g"]), (0, 2, 1))
    sh["swag"] = np.ascontiguousarray(np.concatenate([sw, sw], axis=1))
    sh["sinkb"] = np.ascontiguousarray(np.broadcast_to(f(inp["swa_sink"])[:, None, :], (L, 128, 8)))

    def pairlay(a):
        return np.ascontiguousarray(np.transpose(a.reshape(L, 2, 16, 2, 64), (0, 3, 4, 1, 2)).reshape(L, 128, 2, 16))
    sh["s5lre"] = pairlay(f(inp["s5_lambda_re"])); sh["s5lim"] = pairlay(f(inp["s5_lambda_im"]))
    ls = np.broadcast_to(f(inp["s5_log_step"])[:, :, :, None], (L, 2, 32, 64))
    sh["s5ls"] = pairlay(np.ascontiguousarray(ls))

    def pairlay_b(a):
        return np.ascontiguousarray(np.transpose(a.reshape(L, 2, 16, 2, 64, 16), (0, 3, 4, 1, 2, 5)).reshape(L, 128, 2, 16, 16))
    sh["s5bre"] = pairlay_b(f(inp["s5_b_re"])); sh["s5bim"] = pairlay_b(f(inp["s5_b_im"]))
    sh["s5cre"] = pairlay_b(np.ascontiguousarray(np.transpose(f(inp["s5_c_re"]), (0, 1, 2, 4, 3))))
    sh["s5cim"] = pairlay_b(np.ascontiguousarray(np.transpose(f(inp["s5_c_im"]), (0, 1, 2, 4, 3))))
    sh["s5dc"] = _colv(f(inp["s5_d"]), 4); sh["s5bgc"] = _colv(f(inp["s5_b_glu"]), 4)
    sh["w_glu"] = f(inp["s5_w_glu"]); sh["w_br"] = f(inp["w_branch"]); sh["w_o"] = f(inp["w_out"])
    sh["f_in"] = f(inp["ffn_w_in"]); sh["f_out"] = f(inp["ffn_w_out"])
    sh.update(_consts())
    return sh


def _core_inputs(inp, b, sh):
    x = np.asarray(inp["x"][b], np.float32); ctx = np.asarray(inp["ctx"][b], np.float32)
    m = dict(sh)
    m["xT"] = np.ascontiguousarray(np.concatenate([ctx, x], axis=0).T)
    cc = np.stack([np.asarray(inp["c"][b], np.float32), np.asarray(inp["c_ctx"], np.float32)], axis=-1)
    m["ccol"] = np.ascontiguousarray(np.transpose(cc.reshape(8, 128, 2), (1, 0, 2)))
    return m


_CACHE = {}


def kernel(**inputs):
    if "nc" not in _CACHE:
        _CACHE["nc"] = build()[0]
    nc = _CACHE["nc"]
    sh = _shared_inputs(inputs)
    in_maps = [_core_inputs(inputs, b, sh) for b in range(8)]
    res = run_bass_kernel_spmd(nc, in_maps, core_ids=list(range(8)))
    out = np.stack([np.ascontiguousarray(r["outT"].T) for r in res.results], axis=0)
    return out.astype(np.float32)
```
